# Optimizing a Trainium2 kernel written in Bass

```python
import jax
import jax.numpy as jnp
from jax import lax
import numpy as np

D_MODEL = 1024
BATCH = 4
SEQ = 4096
DEPTH = 1

GRID_W = 64
CTX_LEN = 256
HEAD_DIM = 128
N_Q_HEADS = 8
N_KV_HEADS = 2
Q_PER_KV = N_Q_HEADS // N_KV_HEADS
Q_BLOCK = 128
ROPE_THETA = 10000.0
GLA_HEADS = 4
GLA_DK = D_MODEL // (2 * GLA_HEADS)
GLA_DV = D_MODEL // GLA_HEADS
GLA_GATE_RANK = 16
GLA_GATE_NORM = 16.0
GLA_CHUNK = 64
D_FF = 4 * D_MODEL
EPS = 1e-6

IN_SIZES = (
    N_KV_HEADS * HEAD_DIM,
    N_KV_HEADS * HEAD_DIM,
    GLA_HEADS * GLA_DK,
    GLA_HEADS * GLA_DV,
    GLA_GATE_RANK,
    GLA_GATE_RANK,
    N_Q_HEADS * HEAD_DIM,
    GLA_HEADS * GLA_DK,
    GLA_HEADS * GLA_DV,
    D_MODEL,
    D_MODEL,
)
N_CTX_GROUPS = 6
D_IN = int(sum(IN_SIZES))
IN_OFFSETS = tuple(int(o) for o in np.cumsum(IN_SIZES)[:-1])
CTX_COLS = int(sum(IN_SIZES[:N_CTX_GROUPS]))

kernel_name = 'hybrid_gqa_gla_dit_layer'


def rms_norm(x, g):
    xf = x.astype(jnp.float32)
    y = xf * lax.rsqrt(jnp.mean(xf * xf, axis=-1, keepdims=True) + EPS)
    return (y * g.astype(jnp.float32)).astype(x.dtype)


def modulate(h, shift, scale):
    return h * (1 + scale[..., None, :]) + shift[..., None, :]


def split_heads(a, n_heads, d):
    return a.reshape(*a.shape[:-1], n_heads, d)


def axial_rope_tables(n_rows):
    rows = jnp.repeat(jnp.arange(n_rows), GRID_W).astype(jnp.float32)
    cols = jnp.tile(jnp.arange(GRID_W), n_rows).astype(jnp.float32)
    half = HEAD_DIM // 2
    freqs = ROPE_THETA ** (-jnp.arange(0, half, 2, dtype=jnp.float32) / half)
    ang = jnp.stack([rows[:, None] * freqs, cols[:, None] * freqs], axis=1)
    return jnp.cos(ang)[None, :, None], jnp.sin(ang)[None, :, None]


def apply_rope(x, cos, sin):
    xf = x.astype(jnp.float32).reshape(*x.shape[:-1], 2, 2, HEAD_DIM // 4)
    a, b = xf[..., 0, :], xf[..., 1, :]
    out = jnp.stack([a * cos - b * sin, b * cos + a * sin], axis=-2)
    return out.reshape(x.shape).astype(x.dtype)


def gqa_attention(q, k, v):
    b, t = q.shape[:2]
    qb = jnp.moveaxis(q.reshape(b, t // Q_BLOCK, Q_BLOCK, N_KV_HEADS, Q_PER_KV, HEAD_DIM), 1, 0)

    def one_block(q_blk):
        s = jnp.einsum('bqhgd,bkhd->bhgqk', q_blk, k, preferred_element_type=jnp.float32) * (HEAD_DIM ** -0.5)
        p = jax.nn.softmax(s, axis=-1).astype(v.dtype)
        return jnp.einsum('bhgqk,bkhd->bqhgd', p, v)

    o = lax.map(one_block, qb)
    return jnp.moveaxis(o, 0, 1).reshape(b, t, N_Q_HEADS * HEAD_DIM)


def gla_heads(a, d):
    return jnp.moveaxis(split_heads(a.astype(jnp.float32), GLA_HEADS, d), 2, 1)


def gla_log_decay(lowrank, w_gk, b_gk):
    g = jax.nn.log_sigmoid((lowrank @ w_gk + b_gk).astype(jnp.float32)) / GLA_GATE_NORM
    return gla_heads(g, GLA_DK)


def flip_t(a):
    return a[:, :, ::-1]


def gla_chunked(q, k, v, g, s0):
    b_, h_, t, _ = q.shape
    n = t // GLA_CHUNK

    def to_chunks(a):
        return jnp.moveaxis(a.reshape(b_, h_, n, GLA_CHUNK, a.shape[-1]), 2, 0)

    mask = jnp.tril(jnp.ones((GLA_CHUNK, GLA_CHUNK), dtype=bool))[:, :, None]

    def step(state, inp):
        qc, kc, vc, gc = inp
        b = jnp.cumsum(gc, axis=2)
        o_inter = jnp.einsum('bhid,bhde->bhie', qc * jnp.exp(b), state)
        diff = jnp.where(mask, b[:, :, :, None, :] - b[:, :, None, :, :], -jnp.inf)
        a = jnp.einsum('bhid,bhjd,bhijd->bhij', qc, kc, jnp.exp(diff))
        o = o_inter + jnp.einsum('bhij,bhje->bhie', a, vc)
        b_last = b[:, :, -1:, :]
        state = jnp.exp(b_last[:, :, 0, :])[..., None] * state + jnp.einsum('bhjd,bhje->bhde', kc * jnp.exp(b_last - b), vc)
        return state, o

    s_fin, o = lax.scan(step, s0, (to_chunks(q), to_chunks(k), to_chunks(v), to_chunks(g)))
    o = jnp.moveaxis(o, 0, 2).reshape(b_, h_, t, v.shape[-1])
    return o, s_fin


def gla_final_state(k, v, g):
    b = jnp.cumsum(g, axis=2)
    return jnp.einsum('bhtd,bhte->bhde', k * jnp.exp(b[:, :, -1:] - b), v)


def gla_output(o, gate, g_norm):
    o = jnp.moveaxis(o, 1, 2)
    y = rms_norm(o, g_norm) * jax.nn.silu(split_heads(gate, GLA_HEADS, GLA_DV).astype(jnp.float32))
    return y.reshape(*y.shape[:2], GLA_HEADS * GLA_DV).astype(gate.dtype)


def branch_merge(attn, gla, gate_a, gate_g, w_br_attn, w_br_gla, w_out):
    y = jax.nn.sigmoid(gate_a) * (attn @ w_br_attn) + jax.nn.sigmoid(gate_g) * (gla @ w_br_gla)
    return y @ w_out


def sq_relu_mlp(h, w1, w2):
    return jnp.square(jax.nn.relu(h @ w1)) @ w2


def hybrid_layer(x, ctx, mod_x, mod_c, rope, norm1, w_in, q_norm, k_norm, w_gk_fwd, b_gk_fwd,
                 w_gk_bwd, b_gk_bwd, gla_norm, w_br_attn, w_br_gla, w_out, norm2, w_mlp1, w_mlp2,
                 update_ctx):
    cos, sin = rope
    sh1, sc1, gt1, sh2, sc2, gt2 = mod_x
    bsz = x.shape[0]
    hx = modulate(rms_norm(x, norm1), sh1, sc1)
    hc = modulate(rms_norm(ctx, norm1), mod_c[0], mod_c[1])
    ak, av, gk, gv, lrf, lrb, aq, gq, go, ga, gg = jnp.split(hx @ w_in, IN_OFFSETS, axis=-1)
    if update_ctx:
        ak_c, av_c, gk_c, gv_c, lrf_c, lrb_c, aq_c, gq_c, go_c, ga_c, gg_c = jnp.split(hc @ w_in, IN_OFFSETS, axis=-1)
    else:
        ak_c, av_c, gk_c, gv_c, lrf_c, lrb_c = jnp.split(hc @ w_in[:, :CTX_COLS], IN_OFFSETS[:N_CTX_GROUPS - 1], axis=-1)

    qx = apply_rope(rms_norm(split_heads(aq, N_Q_HEADS, HEAD_DIM), q_norm), cos, sin)
    kx = apply_rope(rms_norm(split_heads(ak, N_KV_HEADS, HEAD_DIM), k_norm), cos, sin)
    vx = split_heads(av, N_KV_HEADS, HEAD_DIM)
    kc = rms_norm(split_heads(ak_c, N_KV_HEADS, HEAD_DIM), k_norm)
    vc = split_heads(av_c, N_KV_HEADS, HEAD_DIM)
    attn_x = gqa_attention(qx, jnp.concatenate([kx, kc], axis=1), jnp.concatenate([vx, vc], axis=1))

    kgc, vgc = gla_heads(gk_c, GLA_DK), gla_heads(gv_c, GLA_DV)
    gfc = gla_log_decay(lrf_c, w_gk_fwd, b_gk_fwd)
    gbc = gla_log_decay(lrb_c, w_gk_bwd, b_gk_bwd)
    if update_ctx:
        zeros = jnp.zeros((bsz, GLA_HEADS, GLA_DK, GLA_DV), jnp.float32)
        qgc = gla_heads(gq_c, GLA_DK) * (GLA_DK ** -0.5)
        oc_f, s_f = gla_chunked(qgc, kgc, vgc, gfc, zeros)
        oc_b, s_b = gla_chunked(flip_t(qgc), flip_t(kgc), flip_t(vgc), flip_t(gbc), zeros)
    else:
        s_f = gla_final_state(kgc, vgc, gfc)
        s_b = gla_final_state(flip_t(kgc), flip_t(vgc), flip_t(gbc))
    qgx = gla_heads(gq, GLA_DK) * (GLA_DK ** -0.5)
    kgx, vgx = gla_heads(gk, GLA_DK), gla_heads(gv, GLA_DV)
    gfx = gla_log_decay(lrf, w_gk_fwd, b_gk_fwd)
    gbx = gla_log_decay(lrb, w_gk_bwd, b_gk_bwd)
    ox_f, _ = gla_chunked(qgx, kgx, vgx, gfx, s_f)
    ox_b, _ = gla_chunked(flip_t(qgx), flip_t(kgx), flip_t(vgx), flip_t(gbx), s_b)
    gla_x = gla_output(ox_f + flip_t(ox_b), go, gla_norm)

    x = x + gt1[..., None, :] * branch_merge(attn_x, gla_x, ga, gg, w_br_attn, w_br_gla, w_out)
    x = x + gt2[..., None, :] * sq_relu_mlp(modulate(rms_norm(x, norm2), sh2, sc2), w_mlp1, w_mlp2)

    if update_ctx:
        _, _, gt1c, sh2c, sc2c, gt2c = mod_c
        qc = rms_norm(split_heads(aq_c, N_Q_HEADS, HEAD_DIM), q_norm)
        attn_c = gqa_attention(qc, kc, vc)
        gla_c = gla_output(oc_f + flip_t(oc_b), go_c, gla_norm)
        ctx = ctx + gt1c[..., None, :] * branch_merge(attn_c, gla_c, ga_c, gg_c, w_br_attn, w_br_gla, w_out)
        ctx = ctx + gt2c[..., None, :] * sq_relu_mlp(modulate(rms_norm(ctx, norm2), sh2c, sc2c), w_mlp1, w_mlp2)
    return x, ctx


def setup_inputs(seed: int = 0) -> dict:
    key = jax.random.key(seed)
    ks = jax.random.split(key, 24)
    f32 = jnp.float32

    def nrm(k, shape, scale=1.0):
        return jax.random.normal(k, shape, f32) * scale

    L, D = DEPTH, D_MODEL
    return {
        'x': nrm(ks[0], (BATCH, SEQ, D)),
        'c': nrm(ks[1], (BATCH, D)),
        'ctx': nrm(ks[2], (BATCH, CTX_LEN, D)),
        'c_ctx': nrm(ks[3], (D,)),
        'w_ada': nrm(ks[4], (L, D, 6 * D), 0.5 * D ** -0.5),
        'b_ada': nrm(ks[5], (L, 6 * D), 0.02),
        'norm1': 1.0 + nrm(ks[6], (L, D), 0.02),
        'w_in': nrm(ks[7], (L, D, D_IN), D ** -0.5),
        'q_norm': 1.0 + nrm(ks[8], (L, HEAD_DIM), 0.02),
        'k_norm': 1.0 + nrm(ks[9], (L, HEAD_DIM), 0.02),
        'w_gk_fwd': nrm(ks[10], (L, GLA_GATE_RANK, GLA_HEADS * GLA_DK), GLA_GATE_RANK ** -0.5),
        'b_gk_fwd': nrm(ks[11], (L, GLA_HEADS * GLA_DK), 0.1),
        'w_gk_bwd': nrm(ks[12], (L, GLA_GATE_RANK, GLA_HEADS * GLA_DK), GLA_GATE_RANK ** -0.5),
        'b_gk_bwd': nrm(ks[13], (L, GLA_HEADS * GLA_DK), 0.1),
        'gla_norm': 1.0 + nrm(ks[14], (L, GLA_DV), 0.02),
        'w_br_attn': nrm(ks[15], (L, N_Q_HEADS * HEAD_DIM, D), (N_Q_HEADS * HEAD_DIM) ** -0.5),
        'w_br_gla': nrm(ks[16], (L, GLA_HEADS * GLA_DV, D), (GLA_HEADS * GLA_DV) ** -0.5),
        'w_out': nrm(ks[17], (L, D, D), D ** -0.5),
        'norm2': 1.0 + nrm(ks[18], (L, D), 0.02),
        'w_mlp1': nrm(ks[19], (L, D, D_FF), D ** -0.5),
        'w_mlp2': nrm(ks[20], (L, D_FF, D), D_FF ** -0.5),
    }


def reference(x, c, ctx, c_ctx, w_ada, b_ada, norm1, w_in, q_norm, k_norm, w_gk_fwd, b_gk_fwd,
              w_gk_bwd, b_gk_bwd, gla_norm, w_br_attn, w_br_gla, w_out, norm2, w_mlp1, w_mlp2):
    ROWS = x.shape[1] // GRID_W
    rope = axial_rope_tables(ROWS)
    for l in range(DEPTH):
        update_ctx = l < DEPTH - 1
        n_mod = 6 if update_ctx else 2
        mod_x = jnp.split(jax.nn.silu(c) @ w_ada[l] + b_ada[l], 6, axis=-1)
        mod_c = jnp.split(jax.nn.silu(c_ctx) @ w_ada[l][:, :n_mod * D_MODEL] + b_ada[l][:n_mod * D_MODEL], n_mod, axis=-1)
        x, ctx = hybrid_layer(x, ctx, mod_x, mod_c, rope, norm1[l], w_in[l], q_norm[l], k_norm[l],
                              w_gk_fwd[l], b_gk_fwd[l], w_gk_bwd[l], b_gk_bwd[l], gla_norm[l],
                              w_br_attn[l], w_br_gla[l], w_out[l], norm2[l], w_mlp1[l], w_mlp2[l],
                              update_ctx)
    return x
```

```python
import numpy as np
from contextlib import ExitStack
import concourse.bass as bass
import concourse.mybir as mybir
from concourse.bass_utils import run_bass_kernel_spmd

F32 = mybir.dt.float32
BF16 = mybir.dt.bfloat16
AF = mybir.ActivationFunctionType
ALU = mybir.AluOpType

D = 1024
SEQ = 4096
OWN = 2048
CTXL = 256
NKEY = SEQ + CTXL
EPS = 1e-6
C_AK, C_AKP, C_AV, C_GK, C_GV, C_LRA, C_LRB, C_GQ, C_AQ, C_AQP, C_GO, C_GA, C_GG = (
    0, 256, 512, 768, 1280, 2304, 2320, 2336, 2848, 3872, 4896, 5920, 6944)
NIN = 7968
SAME_WIN = 3


class Tok:
    __slots__ = ("name", "w", "r", "excl")

    def __init__(self, name, excl=False):
        self.name = name
        self.w = None
        self.r = []
        self.excl = excl


class Chan:
    def __init__(self, idx):
        self.idx = idx
        self.count = 0
        self.sem = None


class Op:
    __slots__ = ("fn", "deps", "signal", "chan")

    def __init__(self, fn, deps, chan):
        self.fn = fn
        self.deps = deps
        self.signal = False
        self.chan = chan


ENGS = ("pe", "act", "dve", "pool", "sp")


class Prog:
    def __init__(self):
        self.ops = {e: [] for e in ENGS}
        self.chans = []
        self.wm = {e: {} for e in ENGS}

    def chan(self):
        c = Chan(len(self.chans))
        self.chans.append(c)
        return c

    def add(self, eng, fn, reads=(), writes=(), chan=None):
        idx = len(self.ops[eng])
        deps = []
        for t in reads:
            if t.w is not None:
                deps.append(t.w)
            if t.excl:
                deps.extend(t.r)
        for t in writes:
            if t.w is not None:
                deps.append(t.w)
            deps.extend(t.r)
        need = []
        wm = self.wm[eng]
        best = {}
        for d in deps:
            if d[0] == "e":
                _, e2, i2 = d
                if e2 == eng:
                    if chan is not None:
                        pass
                    elif eng in ("pe", "sp"):
                        continue
                    elif idx - i2 > SAME_WIN:
                        continue
                key = ("e", e2)
                val = i2
            else:
                _, c, v = d
                if chan is not None and c is chan:
                    continue
                key = ("c", c.idx)
                val = v
            if wm.get(key, -1) >= val:
                continue
            if key not in best or best[key][0] < val:
                best[key] = (val, d)
        for key, (val, d) in best.items():
            wm[key] = val
            need.append(d)
            if d[0] == "e":
                self.ops[d[1]][d[2]].signal = True
        op = Op(fn, need, chan)
        self.ops[eng].append(op)
        if chan is None:
            ref = ("e", eng, idx)
        else:
            chan.count += 16
            ref = ("c", chan, chan.count)
        for t in reads:
            if t.excl:
                t.r = [ref]
            else:
                t.r.append(ref)
        for t in writes:
            t.w = ref
            t.r = []
        return op

    def barrier(self):
        refs = []
        for e in ENGS:
            if self.ops[e]:
                for i in range(len(self.ops[e]) - 1, -1, -1):
                    if self.ops[e][i].chan is None and self.ops[e][i].fn is not None:
                        refs.append(("e", e, i))
                        break
        for c in self.chans:
            if c.count:
                refs.append(("c", c, c.count))
        bt = Tok("barrier")
        for e in ENGS:
            need = []
            wm = self.wm[e]
            for d in refs:
                if d[0] == "e":
                    if d[1] == e:
                        continue
                    key = ("e", d[1]); val = d[2]
                else:
                    key = ("c", d[1].idx); val = d[2]
                if wm.get(key, -1) >= val:
                    continue
                wm[key] = val
                need.append(d)
                if d[0] == "e":
                    self.ops[d[1]][d[2]].signal = True
            self.ops[e].append(Op(None, need, None))

    def emit(self, nc, stack, final_chans):
        sems = {e: stack.enter_context(nc.semaphore("s_" + e)) for e in ENGS}
        for c in self.chans:
            c.sem = stack.enter_context(nc.semaphore("c%d" % c.idx))
        pref = {}
        for e in ENGS:
            cnt = 0
            p = []
            for op in self.ops[e]:
                if op.signal:
                    cnt += 1
                p.append(cnt)
            pref[e] = p
        block = stack.enter_context(nc.Block())
        handles = {"pe": block.tensor, "act": block.scalar, "dve": block.vector,
                   "pool": block.gpsimd, "sp": block.sync}

        def make(e):
            def body(eng):
                for op in self.ops[e]:
                    for d in op.deps:
                        if d[0] == "e":
                            eng.wait_ge(sems[d[1]], pref[d[1]][d[2]])
                        else:
                            eng.wait_ge(d[1].sem, d[2])
                    if op.fn is None:
                        continue
                    ins = op.fn(eng)
                    if op.signal:
                        ins.then_inc(sems[e], 1)
                    if op.chan is not None:
                        ins.then_inc(op.chan.sem, 16)
                if e == "sp":
                    for c in final_chans:
                        if c.count:
                            eng.wait_ge(c.sem, c.count)
            return body

        for e in ENGS:
            handles[e](make(e))


class StopBuild(Exception):
    pass


class Builder:
    def __init__(self, dbg=()):
        self.dbg = set(dbg)
        self.nc = bass.Bass("TRN2", target_bir_lowering=False)
        self.P = Prog()
        self.stack = ExitStack()
        self.dram = {}
        self.dbg_out = []

    def din(self, name, shape, dt=F32):
        ap = self.nc.dram_tensor(name, list(shape), dt, kind="ExternalInput").ap()
        self.dram[name] = ap
        return ap

    def init_arena(self):
        nc = self.nc
        self.ARENA_BYTES = 207 * 1024
        self.arena = self.stack.enter_context(
            nc.sbuf_tensor("arena", [128, self.ARENA_BYTES // 2], BF16))
        self.regions = {}
        self.def_region("P", 0, self.ARENA_BYTES)
        self.cur_region = "P"
        self.psum = [self.stack.enter_context(nc.psum_tensor("ps%d" % i, [128, 512], F32))[:]
                     for i in range(8)]
        self.pstok = [Tok("ps%d" % i, excl=True) for i in range(8)]
        self.rot = list(range(8))
        self.rot_i = 0

    def alloc(self, shape, dt, name="", parts=128, region=None):
        esz = 4 if dt == F32 else 2
        n = int(np.prod(shape))
        nbytes = (n * esz + 63) // 64 * 64
        if region is None:
            region = self.cur_region
        r = self.regions[region]
        off = r[1]
        r[1] += nbytes
        assert r[1] <= r[2], ("SBUF overflow", region, name, r[1] - r[2])
        v = self.arena[0:parts, off // 2: off // 2 + n * esz // 2]
        if dt == F32:
            v = v.bitcast(F32)
        if len(shape) == 2:
            v = v.rearrange("p (a b) -> p a b", a=shape[0])
        elif len(shape) == 3:
            v = v.rearrange("p (a b c) -> p a b c", a=shape[0], b=shape[1])
        elif len(shape) == 4:
            v = v.rearrange("p (a b c d) -> p a b c d", a=shape[0], b=shape[1], c=shape[2])
        return v

    def def_region(self, name, start, end):
        self.regions[name] = [start, start, end]

    def reset_region(self, *names):
        for n in names:
            self.regions[n][1] = self.regions[n][0]

    def use(self, name):
        self.cur_region = name

    def set_rot(self, banks):
        self.rot = list(banks)
        self.rot_i = 0

    def bank(self):
        b = self.rot[self.rot_i % len(self.rot)]
        self.rot_i += 1
        return b

    def mm(self, out, lhsT, rhs, start, stop, reads, writes, tile_position=None):
        if tile_position is not None:
            return self.P.add("pe", lambda e: e.matmul(out, lhsT, rhs, start=start, stop=stop,
                                                       tile_position=tile_position), reads, writes)
        return self.P.add("pe", lambda e: e.matmul(out, lhsT, rhs, start=start, stop=stop),
                          reads, writes)

    def tr(self, out, in_, ident, reads, writes):
        return self.P.add("pe", lambda e: e.transpose(out, in_, ident), reads, writes)

    def act(self, out, in_, func, reads, writes, bias=None, scale=None, accum=None):
        kw = {}
        if bias is not None:
            kw["bias"] = bias
        if scale is not None:
            kw["scale"] = scale
        if accum is not None:
            kw["accum_out"] = accum
        return self.P.add("act", lambda e: e.activation(out, in_, func, **kw), reads, writes)

    def tt(self, eng, out, in0, in1, op, reads, writes):
        return self.P.add(eng, lambda e: e.tensor_tensor(out, in0, in1, op), reads, writes)

    def ts(self, eng, out, in0, s1, s2, op0, op1, reads, writes):
        if op1 is None:
            return self.P.add(eng, lambda e: e.tensor_scalar(out, in0, s1, None, op0),
                              reads, writes)
        return self.P.add(eng, lambda e: e.tensor_scalar(out, in0, s1, s2, op0, op1),
                          reads, writes)

    def stt(self, eng, out, in0, scalar, in1, op0, op1, reads, writes):
        return self.P.add(eng, lambda e: e.scalar_tensor_tensor(out, in0, scalar, in1, op0, op1),
                          reads, writes)

    def cp(self, eng, out, in_, reads, writes):
        if eng == "act":
            return self.P.add("act", lambda e: e.copy(out, in_), reads, writes)
        return self.P.add(eng, lambda e: e.tensor_copy(out, in_), reads, writes)

    def recip(self, out, in_, reads, writes):
        return self.P.add("dve", lambda e: e.reciprocal(out, in_), reads, writes)

    def memset(self, eng, ap, val, writes):
        return self.P.add(eng, lambda e: e.memset(ap, val), (), writes)

    def dma(self, q, out, in_, chan, reads, writes, slow=False):
        if slow:
            return self.P.add(q, lambda e: e.dma_start(out=out, in_=in_,
                                                       allow_slow_non_contiguous=True),
                              reads, writes, chan=chan)
        return self.P.add(q, lambda e: e.dma_start(out=out, in_=in_), reads, writes, chan=chan)

    def dump(self, name, ap, shape, dt, tok):
        if name not in self.dbg:
            return
        o = self.nc.dram_tensor("dbg_" + name, list(shape), dt, kind="ExternalOutput").ap()
        c = self.P.chan()
        self.final_chans.append(c)
        self.dma("sp", o, ap, c, [tok] if not isinstance(tok, (list, tuple)) else list(tok), [])
        self.dbg_out.append("dbg_" + name)

    def build(self):
        try:
            return self._build_body()
        except StopBuild:
            return self.finish_stub(None, None)

    def chk(self, label):
        if ("stop_" + label) in self.dbg:
            raise StopBuild()

    def _build_body(self):
        nc = self.nc
        P = self.P
        din = self.din
        self.final_chans = []
        xs = din("xs", [SEQ, D])
        ctx = din("ctx", [CTXL, D])
        cvec = din("cvec", [2, D])
        w_ada = din("w_ada", [D, 6 * D])
        b_ada = din("b_ada", [6 * D])
        norm1 = din("norm1", [D])
        norm2 = din("norm2", [D])
        w_in = din("w_in", [D, NIN])
        wgk = din("wgk", [2, 17, 512])
        qkg = din("qkg", [128, 4])
        gla_norm = din("gla_norm", [256])
        w_bra = din("w_br_attn", [D, D])
        w_brg = din("w_br_gla", [D, D])
        w_out = din("w_out", [D, D])
        w_m1 = din("w_mlp1", [D, 4 * D])
        w_m2 = din("w_mlp2", [4 * D, D])
        ident_d = din("ident", [128, 128])
        tri_d = din("tri", [128, 4, 128])
        msk_d = din("msk", [128, 2, 128])
        ropec_d = din("ropec", [128, 2, 72])
        out_d = nc.dram_tensor("out", [OWN, D], F32, kind="ExternalOutput").ap()
        self.out_d = out_d

        self.init_arena()
        A = self.alloc
        ps = self.psum
        pt = self.pstok

        ident = A([128], F32, "ident"); t_ident = Tok("ident")
        onesf = A([128], F32, "onesf"); t_onesf = Tok("onesf")
        onesb = A([128], BF16, "onesb"); t_onesb = Tok("onesb")
        tri = A([4, 128], F32, "tri"); t_tri = Tok("tri")
        msk = A([2, 128], F32, "msk"); t_msk = Tok("msk")
        ropec = A([2, 72], F32, "ropec"); t_ropec = Tok("ropec")
        qkgs = A([4], F32, "qkg"); t_qkg = Tok("qkg")
        GT = A([2, 2, 72], F32, "GT"); t_GT = Tok("GT")
        scT = A([8, 2], BF16, "scT"); t_scT = Tok("scT")
        modT = A([48, 2], F32, "modT"); t_mod = Tok("mod")
        a1 = A([8], F32, "a1"); ac = A([8], F32, "ac"); a2 = A([8], F32, "a2"); t_av = Tok("avec"); t_av2 = Tok("avec2")
        gt1bc = A([1024], F32, "gt1bc"); t_gt1 = Tok("gt1bc")
        gt2bc = A([1024], F32, "gt2bc"); t_gt2 = Tok("gt2bc")
        glan = A([256], F32, "glan"); t_glan = Tok("glan")
        wgka = A([2, 512], BF16, "wgka", parts=32); t_wgka = Tok("wgka")
        nhalf = A([1], F32, "nhalf"); t_nhalf = Tok("nhalf")
        tiny = A([16], F32, "tiny"); t_tiny = Tok("tiny")

        for (dst, src, tk) in ((ident, ident_d, t_ident), (tri, tri_d, t_tri), (msk, msk_d, t_msk),
                               (ropec, ropec_d, t_ropec), (qkgs, qkg, t_qkg)):
            self.dma("sp", dst, src, P.chan(), [], [tk])
        vst = A([128], F32, "vst", parts=128); t_vst = Tok("vst")
        vT = A([80], F32, "vT"); t_vT = Tok("vT")
        self.memset("dve", vst, 0.0, [t_vst])
        c_v = P.chan()
        self.dma("sp", vst[0:48], b_ada.rearrange("(k p) -> k p", p=128), c_v, [], [t_vst])
        self.dma("sp", vst[48:56], norm1.rearrange("(k p) -> k p", p=128), c_v, [], [t_vst])
        self.dma("sp", vst[56:64], norm2.rearrange("(k p) -> k p", p=128), c_v, [], [t_vst])
        self.dma("sp", vst[64:80], cvec.rearrange("r (k p) -> (r k) p", p=128), c_v, [], [t_vst])
        self.dma("sp", glan, gla_norm.partition_broadcast(128), P.chan(), [], [t_glan])
        c_wgk = P.chan()
        self.dma("pool", wgka[0:17], wgk.rearrange("x r n -> r x n"), c_wgk, [], [t_wgka])
        self.memset("dve", onesf, 1.0, [t_onesf])
        self.memset("dve", onesb, 1.0, [t_onesb])
        self.memset("dve", nhalf, -0.5, [t_nhalf])
        self.tr(ps[0][:, 0:128], vst, ident, [t_vst, t_ident], [pt[0]])
        self.cp("dve", vT, ps[0][:, 0:80], [pt[0]], [t_vT])
        badaT = vT[:, 0:48]; n1T = vT[:, 48:56]; n2T = vT[:, 56:64]
        cT = vT[:, 64:80].rearrange("p (r k) -> p k r", r=2)
        t_bada = t_vT; t_nT = t_vT; t_cT = t_vT

        hxT = A([4, 8, 512], BF16, "hxT_own")
        t_hx = [Tok("hxT%d" % j) for j in range(4)]
        t_ag = [Tok("AG%d" % j) for j in range(4)]

        SA = A([4, 256], F32, "SA"); SB = A([4, 256], F32, "SB")
        SAb = A([4, 256], BF16, "SAb"); SBb = A([4, 256], BF16, "SBb")
        t_S = {"A": Tok("SA"), "B": Tok("SB")}
        t_Sb = {"A": Tok("SAb"), "B": Tok("SBb")}
        Sf = {"A": SA, "B": SB}
        Sb = {"A": SAb, "B": SBb}
        pend = self.regions["P"][1]
        s_start = pend - 12288
        self.def_region("S", s_start, pend)
        self.def_region("R1", pend, pend + 65536)
        self.def_region("R2", pend + 65536, pend + 65536 + 34816)
        self.def_region("R3", pend + 65536 + 34816, self.ARENA_BYTES)
        self.use("R2")
        KT = A([2, NKEY], BF16, "KT")
        V = A([34, 256], BF16, "V")
        t_kv = [Tok("kv%d" % g) for g in range(9)]

        self.use("R1")
        ws1 = A([8, 2336], BF16, "ws1"); t_ws1 = Tok("ws1"); c_ws1 = P.chan()
        xt = [A([1024], F32, "xt%d" % i) for i in range(2)]
        t_xt = [Tok("xt%d" % i) for i in range(2)]
        c_xt = [P.chan() for _ in range(2)]
        wada = [A([8, 512], BF16, "wada%d" % i) for i in range(2)]
        t_wada = [Tok("wada%d" % i) for i in range(2)]
        c_wada = [P.chan() for _ in range(2)]
        Tkc = A([2, 256], F32, "Tkc"); t_Tkc = Tok("Tkc")
        self.use("R3")
        xn = A([1024], F32, "xn"); t_xn = Tok("xn")
        hxg = A([8, 512], BF16, "hxg"); t_hxg = Tok("hxg")
        gkt = A([4, 512], BF16, "gkt"); t_gkt = Tok("gkt")
        gvt = A([4, 1024], BF16, "gvt"); t_gvt = Tok("gvt")
        lrT = {"A": A([512], BF16, "lrTA", parts=32), "B": A([512], BF16, "lrTB", parts=32)}
        t_lrT = {"A": Tok("lrTA"), "B": Tok("lrTB")}
        Tk = A([2, 512], F32, "Tk"); t_Tk = Tok("Tk")
        r_sq = A([512], BF16, "r_sq"); t_rsq = Tok("r_sq")
        r_rs = A([512], F32, "r_rs"); t_rrs = Tok("r_rs")
        r_t1 = A([512], F32, "r_t1"); t_rt1 = Tok("r_t1")
        r_t2 = A([512], F32, "r_t2"); t_rt2 = Tok("r_t2")
        def make_gset(i, full):
            G = {"sp": (A([512], F32, "g_sp%d" % i), Tok("g_sp")),
                 "EC": (A([512], F32, "g_EC%d" % i), Tok("g_EC")),
                 "kh": (A([512], BF16, "g_kh%d" % i), Tok("g_kh")),
                 "EL": (A([4], F32, "g_EL%d" % i), Tok("g_EL"))}
            if full:
                G["E1"] = (A([512], F32, "g_E1%d" % i), Tok("g_E1"))
                G["E2"] = (A([512], F32, "g_E2%d" % i), Tok("g_E2"))
                G["qt"] = (A([4, 128], BF16, "g_qt%d" % i), Tok("g_qt"))
                G["kt"] = (A([4, 128], BF16, "g_kt%d" % i), Tok("g_kt"))
                G["AT"] = (A([4, 128], BF16, "g_AT%d" % i), Tok("g_AT"))
            return G
        gsets = [make_gset(0, False),
                 {"sp": (r_rs, t_rrs), "EC": (r_t1, t_rt1), "kh": (r_sq, t_rsq),
                  "EL": (A([4], F32, "g_EL1"), Tok("g_EL1"))}]
        self.gcnt = 0
        sqj = r_t2.bitcast(BF16)
        t_sqj = t_rt2
        diag = A([128], F32, "diag"); t_diag = Tok("diag")

        for X in ("A", "B"):
            self.memset("dve", lrT[X], 1.0, [t_lrT[X]])
            self.memset("dve", Sf[X], 0.0, [t_S[X]])
            self.memset("dve", Sb[X], 0.0, [t_Sb[X]])

        w_in_v = w_in.rearrange("(k p) n -> p k n", p=128)
        for kc in range(8):
            self.dma("pool", ws1[:, kc, :], w_in_v[:, kc, 0:2336], c_ws1, [], [t_ws1])

        self.set_rot([0, 1, 2, 3, 4, 5, 6, 7])
        ec = tiny
        ecv = tiny.rearrange("p (a b) -> p a b", a=8)
        self.act(ecv, cT, AF.Exp, [t_cT], [t_tiny], scale=-1.0)
        self.ts("dve", ecv, ecv, 1.0, None, ALU.add, None, [t_tiny], [t_tiny])
        self.recip(ecv, ecv, [t_tiny], [t_tiny])
        self.tt("dve", scT, ecv, cT, ALU.mult, [t_tiny, t_cT], [t_scT])

        w_ada_v = w_ada.rearrange("(k p) n -> p k n", p=128)
        self.set_rot([0, 1, 2, 3, 4, 5, 6])
        bmod = 7
        modps = ps[bmod][:, 0:96].rearrange("p (a b) -> p a b", a=48)

        def ada_dma(cb):
            for kc in range(8):
                self.dma("pool", wada[cb % 2][:, kc, :], w_ada_v[:, kc, cb * 512:(cb + 1) * 512],
                         c_wada[cb % 2], [], [t_wada[cb % 2]])

        def ada_mm(cb):
            wb = wada[cb % 2]
            for fc in range(4):
                j = cb * 4 + fc
                for kc in range(8):
                    self.mm(modps[:, j, :], wb[:, kc, fc * 128:(fc + 1) * 128], scT[:, kc, :],
                            kc == 0, kc == 7, [t_wada[cb % 2], t_scT], [pt[bmod]])

        ada_dma(0)
        ada_dma(1)
        for cb in range(4):
            ada_mm(cb)
            ada_dma(cb + 2)
        self.tt("dve", modT[:, 0:16, :], modps[:, 0:16, :],
                badaT[:, 0:16].unsqueeze(2).broadcast_to([128, 16, 2]), ALU.add,
                [pt[bmod], t_bada], [t_mod])
        self.ts("dve", a1, modT[:, 8:16, 0], 1.0, 32.0, ALU.add, ALU.mult, [t_mod], [t_av])
        self.tt("dve", a1, a1, n1T, ALU.mult, [t_av, t_nT], [t_av])
        self.ts("dve", ac, modT[:, 8:16, 1], 1.0, 32.0, ALU.add, ALU.mult, [t_mod], [t_av])
        self.tt("dve", ac, ac, n1T, ALU.mult, [t_av, t_nT], [t_av])

        def ada_part2():
            for cb in range(4, 12):
                ada_mm(cb)
                if cb + 2 < 12:
                    ada_dma(cb + 2)
            self.tt("dve", modT[:, 16:48, :], modps[:, 16:48, :],
                    badaT[:, 16:48].unsqueeze(2).broadcast_to([128, 32, 2]), ALU.add,
                    [pt[bmod], t_bada], [t_mod])
            self.ts("dve", a2, modT[:, 32:40, 0], 1.0, 32.0, ALU.add, ALU.mult, [t_mod], [t_av2])
            self.tt("dve", a2, a2, n2T, ALU.mult, [t_av2, t_nT], [t_av2])
            for (dst, tk, j0) in ((gt1bc, t_gt1, 16), (gt2bc, t_gt2, 40)):
                for half in range(2):
                    b = self.bank()
                    for q in range(4):
                        kc = half * 4 + q
                        self.ts("dve", diag, ident, modT[:, j0 + kc, 0:1], None, ALU.mult, None,
                                [t_ident, t_mod], [t_diag])
                        self.mm(ps[b][:, q * 128:(q + 1) * 128], onesf, diag, True, True,
                                [t_onesf, t_diag], [pt[b]])
                    self.cp("dve", dst[:, half * 512:(half + 1) * 512], ps[b], [pt[b]], [tk])
        SQ128 = float(np.sqrt(128.0))
        self.ts("dve", GT[:, 0, 0, :], ropec[:, 0, :], qkgs[:, 0:1], None, ALU.mult, None,
                [t_ropec, t_qkg], [t_GT])
        self.ts("dve", GT[:, 0, 1, :], ropec[:, 1, :], qkgs[:, 1:2], None, ALU.mult, None,
                [t_ropec, t_qkg], [t_GT])
        self.ts("dve", GT[:, 1, 0, :], ropec[:, 0, :], qkgs[:, 2:3], SQ128, ALU.mult, ALU.mult,
                [t_ropec, t_qkg], [t_GT])
        self.ts("dve", GT[:, 1, 1, :], ropec[:, 1, :], qkgs[:, 3:4], SQ128, ALU.mult, ALU.mult,
                [t_ropec, t_qkg], [t_GT])
        Tk4 = Tk.rearrange("p c (r w) -> p c r w", r=8)
        self.cp("dve", Tk4[64:128], GT[64:128, 1, :, 0:64].unsqueeze(2).broadcast_to([64, 2, 8, 64]),
                [t_GT], [t_Tk])
        Tkc4 = Tkc.rearrange("p c (r w) -> p c r w", r=4)
        self.cp("dve", Tkc4, GT[:, 1, :, 64:68].unsqueeze(3).broadcast_to([128, 2, 4, 64]),
                [t_GT], [t_Tkc])

        if "stop_setup" in self.dbg:
            return self.finish_stub(t_mod, modT)
        def norm_transpose(src_ap, xt_i, avec, shcol, dst_fn, dst_toks, from_sbuf_tok=None):
            xin = src_ap
            rt = [t_xt[xt_i]] if from_sbuf_tok is None else [from_sbuf_tok]
            ssc = tiny[:, 0:1]
            self.act(sqj, xin, AF.Square, rt, [t_sqj, t_tiny], accum=ssc)
            self.ts("pool", tiny[:, 1:2], ssc, float(D * EPS), None, ALU.add, None, [t_tiny], [t_tiny])
            self.tt("pool", tiny[:, 2:3], tiny[:, 1:2], nhalf, ALU.pow, [t_tiny, t_nhalf], [t_tiny])
            self.act(xn, xin, AF.Copy, rt + [t_tiny], [t_xn], scale=tiny[:, 2:3])
            for half in range(2):
                b = self.bank()
                for q in range(4):
                    kc = half * 4 + q
                    self.tr(ps[b][:, q * 128:(q + 1) * 128], xn[:, kc * 128:(kc + 1) * 128], ident,
                            [t_xn, t_ident], [pt[b]])
                for q in range(4):
                    kc = half * 4 + q
                    dst = dst_fn(kc)
                    if half == 0:
                        self.act(dst, ps[b][:, q * 128:(q + 1) * 128], AF.Identity,
                                 [pt[b], t_av, t_av2, t_mod], dst_toks,
                                 scale=avec[:, kc:kc + 1], bias=shcol(kc))
                    else:
                        self.ts("dve", dst, ps[b][:, q * 128:(q + 1) * 128], avec[:, kc:kc + 1],
                                shcol(kc), ALU.mult, ALU.add, [pt[b], t_av, t_av2, t_mod], dst_toks)

        def rope_norm(b0, b1, n, Tc, Ts, t_tab, dst, dst_toks, eps_scaled):
            self.act(r_sq[:, 0:n], ps[b0][:, 0:n], AF.Square, [pt[b0]], [t_rsq])
            bs = self.bank()
            self.mm(ps[bs][:, 0:n], onesb, r_sq[:, 0:n], True, True, [t_onesb, t_rsq], [pt[bs]])
            self.chk("r1")
            import os
            EXP = os.environ.get("EXP", "")
            if EXP == "copyfirst":
                self.cp("dve", r_t1[:, 0:n], ps[b0][:, 0:n], [pt[b0]], [t_rt1])
            elif EXP == "serial":
                self.cp("dve", r_t1[:, 0:n], ps[b0][:, 0:n], [pt[b0], t_rsq], [t_rt1])
                self.chk("r1b")
                self.chk("r1b")
                self.tt("dve", r_t1[:, 0:n], r_t1[:, 0:n], Tc, ALU.mult, [t_rt1, t_tab], [t_rt1])
                self.chk("r1c")
            else:
                self.tt("dve", r_t1[:, 0:n], ps[b0][:, 0:n], Tc, ALU.mult, [pt[b0], t_tab], [t_rt1])
            self.chk("r1a")
            self.tt("dve", r_t2[:, 0:n], ps[b1][:, 0:n], Ts, ALU.mult, [pt[b1], t_tab], [t_rt2])
            self.chk("r2")
            self.act(r_rs[:, 0:n], ps[bs][:, 0:n], AF.Ln, [pt[bs]], [t_rrs], bias=eps_scaled)
            self.chk("r3")
            self.act(r_rs[:, 0:n], r_rs[:, 0:n], AF.Exp, [t_rrs], [t_rrs], scale=-0.5)
            self.chk("r4")
            self.tt("dve", r_t1[:, 0:n], r_t1[:, 0:n], r_t2[:, 0:n], ALU.add, [t_rt1, t_rt2], [t_rt1])
            self.tt("dve", dst, r_t1[:, 0:n], r_rs[:, 0:n], ALU.mult, [t_rt1, t_rrs], dst_toks)

        def proj_fm(w, t_w, c0, ncols_chunk, rhs_fn, n, rhs_toks):
            b = self.bank()
            for kc in range(8):
                self.mm(ps[b][0:ncols_chunk, 0:n], w[:, kc, c0:c0 + ncols_chunk], rhs_fn(kc),
                        kc == 0, kc == 7, [t_w] + rhs_toks, [pt[b]])
            return b

        def gla_stage1(X, full, lr_ap, gk_tile, gv_tile, t_in, qT=None, kT=None, o_dst=None,
                       o_add=False, o_tok=None):
            G = gsets[self.gcnt % len(gsets)]
            self.gcnt += 1
            g_sp, t_gsp = G["sp"]; g_EC, t_gEC = G["EC"]; g_kh, t_gkh = G["kh"]; g_EL, t_gEL = G["EL"]
            xi = 0 if X == "A" else 1
            cum = tri[:, 2 * xi, :]
            cmat = tri[:, 2 * xi + 1, :]
            last = 127 if X == "A" else 0
            bz = self.bank()
            self.mm(ps[bz], lr_ap, wgka[0:17, xi, :], True, True, t_in + [t_wgka], [pt[bz]])
            self.act(g_sp, ps[bz], AF.Exp, [pt[bz]], [t_gsp], scale=-1.0)
            self.act(g_sp, g_sp, AF.Ln, [t_gsp], [t_gsp], bias=1.0)
            bc = self.bank()
            self.mm(ps[bc], cmat, g_sp, True, True, [t_tri, t_gsp], [pt[bc]])
            if full:
                bb = self.bank()
                for h in range(4):
                    self.mm(ps[bb][:, h * 128:(h + 1) * 128], g_sp[:, h * 128:(h + 1) * 128], cum,
                            True, True, [t_gsp, t_tri], [pt[bb]])
            else:
                bl = self.bank()
                for h in range(4):
                    self.mm(ps[bl][:, h:h + 1], g_sp[:, h * 128:(h + 1) * 128],
                            cum[:, last:last + 1], True, True, [t_gsp, t_tri], [pt[bl]])
            self.act(g_EC, ps[bc], AF.Exp, [pt[bc]], [t_gEC])
            self.tt("dve", g_kh, gk_tile, g_EC, ALU.mult, t_in + [t_gEC], [t_gkh])
            st = dict(X=X, full=full, G=G, gv_tile=gv_tile, t_in=t_in, o_dst=o_dst, o_add=o_add,
                      o_tok=o_tok, xi=xi)
            if full:
                g_E1, t_gE1 = G["E1"]; g_E2, t_gE2 = G["E2"]; g_qt, t_gqtl = G["qt"]
                g_kt, t_gktl = G["kt"]
                self.act(g_E1, ps[bb], AF.Exp, [pt[bb]], [t_gE1])
                self.act(g_E2, ps[bb], AF.Exp, [pt[bb]], [t_gE2], scale=-1.0)
                E1v = g_E1.rearrange("p (h t) -> p h t", h=4)
                E2v = g_E2.rearrange("p (h t) -> p h t", h=4)
                self.tt("dve", g_qt, qT, E1v, ALU.mult, t_in + [t_gE1], [t_gqtl])
                self.tt("dve", g_kt, kT, E2v, ALU.mult, t_in + [t_gE2], [t_gktl])
                st["el"] = lambda h: g_E1[:, h * 128 + last: h * 128 + last + 1]
                st["el_tok"] = t_gE1
            else:
                self.act(g_EL, ps[bl][:, 0:4], AF.Exp, [pt[bl]], [t_gEL])
                st["el"] = lambda h: g_EL[:, h:h + 1]
                st["el_tok"] = t_gEL
            return st

        def gla_stage2(st):
            X = st["X"]; G = st["G"]; gv_tile = st["gv_tile"]; t_in = st["t_in"]; xi = st["xi"]
            g_kh, t_gkh = G["kh"]
            if st["full"]:
                g_qt, t_gqtl = G["qt"]; g_kt, t_gktl = G["kt"]; g_AT, t_gAT = G["AT"]
                ba = self.bank()
                for h in range(4):
                    self.mm(ps[ba][:, h * 128:(h + 1) * 128], g_kt[:, h, :], g_qt[:, h, :],
                            True, True, [t_gktl, t_gqtl], [pt[ba]])
                self.tt("dve", g_AT, ps[ba].rearrange("p (h t) -> p h t", h=4),
                        msk[:, xi, :].unsqueeze(1).broadcast_to([128, 4, 128]), ALU.mult,
                        [pt[ba], t_msk], [t_gAT])
                for hp in range(2):
                    bo = self.bank()
                    for hh in range(2):
                        h = hp * 2 + hh
                        self.mm(ps[bo][:, hh * 256:(hh + 1) * 256], g_qt[:, h, :], Sb[X][:, h, :],
                                True, False, [t_gqtl, t_Sb[X]], [pt[bo]])
                        self.mm(ps[bo][:, hh * 256:(hh + 1) * 256], g_AT[:, h, :],
                                gv_tile[:, h * 256:(h + 1) * 256], False, True,
                                [t_gAT] + t_in, [pt[bo]])
                    od = st["o_dst"][:, hp * 512:(hp + 1) * 512]
                    if st["o_add"]:
                        self.tt("dve", od, ps[bo], od, ALU.add, [pt[bo], st["o_tok"]], [st["o_tok"]])
                    else:
                        self.cp("act", od, ps[bo], [pt[bo]], [st["o_tok"]])
            el = st["el"]; el_tok = st["el_tok"]
            for hp in range(2):
                bu = self.bank()
                for hh in range(2):
                    h = hp * 2 + hh
                    self.mm(ps[bu][:, hh * 256:(hh + 1) * 256], g_kh[:, h * 128:(h + 1) * 128],
                            gv_tile[:, h * 256:(h + 1) * 256], True, True, [t_gkh] + t_in, [pt[bu]])
                for hh in range(2):
                    h = hp * 2 + hh
                    self.stt("dve", Sf[X][:, h, :], Sf[X][:, h, :], el(h),
                             ps[bu][:, hh * 256:(hh + 1) * 256], ALU.mult, ALU.add,
                             [t_S[X], el_tok, pt[bu]], [t_S[X]])
            self.cp("act", Sb[X], Sf[X], [t_S[X]], [t_Sb[X]])

        def gla_run(step_args):
            prev = None
            for a in step_args:
                cur = gla_stage1(*a[0], **a[1])
                if prev is not None:
                    gla_stage2(prev)
                prev = cur
            if prev is not None:
                gla_stage2(prev)

        wf0 = wada[0].rearrange("p a b -> p (a b)")
        wf1 = wada[1].rearrange("p a b -> p (a b)")
        bsets = [
            dict(gkt=gkt, t_gkt=t_gkt, gvt=gvt, t_gvt=t_gvt, lrT=lrT, t_lrT=t_lrT),
            dict(gkt=wf0[:, 0:2048].rearrange("p (t n) -> p t n", t=4), t_gkt=t_wada[0],
                 gvt=wf1.rearrange("p (t n) -> p t n", t=4), t_gvt=t_wada[1],
                 lrT={"A": wf0[0:32, 2048:2560], "B": wf0[0:32, 2560:3072]},
                 t_lrT={"A": t_wada[0], "B": t_wada[0]}),
        ]
        self.xcount = 0

        def front(kind, g, bs):
            ntile = 2 if kind == "ctx" else 4
            n = ntile * 128
            keyoff = SEQ if kind == "ctx" else g * 512
            avec = ac if kind == "ctx" else a1
            rcol = 1 if kind == "ctx" else 0
            shcol = lambda kc, rcol=rcol: modT[:, kc, rcol:rcol + 1]
            src = ctx if kind == "ctx" else xs
            for t in range(ntile):
                xi = self.xcount % 2
                self.xcount += 1
                r0 = t * 128 if kind == "ctx" else g * 512 + t * 128
                self.dma("sp", xt[xi], src[r0:r0 + 128, :], c_xt[xi], [], [t_xt[xi]])
                if kind == "own":
                    dst_fn = lambda kc, t=t, g=g: hxT[:, g, kc, t * 128:(t + 1) * 128]
                    dtoks = [t_hx[g]]
                else:
                    dst_fn = lambda kc, t=t: hxg[:, kc, t * 128:(t + 1) * 128]
                    dtoks = [t_hxg]
                norm_transpose(xt[xi], xi, avec, shcol, dst_fn, dtoks)
            if kind == "own":
                rhs_fn = lambda kc, g=g: hxT[:, g, kc, :]
                rtoks = [t_hx[g]]
                lhs_fn = lambda kc, t, g=g: hxT[:, g, kc, t * 128:(t + 1) * 128]
            else:
                rhs_fn = lambda kc, n=n: hxg[:, kc, 0:n]
                rtoks = [t_hxg]
                lhs_fn = lambda kc, t: hxg[:, kc, t * 128:(t + 1) * 128]
            if kind == "ctx":
                Tc, Ts, t_tab = Tkc[:, 0, :], Tkc[:, 1, :], t_Tkc
            else:
                self.cp("dve", Tk4[0:64],
                        GT[0:64, 1, :, g * 8:(g + 1) * 8].unsqueeze(3).broadcast_to([64, 2, 8, 64]),
                        [t_GT], [t_Tk])
                Tc, Ts, t_tab = Tk[:, 0, :], Tk[:, 1, :], t_Tk
            gi = 8 if kind == "ctx" else g
            for kvh in range(2):
                b0 = proj_fm(ws1, t_ws1, C_AK + kvh * 128, 128, rhs_fn, n, rtoks)
                b1 = proj_fm(ws1, t_ws1, C_AKP + kvh * 128, 128, rhs_fn, n, rtoks)
                rope_norm(b0, b1, n, Tc, Ts, t_tab, KT[:, kvh, keyoff:keyoff + n], [t_kv[gi]],
                          float(128 * EPS))
            for t in range(ntile):
                b = self.bank()
                for kc in range(8):
                    self.mm(ps[b][:, 0:256], lhs_fn(kc, t), ws1[:, kc, C_AV:C_AV + 256],
                            kc == 0, kc == 7, rtoks + [t_ws1], [pt[b]])
                self.cp("act", V[:, keyoff // 128 + t, :], ps[b][:, 0:256], [pt[b]], [t_kv[gi]])
            if kind == "own":
                return
            for t in range(ntile):
                b = self.bank()
                for kc in range(8):
                    self.mm(ps[b], lhs_fn(kc, t), ws1[:, kc, C_GK:C_GK + 512],
                            kc == 0, kc == 7, rtoks + [t_ws1], [pt[b]])
                self.cp("dve", bs["gkt"][:, t, :], ps[b], [pt[b]], [bs["t_gkt"]])
                for hf in range(2):
                    b = self.bank()
                    for kc in range(8):
                        self.mm(ps[b], lhs_fn(kc, t),
                                ws1[:, kc, C_GV + hf * 512:C_GV + (hf + 1) * 512],
                                kc == 0, kc == 7, rtoks + [t_ws1], [pt[b]])
                    self.cp("act", bs["gvt"][:, t, hf * 512:(hf + 1) * 512], ps[b], [pt[b]],
                            [bs["t_gvt"]])
            dirs = ("A", "B") if kind == "ctx" else ("B",)
            for X in dirs:
                c0 = C_LRA if X == "A" else C_LRB
                b = proj_fm(ws1, t_ws1, c0, 16, rhs_fn, n, rtoks)
                self.cp("dve", bs["lrT"][X][0:16, 0:n], ps[b][0:16, 0:n], [pt[b]], [bs["t_lrT"][X]])

        def steps(kind, g, bs):
            ntile = 2 if kind == "ctx" else 4
            dirs = ("A", "B") if kind == "ctx" else ("B",)
            args = []
            for X in dirs:
                order = range(ntile) if X == "A" else range(ntile - 1, -1, -1)
                for t in order:
                    args.append(((X, False, bs["lrT"][X][0:17, t * 128:(t + 1) * 128], bs["gkt"][:, t, :],
                                  bs["gvt"][:, t, :], [bs["t_lrT"][X], bs["t_gkt"], bs["t_gvt"]]), {}))
            gla_run(args)

        front("ctx", 8, bsets[0])
        ada_part2()
        for X in ("A", "B"):
            self.memset("dve", bsets[1]["lrT"][X], 1.0, [t_wada[0]])
        front("oth", 7, bsets[1])
        steps("ctx", 8, bsets[0])
        front("oth", 6, bsets[0])
        steps("oth", 7, bsets[1])
        front("oth", 5, bsets[1])
        steps("oth", 6, bsets[0])
        front("oth", 4, bsets[0])
        steps("oth", 5, bsets[1])
        front("own", 0, None)
        steps("oth", 4, bsets[0])
        for g in (1, 2, 3):
            front("own", g, None)
        self.dump("modT", modT, [128, 48, 2], F32, t_mod)
        self.dump("gt1bc", gt1bc, [128, 1024], F32, t_gt1)
        self.dump("KT", KT, [128, 2, NKEY], BF16, t_kv)
        self.dump("V", V, [128, 34, 256], BF16, t_kv)
        self.dump("hxT", hxT, [128, 4, 8, 512], BF16, t_hx)
        self.dump("SA", SA, [128, 4, 256], F32, t_S["A"])
        self.dump("SB", SB, [128, 4, 256], F32, t_S["B"])

        if "stop_s1" in self.dbg:
            return self.finish_stub(t_mod, modT)

        P.barrier()
        self.reset_region("R1", "R3")
        self.use("R1")
        AG = A([4, 2, 8, 512], BF16, "AG")
        self.use("R3")
        wq = A([8, 1024], BF16, "wq"); t_wq = Tok("wq"); c_wq = P.chan()
        Qbs = [A([4, 512], BF16, "Qb%d" % i) for i in range(2)]
        t_Qbs = [Tok("Qb%d" % i) for i in range(2)]
        Tq = A([2, 512], F32, "Tq"); t_Tq = Tok("Tq")
        r_sq = A([512], BF16, "r_sq2"); t_rsq = Tok("r_sq2")
        r_rs = A([512], F32, "r_rs2"); t_rrs = Tok("r_rs2")
        r_t1 = A([512], F32, "r_t12"); t_rt1 = Tok("r_t12")
        r_t2 = A([512], F32, "r_t22"); t_rt2 = Tok("r_t22")
        PTN = 8
        PT = [A([512], BF16, "PT%d" % i) for i in range(PTN)]
        t_PT = [Tok("PT%d" % i) for i in range(PTN)]
        sst = r_t2; t_sst = t_rt2
        ones32 = A([128], F32, "ones32"); t_ones32 = Tok("ones32")
        self.memset("dve", ones32, 1.0 / 32.0, [t_ones32])
        rec = A([512], F32, "rec"); t_rec = Tok("rec")
        Tq4 = Tq.rearrange("p c (r w) -> p c r w", r=8)
        self.cp("dve", Tq4[64:128], GT[64:128, 0, :, 0:64].unsqueeze(2).broadcast_to([64, 2, 8, 64]),
                [t_GT], [t_Tq])
        self.set_rot([0, 1, 2, 3])
        iters = [(hh, blk) for hh in range(2) for blk in range(4)]
        self.wq_loaded = -1

        def emit_q(it):
            hh, blk = iters[it]
            if self.wq_loaded != hh:
                self.wq_loaded = hh
                for kc in range(8):
                    self.dma("pool", wq[:, kc, 0:512],
                             w_in_v[:, kc, C_AQ + hh * 512:C_AQ + (hh + 1) * 512], c_wq, [], [t_wq])
                    self.dma("pool", wq[:, kc, 512:1024],
                             w_in_v[:, kc, C_AQP + hh * 512:C_AQP + (hh + 1) * 512], c_wq, [], [t_wq])
            self.cp("dve", Tq4[0:64],
                    GT[0:64, 0, :, blk * 8:(blk + 1) * 8].unsqueeze(3).broadcast_to([64, 2, 8, 64]),
                    [t_GT], [t_Tq])
            rhs_fn = lambda kc: hxT[:, blk, kc, :]
            for hl in range(4):
                b0 = proj_fm(wq, t_wq, hl * 128, 128, rhs_fn, 512, [t_hx[blk]])
                b1 = proj_fm(wq, t_wq, 512 + hl * 128, 128, rhs_fn, 512, [t_hx[blk]])
                rope_norm(b0, b1, 512, Tq[:, 0, :], Tq[:, 1, :], t_Tq, Qbs[it % 2][:, hl, :],
                          [t_Qbs[it % 2]], float(128 * EPS))

        self.pcount = 0

        def emit_unit(it, hl):
            hh, blk = iters[it]
            Qb, t_Qb = Qbs[it % 2], t_Qbs[it % 2]
            h = hh * 4 + hl
            kvh = h // 4
            bo = 4 + (self.pcount % 2) * 2
            bsum = bo + 1
            self.pcount += 1

            def smm(kt):
                b = self.bank()
                gi = 8 if kt >= 32 else kt // 4
                self.mm(ps[b], KT[:, kvh, kt * 128:(kt + 1) * 128], Qb[:, hl, :], True, True,
                        [t_kv[gi], t_Qb], [pt[b]])
                return b
            bcur = smm(0)
            bnext = None
            for kt in range(34):
                pi = kt % PTN
                self.act(PT[pi], ps[bcur], AF.Exp, [pt[bcur]], [t_PT[pi]])
                if kt + 1 < 34:
                    bnext = smm(kt + 1)
                gi = 8 if kt >= 32 else kt // 4
                self.mm(ps[bo], V[:, kt, kvh * 128:(kvh + 1) * 128], PT[pi], kt == 0, kt == 33,
                        [t_kv[gi], t_PT[pi]], [pt[bo]])
                if kt % 4 == 3 or kt == 33:
                    for k2 in range(kt - (kt % 4), kt + 1):
                        j = k2 % 4
                        self.mm(ps[bsum][32 * j:32 * j + 32, :], onesb[:, 32 * j:32 * j + 32],
                                PT[k2 % PTN], k2 < 4, k2 >= 30, [t_onesb, t_PT[k2 % PTN]], [pt[bsum]],
                                tile_position=(0, 32 * j))
                bcur = bnext
            self.cp("dve", sst, ps[bsum], [pt[bsum]], [t_sst])
            bt = self.bank()
            self.mm(ps[bt], ones32, sst, True, True, [t_ones32, t_sst], [pt[bt]])
            self.recip(rec, ps[bt], [pt[bt]], [t_rec])
            self.tt("dve", AG[:, blk, 0, h, :], ps[bo], rec, ALU.mult, [pt[bo], t_rec],
                    [t_ag[blk]])

        emit_q(0)
        for it in range(len(iters)):
            emit_unit(it, 0)
            emit_unit(it, 1)
            if it + 1 < len(iters):
                emit_q(it + 1)
            emit_unit(it, 2)
            emit_unit(it, 3)
        self.dump("attnT", AG[:, :, 0, :, :], [128, 4, 8, 512], BF16, t_ag)
        if "stop_att" in self.dbg:
            return self.finish_stub(t_mod, modT)

        P.barrier()
        self.reset_region("R2", "R3")
        self.set_rot([0, 1, 2, 3, 4, 5, 6, 7])
        self.use("R2")
        wg = A([8, 2080], BF16, "wg"); t_wg = Tok("wg"); c_wg = P.chan()
        self.use("R3")
        WG0 = 768
        gkT = A([4, 512], BF16, "gkT"); t_gkT = Tok("gkT")
        gqT = A([4, 512], BF16, "gqT"); t_gqT = Tok("gqT")
        gkt = A([4, 512], BF16, "gkt2"); t_gkt = Tok("gkt2")
        gvt = A([4, 1024], BF16, "gvt2"); t_gvt = Tok("gvt2")
        lrT = {"A": A([512], BF16, "lrTA2", parts=32), "B": A([512], BF16, "lrTB2", parts=32)}
        t_lrT = {"A": Tok("lrTA2"), "B": Tok("lrTB2")}
        gsets = [make_gset(10 + i, True) for i in range(2)]
        for X in ("A", "B"):
            self.memset("dve", lrT[X], 1.0, [t_lrT[X]])
        for kc in range(8):
            self.dma("pool", wg[:, kc, :], w_in_v[:, kc, WG0:WG0 + 2080], c_wg, [], [t_wg])

        def gla_group(X, g):
            rhs_fn = lambda kc: hxT[:, g, kc, :]
            rtoks = [t_hx[g]]
            for h in range(4):
                b = proj_fm(wg, t_wg, C_GK - WG0 + h * 128, 128, rhs_fn, 512, rtoks)
                self.cp("act", gkT[:, h, :], ps[b], [pt[b]], [t_gkT])
                b = proj_fm(wg, t_wg, C_GQ - WG0 + h * 128, 128, rhs_fn, 512, rtoks)
                self.ts("dve", gqT[:, h, :], ps[b], float(128.0 ** -0.5), None, ALU.mult, None,
                        [pt[b]], [t_gqT])
            for t in range(4):
                b = self.bank()
                for kc in range(8):
                    self.mm(ps[b], hxT[:, g, kc, t * 128:(t + 1) * 128],
                            wg[:, kc, C_GK - WG0:C_GK - WG0 + 512], kc == 0, kc == 7,
                            rtoks + [t_wg], [pt[b]])
                self.cp("dve", gkt[:, t, :], ps[b], [pt[b]], [t_gkt])
                for hf in range(2):
                    b = self.bank()
                    for kc in range(8):
                        self.mm(ps[b], hxT[:, g, kc, t * 128:(t + 1) * 128],
                                wg[:, kc, C_GV - WG0 + hf * 512:C_GV - WG0 + (hf + 1) * 512],
                                kc == 0, kc == 7, rtoks + [t_wg], [pt[b]])
                    self.cp("act", gvt[:, t, hf * 512:(hf + 1) * 512], ps[b], [pt[b]], [t_gvt])
            c0 = (C_LRA if X == "A" else C_LRB) - WG0
            b = proj_fm(wg, t_wg, c0, 16, rhs_fn, 512, rtoks)
            self.cp("dve", lrT[X][0:16, :], ps[b][0:16, :], [pt[b]], [t_lrT[X]])
            order = range(4) if X == "A" else range(3, -1, -1)
            args = []
            for t in order:
                osl = AG[:, g, 1, :, :].rearrange("p a b -> p (a b)")[:, t * 1024:(t + 1) * 1024]
                args.append(((X, True, lrT[X][0:17, t * 128:(t + 1) * 128], gkt[:, t, :], gvt[:, t, :],
                              [t_lrT[X], t_gkt, t_gvt, t_gkT, t_gqT]),
                             dict(qT=gqT[:, :, t * 128:(t + 1) * 128], kT=gkT[:, :, t * 128:(t + 1) * 128],
                                  o_dst=osl, o_add=(X == "A"), o_tok=t_ag[g])))
            gla_run(args)

        for g in (3, 2, 1, 0):
            gla_group("B", g)
        for g in (0, 1, 2, 3):
            gla_group("A", g)
        self.dump("osum", AG[:, :, 1, :, :], [128, 4, 8, 512], BF16, t_ag)
        if "stop_gla" in self.dbg:
            return self.finish_stub(t_mod, modT)

        P.barrier()
        self.reset_region("S", "R2", "R3")
        self.set_rot([0, 1, 2, 3, 4, 5, 6, 7])
        self.use("R3")
        wgo = A([8, 1024], BF16, "wgo"); t_wgo = Tok("wgo"); c_wgo = P.chan()
        for kc in range(8):
            self.dma("pool", wgo[:, kc, :], w_in_v[:, kc, C_GO:C_GO + 1024], c_wgo, [], [t_wgo])
        self.use("R2")
        wga = A([8, 1024], BF16, "wga"); t_wga = Tok("wga"); c_wga = P.chan()
        wba = A([8, 1024], BF16, "wba"); t_wba = Tok("wba"); c_wba = P.chan()
        w_bra_v = w_bra.rearrange("(k p) n -> p k n", p=128)
        w_brg_v = w_brg.rearrange("(k p) n -> p k n", p=128)
        w_out_v = w_out.rearrange("(k p) n -> p k n", p=128)
        for kc in range(8):
            self.dma("pool", wga[:, kc, :], w_in_v[:, kc, C_GA:C_GA + 1024], c_wga, [], [t_wga])
            self.dma("pool", wba[:, kc, :], w_bra_v[:, kc, :], c_wba, [], [t_wba])
        self.use("R3")
        gx = A([4, 1024], F32, "gx"); t_gx = Tok("gx")
        sgs = [A([1024], F32, "sg%d" % i) for i in range(2)]
        t_sgs = [Tok("sg%d" % i) for i in range(2)]
        ssgs = [A([8], F32, "ssg%d" % i) for i in range(2)]
        t_ssgs = [Tok("ssg%d" % i) for i in range(2)]
        o2j = A([256], BF16, "o2j"); t_o2j = Tok("o2j")
        acnt = 0
        for blk in range(4):
            for t in range(4):
                sg, t_sg = sgs[acnt % 2], t_sgs[acnt % 2]
                ssg, t_ssg = ssgs[acnt % 2], t_ssgs[acnt % 2]
                acnt += 1
                osl = AG[:, blk, 1, :, :].rearrange("p a b -> p (a b)")[:, t * 1024:(t + 1) * 1024]
                for hf in range(2):
                    b = self.bank()
                    for kc in range(8):
                        self.mm(ps[b], hxT[:, blk, kc, t * 128:(t + 1) * 128],
                                wgo[:, kc, hf * 512:(hf + 1) * 512], kc == 0, kc == 7,
                                [t_hx[blk], t_wgo], [pt[b]])
                    sgh = sg[:, hf * 512:(hf + 1) * 512]
                    self.act(sgh, ps[b], AF.Silu, [pt[b]], [t_sg])
                    sgh3 = sgh.rearrange("p (h e) -> p h e", h=2)
                    self.tt("dve", sgh3, sgh3, glan.unsqueeze(1).broadcast_to([128, 2, 256]), ALU.mult,
                            [t_sg, t_glan], [t_sg])
                for h in range(4):
                    self.act(o2j, osl[:, h * 256:(h + 1) * 256], AF.Square,
                             [t_ag[blk]], [t_o2j, t_ssg], accum=ssg[:, h:h + 1])
                self.ts("pool", ssg[:, 0:4], ssg[:, 0:4], float(1.0 / 256), float(EPS), ALU.mult, ALU.add,
                        [t_ssg], [t_ssg])
                self.tt("pool", ssg[:, 4:8], ssg[:, 0:4], nhalf.broadcast_to([128, 4]), ALU.pow,
                        [t_ssg, t_nhalf], [t_ssg])
                for h in range(4):
                    self.stt("dve", gx[:, t, h * 256:(h + 1) * 256], osl[:, h * 256:(h + 1) * 256],
                             ssg[:, 4 + h:5 + h], sg[:, h * 256:(h + 1) * 256], ALU.mult, ALU.mult,
                             [t_ag[blk], t_ssg, t_sg], [t_gx])
            for t in range(4):
                for half in range(2):
                    b = self.bank()
                    for q in range(4):
                        kc = half * 4 + q
                        self.tr(ps[b][:, q * 128:(q + 1) * 128], gx[:, t, kc * 128:(kc + 1) * 128], ident,
                                [t_gx, t_ident], [pt[b]])
                    dstv = AG[:, blk, 1, half * 4:half * 4 + 4, t * 128:(t + 1) * 128]
                    self.cp("act" if half == 0 else "dve", dstv,
                            ps[b].rearrange("p (q t) -> p q t", q=4), [pt[b]], [t_ag[blk]])
        self.dump("glaT", AG[:, :, 1, :, :], [128, 4, 8, 512], BF16, t_ag)
        self.chk("a")
        P.barrier()
        self.reset_region("S", "R3")
        self.use("R3")
        yT = A([8, 512], BF16, "yT"); t_yT = Tok("yT")
        sgts = [A([512], F32, "sgt%d" % i) for i in range(2)]
        t_sgts = [Tok("sgt%d" % i) for i in range(2)]
        self.sgc = 0
        wo = A([8, 1024], BF16, "wo"); t_wo = Tok("wo"); c_wo = P.chan()
        for kc in range(8):
            self.dma("pool", wo[:, kc, :], w_out_v[:, kc, :], c_wo, [], [t_wo])
        for kc in range(8):
            self.tt("dve", wo[:, kc, :], wo[:, kc, :], gt1bc, ALU.mult, [t_wo, t_gt1], [t_wo])

        def gated_proj(blk, wgate, t_wgate, wbr, t_wbr, src_half, fc, dst, dst_toks, add_src=None,
                       add_toks=()):
            bg = proj_fm(wgate, t_wgate, fc * 128, 128, lambda kc: hxT[:, blk, kc, :], 512, [t_hx[blk]])
            sgt, t_sgt = sgts[self.sgc % 2], t_sgts[self.sgc % 2]
            self.sgc += 1
            self.act(sgt, ps[bg], AF.Sigmoid, [pt[bg]], [t_sgt])
            bp = proj_fm(wbr, t_wbr, fc * 128, 128, lambda kc: AG[:, blk, src_half, kc, :], 512,
                         [t_ag[blk]])
            if add_src is None:
                self.tt("dve", dst, ps[bp], sgt, ALU.mult, [pt[bp], t_sgt], dst_toks)
            else:
                self.tt("dve", sgt, ps[bp], sgt, ALU.mult, [pt[bp], t_sgt], [t_sgt])
                self.tt("dve", dst, sgt, add_src, ALU.add, [t_sgt] + list(add_toks), dst_toks)

        for blk in range(4):
            for fc in range(8):
                gated_proj(blk, wga, t_wga, wba, t_wba, 0, fc, yT[:, fc, :], [t_yT])
            self.cp("dve", AG[:, blk, 0, :, :], yT, [t_yT], [t_ag[blk]])
        self.dump("y1T", AG[:, :, 0, :, :], [128, 4, 8, 512], BF16, t_ag)
        self.chk("b1")
        P.barrier()
        self.reset_region("S", "R2")
        self.use("R2")
        wgg = A([8, 1024], BF16, "wgg"); t_wgg = Tok("wgg"); c_wgg = P.chan()
        wbg = A([8, 1024], BF16, "wbg"); t_wbg = Tok("wbg"); c_wbg = P.chan()
        self.use("R3")
        for kc in range(8):
            self.dma("pool", wgg[:, kc, :], w_in_v[:, kc, C_GG:C_GG + 1024], c_wgg, [], [t_wgg])
            self.dma("pool", wbg[:, kc, :], w_brg_v[:, kc, :], c_wbg, [], [t_wbg])
        xt = [A([1024], F32, "xtb%d" % i) for i in range(2)]
        t_xt = [Tok("xtb%d" % i) for i in range(2)]
        c_xt = [P.chan() for _ in range(2)]
        xn = A([1024], F32, "xn2"); t_xn = Tok("xn2")
        sqj = A([1024], BF16, "sqj2"); t_sqj = Tok("sqj2")
        x2v = [AG[:, blk].rearrange("p a b c -> p (a b c)").bitcast(F32).rearrange(
            "p (t d) -> p t d", t=4) for blk in range(4)]
        xcount = 0
        for blk in range(4):
            for fc in range(8):
                gated_proj(blk, wgg, t_wgg, wbg, t_wbg, 1, fc, yT[:, fc, :], [t_yT],
                           add_src=AG[:, blk, 0, fc, :], add_toks=[t_ag[blk]])
            for t in range(4):
                xi = xcount % 2
                xcount += 1
                r0 = blk * 512 + t * 128
                self.dma("sp", xt[xi], xs[r0:r0 + 128, :], c_xt[xi], [], [t_xt[xi]])
                for hf in range(2):
                    b = self.bank()
                    for kc in range(8):
                        self.mm(ps[b], yT[:, kc, t * 128:(t + 1) * 128], wo[:, kc, hf * 512:(hf + 1) * 512],
                                kc == 0, kc == 7, [t_yT, t_wo], [pt[b]])
                    xh = xt[xi][:, hf * 512:(hf + 1) * 512]
                    self.tt("dve", x2v[blk][:, t, hf * 512:(hf + 1) * 512], ps[b], xh, ALU.add,
                            [pt[b], t_xt[xi]], [t_ag[blk]])
            for t in range(4):
                norm_transpose(x2v[blk][:, t, :], 0, a2, lambda kc: modT[:, 24 + kc, 0:1],
                               lambda kc, t=t, blk=blk: hxT[:, blk, kc, t * 128:(t + 1) * 128],
                               [t_hx[blk]], from_sbuf_tok=t_ag[blk])
        self.dump("x2", AG.rearrange("p a b c d -> p (a b c d)").bitcast(F32), [128, 16384], F32, t_ag)
        self.dump("hmT", hxT, [128, 4, 8, 512], BF16, t_hx)
        self.chk("b2")

        P.barrier()
        self.reset_region("S", "R2", "R3")
        self.use("R2")
        w1q = [A([8, 1024], BF16, "w1q%d" % i) for i in range(2)]
        self.use("R3")
        w2q = [A([8, 1024], BF16, "w2q%d" % i) for i in range(2)]
        t_w1q = [Tok("w1q%d" % i) for i in range(2)]
        t_w2q = [Tok("w2q%d" % i) for i in range(2)]
        c_w1q = [P.chan() for _ in range(2)]
        c_w2q = [P.chan() for _ in range(2)]
        h1 = A([8, 512], BF16, "h1"); t_h1 = Tok("h1")
        rl = [A([512], F32, "rl%d" % i) for i in range(2)]
        t_rl = [Tok("rl%d" % i) for i in range(2)]
        self.use("S")
        tmp = A([512], F32, "mtmp"); t_tmp = Tok("mtmp")
        ost = [A([1024], F32, "ost%d" % i) for i in range(2)]
        t_ost = [Tok("ost%d" % i) for i in range(2)]
        c_ost = [P.chan() for _ in range(2)]
        self.final_chans.extend(c_ost)
        w_m1_v = w_m1.rearrange("(k p) n -> p k n", p=128)
        w_m2_v = w_m2.rearrange("(k p) n -> p k n", p=128)
        ocount = 0
        rcount = 0
        for q in range(4):
            wi = q % 2
            for kc in range(8):
                self.dma("pool", w1q[wi][:, kc, :], w_m1_v[:, kc, q * 1024:(q + 1) * 1024], c_w1q[wi],
                         [], [t_w1q[wi]])
            for kc in range(8):
                self.dma("pool", w2q[wi][:, kc, :], w_m2_v[:, q * 8 + kc, :], c_w2q[wi], [], [t_w2q[wi]])
            for blk in range(4):
                for fc in range(8):
                    b = proj_fm(w1q[wi], t_w1q[wi], fc * 128, 128, lambda kc: hxT[:, blk, kc, :], 512,
                                [t_hx[blk]])
                    ri = rcount % 2
                    rcount += 1
                    self.act(rl[ri], ps[b], AF.Relu, [pt[b]], [t_rl[ri]])
                    self.tt("dve", h1[:, fc, :], rl[ri], rl[ri], ALU.mult, [t_rl[ri]], [t_h1])
                for t in range(4):
                    for hf in range(2):
                        b = self.bank()
                        for kc in range(8):
                            self.mm(ps[b], h1[:, kc, t * 128:(t + 1) * 128],
                                    w2q[wi][:, kc, hf * 512:(hf + 1) * 512], kc == 0, kc == 7,
                                    [t_h1, t_w2q[wi]], [pt[b]])
                        self.tt("dve", tmp, ps[b], gt2bc[:, hf * 512:(hf + 1) * 512], ALU.mult,
                                [pt[b], t_gt2], [t_tmp])
                        x2h = x2v[blk][:, t, hf * 512:(hf + 1) * 512]
                        if q < 3:
                            self.tt("dve", x2h, tmp, x2h, ALU.add, [t_tmp, t_ag[blk]], [t_ag[blk]])
                        else:
                            oi = ocount % 2
                            self.tt("dve", ost[oi][:, hf * 512:(hf + 1) * 512], tmp, x2h, ALU.add,
                                    [t_tmp, t_ag[blk]], [t_ost[oi]])
                    if q == 3:
                        oi = ocount % 2
                        ocount += 1
                        r0 = blk * 512 + t * 128
                        self.dma("sp", out_d[r0:r0 + 128, :], ost[oi], c_ost[oi], [t_ost[oi]], [])
        self.finalize()
        return self.nc

    def finish_stub(self, tok, ap):
        self.reset_region("R3")
        z = self.alloc([1024], F32, "zstub", region="R3")
        tz = Tok("z")
        self.memset("dve", z, 0.0, [tz])
        c = self.P.chan()
        self.final_chans.append(c)
        for i in range(16):
            self.dma("sp", self.out_d[i * 128:(i + 1) * 128, :], z, c, [tz], [])
        self.finalize()
        return self.nc

    def finalize(self):
        self.P.emit(self.nc, self.stack, self.final_chans)
        self.stack.close()


def _perm_half_swap(nheads):
    idx = []
    for h in range(nheads):
        for d in range(128):
            axis, rem = divmod(d, 64)
            half, f = divmod(rem, 32)
            idx.append(h * 128 + axis * 64 + (1 - half) * 32 + f)
    return np.array(idx)


def _consts(h):
    ident = np.eye(128, dtype=np.float32)
    r = np.arange(128)[:, None]
    c = np.arange(128)[None, :]
    s = np.float32(-1.0 / 16.0)
    tri = np.zeros((128, 4, 128), np.float32)
    tri[:, 0, :] = (r <= c) * s
    tri[:, 1, :] = (r > c) * s
    tri[:, 2, :] = (r >= c) * s
    tri[:, 3, :] = (r < c) * s
    msk = np.zeros((128, 2, 128), np.float32)
    msk[:, 0, :] = (r <= c)
    msk[:, 1, :] = (r >= c)
    half = 64
    freqs = (10000.0 ** (-np.arange(0, half, 2, dtype=np.float32) / half)).astype(np.float32)
    ropec = np.zeros((128, 2, 72), np.float32)
    for d in range(128):
        axis, rem = divmod(d, 64)
        hf, f = divmod(rem, 32)
        for i in range(64):
            pos = i if h == 0 else 63 - i
            ang = np.float32(pos) * freqs[f]
            ropec[d, 0, i] = np.cos(ang)
            ropec[d, 1, i] = (-np.sin(ang)) if hf == 0 else np.sin(ang)
        ropec[d, 0, 64:72] = 1.0
        ropec[d, 1, 64:72] = 0.0
    return ident, tri, msk, ropec


_NC_CACHE = {}


def _get_nc(dbg=()):
    key = tuple(sorted(dbg))
    if key not in _NC_CACHE:
        b = Builder(dbg)
        nc = b.build()
        _NC_CACHE[key] = (nc, b.dbg_out)
    return _NC_CACHE[key]


def make_in_maps(x, c, ctx, c_ctx, w_ada, b_ada, norm1, w_in, q_norm, k_norm, w_gk_fwd, b_gk_fwd,
                 w_gk_bwd, b_gk_bwd, gla_norm, w_br_attn, w_br_gla, w_out, norm2, w_mlp1, w_mlp2):
    f = lambda a: np.ascontiguousarray(np.asarray(a, dtype=np.float32))
    x = f(x); c = f(c); ctx = f(ctx); c_ctx = f(c_ctx)
    w_in0 = f(w_in)[0]
    off = np.cumsum([0, 256, 256, 512, 1024, 16, 16, 1024, 512, 1024, 1024, 1024])
    ak, av, gk, gv, lrf, lrb, aq, gq, go, ga, gg = [w_in0[:, off[i]:off[i + 1]] for i in range(11)]
    pk = _perm_half_swap(2)
    pq = _perm_half_swap(8)
    pd = _perm_half_swap(1)
    qn = f(q_norm)[0]; kn = f(k_norm)[0]
    qkg = np.ascontiguousarray(np.stack([qn, qn[pd], kn, kn[pd]], axis=1))
    win = {}
    wgk = {}
    for h in (0, 1):
        lra, lrbb = (lrf, lrb) if h == 0 else (lrb, lrf)
        win[h] = np.ascontiguousarray(np.concatenate(
            [ak, ak[:, pk], av, gk, gv, lra, lrbb, gq, aq, aq[:, pq], go, ga, gg], axis=1))
        assert win[h].shape[1] == NIN
        wf = np.concatenate([f(w_gk_fwd)[0], f(b_gk_fwd)[0][None, :]], axis=0)
        wb = np.concatenate([f(w_gk_bwd)[0], f(b_gk_bwd)[0][None, :]], axis=0)
        wgk[h] = np.ascontiguousarray(np.stack([wf, wb] if h == 0 else [wb, wf], axis=0))
    consts = {h: _consts(h) for h in (0, 1)}
    shared = dict(w_ada=f(w_ada)[0], b_ada=f(b_ada)[0], norm1=f(norm1)[0], norm2=f(norm2)[0],
                  gla_norm=f(gla_norm)[0], w_br_attn=f(w_br_attn)[0], w_br_gla=f(w_br_gla)[0],
                  w_out=f(w_out)[0], w_mlp1=f(w_mlp1)[0], w_mlp2=f(w_mlp2)[0], qkg=qkg)
    in_maps = []
    for core in range(8):
        b, h = divmod(core, 2)
        xb = x[b] if h == 0 else x[b][::-1]
        cb = ctx[b] if h == 0 else ctx[b][::-1]
        ident, tri, msk, ropec = consts[h]
        m = dict(shared)
        m.update(xs=np.ascontiguousarray(xb), ctx=np.ascontiguousarray(cb),
                 cvec=np.ascontiguousarray(np.stack([c[b], c_ctx], axis=0)),
                 w_in=win[h], wgk=wgk[h], ident=ident, tri=tri, msk=msk, ropec=ropec)
        in_maps.append(m)
    return in_maps


def assemble(results):
    out = np.empty((4, SEQ, D), np.float32)
    for core in range(8):
        b, h = divmod(core, 2)
        o = np.asarray(results[core]["out"], dtype=np.float32)
        if h == 0:
            out[b, 0:OWN] = o
        else:
            out[b, OWN:SEQ] = o[::-1]
    return out


def kernel(**inputs):
    nc, _ = _get_nc(())
    in_maps = make_in_maps(**inputs)
    res = run_bass_kernel_spmd(nc, in_maps, core_ids=list(range(8)))
    return assemble(res.results)
```

```python
import numpy as np
from contextlib import ExitStack
import concourse.bass as bass
import concourse.mybir as mybir
from concourse.bass_utils import run_bass_kernel_spmd

F32 = mybir.dt.float32
BF16 = mybir.dt.bfloat16
AF = mybir.ActivationFunctionType
ALU = mybir.AluOpType

D = 1024
SEQ = 4096
OWN = 2048
CTXL = 256
NKEY = SEQ + CTXL
EPS = 1e-6
C_AK, C_AKP, C_AV, C_GK, C_GV, C_LRA, C_LRB, C_GQ, C_AQ, C_AQP, C_GO, C_GA, C_GG = (
    0, 256, 512, 768, 1280, 2304, 2320, 2336, 2848, 3872, 4896, 5920, 6944)
NIN = 7968
SAME_WIN = 3


class Tok:
    __slots__ = ("name", "w", "r", "excl")

    def __init__(self, name, excl=False):
        self.name = name
        self.w = None
        self.r = []
        self.excl = excl


class Chan:
    def __init__(self, idx):
        self.idx = idx
        self.count = 0
        self.sem = None


class Op:
    __slots__ = ("fn", "deps", "signal", "chan")

    def __init__(self, fn, deps, chan):
        self.fn = fn
        self.deps = deps
        self.signal = False
        self.chan = chan


ENGS = ("pe", "act", "dve", "pool", "sp")


class Prog:
    def __init__(self):
        self.ops = {e: [] for e in ENGS}
        self.chans = []
        self.wm = {e: {} for e in ENGS}

    def chan(self):
        c = Chan(len(self.chans))
        self.chans.append(c)
        return c

    def add(self, eng, fn, reads=(), writes=(), chan=None):
        idx = len(self.ops[eng])
        deps = []
        for t in reads:
            if t.w is not None:
                deps.append(t.w)
            if t.excl:
                deps.extend(t.r)
        for t in writes:
            if t.w is not None:
                deps.append(t.w)
            deps.extend(t.r)
        need = []
        wm = self.wm[eng]
        best = {}
        for d in deps:
            if d[0] == "e":
                _, e2, i2 = d
                if e2 == eng:
                    if chan is not None:
                        pass
                    elif eng in ("pe", "sp"):
                        continue
                    elif idx - i2 > SAME_WIN:
                        continue
                key = ("e", e2)
                val = i2
            else:
                _, c, v = d
                if chan is not None and c is chan:
                    continue
                key = ("c", c.idx)
                val = v
            if wm.get(key, -1) >= val:
                continue
            if key not in best or best[key][0] < val:
                best[key] = (val, d)
        for key, (val, d) in best.items():
            wm[key] = val
            need.append(d)
            if d[0] == "e":
                self.ops[d[1]][d[2]].signal = True
        op = Op(fn, need, chan)
        self.ops[eng].append(op)
        if chan is None:
            ref = ("e", eng, idx)
        else:
            chan.count += 16
            ref = ("c", chan, chan.count)
        for t in reads:
            if t.excl:
                t.r = [ref]
            else:
                t.r.append(ref)
        for t in writes:
            t.w = ref
            t.r = []
        return op

    def barrier(self):
        refs = []
        for e in ENGS:
            if self.ops[e]:
                for i in range(len(self.ops[e]) - 1, -1, -1):
                    if self.ops[e][i].chan is None and self.ops[e][i].fn is not None:
                        refs.append(("e", e, i))
                        break
        for c in self.chans:
            if c.count:
                refs.append(("c", c, c.count))
        bt = Tok("barrier")
        for e in ENGS:
            need = []
            wm = self.wm[e]
            for d in refs:
                if d[0] == "e":
                    if d[1] == e:
                        continue
                    key = ("e", d[1]); val = d[2]
                else:
                    key = ("c", d[1].idx); val = d[2]
                if wm.get(key, -1) >= val:
                    continue
                wm[key] = val
                need.append(d)
                if d[0] == "e":
                    self.ops[d[1]][d[2]].signal = True
            self.ops[e].append(Op(None, need, None))

    def emit(self, nc, stack, final_chans):
        sems = {e: stack.enter_context(nc.semaphore("s_" + e)) for e in ENGS}
        for c in self.chans:
            c.sem = stack.enter_context(nc.semaphore("c%d" % c.idx))
        pref = {}
        for e in ENGS:
            cnt = 0
            p = []
            for op in self.ops[e]:
                if op.signal:
                    cnt += 1
                p.append(cnt)
            pref[e] = p
        block = stack.enter_context(nc.Block())
        handles = {"pe": block.tensor, "act": block.scalar, "dve": block.vector,
                   "pool": block.gpsimd, "sp": block.sync}

        def make(e):
            def body(eng):
                for op in self.ops[e]:
                    for d in op.deps:
                        if d[0] == "e":
                            eng.wait_ge(sems[d[1]], pref[d[1]][d[2]])
                        else:
                            eng.wait_ge(d[1].sem, d[2])
                    if op.fn is None:
                        continue
                    ins = op.fn(eng)
                    if op.signal:
                        ins.then_inc(sems[e], 1)
                    if op.chan is not None:
                        ins.then_inc(op.chan.sem, 16)
                if e == "sp":
                    for c in final_chans:
                        if c.count:
                            eng.wait_ge(c.sem, c.count)
            return body

        for e in ENGS:
            handles[e](make(e))


class StopBuild(Exception):
    pass


class Builder:
    def __init__(self, dbg=()):
        self.dbg = set(dbg)
        self.nc = bass.Bass("TRN2", target_bir_lowering=False)
        self.P = Prog()
        self.stack = ExitStack()
        self.dram = {}
        self.dbg_out = []

    def din(self, name, shape, dt=F32):
        ap = self.nc.dram_tensor(name, list(shape), dt, kind="ExternalInput").ap()
        self.dram[name] = ap
        return ap

    def init_arena(self):
        nc = self.nc
        self.ARENA_BYTES = 207 * 1024
        self.arena = self.stack.enter_context(
            nc.sbuf_tensor("arena", [128, self.ARENA_BYTES // 2], BF16))
        self.regions = {}
        self.def_region("P", 0, self.ARENA_BYTES)
        self.cur_region = "P"
        self.psum = [self.stack.enter_context(nc.psum_tensor("ps%d" % i, [128, 512], F32))[:]
                     for i in range(8)]
        self.pstok = [Tok("ps%d" % i, excl=True) for i in range(8)]
        self.rot = list(range(8))
        self.rot_i = 0

    def alloc(self, shape, dt, name="", parts=128, region=None):
        esz = 4 if dt == F32 else 2
        n = int(np.prod(shape))
        nbytes = (n * esz + 63) // 64 * 64
        if region is None:
            region = self.cur_region
        r = self.regions[region]
        off = r[1]
        r[1] += nbytes
        assert r[1] <= r[2], ("SBUF overflow", region, name, r[1] - r[2])
        v = self.arena[0:parts, off // 2: off // 2 + n * esz // 2]
        if dt == F32:
            v = v.bitcast(F32)
        if len(shape) == 2:
            v = v.rearrange("p (a b) -> p a b", a=shape[0])
        elif len(shape) == 3:
            v = v.rearrange("p (a b c) -> p a b c", a=shape[0], b=shape[1])
        elif len(shape) == 4:
            v = v.rearrange("p (a b c d) -> p a b c d", a=shape[0], b=shape[1], c=shape[2])
        return v

    def def_region(self, name, start, end):
        self.regions[name] = [start, start, end]

    def reset_region(self, *names):
        for n in names:
            self.regions[n][1] = self.regions[n][0]

    def use(self, name):
        self.cur_region = name

    def set_rot(self, banks):
        self.rot = list(banks)
        self.rot_i = 0

    def bank(self):
        b = self.rot[self.rot_i % len(self.rot)]
        self.rot_i += 1
        return b

    def mm(self, out, lhsT, rhs, start, stop, reads, writes, tile_position=None):
        if tile_position is not None:
            return self.P.add("pe", lambda e: e.matmul(out, lhsT, rhs, start=start, stop=stop,
                                                       tile_position=tile_position), reads, writes)
        return self.P.add("pe", lambda e: e.matmul(out, lhsT, rhs, start=start, stop=stop),
                          reads, writes)

    def tr(self, out, in_, ident, reads, writes):
        return self.P.add("pe", lambda e: e.transpose(out, in_, ident), reads, writes)

    def act(self, out, in_, func, reads, writes, bias=None, scale=None, accum=None):
        kw = {}
        if bias is not None:
            kw["bias"] = bias
        if scale is not None:
            kw["scale"] = scale
        if accum is not None:
            kw["accum_out"] = accum
        return self.P.add("act", lambda e: e.activation(out, in_, func, **kw), reads, writes)

    def tt(self, eng, out, in0, in1, op, reads, writes):
        return self.P.add(eng, lambda e: e.tensor_tensor(out, in0, in1, op), reads, writes)

    def ts(self, eng, out, in0, s1, s2, op0, op1, reads, writes):
        if op1 is None:
            return self.P.add(eng, lambda e: e.tensor_scalar(out, in0, s1, None, op0),
                              reads, writes)
        return self.P.add(eng, lambda e: e.tensor_scalar(out, in0, s1, s2, op0, op1),
                          reads, writes)

    def stt(self, eng, out, in0, scalar, in1, op0, op1, reads, writes):
        return self.P.add(eng, lambda e: e.scalar_tensor_tensor(out, in0, scalar, in1, op0, op1),
                          reads, writes)

    def cp(self, eng, out, in_, reads, writes):
        if eng == "act":
            return self.P.add("act", lambda e: e.copy(out, in_), reads, writes)
        return self.P.add(eng, lambda e: e.tensor_copy(out, in_), reads, writes)

    def recip(self, out, in_, reads, writes):
        return self.P.add("dve", lambda e: e.reciprocal(out, in_), reads, writes)

    def memset(self, eng, ap, val, writes):
        return self.P.add(eng, lambda e: e.memset(ap, val), (), writes)

    def dma(self, q, out, in_, chan, reads, writes, slow=False):
        if slow:
            return self.P.add(q, lambda e: e.dma_start(out=out, in_=in_,
                                                       allow_slow_non_contiguous=True),
                              reads, writes, chan=chan)
        return self.P.add(q, lambda e: e.dma_start(out=out, in_=in_), reads, writes, chan=chan)

    def dump(self, name, ap, shape, dt, tok):
        if name not in self.dbg:
            return
        o = self.nc.dram_tensor("dbg_" + name, list(shape), dt, kind="ExternalOutput").ap()
        c = self.P.chan()
        self.final_chans.append(c)
        self.dma("sp", o, ap, c, [tok] if not isinstance(tok, (list, tuple)) else list(tok), [])
        self.dbg_out.append("dbg_" + name)

    def build(self):
        try:
            return self._build_body()
        except StopBuild:
            return self.finish_stub(None, None)

    def chk(self, label):
        if ("stop_" + label) in self.dbg:
            raise StopBuild()

    def _build_body(self):
        nc = self.nc
        P = self.P
        din = self.din
        self.final_chans = []
        xs = din("xs", [SEQ, D])
        ctx = din("ctx", [CTXL, D])
        cvec = din("cvec", [2, D])
        w_ada = din("w_ada", [D, 6 * D])
        b_ada = din("b_ada", [6 * D])
        norm1 = din("norm1", [D])
        norm2 = din("norm2", [D])
        w_in = din("w_in", [D, NIN])
        wgk = din("wgk", [2, 17, 512])
        qkg = din("qkg", [128, 4])
        gla_norm = din("gla_norm", [256])
        w_bra = din("w_br_attn", [D, D])
        w_brg = din("w_br_gla", [D, D])
        w_out = din("w_out", [D, D])
        w_m1 = din("w_mlp1", [D, 4 * D])
        w_m2 = din("w_mlp2", [4 * D, D])
        ident_d = din("ident", [128, 128])
        tri_d = din("tri", [128, 4, 128])
        msk_d = din("msk", [128, 2, 128])
        ropec_d = din("ropec", [128, 2, 72])
        out_d = nc.dram_tensor("out", [OWN, D], F32, kind="ExternalOutput").ap()
        self.out_d = out_d

        self.init_arena()
        A = self.alloc
        ps = self.psum
        pt = self.pstok

        ident = A([128], F32, "ident"); t_ident = Tok("ident")
        onesf = A([128], F32, "onesf"); t_onesf = Tok("onesf")
        onesb = A([128], BF16, "onesb"); t_onesb = Tok("onesb")
        tri = A([4, 128], F32, "tri"); t_tri = Tok("tri")
        msk = A([2, 128], F32, "msk"); t_msk = Tok("msk")
        ropec = A([2, 72], F32, "ropec"); t_ropec = Tok("ropec")
        qkgs = A([4], F32, "qkg"); t_qkg = Tok("qkg")
        GT = A([2, 2, 72], F32, "GT"); t_GT = Tok("GT")
        scT = A([8, 2], BF16, "scT"); t_scT = Tok("scT")
        modT = A([48, 2], F32, "modT"); t_mod = Tok("mod")
        a1 = A([8], F32, "a1"); ac = A([8], F32, "ac"); a2 = A([8], F32, "a2"); t_av = Tok("avec"); t_av2 = Tok("avec2")
        gt1bc = A([1024], F32, "gt1bc"); t_gt1 = Tok("gt1bc")
        gt2bc = A([1024], F32, "gt2bc"); t_gt2 = Tok("gt2bc")
        glan = A([256], F32, "glan"); t_glan = Tok("glan")
        wgka = A([2, 512], BF16, "wgka", parts=32); t_wgka = Tok("wgka")
        nhalf = A([1], F32, "nhalf"); t_nhalf = Tok("nhalf")
        tiny = A([16], F32, "tiny"); t_tiny = Tok("tiny")

        for (dst, src, tk) in ((ident, ident_d, t_ident), (tri, tri_d, t_tri), (msk, msk_d, t_msk),
                               (ropec, ropec_d, t_ropec), (qkgs, qkg, t_qkg)):
            self.dma("sp", dst, src, P.chan(), [], [tk])
        vst = A([128], F32, "vst", parts=128); t_vst = Tok("vst")
        vT = A([80], F32, "vT"); t_vT = Tok("vT")
        self.memset("dve", vst, 0.0, [t_vst])
        c_v = P.chan()
        self.dma("sp", vst[0:48], b_ada.rearrange("(k p) -> k p", p=128), c_v, [], [t_vst])
        self.dma("sp", vst[48:56], norm1.rearrange("(k p) -> k p", p=128), c_v, [], [t_vst])
        self.dma("sp", vst[56:64], norm2.rearrange("(k p) -> k p", p=128), c_v, [], [t_vst])
        self.dma("sp", vst[64:80], cvec.rearrange("r (k p) -> (r k) p", p=128), c_v, [], [t_vst])
        self.dma("sp", glan, gla_norm.partition_broadcast(128), P.chan(), [], [t_glan])
        c_wgk = P.chan()
        self.dma("pool", wgka[0:17], wgk.rearrange("x r n -> r x n"), c_wgk, [], [t_wgka])
        self.memset("dve", onesf, 1.0, [t_onesf])
        self.memset("dve", onesb, 1.0, [t_onesb])
        self.memset("dve", nhalf, -0.5, [t_nhalf])
        self.tr(ps[0][:, 0:128], vst, ident, [t_vst, t_ident], [pt[0]])
        self.cp("dve", vT, ps[0][:, 0:80], [pt[0]], [t_vT])
        badaT = vT[:, 0:48]; n1T = vT[:, 48:56]; n2T = vT[:, 56:64]
        cT = vT[:, 64:80].rearrange("p (r k) -> p k r", r=2)
        t_bada = t_vT; t_nT = t_vT; t_cT = t_vT

        hxT = A([4, 8, 512], BF16, "hxT_own")
        t_hx = [Tok("hxT%d" % j) for j in range(4)]
        t_ag = [Tok("AG%d" % j) for j in range(4)]

        SA = A([4, 256], F32, "SA"); SB = A([4, 256], F32, "SB")
        SAb = A([4, 256], BF16, "SAb"); SBb = A([4, 256], BF16, "SBb")
        t_S = {"A": Tok("SA"), "B": Tok("SB")}
        t_Sb = {"A": Tok("SAb"), "B": Tok("SBb")}
        Sf = {"A": SA, "B": SB}
        Sb = {"A": SAb, "B": SBb}
        pend = self.regions["P"][1]
        s_start = pend - 12288
        self.def_region("S", s_start, pend)
        self.def_region("R1", pend, pend + 65536)
        self.def_region("R2", pend + 65536, pend + 65536 + 34816)
        self.def_region("R3", pend + 65536 + 34816, self.ARENA_BYTES)
        self.use("R2")
        KT = A([2, NKEY], BF16, "KT")
        V = A([34, 256], BF16, "V")
        t_kv = [Tok("kv%d" % g) for g in range(9)]

        self.use("R1")
        ws1 = A([8, 2336], BF16, "ws1"); t_ws1 = Tok("ws1"); c_ws1 = P.chan()
        xt = [A([1024], F32, "xt%d" % i) for i in range(2)]
        t_xt = [Tok("xt%d" % i) for i in range(2)]
        c_xt = [P.chan() for _ in range(2)]
        wada = [A([8, 512], BF16, "wada%d" % i) for i in range(2)]
        t_wada = [Tok("wada%d" % i) for i in range(2)]
        c_wada = [P.chan() for _ in range(2)]
        Tkc = A([2, 256], F32, "Tkc"); t_Tkc = Tok("Tkc")
        self.use("R3")
        xn = A([1024], F32, "xn"); t_xn = Tok("xn")
        hxg = A([8, 512], BF16, "hxg"); t_hxg = Tok("hxg")
        gkt = A([4, 512], BF16, "gkt"); t_gkt = Tok("gkt")
        gvt = A([4, 1024], BF16, "gvt"); t_gvt = Tok("gvt")
        lrT = {"A": A([512], BF16, "lrTA", parts=32), "B": A([512], BF16, "lrTB", parts=32)}
        t_lrT = {"A": Tok("lrTA"), "B": Tok("lrTB")}
        Tk = A([2, 512], F32, "Tk"); t_Tk = Tok("Tk")
        r_sq = A([512], BF16, "r_sq"); t_rsq = Tok("r_sq")
        r_rs = A([512], F32, "r_rs"); t_rrs = Tok("r_rs")
        r_t1 = A([512], F32, "r_t1"); t_rt1 = Tok("r_t1")
        r_t2 = A([512], F32, "r_t2"); t_rt2 = Tok("r_t2")
        def make_gset(i, full):
            G = {"sp": (A([512], F32, "g_sp%d" % i), Tok("g_sp")),
                 "EC": (A([512], F32, "g_EC%d" % i), Tok("g_EC")),
                 "kh": (A([512], BF16, "g_kh%d" % i), Tok("g_kh")),
                 "EL": (A([4], F32, "g_EL%d" % i), Tok("g_EL"))}
            if full:
                G["E1"] = (A([512], F32, "g_E1%d" % i), Tok("g_E1"))
                G["E2"] = (A([512], F32, "g_E2%d" % i), Tok("g_E2"))
                G["qt"] = (A([4, 128], BF16, "g_qt%d" % i), Tok("g_qt"))
                G["kt"] = (A([4, 128], BF16, "g_kt%d" % i), Tok("g_kt"))
                G["AT"] = (A([4, 128], BF16, "g_AT%d" % i), Tok("g_AT"))
            return G
        gsets = [make_gset(0, False),
                 {"sp": (r_rs, t_rrs), "EC": (r_t1, t_rt1), "kh": (r_sq, t_rsq),
                  "EL": (A([4], F32, "g_EL1"), Tok("g_EL1"))}]
        self.gcnt = 0
        sqj = r_t2.bitcast(BF16)
        t_sqj = t_rt2
        diag = A([128], F32, "diag"); t_diag = Tok("diag")

        for X in ("A", "B"):
            self.memset("dve", lrT[X], 1.0, [t_lrT[X]])
            self.memset("dve", Sf[X], 0.0, [t_S[X]])
            self.memset("dve", Sb[X], 0.0, [t_Sb[X]])

        w_in_v = w_in.rearrange("(k p) n -> p k n", p=128)
        for kc in range(8):
            self.dma("pool", ws1[:, kc, :], w_in_v[:, kc, 0:2336], c_ws1, [], [t_ws1])

        self.set_rot([0, 1, 2, 3, 4, 5, 6, 7])
        ec = tiny
        ecv = tiny.rearrange("p (a b) -> p a b", a=8)
        self.act(ecv, cT, AF.Exp, [t_cT], [t_tiny], scale=-1.0)
        self.ts("dve", ecv, ecv, 1.0, None, ALU.add, None, [t_tiny], [t_tiny])
        self.recip(ecv, ecv, [t_tiny], [t_tiny])
        self.tt("dve", scT, ecv, cT, ALU.mult, [t_tiny, t_cT], [t_scT])

        w_ada_v = w_ada.rearrange("(k p) n -> p k n", p=128)
        self.set_rot([0, 1, 2, 3, 4, 5, 6])
        bmod = 7
        modps = ps[bmod][:, 0:96].rearrange("p (a b) -> p a b", a=48)

        def ada_dma(cb):
            for kc in range(8):
                self.dma("pool", wada[cb % 2][:, kc, :], w_ada_v[:, kc, cb * 512:(cb + 1) * 512],
                         c_wada[cb % 2], [], [t_wada[cb % 2]])

        def ada_mm(cb):
            wb = wada[cb % 2]
            for fc in range(4):
                j = cb * 4 + fc
                for kc in range(8):
                    self.mm(modps[:, j, :], wb[:, kc, fc * 128:(fc + 1) * 128], scT[:, kc, :],
                            kc == 0, kc == 7, [t_wada[cb % 2], t_scT], [pt[bmod]])

        ada_dma(0)
        ada_dma(1)
        for cb in range(4):
            ada_mm(cb)
            ada_dma(cb + 2)
        self.tt("dve", modT[:, 0:16, :], modps[:, 0:16, :],
                badaT[:, 0:16].unsqueeze(2).broadcast_to([128, 16, 2]), ALU.add,
                [pt[bmod], t_bada], [t_mod])
        self.ts("dve", a1, modT[:, 8:16, 0], 1.0, 32.0, ALU.add, ALU.mult, [t_mod], [t_av])
        self.tt("dve", a1, a1, n1T, ALU.mult, [t_av, t_nT], [t_av])
        self.ts("dve", ac, modT[:, 8:16, 1], 1.0, 32.0, ALU.add, ALU.mult, [t_mod], [t_av])
        self.tt("dve", ac, ac, n1T, ALU.mult, [t_av, t_nT], [t_av])

        def ada_part2():
            for cb in range(4, 12):
                ada_mm(cb)
                if cb + 2 < 12:
                    ada_dma(cb + 2)
            self.tt("dve", modT[:, 16:48, :], modps[:, 16:48, :],
                    badaT[:, 16:48].unsqueeze(2).broadcast_to([128, 32, 2]), ALU.add,
                    [pt[bmod], t_bada], [t_mod])
            self.ts("dve", a2, modT[:, 32:40, 0], 1.0, 32.0, ALU.add, ALU.mult, [t_mod], [t_av2])
            self.tt("dve", a2, a2, n2T, ALU.mult, [t_av2, t_nT], [t_av2])
            for (dst, tk, j0) in ((gt1bc, t_gt1, 16), (gt2bc, t_gt2, 40)):
                for half in range(2):
                    b = self.bank()
                    for q in range(4):
                        kc = half * 4 + q
                        self.ts("dve", diag, ident, modT[:, j0 + kc, 0:1], None, ALU.mult, None,
                                [t_ident, t_mod], [t_diag])
                        self.mm(ps[b][:, q * 128:(q + 1) * 128], onesf, diag, True, True,
                                [t_onesf, t_diag], [pt[b]])
                    self.cp("dve", dst[:, half * 512:(half + 1) * 512], ps[b], [pt[b]], [tk])
        SQ128 = float(np.sqrt(128.0))
        self.ts("dve", GT[:, 0, 0, :], ropec[:, 0, :], qkgs[:, 0:1], None, ALU.mult, None,
                [t_ropec, t_qkg], [t_GT])
        self.ts("dve", GT[:, 0, 1, :], ropec[:, 1, :], qkgs[:, 1:2], None, ALU.mult, None,
                [t_ropec, t_qkg], [t_GT])
        self.ts("dve", GT[:, 1, 0, :], ropec[:, 0, :], qkgs[:, 2:3], SQ128, ALU.mult, ALU.mult,
                [t_ropec, t_qkg], [t_GT])
        self.ts("dve", GT[:, 1, 1, :], ropec[:, 1, :], qkgs[:, 3:4], SQ128, ALU.mult, ALU.mult,
                [t_ropec, t_qkg], [t_GT])
        Tk4 = Tk.rearrange("p c (r w) -> p c r w", r=8)
        self.cp("dve", Tk4[64:128], GT[64:128, 1, :, 0:64].unsqueeze(2).broadcast_to([64, 2, 8, 64]),
                [t_GT], [t_Tk])
        Tkc4 = Tkc.rearrange("p c (r w) -> p c r w", r=4)
        self.cp("dve", Tkc4, GT[:, 1, :, 64:68].unsqueeze(3).broadcast_to([128, 2, 4, 64]),
                [t_GT], [t_Tkc])

        if "stop_setup" in self.dbg:
            return self.finish_stub(t_mod, modT)
        def norm_transpose(src_ap, xt_i, avec, shcol, dst_fn, dst_toks, from_sbuf_tok=None):
            xin = src_ap
            rt = [t_xt[xt_i]] if from_sbuf_tok is None else [from_sbuf_tok]
            ssc = tiny[:, 0:1]
            self.act(sqj, xin, AF.Square, rt, [t_sqj, t_tiny], accum=ssc)
            self.ts("pool", tiny[:, 1:2], ssc, float(D * EPS), None, ALU.add, None, [t_tiny], [t_tiny])
            self.tt("pool", tiny[:, 2:3], tiny[:, 1:2], nhalf, ALU.pow, [t_tiny, t_nhalf], [t_tiny])
            self.act(xn, xin, AF.Copy, rt + [t_tiny], [t_xn], scale=tiny[:, 2:3])
            for half in range(2):
                b = self.bank()
                for q in range(4):
                    kc = half * 4 + q
                    self.tr(ps[b][:, q * 128:(q + 1) * 128], xn[:, kc * 128:(kc + 1) * 128], ident,
                            [t_xn, t_ident], [pt[b]])
                for q in range(4):
                    kc = half * 4 + q
                    dst = dst_fn(kc)
                    if half == 0:
                        self.act(dst, ps[b][:, q * 128:(q + 1) * 128], AF.Identity,
                                 [pt[b], t_av, t_av2, t_mod], dst_toks,
                                 scale=avec[:, kc:kc + 1], bias=shcol(kc))
                    else:
                        self.ts("dve", dst, ps[b][:, q * 128:(q + 1) * 128], avec[:, kc:kc + 1],
                                shcol(kc), ALU.mult, ALU.add, [pt[b], t_av, t_av2, t_mod], dst_toks)

        def rope_norm(b0, b1, n, Tc, Ts, t_tab, dst, dst_toks, eps_scaled):
            self.act(r_sq[:, 0:n], ps[b0][:, 0:n], AF.Square, [pt[b0]], [t_rsq])
            bs = self.bank()
            self.mm(ps[bs][:, 0:n], onesb, r_sq[:, 0:n], True, True, [t_onesb, t_rsq], [pt[bs]])
            self.chk("r1")
            import os
            EXP = os.environ.get("EXP", "")
            if EXP == "copyfirst":
                self.cp("dve", r_t1[:, 0:n], ps[b0][:, 0:n], [pt[b0]], [t_rt1])
            elif EXP == "serial":
                self.cp("dve", r_t1[:, 0:n], ps[b0][:, 0:n], [pt[b0], t_rsq], [t_rt1])
                self.chk("r1b")
                self.chk("r1b")
                self.tt("dve", r_t1[:, 0:n], r_t1[:, 0:n], Tc, ALU.mult, [t_rt1, t_tab], [t_rt1])
                self.chk("r1c")
            else:
                self.tt("dve", r_t1[:, 0:n], ps[b0][:, 0:n], Tc, ALU.mult, [pt[b0], t_tab], [t_rt1])
            self.chk("r1a")
            self.tt("dve", r_t2[:, 0:n], ps[b1][:, 0:n], Ts, ALU.mult, [pt[b1], t_tab], [t_rt2])
            self.chk("r2")
            self.act(r_rs[:, 0:n], ps[bs][:, 0:n], AF.Ln, [pt[bs]], [t_rrs], bias=eps_scaled)
            self.chk("r3")
            self.act(r_rs[:, 0:n], r_rs[:, 0:n], AF.Exp, [t_rrs], [t_rrs], scale=-0.5)
            self.chk("r4")
            self.tt("dve", r_t1[:, 0:n], r_t1[:, 0:n], r_t2[:, 0:n], ALU.add, [t_rt1, t_rt2], [t_rt1])
            self.tt("dve", dst, r_t1[:, 0:n], r_rs[:, 0:n], ALU.mult, [t_rt1, t_rrs], dst_toks)

        def proj_fm(w, t_w, c0, ncols_chunk, rhs_fn, n, rhs_toks):
            b = self.bank()
            for kc in range(8):
                self.mm(ps[b][0:ncols_chunk, 0:n], w[:, kc, c0:c0 + ncols_chunk], rhs_fn(kc),
                        kc == 0, kc == 7, [t_w] + rhs_toks, [pt[b]])
            return b

        def gla_stage1(X, full, lr_ap, gk_tile, gv_tile, t_in, qT=None, kT=None, o_dst=None,
                       o_add=False, o_tok=None):
            G = gsets[self.gcnt % len(gsets)]
            self.gcnt += 1
            g_sp, t_gsp = G["sp"]; g_EC, t_gEC = G["EC"]; g_kh, t_gkh = G["kh"]; g_EL, t_gEL = G["EL"]
            xi = 0 if X == "A" else 1
            cum = tri[:, 2 * xi, :]
            cmat = tri[:, 2 * xi + 1, :]
            last = 127 if X == "A" else 0
            bz = self.bank()
            self.mm(ps[bz], lr_ap, wgka[0:17, xi, :], True, True, t_in + [t_wgka], [pt[bz]])
            self.act(g_sp, ps[bz], AF.Exp, [pt[bz]], [t_gsp], scale=-1.0)
            self.act(g_sp, g_sp, AF.Ln, [t_gsp], [t_gsp], bias=1.0)
            bc = self.bank()
            self.mm(ps[bc], cmat, g_sp, True, True, [t_tri, t_gsp], [pt[bc]])
            if full:
                bb = self.bank()
                for h in range(4):
                    self.mm(ps[bb][:, h * 128:(h + 1) * 128], g_sp[:, h * 128:(h + 1) * 128], cum,
                            True, True, [t_gsp, t_tri], [pt[bb]])
            else:
                bl = self.bank()
                for h in range(4):
                    self.mm(ps[bl][:, h:h + 1], g_sp[:, h * 128:(h + 1) * 128],
                            cum[:, last:last + 1], True, True, [t_gsp, t_tri], [pt[bl]])
            self.act(g_EC, ps[bc], AF.Exp, [pt[bc]], [t_gEC])
            self.tt("dve", g_kh, gk_tile, g_EC, ALU.mult, t_in + [t_gEC], [t_gkh])
            st = dict(X=X, full=full, G=G, gv_tile=gv_tile, t_in=t_in, o_dst=o_dst, o_add=o_add,
                      o_tok=o_tok, xi=xi)
            if full:
                g_E1, t_gE1 = G["E1"]; g_E2, t_gE2 = G["E2"]; g_qt, t_gqtl = G["qt"]
                g_kt, t_gktl = G["kt"]
                self.act(g_E1, ps[bb], AF.Exp, [pt[bb]], [t_gE1])
                self.act(g_E2, ps[bb], AF.Exp, [pt[bb]], [t_gE2], scale=-1.0)
                E1v = g_E1.rearrange("p (h t) -> p h t", h=4)
                E2v = g_E2.rearrange("p (h t) -> p h t", h=4)
                self.tt("dve", g_qt, qT, E1v, ALU.mult, t_in + [t_gE1], [t_gqtl])
                self.tt("dve", g_kt, kT, E2v, ALU.mult, t_in + [t_gE2], [t_gktl])
                st["el"] = lambda h: g_E1[:, h * 128 + last: h * 128 + last + 1]
                st["el_tok"] = t_gE1
            else:
                self.act(g_EL, ps[bl][:, 0:4], AF.Exp, [pt[bl]], [t_gEL])
                st["el"] = lambda h: g_EL[:, h:h + 1]
                st["el_tok"] = t_gEL
            return st

        def gla_stage2(st):
            X = st["X"]; G = st["G"]; gv_tile = st["gv_tile"]; t_in = st["t_in"]; xi = st["xi"]
            g_kh, t_gkh = G["kh"]
            if st["full"]:
                g_qt, t_gqtl = G["qt"]; g_kt, t_gktl = G["kt"]; g_AT, t_gAT = G["AT"]
                ba = self.bank()
                for h in range(4):
                    self.mm(ps[ba][:, h * 128:(h + 1) * 128], g_kt[:, h, :], g_qt[:, h, :],
                            True, True, [t_gktl, t_gqtl], [pt[ba]])
                self.tt("dve", g_AT, ps[ba].rearrange("p (h t) -> p h t", h=4),
                        msk[:, xi, :].unsqueeze(1).broadcast_to([128, 4, 128]), ALU.mult,
                        [pt[ba], t_msk], [t_gAT])
                for hp in range(2):
                    bo = self.bank()
                    for hh in range(2):
                        h = hp * 2 + hh
                        self.mm(ps[bo][:, hh * 256:(hh + 1) * 256], g_qt[:, h, :], Sb[X][:, h, :],
                                True, False, [t_gqtl, t_Sb[X]], [pt[bo]])
                        self.mm(ps[bo][:, hh * 256:(hh + 1) * 256], g_AT[:, h, :],
                                gv_tile[:, h * 256:(h + 1) * 256], False, True,
                                [t_gAT] + t_in, [pt[bo]])
                    od = st["o_dst"][:, hp * 512:(hp + 1) * 512]
                    if st["o_add"]:
                        self.tt("dve", od, ps[bo], od, ALU.add, [pt[bo], st["o_tok"]], [st["o_tok"]])
                    else:
                        self.cp("act", od, ps[bo], [pt[bo]], [st["o_tok"]])
            el = st["el"]; el_tok = st["el_tok"]
            for hp in range(2):
                bu = self.bank()
                for hh in range(2):
                    h = hp * 2 + hh
                    self.mm(ps[bu][:, hh * 256:(hh + 1) * 256], g_kh[:, h * 128:(h + 1) * 128],
                            gv_tile[:, h * 256:(h + 1) * 256], True, True, [t_gkh] + t_in, [pt[bu]])
                for hh in range(2):
                    h = hp * 2 + hh
                    self.stt("dve", Sf[X][:, h, :], Sf[X][:, h, :], el(h),
                             ps[bu][:, hh * 256:(hh + 1) * 256], ALU.mult, ALU.add,
                             [t_S[X], el_tok, pt[bu]], [t_S[X]])
            self.cp("act", Sb[X], Sf[X], [t_S[X]], [t_Sb[X]])

        def gla_run(step_args, inter=None):
            prev = None
            for a in step_args:
                cur = gla_stage1(*a[0], **a[1])
                if inter is not None:
                    next(inter, None)
                if prev is not None:
                    gla_stage2(prev)
                prev = cur
            if prev is not None:
                gla_stage2(prev)
            if inter is not None:
                for _ in inter:
                    pass

        wf0 = wada[0].rearrange("p a b -> p (a b)")
        wf1 = wada[1].rearrange("p a b -> p (a b)")
        bsets = [
            dict(gkt=gkt, t_gkt=t_gkt, gvt=gvt, t_gvt=t_gvt, lrT=lrT, t_lrT=t_lrT),
            dict(gkt=wf0[:, 0:2048].rearrange("p (t n) -> p t n", t=4), t_gkt=t_wada[0],
                 gvt=wf1.rearrange("p (t n) -> p t n", t=4), t_gvt=t_wada[1],
                 lrT={"A": wf0[0:32, 2048:2560], "B": wf0[0:32, 2560:3072]},
                 t_lrT={"A": t_wada[0], "B": t_wada[0]}),
        ]
        self.xcount = 0

        def front_tiles(kind, g):
            ntile = 2 if kind == "ctx" else 4
            avec = ac if kind == "ctx" else a1
            rcol = 1 if kind == "ctx" else 0
            shcol = lambda kc, rcol=rcol: modT[:, kc, rcol:rcol + 1]
            src = ctx if kind == "ctx" else xs
            for t in range(ntile):
                xi = self.xcount % 2
                self.xcount += 1
                r0 = t * 128 if kind == "ctx" else g * 512 + t * 128
                self.dma("sp", xt[xi], src[r0:r0 + 128, :], c_xt[xi], [], [t_xt[xi]])
                if kind == "own":
                    dst_fn = lambda kc, t=t, g=g: hxT[:, g, kc, t * 128:(t + 1) * 128]
                    dtoks = [t_hx[g]]
                else:
                    dst_fn = lambda kc, t=t: hxg[:, kc, t * 128:(t + 1) * 128]
                    dtoks = [t_hxg]
                norm_transpose(xt[xi], xi, avec, shcol, dst_fn, dtoks)
                yield t

        def front_proj(kind, g, bs):
            ntile = 2 if kind == "ctx" else 4
            n = ntile * 128
            keyoff = SEQ if kind == "ctx" else g * 512
            if kind == "own":
                rhs_fn = lambda kc, g=g: hxT[:, g, kc, :]
                rtoks = [t_hx[g]]
                lhs_fn = lambda kc, t, g=g: hxT[:, g, kc, t * 128:(t + 1) * 128]
            else:
                rhs_fn = lambda kc, n=n: hxg[:, kc, 0:n]
                rtoks = [t_hxg]
                lhs_fn = lambda kc, t: hxg[:, kc, t * 128:(t + 1) * 128]
            if kind == "ctx":
                Tc, Ts, t_tab = Tkc[:, 0, :], Tkc[:, 1, :], t_Tkc
            else:
                self.cp("dve", Tk4[0:64],
                        GT[0:64, 1, :, g * 8:(g + 1) * 8].unsqueeze(3).broadcast_to([64, 2, 8, 64]),
                        [t_GT], [t_Tk])
                Tc, Ts, t_tab = Tk[:, 0, :], Tk[:, 1, :], t_Tk
            gi = 8 if kind == "ctx" else g
            for kvh in range(2):
                b0 = proj_fm(ws1, t_ws1, C_AK + kvh * 128, 128, rhs_fn, n, rtoks)
                b1 = proj_fm(ws1, t_ws1, C_AKP + kvh * 128, 128, rhs_fn, n, rtoks)
                rope_norm(b0, b1, n, Tc, Ts, t_tab, KT[:, kvh, keyoff:keyoff + n], [t_kv[gi]],
                          float(128 * EPS))
            for t in range(ntile):
                b = self.bank()
                for kc in range(8):
                    self.mm(ps[b][:, 0:256], lhs_fn(kc, t), ws1[:, kc, C_AV:C_AV + 256],
                            kc == 0, kc == 7, rtoks + [t_ws1], [pt[b]])
                self.cp("act", V[:, keyoff // 128 + t, :], ps[b][:, 0:256], [pt[b]], [t_kv[gi]])
            if kind == "own":
                return
            for t in range(ntile):
                b = self.bank()
                for kc in range(8):
                    self.mm(ps[b], lhs_fn(kc, t), ws1[:, kc, C_GK:C_GK + 512],
                            kc == 0, kc == 7, rtoks + [t_ws1], [pt[b]])
                self.cp("dve", bs["gkt"][:, t, :], ps[b], [pt[b]], [bs["t_gkt"]])
                for hf in range(2):
                    b = self.bank()
                    for kc in range(8):
                        self.mm(ps[b], lhs_fn(kc, t),
                                ws1[:, kc, C_GV + hf * 512:C_GV + (hf + 1) * 512],
                                kc == 0, kc == 7, rtoks + [t_ws1], [pt[b]])
                    self.cp("act", bs["gvt"][:, t, hf * 512:(hf + 1) * 512], ps[b], [pt[b]],
                            [bs["t_gvt"]])
            dirs = ("A", "B") if kind == "ctx" else ("B",)
            for X in dirs:
                c0 = C_LRA if X == "A" else C_LRB
                b = proj_fm(ws1, t_ws1, c0, 16, rhs_fn, n, rtoks)
                self.cp("dve", bs["lrT"][X][0:16, 0:n], ps[b][0:16, 0:n], [pt[b]], [bs["t_lrT"][X]])

        def steps(kind, g, bs, inter=None):
            ntile = 2 if kind == "ctx" else 4
            dirs = ("A", "B") if kind == "ctx" else ("B",)
            args = []
            for X in dirs:
                order = range(ntile) if X == "A" else range(ntile - 1, -1, -1)
                for t in order:
                    args.append(((X, False, bs["lrT"][X][0:17, t * 128:(t + 1) * 128], bs["gkt"][:, t, :],
                                  bs["gvt"][:, t, :], [bs["t_lrT"][X], bs["t_gkt"], bs["t_gvt"]]), {}))
            gla_run(args, inter)

        def front(kind, g, bs):
            for _ in front_tiles(kind, g):
                pass
            front_proj(kind, g, bs)

        front("ctx", 8, bsets[0])
        ada_part2()
        for X in ("A", "B"):
            self.memset("dve", bsets[1]["lrT"][X], 1.0, [t_wada[0]])
        front("oth", 7, bsets[1])
        steps("ctx", 8, bsets[0], front_tiles("oth", 6))
        front_proj("oth", 6, bsets[0])
        steps("oth", 7, bsets[1], front_tiles("oth", 5))
        front_proj("oth", 5, bsets[1])
        steps("oth", 6, bsets[0], front_tiles("oth", 4))
        front_proj("oth", 4, bsets[0])
        steps("oth", 5, bsets[1], front_tiles("own", 0))
        front_proj("own", 0, None)
        steps("oth", 4, bsets[0], front_tiles("own", 1))
        front_proj("own", 1, None)
        for g in (2, 3):
            front("own", g, None)
        self.dump("modT", modT, [128, 48, 2], F32, t_mod)
        self.dump("gt1bc", gt1bc, [128, 1024], F32, t_gt1)
        self.dump("KT", KT, [128, 2, NKEY], BF16, t_kv)
        self.dump("V", V, [128, 34, 256], BF16, t_kv)
        self.dump("hxT", hxT, [128, 4, 8, 512], BF16, t_hx)
        self.dump("SA", SA, [128, 4, 256], F32, t_S["A"])
        self.dump("SB", SB, [128, 4, 256], F32, t_S["B"])

        if "stop_s1" in self.dbg:
            return self.finish_stub(t_mod, modT)

        P.barrier()
        self.reset_region("R1", "R3")
        self.use("R1")
        AG = A([4, 2, 8, 512], BF16, "AG")
        self.use("R3")
        wq = A([8, 1024], BF16, "wq"); t_wq = Tok("wq"); c_wq = P.chan()
        Qbs = [A([4, 512], BF16, "Qb%d" % i) for i in range(2)]
        t_Qbs = [Tok("Qb%d" % i) for i in range(2)]
        Tq = A([2, 512], F32, "Tq"); t_Tq = Tok("Tq")
        r_sq = A([512], BF16, "r_sq2"); t_rsq = Tok("r_sq2")
        r_rs = A([512], F32, "r_rs2"); t_rrs = Tok("r_rs2")
        r_t1 = A([512], F32, "r_t12"); t_rt1 = Tok("r_t12")
        r_t2 = A([512], F32, "r_t22"); t_rt2 = Tok("r_t22")
        PTN = 4
        PT = [A([512], BF16, "PT%d" % i) for i in range(PTN)]
        t_PT = [Tok("PT%d" % i) for i in range(PTN)]
        sst = r_t2; t_sst = t_rt2
        ones32 = A([128], F32, "ones32"); t_ones32 = Tok("ones32")
        self.memset("dve", ones32, 1.0 / 32.0, [t_ones32])
        rec = A([512], F32, "rec"); t_rec = Tok("rec")
        Tq4 = Tq.rearrange("p c (r w) -> p c r w", r=8)
        self.cp("dve", Tq4[64:128], GT[64:128, 0, :, 0:64].unsqueeze(2).broadcast_to([64, 2, 8, 64]),
                [t_GT], [t_Tq])
        self.set_rot([0, 1, 2, 3])
        iters = [(hh, blk) for hh in range(2) for blk in range(4)]
        self.wq_loaded = -1

        def emit_q(it):
            hh, blk = iters[it]
            if self.wq_loaded != hh:
                self.wq_loaded = hh
                for kc in range(8):
                    self.dma("pool", wq[:, kc, 0:512],
                             w_in_v[:, kc, C_AQ + hh * 512:C_AQ + (hh + 1) * 512], c_wq, [], [t_wq])
                    self.dma("pool", wq[:, kc, 512:1024],
                             w_in_v[:, kc, C_AQP + hh * 512:C_AQP + (hh + 1) * 512], c_wq, [], [t_wq])
            self.cp("dve", Tq4[0:64],
                    GT[0:64, 0, :, blk * 8:(blk + 1) * 8].unsqueeze(3).broadcast_to([64, 2, 8, 64]),
                    [t_GT], [t_Tq])
            rhs_fn = lambda kc: hxT[:, blk, kc, :]
            for hl in range(4):
                b0 = proj_fm(wq, t_wq, hl * 128, 128, rhs_fn, 512, [t_hx[blk]])
                b1 = proj_fm(wq, t_wq, 512 + hl * 128, 128, rhs_fn, 512, [t_hx[blk]])
                rope_norm(b0, b1, 512, Tq[:, 0, :], Tq[:, 1, :], t_Tq, Qbs[it % 2][:, hl, :],
                          [t_Qbs[it % 2]], float(128 * EPS))

        self.pcount = 0

        def emit_unit(it, hl):
            hh, blk = iters[it]
            Qb, t_Qb = Qbs[it % 2], t_Qbs[it % 2]
            h = hh * 4 + hl
            kvh = h // 4
            bo = 4 + (self.pcount % 2) * 2
            bsum = bo + 1
            self.pcount += 1

            def smm(kt):
                b = self.bank()
                gi = 8 if kt >= 32 else kt // 4
                self.mm(ps[b], KT[:, kvh, kt * 128:(kt + 1) * 128], Qb[:, hl, :], True, True,
                        [t_kv[gi], t_Qb], [pt[b]])
                return b
            bcur = smm(0)
            bnext = None
            for kt in range(34):
                pi = kt % PTN
                self.act(PT[pi], ps[bcur], AF.Exp, [pt[bcur]], [t_PT[pi]])
                if kt + 1 < 34:
                    bnext = smm(kt + 1)
                gi = 8 if kt >= 32 else kt // 4
                self.mm(ps[bo], V[:, kt, kvh * 128:(kvh + 1) * 128], PT[pi], kt == 0, kt == 33,
                        [t_kv[gi], t_PT[pi]], [pt[bo]])
                self.mm(ps[bsum], onesb, PT[pi], kt == 0, kt == 33,
                        [t_onesb, t_PT[pi]], [pt[bsum]])
                bcur = bnext
            self.recip(rec, ps[bsum], [pt[bsum]], [t_rec])
            self.tt("dve", AG[:, blk, 0, h, :], ps[bo], rec, ALU.mult, [pt[bo], t_rec],
                    [t_ag[blk]])

        emit_q(0)
        for it in range(len(iters)):
            emit_unit(it, 0)
            emit_unit(it, 1)
            if it + 1 < len(iters):
                emit_q(it + 1)
            emit_unit(it, 2)
            emit_unit(it, 3)
        self.dump("attnT", AG[:, :, 0, :, :], [128, 4, 8, 512], BF16, t_ag)
        if "stop_att" in self.dbg:
            return self.finish_stub(t_mod, modT)

        P.barrier()
        self.reset_region("R2", "R3")
        self.set_rot([0, 1, 2, 3, 4, 5, 6, 7])
        self.use("R2")
        wg = A([8, 2080], BF16, "wg"); t_wg = Tok("wg"); c_wg = P.chan()
        self.use("R3")
        WG0 = 768
        gkT = A([4, 512], BF16, "gkT"); t_gkT = Tok("gkT")
        gqT = A([4, 512], BF16, "gqT"); t_gqT = Tok("gqT")
        gkt = A([4, 512], BF16, "gkt2"); t_gkt = Tok("gkt2")
        gvt = A([4, 1024], BF16, "gvt2"); t_gvt = Tok("gvt2")
        lrT = {"A": A([512], BF16, "lrTA2", parts=32), "B": A([512], BF16, "lrTB2", parts=32)}
        t_lrT = {"A": Tok("lrTA2"), "B": Tok("lrTB2")}
        gsets = [make_gset(10 + i, True) for i in range(2)]
        for X in ("A", "B"):
            self.memset("dve", lrT[X], 1.0, [t_lrT[X]])
        for kc in range(8):
            self.dma("pool", wg[:, kc, :], w_in_v[:, kc, WG0:WG0 + 2080], c_wg, [], [t_wg])

        def gla_group(X, g):
            rhs_fn = lambda kc: hxT[:, g, kc, :]
            rtoks = [t_hx[g]]
            for h in range(4):
                b = proj_fm(wg, t_wg, C_GK - WG0 + h * 128, 128, rhs_fn, 512, rtoks)
                self.cp("act", gkT[:, h, :], ps[b], [pt[b]], [t_gkT])
                b = proj_fm(wg, t_wg, C_GQ - WG0 + h * 128, 128, rhs_fn, 512, rtoks)
                self.ts("dve", gqT[:, h, :], ps[b], float(128.0 ** -0.5), None, ALU.mult, None,
                        [pt[b]], [t_gqT])
            for t in range(4):
                b = self.bank()
                for kc in range(8):
                    self.mm(ps[b], hxT[:, g, kc, t * 128:(t + 1) * 128],
                            wg[:, kc, C_GK - WG0:C_GK - WG0 + 512], kc == 0, kc == 7,
                            rtoks + [t_wg], [pt[b]])
                self.cp("dve", gkt[:, t, :], ps[b], [pt[b]], [t_gkt])
                for hf in range(2):
                    b = self.bank()
                    for kc in range(8):
                        self.mm(ps[b], hxT[:, g, kc, t * 128:(t + 1) * 128],
                                wg[:, kc, C_GV - WG0 + hf * 512:C_GV - WG0 + (hf + 1) * 512],
                                kc == 0, kc == 7, rtoks + [t_wg], [pt[b]])
                    self.cp("act", gvt[:, t, hf * 512:(hf + 1) * 512], ps[b], [pt[b]], [t_gvt])
            c0 = (C_LRA if X == "A" else C_LRB) - WG0
            b = proj_fm(wg, t_wg, c0, 16, rhs_fn, 512, rtoks)
            self.cp("dve", lrT[X][0:16, :], ps[b][0:16, :], [pt[b]], [t_lrT[X]])
            order = range(4) if X == "A" else range(3, -1, -1)
            args = []
            for t in order:
                osl = AG[:, g, 1, :, :].rearrange("p a b -> p (a b)")[:, t * 1024:(t + 1) * 1024]
                args.append(((X, True, lrT[X][0:17, t * 128:(t + 1) * 128], gkt[:, t, :], gvt[:, t, :],
                              [t_lrT[X], t_gkt, t_gvt, t_gkT, t_gqT]),
                             dict(qT=gqT[:, :, t * 128:(t + 1) * 128], kT=gkT[:, :, t * 128:(t + 1) * 128],
                                  o_dst=osl, o_add=(X == "A"), o_tok=t_ag[g])))
            gla_run(args)

        for g in (3, 2, 1, 0):
            gla_group("B", g)
        for g in (0, 1, 2, 3):
            gla_group("A", g)
        self.dump("osum", AG[:, :, 1, :, :], [128, 4, 8, 512], BF16, t_ag)
        if "stop_gla" in self.dbg:
            return self.finish_stub(t_mod, modT)

        P.barrier()
        self.reset_region("S", "R2", "R3")
        self.set_rot([0, 1, 2, 3, 4, 5, 6, 7])
        self.use("R3")
        wgo = A([8, 1024], BF16, "wgo"); t_wgo = Tok("wgo"); c_wgo = P.chan()
        for kc in range(8):
            self.dma("pool", wgo[:, kc, :], w_in_v[:, kc, C_GO:C_GO + 1024], c_wgo, [], [t_wgo])
        self.use("R2")
        wga = A([8, 1024], BF16, "wga"); t_wga = Tok("wga"); c_wga = P.chan()
        wba = A([8, 1024], BF16, "wba"); t_wba = Tok("wba"); c_wba = P.chan()
        w_bra_v = w_bra.rearrange("(k p) n -> p k n", p=128)
        w_brg_v = w_brg.rearrange("(k p) n -> p k n", p=128)
        w_out_v = w_out.rearrange("(k p) n -> p k n", p=128)
        for kc in range(8):
            self.dma("pool", wga[:, kc, :], w_in_v[:, kc, C_GA:C_GA + 1024], c_wga, [], [t_wga])
            self.dma("pool", wba[:, kc, :], w_bra_v[:, kc, :], c_wba, [], [t_wba])
        self.use("R3")
        gx = A([4, 1024], F32, "gx"); t_gx = Tok("gx")
        sgs = [A([1024], F32, "sg%d" % i) for i in range(2)]
        t_sgs = [Tok("sg%d" % i) for i in range(2)]
        ssgs = [A([8], F32, "ssg%d" % i) for i in range(2)]
        t_ssgs = [Tok("ssg%d" % i) for i in range(2)]
        o2j = A([256], BF16, "o2j"); t_o2j = Tok("o2j")
        acnt = 0
        for blk in range(4):
            for t in range(4):
                sg, t_sg = sgs[acnt % 2], t_sgs[acnt % 2]
                ssg, t_ssg = ssgs[acnt % 2], t_ssgs[acnt % 2]
                acnt += 1
                osl = AG[:, blk, 1, :, :].rearrange("p a b -> p (a b)")[:, t * 1024:(t + 1) * 1024]
                for hf in range(2):
                    b = self.bank()
                    for kc in range(8):
                        self.mm(ps[b], hxT[:, blk, kc, t * 128:(t + 1) * 128],
                                wgo[:, kc, hf * 512:(hf + 1) * 512], kc == 0, kc == 7,
                                [t_hx[blk], t_wgo], [pt[b]])
                    sgh = sg[:, hf * 512:(hf + 1) * 512]
                    self.act(sgh, ps[b], AF.Silu, [pt[b]], [t_sg])
                    sgh3 = sgh.rearrange("p (h e) -> p h e", h=2)
                    self.tt("dve", sgh3, sgh3, glan.unsqueeze(1).broadcast_to([128, 2, 256]), ALU.mult,
                            [t_sg, t_glan], [t_sg])
                for h in range(4):
                    self.act(o2j, osl[:, h * 256:(h + 1) * 256], AF.Square,
                             [t_ag[blk]], [t_o2j, t_ssg], accum=ssg[:, h:h + 1])
                self.ts("pool", ssg[:, 0:4], ssg[:, 0:4], float(1.0 / 256), float(EPS), ALU.mult, ALU.add,
                        [t_ssg], [t_ssg])
                self.tt("pool", ssg[:, 4:8], ssg[:, 0:4], nhalf.broadcast_to([128, 4]), ALU.pow,
                        [t_ssg, t_nhalf], [t_ssg])
                for h in range(4):
                    self.stt("dve", gx[:, t, h * 256:(h + 1) * 256], osl[:, h * 256:(h + 1) * 256],
                             ssg[:, 4 + h:5 + h], sg[:, h * 256:(h + 1) * 256], ALU.mult, ALU.mult,
                             [t_ag[blk], t_ssg, t_sg], [t_gx])
            for t in range(4):
                for half in range(2):
                    b = self.bank()
                    for q in range(4):
                        kc = half * 4 + q
                        self.tr(ps[b][:, q * 128:(q + 1) * 128], gx[:, t, kc * 128:(kc + 1) * 128], ident,
                                [t_gx, t_ident], [pt[b]])
                    dstv = AG[:, blk, 1, half * 4:half * 4 + 4, t * 128:(t + 1) * 128]
                    self.cp("act" if half == 0 else "dve", dstv,
                            ps[b].rearrange("p (q t) -> p q t", q=4), [pt[b]], [t_ag[blk]])
        self.dump("glaT", AG[:, :, 1, :, :], [128, 4, 8, 512], BF16, t_ag)
        self.chk("a")
        P.barrier()
        self.reset_region("S", "R3")
        self.use("R3")
        yT = A([8, 512], BF16, "yT"); t_yT = Tok("yT")
        sgts = [A([512], F32, "sgt%d" % i) for i in range(2)]
        t_sgts = [Tok("sgt%d" % i) for i in range(2)]
        self.sgc = 0
        wo = A([8, 1024], BF16, "wo"); t_wo = Tok("wo"); c_wo = P.chan()
        for kc in range(8):
            self.dma("pool", wo[:, kc, :], w_out_v[:, kc, :], c_wo, [], [t_wo])
        for kc in range(8):
            self.tt("dve", wo[:, kc, :], wo[:, kc, :], gt1bc, ALU.mult, [t_wo, t_gt1], [t_wo])

        def gated_proj(blk, wgate, t_wgate, wbr, t_wbr, src_half, fc, dst, dst_toks, add_src=None,
                       add_toks=()):
            bg = proj_fm(wgate, t_wgate, fc * 128, 128, lambda kc: hxT[:, blk, kc, :], 512, [t_hx[blk]])
            sgt, t_sgt = sgts[self.sgc % 2], t_sgts[self.sgc % 2]
            self.sgc += 1
            self.act(sgt, ps[bg], AF.Sigmoid, [pt[bg]], [t_sgt])
            bp = proj_fm(wbr, t_wbr, fc * 128, 128, lambda kc: AG[:, blk, src_half, kc, :], 512,
                         [t_ag[blk]])
            if add_src is None:
                self.tt("dve", dst, ps[bp], sgt, ALU.mult, [pt[bp], t_sgt], dst_toks)
            else:
                self.tt("dve", sgt, ps[bp], sgt, ALU.mult, [pt[bp], t_sgt], [t_sgt])
                self.tt("dve", dst, sgt, add_src, ALU.add, [t_sgt] + list(add_toks), dst_toks)

        for blk in range(4):
            for fc in range(8):
                gated_proj(blk, wga, t_wga, wba, t_wba, 0, fc, yT[:, fc, :], [t_yT])
            self.cp("dve", AG[:, blk, 0, :, :], yT, [t_yT], [t_ag[blk]])
        self.dump("y1T", AG[:, :, 0, :, :], [128, 4, 8, 512], BF16, t_ag)
        self.chk("b1")
        P.barrier()
        self.reset_region("S", "R2")
        self.use("R2")
        wgg = A([8, 1024], BF16, "wgg"); t_wgg = Tok("wgg"); c_wgg = P.chan()
        wbg = A([8, 1024], BF16, "wbg"); t_wbg = Tok("wbg"); c_wbg = P.chan()
        self.use("R3")
        for kc in range(8):
            self.dma("pool", wgg[:, kc, :], w_in_v[:, kc, C_GG:C_GG + 1024], c_wgg, [], [t_wgg])
            self.dma("pool", wbg[:, kc, :], w_brg_v[:, kc, :], c_wbg, [], [t_wbg])
        xt = [A([1024], F32, "xtb%d" % i) for i in range(2)]
        t_xt = [Tok("xtb%d" % i) for i in range(2)]
        c_xt = [P.chan() for _ in range(2)]
        xn = A([1024], F32, "xn2"); t_xn = Tok("xn2")
        sqj = A([1024], BF16, "sqj2"); t_sqj = Tok("sqj2")
        x2v = [AG[:, blk].rearrange("p a b c -> p (a b c)").bitcast(F32).rearrange(
            "p (t d) -> p t d", t=4) for blk in range(4)]
        xcount = 0
        for blk in range(4):
            for fc in range(8):
                gated_proj(blk, wgg, t_wgg, wbg, t_wbg, 1, fc, yT[:, fc, :], [t_yT],
                           add_src=AG[:, blk, 0, fc, :], add_toks=[t_ag[blk]])
            for t in range(4):
                xi = xcount % 2
                xcount += 1
                r0 = blk * 512 + t * 128
                self.dma("sp", xt[xi], xs[r0:r0 + 128, :], c_xt[xi], [], [t_xt[xi]])
                for hf in range(2):
                    b = self.bank()
                    for kc in range(8):
                        self.mm(ps[b], yT[:, kc, t * 128:(t + 1) * 128], wo[:, kc, hf * 512:(hf + 1) * 512],
                                kc == 0, kc == 7, [t_yT, t_wo], [pt[b]])
                    xh = xt[xi][:, hf * 512:(hf + 1) * 512]
                    self.tt("dve", x2v[blk][:, t, hf * 512:(hf + 1) * 512], ps[b], xh, ALU.add,
                            [pt[b], t_xt[xi]], [t_ag[blk]])
            for t in range(4):
                norm_transpose(x2v[blk][:, t, :], 0, a2, lambda kc: modT[:, 24 + kc, 0:1],
                               lambda kc, t=t, blk=blk: hxT[:, blk, kc, t * 128:(t + 1) * 128],
                               [t_hx[blk]], from_sbuf_tok=t_ag[blk])
        self.dump("x2", AG.rearrange("p a b c d -> p (a b c d)").bitcast(F32), [128, 16384], F32, t_ag)
        self.dump("hmT", hxT, [128, 4, 8, 512], BF16, t_hx)
        self.chk("b2")

        P.barrier()
        self.reset_region("S", "R2", "R3")
        self.use("R2")
        w1q = [A([8, 1024], BF16, "w1q%d" % i) for i in range(2)]
        self.use("R3")
        w2q = [A([8, 1024], BF16, "w2q%d" % i) for i in range(2)]
        t_w1q = [Tok("w1q%d" % i) for i in range(2)]
        t_w2q = [Tok("w2q%d" % i) for i in range(2)]
        c_w1q = [P.chan() for _ in range(2)]
        c_w2q = [P.chan() for _ in range(2)]
        h1 = A([8, 512], BF16, "h1"); t_h1 = Tok("h1")
        rl = [A([512], F32, "rl%d" % i) for i in range(2)]
        t_rl = [Tok("rl%d" % i) for i in range(2)]
        self.use("S")
        tmp = A([512], F32, "mtmp"); t_tmp = Tok("mtmp")
        ost = [A([1024], F32, "ost%d" % i) for i in range(2)]
        t_ost = [Tok("ost%d" % i) for i in range(2)]
        c_ost = [P.chan() for _ in range(2)]
        self.final_chans.extend(c_ost)
        w_m1_v = w_m1.rearrange("(k p) n -> p k n", p=128)
        w_m2_v = w_m2.rearrange("(k p) n -> p k n", p=128)
        ocount = 0
        rcount = 0
        for q in range(4):
            wi = q % 2
            for kc in range(8):
                self.dma("pool", w1q[wi][:, kc, :], w_m1_v[:, kc, q * 1024:(q + 1) * 1024], c_w1q[wi],
                         [], [t_w1q[wi]])
            for kc in range(8):
                self.dma("pool", w2q[wi][:, kc, :], w_m2_v[:, q * 8 + kc, :], c_w2q[wi], [], [t_w2q[wi]])
            for blk in range(4):
                for fc in range(8):
                    b = proj_fm(w1q[wi], t_w1q[wi], fc * 128, 128, lambda kc: hxT[:, blk, kc, :], 512,
                                [t_hx[blk]])
                    ri = rcount % 2
                    rcount += 1
                    self.act(rl[ri], ps[b], AF.Relu, [pt[b]], [t_rl[ri]])
                    self.tt("dve", h1[:, fc, :], rl[ri], rl[ri], ALU.mult, [t_rl[ri]], [t_h1])
                for t in range(4):
                    for hf in range(2):
                        b = self.bank()
                        for kc in range(8):
                            self.mm(ps[b], h1[:, kc, t * 128:(t + 1) * 128],
                                    w2q[wi][:, kc, hf * 512:(hf + 1) * 512], kc == 0, kc == 7,
                                    [t_h1, t_w2q[wi]], [pt[b]])
                        self.tt("dve", tmp, ps[b], gt2bc[:, hf * 512:(hf + 1) * 512], ALU.mult,
                                [pt[b], t_gt2], [t_tmp])
                        x2h = x2v[blk][:, t, hf * 512:(hf + 1) * 512]
                        if q < 3:
                            self.tt("dve", x2h, tmp, x2h, ALU.add, [t_tmp, t_ag[blk]], [t_ag[blk]])
                        else:
                            oi = ocount % 2
                            self.tt("dve", ost[oi][:, hf * 512:(hf + 1) * 512], tmp, x2h, ALU.add,
                                    [t_tmp, t_ag[blk]], [t_ost[oi]])
                    if q == 3:
                        oi = ocount % 2
                        ocount += 1
                        r0 = blk * 512 + t * 128
                        self.dma("sp", out_d[r0:r0 + 128, :], ost[oi], c_ost[oi], [t_ost[oi]], [])
        self.finalize()
        return self.nc

    def finish_stub(self, tok, ap):
        self.reset_region("R3")
        z = self.alloc([1024], F32, "zstub", region="R3")
        tz = Tok("z")
        self.memset("dve", z, 0.0, [tz])
        c = self.P.chan()
        self.final_chans.append(c)
        for i in range(16):
            self.dma("sp", self.out_d[i * 128:(i + 1) * 128, :], z, c, [tz], [])
        self.finalize()
        return self.nc

    def finalize(self):
        self.P.emit(self.nc, self.stack, self.final_chans)
        self.stack.close()


def _perm_half_swap(nheads):
    idx = []
    for h in range(nheads):
        for d in range(128):
            axis, rem = divmod(d, 64)
            half, f = divmod(rem, 32)
            idx.append(h * 128 + axis * 64 + (1 - half) * 32 + f)
    return np.array(idx)


def _consts(h):
    ident = np.eye(128, dtype=np.float32)
    r = np.arange(128)[:, None]
    c = np.arange(128)[None, :]
    s = np.float32(-1.0 / 16.0)
    tri = np.zeros((128, 4, 128), np.float32)
    tri[:, 0, :] = (r <= c) * s
    tri[:, 1, :] = (r > c) * s
    tri[:, 2, :] = (r >= c) * s
    tri[:, 3, :] = (r < c) * s
    msk = np.zeros((128, 2, 128), np.float32)
    msk[:, 0, :] = (r <= c)
    msk[:, 1, :] = (r >= c)
    half = 64
    freqs = (10000.0 ** (-np.arange(0, half, 2, dtype=np.float32) / half)).astype(np.float32)
    ropec = np.zeros((128, 2, 72), np.float32)
    for d in range(128):
        axis, rem = divmod(d, 64)
        hf, f = divmod(rem, 32)
        for i in range(64):
            pos = i if h == 0 else 63 - i
            ang = np.float32(pos) * freqs[f]
            ropec[d, 0, i] = np.cos(ang)
            ropec[d, 1, i] = (-np.sin(ang)) if hf == 0 else np.sin(ang)
        ropec[d, 0, 64:72] = 1.0
        ropec[d, 1, 64:72] = 0.0
    return ident, tri, msk, ropec


_NC_CACHE = {}


def _get_nc(dbg=()):
    key = tuple(sorted(dbg))
    if key not in _NC_CACHE:
        b = Builder(dbg)
        nc = b.build()
        _NC_CACHE[key] = (nc, b.dbg_out)
    return _NC_CACHE[key]


def make_in_maps(x, c, ctx, c_ctx, w_ada, b_ada, norm1, w_in, q_norm, k_norm, w_gk_fwd, b_gk_fwd,
                 w_gk_bwd, b_gk_bwd, gla_norm, w_br_attn, w_br_gla, w_out, norm2, w_mlp1, w_mlp2):
    f = lambda a: np.ascontiguousarray(np.asarray(a, dtype=np.float32))
    x = f(x); c = f(c); ctx = f(ctx); c_ctx = f(c_ctx)
    w_in0 = f(w_in)[0]
    off = np.cumsum([0, 256, 256, 512, 1024, 16, 16, 1024, 512, 1024, 1024, 1024])
    ak, av, gk, gv, lrf, lrb, aq, gq, go, ga, gg = [w_in0[:, off[i]:off[i + 1]] for i in range(11)]
    pk = _perm_half_swap(2)
    pq = _perm_half_swap(8)
    pd = _perm_half_swap(1)
    qn = f(q_norm)[0]; kn = f(k_norm)[0]
    qkg = np.ascontiguousarray(np.stack([qn, qn[pd], kn, kn[pd]], axis=1))
    win = {}
    wgk = {}
    for h in (0, 1):
        lra, lrbb = (lrf, lrb) if h == 0 else (lrb, lrf)
        win[h] = np.ascontiguousarray(np.concatenate(
            [ak, ak[:, pk], av, gk, gv, lra, lrbb, gq, aq, aq[:, pq], go, ga, gg], axis=1))
        assert win[h].shape[1] == NIN
        wf = np.concatenate([f(w_gk_fwd)[0], f(b_gk_fwd)[0][None, :]], axis=0)
        wb = np.concatenate([f(w_gk_bwd)[0], f(b_gk_bwd)[0][None, :]], axis=0)
        wgk[h] = np.ascontiguousarray(np.stack([wf, wb] if h == 0 else [wb, wf], axis=0))
    consts = {h: _consts(h) for h in (0, 1)}
    shared = dict(w_ada=f(w_ada)[0], b_ada=f(b_ada)[0], norm1=f(norm1)[0], norm2=f(norm2)[0],
                  gla_norm=f(gla_norm)[0], w_br_attn=f(w_br_attn)[0], w_br_gla=f(w_br_gla)[0],
                  w_out=f(w_out)[0], w_mlp1=f(w_mlp1)[0], w_mlp2=f(w_mlp2)[0], qkg=qkg)
    in_maps = []
    for core in range(8):
        b, h = divmod(core, 2)
        xb = x[b] if h == 0 else x[b][::-1]
        cb = ctx[b] if h == 0 else ctx[b][::-1]
        ident, tri, msk, ropec = consts[h]
        m = dict(shared)
        m.update(xs=np.ascontiguousarray(xb), ctx=np.ascontiguousarray(cb),
                 cvec=np.ascontiguousarray(np.stack([c[b], c_ctx], axis=0)),
                 w_in=win[h], wgk=wgk[h], ident=ident, tri=tri, msk=msk, ropec=ropec)
        in_maps.append(m)
    return in_maps


def assemble(results):
    out = np.empty((4, SEQ, D), np.float32)
    for core in range(8):
        b, h = divmod(core, 2)
        o = np.asarray(results[core]["out"], dtype=np.float32)
        if h == 0:
            out[b, 0:OWN] = o
        else:
            out[b, OWN:SEQ] = o[::-1]
    return out


def kernel(**inputs):
    nc, _ = _get_nc(())
    in_maps = make_in_maps(**inputs)
    res = run_bass_kernel_spmd(nc, in_maps, core_ids=list(range(8)))
    return assemble(res.results)
```

```python
import numpy as np
from contextlib import ExitStack
import concourse.bass as bass
import concourse.mybir as mybir
from concourse.bass_utils import run_bass_kernel_spmd

F32 = mybir.dt.float32
BF16 = mybir.dt.bfloat16
AF = mybir.ActivationFunctionType
ALU = mybir.AluOpType

D = 1024
SEQ = 4096
OWN = 2048
CTXL = 256
NKEY = SEQ + CTXL
EPS = 1e-6
C_AK, C_AKP, C_AV, C_GK, C_GV, C_LRA, C_LRB, C_GQ, C_AQ, C_AQP, C_GO, C_GA, C_GG = (
    0, 256, 512, 768, 1280, 2304, 2320, 2336, 2848, 3872, 4896, 5920, 6944)
NIN = 7968
SAME_WIN = 3


class Tok:
    __slots__ = ("name", "w", "r", "excl")

    def __init__(self, name, excl=False):
        self.name = name
        self.w = None
        self.r = []
        self.excl = excl


class Chan:
    def __init__(self, idx):
        self.idx = idx
        self.count = 0
        self.sem = None


class Op:
    __slots__ = ("fn", "deps", "signal", "chan")

    def __init__(self, fn, deps, chan):
        self.fn = fn
        self.deps = deps
        self.signal = False
        self.chan = chan


ENGS = ("pe", "act", "dve", "pool", "sp")


class Prog:
    def __init__(self):
        self.ops = {e: [] for e in ENGS}
        self.chans = []
        self.wm = {e: {} for e in ENGS}

    def chan(self):
        c = Chan(len(self.chans))
        self.chans.append(c)
        return c

    def add(self, eng, fn, reads=(), writes=(), chan=None):
        idx = len(self.ops[eng])
        deps = []
        for t in reads:
            if t.w is not None:
                deps.append(t.w)
            if t.excl:
                deps.extend(t.r)
        for t in writes:
            if t.w is not None:
                deps.append(t.w)
            deps.extend(t.r)
        need = []
        wm = self.wm[eng]
        best = {}
        for d in deps:
            if d[0] == "e":
                _, e2, i2 = d
                if e2 == eng:
                    if chan is not None:
                        pass
                    elif eng in ("pe", "sp"):
                        continue
                    elif idx - i2 > SAME_WIN:
                        continue
                key = ("e", e2)
                val = i2
            else:
                _, c, v = d
                if chan is not None and c is chan:
                    continue
                key = ("c", c.idx)
                val = v
            if wm.get(key, -1) >= val:
                continue
            if key not in best or best[key][0] < val:
                best[key] = (val, d)
        for key, (val, d) in best.items():
            wm[key] = val
            need.append(d)
            if d[0] == "e":
                self.ops[d[1]][d[2]].signal = True
        op = Op(fn, need, chan)
        self.ops[eng].append(op)
        if chan is None:
            ref = ("e", eng, idx)
        else:
            chan.count += 16
            ref = ("c", chan, chan.count)
        for t in reads:
            if t.excl:
                t.r = [ref]
            else:
                t.r.append(ref)
        for t in writes:
            t.w = ref
            t.r = []
        return op

    def barrier(self):
        refs = []
        for e in ENGS:
            if self.ops[e]:
                for i in range(len(self.ops[e]) - 1, -1, -1):
                    if self.ops[e][i].chan is None and self.ops[e][i].fn is not None:
                        refs.append(("e", e, i))
                        break
        for c in self.chans:
            if c.count:
                refs.append(("c", c, c.count))
        bt = Tok("barrier")
        for e in ENGS:
            need = []
            wm = self.wm[e]
            for d in refs:
                if d[0] == "e":
                    if d[1] == e:
                        continue
                    key = ("e", d[1]); val = d[2]
                else:
                    key = ("c", d[1].idx); val = d[2]
                if wm.get(key, -1) >= val:
                    continue
                wm[key] = val
                need.append(d)
                if d[0] == "e":
                    self.ops[d[1]][d[2]].signal = True
            self.ops[e].append(Op(None, need, None))

    def emit(self, nc, stack, final_chans):
        sems = {e: stack.enter_context(nc.semaphore("s_" + e)) for e in ENGS}
        for c in self.chans:
            c.sem = stack.enter_context(nc.semaphore("c%d" % c.idx))
        pref = {}
        for e in ENGS:
            cnt = 0
            p = []
            for op in self.ops[e]:
                if op.signal:
                    cnt += 1
                p.append(cnt)
            pref[e] = p
        block = stack.enter_context(nc.Block())
        handles = {"pe": block.tensor, "act": block.scalar, "dve": block.vector,
                   "pool": block.gpsimd, "sp": block.sync}

        def make(e):
            def body(eng):
                for op in self.ops[e]:
                    for d in op.deps:
                        if d[0] == "e":
                            eng.wait_ge(sems[d[1]], pref[d[1]][d[2]])
                        else:
                            eng.wait_ge(d[1].sem, d[2])
                    if op.fn is None:
                        continue
                    ins = op.fn(eng)
                    if op.signal:
                        ins.then_inc(sems[e], 1)
                    if op.chan is not None:
                        ins.then_inc(op.chan.sem, 16)
                if e == "sp":
                    for c in final_chans:
                        if c.count:
                            eng.wait_ge(c.sem, c.count)
            return body

        for e in ENGS:
            handles[e](make(e))


class StopBuild(Exception):
    pass


class Builder:
    def __init__(self, dbg=()):
        self.dbg = set(dbg)
        self.nc = bass.Bass("TRN2", target_bir_lowering=False)
        self.P = Prog()
        self.stack = ExitStack()
        self.dram = {}
        self.dbg_out = []

    def din(self, name, shape, dt=F32):
        ap = self.nc.dram_tensor(name, list(shape), dt, kind="ExternalInput").ap()
        self.dram[name] = ap
        return ap

    def init_arena(self):
        nc = self.nc
        self.ARENA_BYTES = 207 * 1024
        self.arena = self.stack.enter_context(
            nc.sbuf_tensor("arena", [128, self.ARENA_BYTES // 2], BF16))
        self.regions = {}
        self.def_region("P", 0, self.ARENA_BYTES)
        self.cur_region = "P"
        self.psum = [self.stack.enter_context(nc.psum_tensor("ps%d" % i, [128, 512], F32))[:]
                     for i in range(8)]
        self.pstok = [Tok("ps%d" % i, excl=True) for i in range(8)]
        self.rot = list(range(8))
        self.rot_i = 0

    def alloc(self, shape, dt, name="", parts=128, region=None):
        esz = 4 if dt == F32 else 2
        n = int(np.prod(shape))
        nbytes = (n * esz + 63) // 64 * 64
        if region is None:
            region = self.cur_region
        r = self.regions[region]
        off = r[1]
        r[1] += nbytes
        assert r[1] <= r[2], ("SBUF overflow", region, name, r[1] - r[2])
        v = self.arena[0:parts, off // 2: off // 2 + n * esz // 2]
        if dt == F32:
            v = v.bitcast(F32)
        if len(shape) == 2:
            v = v.rearrange("p (a b) -> p a b", a=shape[0])
        elif len(shape) == 3:
            v = v.rearrange("p (a b c) -> p a b c", a=shape[0], b=shape[1])
        elif len(shape) == 4:
            v = v.rearrange("p (a b c d) -> p a b c d", a=shape[0], b=shape[1], c=shape[2])
        return v

    def def_region(self, name, start, end):
        self.regions[name] = [start, start, end]

    def reset_region(self, *names):
        for n in names:
            self.regions[n][1] = self.regions[n][0]

    def use(self, name):
        self.cur_region = name

    def set_rot(self, banks):
        self.rot = list(banks)
        self.rot_i = 0

    def bank(self):
        b = self.rot[self.rot_i % len(self.rot)]
        self.rot_i += 1
        return b

    def mm(self, out, lhsT, rhs, start, stop, reads, writes, tile_position=None):
        if tile_position is not None:
            return self.P.add("pe", lambda e: e.matmul(out, lhsT, rhs, start=start, stop=stop,
                                                       tile_position=tile_position), reads, writes)
        return self.P.add("pe", lambda e: e.matmul(out, lhsT, rhs, start=start, stop=stop),
                          reads, writes)

    def tr(self, out, in_, ident, reads, writes):
        return self.P.add("pe", lambda e: e.transpose(out, in_, ident), reads, writes)

    def act(self, out, in_, func, reads, writes, bias=None, scale=None, accum=None):
        kw = {}
        if bias is not None:
            kw["bias"] = bias
        if scale is not None:
            kw["scale"] = scale
        if accum is not None:
            kw["accum_out"] = accum
        return self.P.add("act", lambda e: e.activation(out, in_, func, **kw), reads, writes)

    def tt(self, eng, out, in0, in1, op, reads, writes):
        return self.P.add(eng, lambda e: e.tensor_tensor(out, in0, in1, op), reads, writes)

    def ts(self, eng, out, in0, s1, s2, op0, op1, reads, writes):
        if op1 is None:
            return self.P.add(eng, lambda e: e.tensor_scalar(out, in0, s1, None, op0),
                              reads, writes)
        return self.P.add(eng, lambda e: e.tensor_scalar(out, in0, s1, s2, op0, op1),
                          reads, writes)

    def stt(self, eng, out, in0, scalar, in1, op0, op1, reads, writes):
        return self.P.add(eng, lambda e: e.scalar_tensor_tensor(out, in0, scalar, in1, op0, op1),
                          reads, writes)

    def cp(self, eng, out, in_, reads, writes):
        if eng == "act":
            return self.P.add("act", lambda e: e.copy(out, in_), reads, writes)
        return self.P.add(eng, lambda e: e.tensor_copy(out, in_), reads, writes)

    def recip(self, out, in_, reads, writes):
        return self.P.add("dve", lambda e: e.reciprocal(out, in_), reads, writes)

    def memset(self, eng, ap, val, writes):
        return self.P.add(eng, lambda e: e.memset(ap, val), (), writes)

    def dma(self, q, out, in_, chan, reads, writes, slow=False):
        if slow:
            return self.P.add(q, lambda e: e.dma_start(out=out, in_=in_,
                                                       allow_slow_non_contiguous=True),
                              reads, writes, chan=chan)
        return self.P.add(q, lambda e: e.dma_start(out=out, in_=in_), reads, writes, chan=chan)

    def dump(self, name, ap, shape, dt, tok):
        if name not in self.dbg:
            return
        o = self.nc.dram_tensor("dbg_" + name, list(shape), dt, kind="ExternalOutput").ap()
        c = self.P.chan()
        self.final_chans.append(c)
        self.dma("sp", o, ap, c, [tok] if not isinstance(tok, (list, tuple)) else list(tok), [])
        self.dbg_out.append("dbg_" + name)

    def build(self):
        try:
            return self._build_body()
        except StopBuild:
            return self.finish_stub(None, None)

    def chk(self, label):
        if ("stop_" + label) in self.dbg:
            raise StopBuild()

    def _build_body(self):
        nc = self.nc
        P = self.P
        din = self.din
        self.final_chans = []
        xs = din("xs", [SEQ, D])
        ctx = din("ctx", [CTXL, D])
        cvec = din("cvec", [2, D])
        w_ada = din("w_ada", [D, 6 * D])
        b_ada = din("b_ada", [6 * D])
        norm1 = din("norm1", [D])
        norm2 = din("norm2", [D])
        w_in = din("w_in", [D, NIN])
        wgk = din("wgk", [2, 17, 512])
        qkg = din("qkg", [128, 4])
        gla_norm = din("gla_norm", [256])
        w_bra = din("w_br_attn", [D, D])
        w_brg = din("w_br_gla", [D, D])
        w_out = din("w_out", [D, D])
        w_m1 = din("w_mlp1", [D, 4 * D])
        w_m2 = din("w_mlp2", [4 * D, D])
        ident_d = din("ident", [128, 128])
        tri_d = din("tri", [128, 4, 128])
        msk_d = din("msk", [128, 2, 128])
        ropec_d = din("ropec", [128, 2, 72])
        out_d = nc.dram_tensor("out", [OWN, D], F32, kind="ExternalOutput").ap()
        self.out_d = out_d

        self.init_arena()
        A = self.alloc
        ps = self.psum
        pt = self.pstok

        ident = A([128], F32, "ident"); t_ident = Tok("ident")
        onesf = A([128], F32, "onesf"); t_onesf = Tok("onesf")
        onesb = A([128], BF16, "onesb"); t_onesb = Tok("onesb")
        tri = A([4, 128], F32, "tri"); t_tri = Tok("tri")
        msk = A([2, 128], F32, "msk"); t_msk = Tok("msk")
        ropec = A([2, 72], F32, "ropec"); t_ropec = Tok("ropec")
        qkgs = A([4], F32, "qkg"); t_qkg = Tok("qkg")
        GT = A([2, 2, 72], F32, "GT"); t_GT = Tok("GT")
        scT = A([8, 2], BF16, "scT"); t_scT = Tok("scT")
        modT = A([48, 2], F32, "modT"); t_mod = Tok("mod")
        a1 = A([8], F32, "a1"); ac = A([8], F32, "ac"); a2 = A([8], F32, "a2"); t_av = Tok("avec"); t_av2 = Tok("avec2")
        gt1bc = A([1024], F32, "gt1bc"); t_gt1 = Tok("gt1bc")
        gt2bc = A([1024], F32, "gt2bc"); t_gt2 = Tok("gt2bc")
        glan = A([256], F32, "glan"); t_glan = Tok("glan")
        wgka = A([2, 512], BF16, "wgka", parts=32); t_wgka = Tok("wgka")
        nhalf = A([1], F32, "nhalf"); t_nhalf = Tok("nhalf")
        tiny = A([16], F32, "tiny"); t_tiny = Tok("tiny")

        for (dst, src, tk) in ((ident, ident_d, t_ident), (tri, tri_d, t_tri), (msk, msk_d, t_msk),
                               (ropec, ropec_d, t_ropec), (qkgs, qkg, t_qkg)):
            self.dma("sp", dst, src, P.chan(), [], [tk])
        vst = A([128], F32, "vst", parts=128); t_vst = Tok("vst")
        vT = A([80], F32, "vT"); t_vT = Tok("vT")
        self.memset("dve", vst, 0.0, [t_vst])
        c_v = P.chan()
        self.dma("sp", vst[0:48], b_ada.rearrange("(k p) -> k p", p=128), c_v, [], [t_vst])
        self.dma("sp", vst[48:56], norm1.rearrange("(k p) -> k p", p=128), c_v, [], [t_vst])
        self.dma("sp", vst[56:64], norm2.rearrange("(k p) -> k p", p=128), c_v, [], [t_vst])
        self.dma("sp", vst[64:80], cvec.rearrange("r (k p) -> (r k) p", p=128), c_v, [], [t_vst])
        self.dma("sp", glan, gla_norm.partition_broadcast(128), P.chan(), [], [t_glan])
        c_wgk = P.chan()
        self.dma("pool", wgka[0:17], wgk.rearrange("x r n -> r x n"), c_wgk, [], [t_wgka])
        self.memset("dve", onesf, 1.0, [t_onesf])
        self.memset("dve", onesb, 1.0, [t_onesb])
        self.memset("dve", nhalf, -0.5, [t_nhalf])
        self.tr(ps[0][:, 0:128], vst, ident, [t_vst, t_ident], [pt[0]])
        self.cp("dve", vT, ps[0][:, 0:80], [pt[0]], [t_vT])
        badaT = vT[:, 0:48]; n1T = vT[:, 48:56]; n2T = vT[:, 56:64]
        cT = vT[:, 64:80].rearrange("p (r k) -> p k r", r=2)
        t_bada = t_vT; t_nT = t_vT; t_cT = t_vT

        hxT = A([4, 8, 512], BF16, "hxT_own")
        t_hx = [Tok("hxT%d" % j) for j in range(4)]
        t_ag = [Tok("AG%d" % j) for j in range(4)]

        SA = A([4, 256], F32, "SA"); SB = A([4, 256], F32, "SB")
        SAb = A([4, 256], BF16, "SAb"); SBb = A([4, 256], BF16, "SBb")
        t_S = {"A": Tok("SA"), "B": Tok("SB")}
        t_Sb = {"A": Tok("SAb"), "B": Tok("SBb")}
        Sf = {"A": SA, "B": SB}
        Sb = {"A": SAb, "B": SBb}
        pend = self.regions["P"][1]
        s_start = pend - 12288
        self.def_region("S", s_start, pend)
        self.def_region("R1", pend, pend + 65536)
        self.def_region("R2", pend + 65536, pend + 65536 + 34816)
        self.def_region("R3", pend + 65536 + 34816, self.ARENA_BYTES)
        self.use("R2")
        KT = A([2, NKEY], BF16, "KT")
        V = A([34, 256], BF16, "V")
        t_kv = [Tok("kv%d" % g) for g in range(9)]

        self.use("R1")
        ws1 = A([8, 2336], BF16, "ws1"); t_ws1 = Tok("ws1"); c_ws1 = P.chan()
        xt = [A([1024], F32, "xt%d" % i) for i in range(2)]
        t_xt = [Tok("xt%d" % i) for i in range(2)]
        c_xt = [P.chan() for _ in range(2)]
        wada = [A([8, 512], BF16, "wada%d" % i) for i in range(2)]
        t_wada = [Tok("wada%d" % i) for i in range(2)]
        c_wada = [P.chan() for _ in range(2)]
        Tkc = A([2, 256], F32, "Tkc"); t_Tkc = Tok("Tkc")
        self.use("R3")
        xn = A([1024], F32, "xn"); t_xn = Tok("xn")
        hxg = A([8, 512], BF16, "hxg"); t_hxg = Tok("hxg")
        gkt = A([4, 512], BF16, "gkt"); t_gkt = Tok("gkt")
        gvt = A([4, 1024], BF16, "gvt"); t_gvt = Tok("gvt")
        lrT = {"A": A([512], BF16, "lrTA", parts=32), "B": A([512], BF16, "lrTB", parts=32)}
        t_lrT = {"A": Tok("lrTA"), "B": Tok("lrTB")}
        Tk = A([2, 512], F32, "Tk"); t_Tk = Tok("Tk")
        r_sq = A([512], BF16, "r_sq"); t_rsq = Tok("r_sq")
        r_rs = A([512], F32, "r_rs"); t_rrs = Tok("r_rs")
        r_t1 = A([512], F32, "r_t1"); t_rt1 = Tok("r_t1")
        r_t2 = A([512], F32, "r_t2"); t_rt2 = Tok("r_t2")
        def make_gset(i, full):
            G = {"sp": (A([512], F32, "g_sp%d" % i), Tok("g_sp")),
                 "EC": (A([512], F32, "g_EC%d" % i), Tok("g_EC")),
                 "kh": (A([512], BF16, "g_kh%d" % i), Tok("g_kh")),
                 "EL": (A([4], F32, "g_EL%d" % i), Tok("g_EL"))}
            if full:
                G["E1"] = (A([512], F32, "g_E1%d" % i), Tok("g_E1"))
                G["E2"] = (A([512], F32, "g_E2%d" % i), Tok("g_E2"))
                G["qt"] = (A([4, 128], BF16, "g_qt%d" % i), Tok("g_qt"))
                G["kt"] = (A([4, 128], BF16, "g_kt%d" % i), Tok("g_kt"))
                G["AT"] = (A([4, 128], BF16, "g_AT%d" % i), Tok("g_AT"))
            return G
        gsets = [make_gset(0, False),
                 {"sp": (r_rs, t_rrs), "EC": (r_t1, t_rt1), "kh": (r_sq, t_rsq),
                  "EL": (A([4], F32, "g_EL1"), Tok("g_EL1"))}]
        self.gcnt = 0
        sqj = r_t2.bitcast(BF16)
        t_sqj = t_rt2
        diag = A([128], F32, "diag"); t_diag = Tok("diag")

        for X in ("A", "B"):
            self.memset("dve", lrT[X], 1.0, [t_lrT[X]])
            self.memset("dve", Sf[X], 0.0, [t_S[X]])
            self.memset("dve", Sb[X], 0.0, [t_Sb[X]])

        w_in_v = w_in.rearrange("(k p) n -> p k n", p=128)

        self.set_rot([0, 1, 2, 3, 4, 5, 6, 7])
        ec = tiny
        ecv = tiny.rearrange("p (a b) -> p a b", a=8)
        self.act(ecv, cT, AF.Exp, [t_cT], [t_tiny], scale=-1.0)
        self.ts("dve", ecv, ecv, 1.0, None, ALU.add, None, [t_tiny], [t_tiny])
        self.recip(ecv, ecv, [t_tiny], [t_tiny])
        self.tt("dve", scT, ecv, cT, ALU.mult, [t_tiny, t_cT], [t_scT])

        w_ada_v = w_ada.rearrange("(k p) n -> p k n", p=128)
        self.set_rot([0, 1, 2, 3, 4, 5, 6])
        bmod = 7
        modps = ps[bmod][:, 0:96].rearrange("p (a b) -> p a b", a=48)

        def ada_dma(cb):
            for kc in range(8):
                self.dma("pool", wada[cb % 2][:, kc, :], w_ada_v[:, kc, cb * 512:(cb + 1) * 512],
                         c_wada[cb % 2], [], [t_wada[cb % 2]])

        def ada_mm(cb):
            wb = wada[cb % 2]
            for fc in range(4):
                j = cb * 4 + fc
                for kc in range(8):
                    self.mm(modps[:, j, :], wb[:, kc, fc * 128:(fc + 1) * 128], scT[:, kc, :],
                            kc == 0, kc == 7, [t_wada[cb % 2], t_scT], [pt[bmod]])

        ada_dma(0)
        ada_dma(1)
        for kc in range(8):
            self.dma("pool", ws1[:, kc, :], w_in_v[:, kc, 0:2336], c_ws1, [], [t_ws1])
        for cb in range(4):
            ada_mm(cb)
            ada_dma(cb + 2)
        self.tt("dve", modT[:, 0:16, :], modps[:, 0:16, :],
                badaT[:, 0:16].unsqueeze(2).broadcast_to([128, 16, 2]), ALU.add,
                [pt[bmod], t_bada], [t_mod])
        self.ts("dve", a1, modT[:, 8:16, 0], 1.0, 32.0, ALU.add, ALU.mult, [t_mod], [t_av])
        self.tt("dve", a1, a1, n1T, ALU.mult, [t_av, t_nT], [t_av])
        self.ts("dve", ac, modT[:, 8:16, 1], 1.0, 32.0, ALU.add, ALU.mult, [t_mod], [t_av])
        self.tt("dve", ac, ac, n1T, ALU.mult, [t_av, t_nT], [t_av])

        def ada_part2():
            for cb in range(4, 12):
                ada_mm(cb)
                if cb + 2 < 12:
                    ada_dma(cb + 2)
            self.tt("dve", modT[:, 16:48, :], modps[:, 16:48, :],
                    badaT[:, 16:48].unsqueeze(2).broadcast_to([128, 32, 2]), ALU.add,
                    [pt[bmod], t_bada], [t_mod])
            self.ts("dve", a2, modT[:, 32:40, 0], 1.0, 32.0, ALU.add, ALU.mult, [t_mod], [t_av2])
            self.tt("dve", a2, a2, n2T, ALU.mult, [t_av2, t_nT], [t_av2])
            for (dst, tk, j0) in ((gt1bc, t_gt1, 16), (gt2bc, t_gt2, 40)):
                for half in range(2):
                    b = self.bank()
                    for q in range(4):
                        kc = half * 4 + q
                        self.ts("dve", diag, ident, modT[:, j0 + kc, 0:1], None, ALU.mult, None,
                                [t_ident, t_mod], [t_diag])
                        self.mm(ps[b][:, q * 128:(q + 1) * 128], onesf, diag, True, True,
                                [t_onesf, t_diag], [pt[b]])
                    self.cp("dve", dst[:, half * 512:(half + 1) * 512], ps[b], [pt[b]], [tk])
        SQ128 = float(np.sqrt(128.0))
        self.ts("dve", GT[:, 0, 0, :], ropec[:, 0, :], qkgs[:, 0:1], None, ALU.mult, None,
                [t_ropec, t_qkg], [t_GT])
        self.ts("dve", GT[:, 0, 1, :], ropec[:, 1, :], qkgs[:, 1:2], None, ALU.mult, None,
                [t_ropec, t_qkg], [t_GT])
        self.ts("dve", GT[:, 1, 0, :], ropec[:, 0, :], qkgs[:, 2:3], SQ128, ALU.mult, ALU.mult,
                [t_ropec, t_qkg], [t_GT])
        self.ts("dve", GT[:, 1, 1, :], ropec[:, 1, :], qkgs[:, 3:4], SQ128, ALU.mult, ALU.mult,
                [t_ropec, t_qkg], [t_GT])
        Tk4 = Tk.rearrange("p c (r w) -> p c r w", r=8)
        self.cp("dve", Tk4[64:128], GT[64:128, 1, :, 0:64].unsqueeze(2).broadcast_to([64, 2, 8, 64]),
                [t_GT], [t_Tk])
        Tkc4 = Tkc.rearrange("p c (r w) -> p c r w", r=4)
        self.cp("dve", Tkc4, GT[:, 1, :, 64:68].unsqueeze(3).broadcast_to([128, 2, 4, 64]),
                [t_GT], [t_Tkc])

        if "stop_setup" in self.dbg:
            return self.finish_stub(t_mod, modT)
        def norm_transpose(src_ap, xt_i, avec, shcol, dst_fn, dst_toks, from_sbuf_tok=None):
            xin = src_ap
            rt = [t_xt[xt_i]] if from_sbuf_tok is None else [from_sbuf_tok]
            ssc = tiny[:, 0:1]
            self.act(sqj, xin, AF.Square, rt, [t_sqj, t_tiny], accum=ssc)
            self.ts("pool", tiny[:, 1:2], ssc, float(D * EPS), None, ALU.add, None, [t_tiny], [t_tiny])
            self.tt("pool", tiny[:, 2:3], tiny[:, 1:2], nhalf, ALU.pow, [t_tiny, t_nhalf], [t_tiny])
            self.act(xn, xin, AF.Copy, rt + [t_tiny], [t_xn], scale=tiny[:, 2:3])
            for half in range(2):
                b = self.bank()
                for q in range(4):
                    kc = half * 4 + q
                    self.tr(ps[b][:, q * 128:(q + 1) * 128], xn[:, kc * 128:(kc + 1) * 128], ident,
                            [t_xn, t_ident], [pt[b]])
                for q in range(4):
                    kc = half * 4 + q
                    dst = dst_fn(kc)
                    if half == 0:
                        self.act(dst, ps[b][:, q * 128:(q + 1) * 128], AF.Identity,
                                 [pt[b], t_av, t_av2, t_mod], dst_toks,
                                 scale=avec[:, kc:kc + 1], bias=shcol(kc))
                    else:
                        self.ts("dve", dst, ps[b][:, q * 128:(q + 1) * 128], avec[:, kc:kc + 1],
                                shcol(kc), ALU.mult, ALU.add, [pt[b], t_av, t_av2, t_mod], dst_toks)

        def rope_norm(b0, b1, n, Tc, Ts, t_tab, dst, dst_toks, eps_scaled):
            self.act(r_sq[:, 0:n], ps[b0][:, 0:n], AF.Square, [pt[b0]], [t_rsq])
            bs = self.bank()
            self.mm(ps[bs][:, 0:n], onesb, r_sq[:, 0:n], True, True, [t_onesb, t_rsq], [pt[bs]])
            self.chk("r1")
            import os
            EXP = os.environ.get("EXP", "")
            if EXP == "copyfirst":
                self.cp("dve", r_t1[:, 0:n], ps[b0][:, 0:n], [pt[b0]], [t_rt1])
            elif EXP == "serial":
                self.cp("dve", r_t1[:, 0:n], ps[b0][:, 0:n], [pt[b0], t_rsq], [t_rt1])
                self.chk("r1b")
                self.chk("r1b")
                self.tt("dve", r_t1[:, 0:n], r_t1[:, 0:n], Tc, ALU.mult, [t_rt1, t_tab], [t_rt1])
                self.chk("r1c")
            else:
                self.tt("dve", r_t1[:, 0:n], ps[b0][:, 0:n], Tc, ALU.mult, [pt[b0], t_tab], [t_rt1])
            self.chk("r1a")
            self.tt("dve", r_t2[:, 0:n], ps[b1][:, 0:n], Ts, ALU.mult, [pt[b1], t_tab], [t_rt2])
            self.chk("r2")
            self.act(r_rs[:, 0:n], ps[bs][:, 0:n], AF.Ln, [pt[bs]], [t_rrs], bias=eps_scaled)
            self.chk("r3")
            self.act(r_rs[:, 0:n], r_rs[:, 0:n], AF.Exp, [t_rrs], [t_rrs], scale=-0.5)
            self.chk("r4")
            self.tt("dve", r_t1[:, 0:n], r_t1[:, 0:n], r_t2[:, 0:n], ALU.add, [t_rt1, t_rt2], [t_rt1])
            self.tt("dve", dst, r_t1[:, 0:n], r_rs[:, 0:n], ALU.mult, [t_rt1, t_rrs], dst_toks)

        def proj_fm(w, t_w, c0, ncols_chunk, rhs_fn, n, rhs_toks):
            b = self.bank()
            for kc in range(8):
                self.mm(ps[b][0:ncols_chunk, 0:n], w[:, kc, c0:c0 + ncols_chunk], rhs_fn(kc),
                        kc == 0, kc == 7, [t_w] + rhs_toks, [pt[b]])
            return b

        def gla_stage1(X, full, lr_ap, gk_tile, gv_tile, t_in, qT=None, kT=None, o_dst=None,
                       o_add=False, o_tok=None):
            G = gsets[self.gcnt % len(gsets)]
            self.gcnt += 1
            g_sp, t_gsp = G["sp"]; g_EC, t_gEC = G["EC"]; g_kh, t_gkh = G["kh"]; g_EL, t_gEL = G["EL"]
            xi = 0 if X == "A" else 1
            cum = tri[:, 2 * xi, :]
            cmat = tri[:, 2 * xi + 1, :]
            last = 127 if X == "A" else 0
            bz = self.bank()
            self.mm(ps[bz], lr_ap, wgka[0:17, xi, :], True, True, t_in + [t_wgka], [pt[bz]])
            self.act(g_sp, ps[bz], AF.Exp, [pt[bz]], [t_gsp], scale=-1.0)
            self.act(g_sp, g_sp, AF.Ln, [t_gsp], [t_gsp], bias=1.0)
            bc = self.bank()
            self.mm(ps[bc], cmat, g_sp, True, True, [t_tri, t_gsp], [pt[bc]])
            if full:
                bb = self.bank()
                for h in range(4):
                    self.mm(ps[bb][:, h * 128:(h + 1) * 128], g_sp[:, h * 128:(h + 1) * 128], cum,
                            True, True, [t_gsp, t_tri], [pt[bb]])
            else:
                bl = self.bank()
                for h in range(4):
                    self.mm(ps[bl][:, h:h + 1], g_sp[:, h * 128:(h + 1) * 128],
                            cum[:, last:last + 1], True, True, [t_gsp, t_tri], [pt[bl]])
            self.act(g_EC, ps[bc], AF.Exp, [pt[bc]], [t_gEC])
            self.tt("dve", g_kh, gk_tile, g_EC, ALU.mult, t_in + [t_gEC], [t_gkh])
            st = dict(X=X, full=full, G=G, gv_tile=gv_tile, t_in=t_in, o_dst=o_dst, o_add=o_add,
                      o_tok=o_tok, xi=xi)
            if full:
                g_E1, t_gE1 = G["E1"]; g_E2, t_gE2 = G["E2"]; g_qt, t_gqtl = G["qt"]
                g_kt, t_gktl = G["kt"]
                self.act(g_E1, ps[bb], AF.Exp, [pt[bb]], [t_gE1])
                self.act(g_E2, ps[bb], AF.Exp, [pt[bb]], [t_gE2], scale=-1.0)
                E1v = g_E1.rearrange("p (h t) -> p h t", h=4)
                E2v = g_E2.rearrange("p (h t) -> p h t", h=4)
                self.tt("dve", g_qt, qT, E1v, ALU.mult, t_in + [t_gE1], [t_gqtl])
                self.tt("dve", g_kt, kT, E2v, ALU.mult, t_in + [t_gE2], [t_gktl])
                st["el"] = lambda h: g_E1[:, h * 128 + last: h * 128 + last + 1]
                st["el_tok"] = t_gE1
            else:
                self.act(g_EL, ps[bl][:, 0:4], AF.Exp, [pt[bl]], [t_gEL])
                st["el"] = lambda h: g_EL[:, h:h + 1]
                st["el_tok"] = t_gEL
            return st

        def gla_stage2(st):
            X = st["X"]; G = st["G"]; gv_tile = st["gv_tile"]; t_in = st["t_in"]; xi = st["xi"]
            g_kh, t_gkh = G["kh"]
            if st["full"]:
                g_qt, t_gqtl = G["qt"]; g_kt, t_gktl = G["kt"]; g_AT, t_gAT = G["AT"]
                ba = self.bank()
                for h in range(4):
                    self.mm(ps[ba][:, h * 128:(h + 1) * 128], g_kt[:, h, :], g_qt[:, h, :],
                            True, True, [t_gktl, t_gqtl], [pt[ba]])
                self.tt("dve", g_AT, ps[ba].rearrange("p (h t) -> p h t", h=4),
                        msk[:, xi, :].unsqueeze(1).broadcast_to([128, 4, 128]), ALU.mult,
                        [pt[ba], t_msk], [t_gAT])
                for hp in range(2):
                    bo = self.bank()
                    for hh in range(2):
                        h = hp * 2 + hh
                        self.mm(ps[bo][:, hh * 256:(hh + 1) * 256], g_qt[:, h, :], Sb[X][:, h, :],
                                True, False, [t_gqtl, t_Sb[X]], [pt[bo]])
                        self.mm(ps[bo][:, hh * 256:(hh + 1) * 256], g_AT[:, h, :],
                                gv_tile[:, h * 256:(h + 1) * 256], False, True,
                                [t_gAT] + t_in, [pt[bo]])
                    od = st["o_dst"][:, hp * 512:(hp + 1) * 512]
                    if st["o_add"]:
                        self.tt("dve", od, ps[bo], od, ALU.add, [pt[bo], st["o_tok"]], [st["o_tok"]])
                    else:
                        self.cp("act", od, ps[bo], [pt[bo]], [st["o_tok"]])
            el = st["el"]; el_tok = st["el_tok"]
            for hp in range(2):
                bu = self.bank()
                for hh in range(2):
                    h = hp * 2 + hh
                    self.mm(ps[bu][:, hh * 256:(hh + 1) * 256], g_kh[:, h * 128:(h + 1) * 128],
                            gv_tile[:, h * 256:(h + 1) * 256], True, True, [t_gkh] + t_in, [pt[bu]])
                for hh in range(2):
                    h = hp * 2 + hh
                    self.stt("dve", Sf[X][:, h, :], Sf[X][:, h, :], el(h),
                             ps[bu][:, hh * 256:(hh + 1) * 256], ALU.mult, ALU.add,
                             [t_S[X], el_tok, pt[bu]], [t_S[X]])
            self.cp("act", Sb[X], Sf[X], [t_S[X]], [t_Sb[X]])

        def gla_run(step_args, inter=None):
            prev = None
            for a in step_args:
                cur = gla_stage1(*a[0], **a[1])
                if inter is not None:
                    next(inter, None)
                if prev is not None:
                    gla_stage2(prev)
                prev = cur
            if prev is not None:
                gla_stage2(prev)
            if inter is not None:
                for _ in inter:
                    pass

        wf0 = wada[0].rearrange("p a b -> p (a b)")
        wf1 = wada[1].rearrange("p a b -> p (a b)")
        bsets = [
            dict(gkt=gkt, t_gkt=t_gkt, gvt=gvt, t_gvt=t_gvt, lrT=lrT, t_lrT=t_lrT),
            dict(gkt=wf0[:, 0:2048].rearrange("p (t n) -> p t n", t=4), t_gkt=t_wada[0],
                 gvt=wf1.rearrange("p (t n) -> p t n", t=4), t_gvt=t_wada[1],
                 lrT={"A": wf0[0:32, 2048:2560], "B": wf0[0:32, 2560:3072]},
                 t_lrT={"A": t_wada[0], "B": t_wada[0]}),
        ]
        self.xcount = 0

        def front_tiles(kind, g):
            ntile = 2 if kind == "ctx" else 4
            avec = ac if kind == "ctx" else a1
            rcol = 1 if kind == "ctx" else 0
            shcol = lambda kc, rcol=rcol: modT[:, kc, rcol:rcol + 1]
            src = ctx if kind == "ctx" else xs
            for t in range(ntile):
                xi = self.xcount % 2
                self.xcount += 1
                r0 = t * 128 if kind == "ctx" else g * 512 + t * 128
                self.dma("sp", xt[xi], src[r0:r0 + 128, :], c_xt[xi], [], [t_xt[xi]])
                if kind == "own":
                    dst_fn = lambda kc, t=t, g=g: hxT[:, g, kc, t * 128:(t + 1) * 128]
                    dtoks = [t_hx[g]]
                else:
                    dst_fn = lambda kc, t=t: hxg[:, kc, t * 128:(t + 1) * 128]
                    dtoks = [t_hxg]
                norm_transpose(xt[xi], xi, avec, shcol, dst_fn, dtoks)
                yield t

        def front_proj(kind, g, bs):
            ntile = 2 if kind == "ctx" else 4
            n = ntile * 128
            keyoff = SEQ if kind == "ctx" else g * 512
            if kind == "own":
                rhs_fn = lambda kc, g=g: hxT[:, g, kc, :]
                rtoks = [t_hx[g]]
                lhs_fn = lambda kc, t, g=g: hxT[:, g, kc, t * 128:(t + 1) * 128]
            else:
                rhs_fn = lambda kc, n=n: hxg[:, kc, 0:n]
                rtoks = [t_hxg]
                lhs_fn = lambda kc, t: hxg[:, kc, t * 128:(t + 1) * 128]
            if kind == "ctx":
                Tc, Ts, t_tab = Tkc[:, 0, :], Tkc[:, 1, :], t_Tkc
            else:
                self.cp("dve", Tk4[0:64],
                        GT[0:64, 1, :, g * 8:(g + 1) * 8].unsqueeze(3).broadcast_to([64, 2, 8, 64]),
                        [t_GT], [t_Tk])
                Tc, Ts, t_tab = Tk[:, 0, :], Tk[:, 1, :], t_Tk
            gi = 8 if kind == "ctx" else g
            for kvh in range(2):
                b0 = proj_fm(ws1, t_ws1, C_AK + kvh * 128, 128, rhs_fn, n, rtoks)
                b1 = proj_fm(ws1, t_ws1, C_AKP + kvh * 128, 128, rhs_fn, n, rtoks)
                rope_norm(b0, b1, n, Tc, Ts, t_tab, KT[:, kvh, keyoff:keyoff + n], [t_kv[gi]],
                          float(128 * EPS))
            for t in range(ntile):
                b = self.bank()
                for kc in range(8):
                    self.mm(ps[b][:, 0:256], lhs_fn(kc, t), ws1[:, kc, C_AV:C_AV + 256],
                            kc == 0, kc == 7, rtoks + [t_ws1], [pt[b]])
                self.cp("act", V[:, keyoff // 128 + t, :], ps[b][:, 0:256], [pt[b]], [t_kv[gi]])
            if kind == "own":
                return
            for t in range(ntile):
                b = self.bank()
                for kc in range(8):
                    self.mm(ps[b], lhs_fn(kc, t), ws1[:, kc, C_GK:C_GK + 512],
                            kc == 0, kc == 7, rtoks + [t_ws1], [pt[b]])
                self.cp("dve", bs["gkt"][:, t, :], ps[b], [pt[b]], [bs["t_gkt"]])
                for hf in range(2):
                    b = self.bank()
                    for kc in range(8):
                        self.mm(ps[b], lhs_fn(kc, t),
                                ws1[:, kc, C_GV + hf * 512:C_GV + (hf + 1) * 512],
                                kc == 0, kc == 7, rtoks + [t_ws1], [pt[b]])
                    self.cp("act", bs["gvt"][:, t, hf * 512:(hf + 1) * 512], ps[b], [pt[b]],
                            [bs["t_gvt"]])
            dirs = ("A", "B") if kind == "ctx" else ("B",)
            for X in dirs:
                c0 = C_LRA if X == "A" else C_LRB
                b = proj_fm(ws1, t_ws1, c0, 16, rhs_fn, n, rtoks)
                self.cp("dve", bs["lrT"][X][0:16, 0:n], ps[b][0:16, 0:n], [pt[b]], [bs["t_lrT"][X]])

        def steps(kind, g, bs, inter=None):
            ntile = 2 if kind == "ctx" else 4
            dirs = ("A", "B") if kind == "ctx" else ("B",)
            args = []
            for X in dirs:
                order = range(ntile) if X == "A" else range(ntile - 1, -1, -1)
                for t in order:
                    args.append(((X, False, bs["lrT"][X][0:17, t * 128:(t + 1) * 128], bs["gkt"][:, t, :],
                                  bs["gvt"][:, t, :], [bs["t_lrT"][X], bs["t_gkt"], bs["t_gvt"]]), {}))
            gla_run(args, inter)

        def front(kind, g, bs):
            for _ in front_tiles(kind, g):
                pass
            front_proj(kind, g, bs)

        front("ctx", 8, bsets[0])
        ada_part2()
        for X in ("A", "B"):
            self.memset("dve", bsets[1]["lrT"][X], 1.0, [t_wada[0]])
        front("oth", 7, bsets[1])
        steps("ctx", 8, bsets[0], front_tiles("oth", 6))
        front_proj("oth", 6, bsets[0])
        steps("oth", 7, bsets[1], front_tiles("oth", 5))
        front_proj("oth", 5, bsets[1])
        steps("oth", 6, bsets[0], front_tiles("oth", 4))
        front_proj("oth", 4, bsets[0])
        steps("oth", 5, bsets[1], front_tiles("own", 0))
        front_proj("own", 0, None)
        steps("oth", 4, bsets[0], front_tiles("own", 1))
        front_proj("own", 1, None)
        for g in (2, 3):
            front("own", g, None)
        self.dump("modT", modT, [128, 48, 2], F32, t_mod)
        self.dump("gt1bc", gt1bc, [128, 1024], F32, t_gt1)
        self.dump("KT", KT, [128, 2, NKEY], BF16, t_kv)
        self.dump("V", V, [128, 34, 256], BF16, t_kv)
        self.dump("hxT", hxT, [128, 4, 8, 512], BF16, t_hx)
        self.dump("SA", SA, [128, 4, 256], F32, t_S["A"])
        self.dump("SB", SB, [128, 4, 256], F32, t_S["B"])

        if "stop_s1" in self.dbg:
            return self.finish_stub(t_mod, modT)

        P.barrier()
        self.reset_region("R1", "R3")
        self.use("R1")
        AG = A([4, 2, 8, 512], BF16, "AG")
        self.use("R3")
        wq = A([8, 1024], BF16, "wq"); t_wq = Tok("wq"); c_wq = P.chan()
        Qbs = [A([4, 512], BF16, "Qb%d" % i) for i in range(2)]
        t_Qbs = [Tok("Qb%d" % i) for i in range(2)]
        Tq = A([2, 512], F32, "Tq"); t_Tq = Tok("Tq")
        r_sq = A([512], BF16, "r_sq2"); t_rsq = Tok("r_sq2")
        r_rs = A([512], F32, "r_rs2"); t_rrs = Tok("r_rs2")
        r_t1 = A([512], F32, "r_t12"); t_rt1 = Tok("r_t12")
        r_t2 = A([512], F32, "r_t22"); t_rt2 = Tok("r_t22")
        PTN = 4
        PT = [A([512], BF16, "PT%d" % i) for i in range(PTN)]
        t_PT = [Tok("PT%d" % i) for i in range(PTN)]
        sst = r_t2; t_sst = t_rt2
        ones32 = A([128], F32, "ones32"); t_ones32 = Tok("ones32")
        self.memset("dve", ones32, 1.0 / 32.0, [t_ones32])
        rec = A([512], F32, "rec"); t_rec = Tok("rec")
        Tq4 = Tq.rearrange("p c (r w) -> p c r w", r=8)
        self.cp("dve", Tq4[64:128], GT[64:128, 0, :, 0:64].unsqueeze(2).broadcast_to([64, 2, 8, 64]),
                [t_GT], [t_Tq])
        self.set_rot([0, 1, 2, 3])
        iters = [(hh, blk) for hh in range(2) for blk in range(4)]
        self.wq_loaded = -1

        def emit_q(it):
            hh, blk = iters[it]
            if self.wq_loaded != hh:
                self.wq_loaded = hh
                for kc in range(8):
                    self.dma("pool", wq[:, kc, 0:512],
                             w_in_v[:, kc, C_AQ + hh * 512:C_AQ + (hh + 1) * 512], c_wq, [], [t_wq])
                    self.dma("pool", wq[:, kc, 512:1024],
                             w_in_v[:, kc, C_AQP + hh * 512:C_AQP + (hh + 1) * 512], c_wq, [], [t_wq])
            self.cp("dve", Tq4[0:64],
                    GT[0:64, 0, :, blk * 8:(blk + 1) * 8].unsqueeze(3).broadcast_to([64, 2, 8, 64]),
                    [t_GT], [t_Tq])
            rhs_fn = lambda kc: hxT[:, blk, kc, :]
            for hl in range(4):
                b0 = proj_fm(wq, t_wq, hl * 128, 128, rhs_fn, 512, [t_hx[blk]])
                b1 = proj_fm(wq, t_wq, 512 + hl * 128, 128, rhs_fn, 512, [t_hx[blk]])
                rope_norm(b0, b1, 512, Tq[:, 0, :], Tq[:, 1, :], t_Tq, Qbs[it % 2][:, hl, :],
                          [t_Qbs[it % 2]], float(128 * EPS))

        self.pcount = 0

        def emit_unit(it, hl):
            hh, blk = iters[it]
            Qb, t_Qb = Qbs[it % 2], t_Qbs[it % 2]
            h = hh * 4 + hl
            kvh = h // 4
            bo = 4 + (self.pcount % 2) * 2
            bsum = bo + 1
            self.pcount += 1

            def smm(kt):
                b = self.bank()
                gi = 8 if kt >= 32 else kt // 4
                self.mm(ps[b], KT[:, kvh, kt * 128:(kt + 1) * 128], Qb[:, hl, :], True, True,
                        [t_kv[gi], t_Qb], [pt[b]])
                return b
            bcur = smm(0)
            bnext = None
            for kt in range(34):
                pi = kt % PTN
                self.act(PT[pi], ps[bcur], AF.Exp, [pt[bcur]], [t_PT[pi]])
                if kt + 1 < 34:
                    bnext = smm(kt + 1)
                gi = 8 if kt >= 32 else kt // 4
                self.mm(ps[bo], V[:, kt, kvh * 128:(kvh + 1) * 128], PT[pi], kt == 0, kt == 33,
                        [t_kv[gi], t_PT[pi]], [pt[bo]])
                self.mm(ps[bsum], onesb, PT[pi], kt == 0, kt == 33,
                        [t_onesb, t_PT[pi]], [pt[bsum]])
                bcur = bnext
            self.recip(rec, ps[bsum], [pt[bsum]], [t_rec])
            self.tt("dve", AG[:, blk, 0, h, :], ps[bo], rec, ALU.mult, [pt[bo], t_rec],
                    [t_ag[blk]])

        emit_q(0)
        for it in range(len(iters)):
            emit_unit(it, 0)
            emit_unit(it, 1)
            if it + 1 < len(iters):
                emit_q(it + 1)
            emit_unit(it, 2)
            emit_unit(it, 3)
        self.dump("attnT", AG[:, :, 0, :, :], [128, 4, 8, 512], BF16, t_ag)
        if "stop_att" in self.dbg:
            return self.finish_stub(t_mod, modT)

        P.barrier()
        self.reset_region("R2", "R3")
        self.set_rot([0, 1, 2, 3, 4, 5, 6, 7])
        self.use("R2")
        wg = A([8, 2080], BF16, "wg"); t_wg = Tok("wg"); c_wg = P.chan()
        self.use("R3")
        WG0 = 768
        gkT = A([4, 512], BF16, "gkT"); t_gkT = Tok("gkT")
        gqT = A([4, 512], BF16, "gqT"); t_gqT = Tok("gqT")
        gkt = A([4, 512], BF16, "gkt2"); t_gkt = Tok("gkt2")
        gvt = A([4, 1024], BF16, "gvt2"); t_gvt = Tok("gvt2")
        lrT = {"A": A([512], BF16, "lrTA2", parts=32), "B": A([512], BF16, "lrTB2", parts=32)}
        t_lrT = {"A": Tok("lrTA2"), "B": Tok("lrTB2")}
        gsets = [make_gset(10 + i, True) for i in range(2)]
        for X in ("A", "B"):
            self.memset("dve", lrT[X], 1.0, [t_lrT[X]])
        for kc in range(8):
            self.dma("pool", wg[:, kc, :], w_in_v[:, kc, WG0:WG0 + 2080], c_wg, [], [t_wg])

        def gla_group(X, g, reuse=False):
            rhs_fn = lambda kc: hxT[:, g, kc, :]
            rtoks = [t_hx[g]]
            for h in range(0 if reuse else 4):
                b = proj_fm(wg, t_wg, C_GK - WG0 + h * 128, 128, rhs_fn, 512, rtoks)
                self.cp("act", gkT[:, h, :], ps[b], [pt[b]], [t_gkT])
                b = proj_fm(wg, t_wg, C_GQ - WG0 + h * 128, 128, rhs_fn, 512, rtoks)
                self.ts("dve", gqT[:, h, :], ps[b], float(128.0 ** -0.5), None, ALU.mult, None,
                        [pt[b]], [t_gqT])
            for t in range(0 if reuse else 4):
                b = self.bank()
                for kc in range(8):
                    self.mm(ps[b], hxT[:, g, kc, t * 128:(t + 1) * 128],
                            wg[:, kc, C_GK - WG0:C_GK - WG0 + 512], kc == 0, kc == 7,
                            rtoks + [t_wg], [pt[b]])
                self.cp("dve", gkt[:, t, :], ps[b], [pt[b]], [t_gkt])
                for hf in range(2):
                    b = self.bank()
                    for kc in range(8):
                        self.mm(ps[b], hxT[:, g, kc, t * 128:(t + 1) * 128],
                                wg[:, kc, C_GV - WG0 + hf * 512:C_GV - WG0 + (hf + 1) * 512],
                                kc == 0, kc == 7, rtoks + [t_wg], [pt[b]])
                    self.cp("act", gvt[:, t, hf * 512:(hf + 1) * 512], ps[b], [pt[b]], [t_gvt])
            c0 = (C_LRA if X == "A" else C_LRB) - WG0
            b = proj_fm(wg, t_wg, c0, 16, rhs_fn, 512, rtoks)
            self.cp("dve", lrT[X][0:16, :], ps[b][0:16, :], [pt[b]], [t_lrT[X]])
            order = range(4) if X == "A" else range(3, -1, -1)
            args = []
            for t in order:
                osl = AG[:, g, 1, :, :].rearrange("p a b -> p (a b)")[:, t * 1024:(t + 1) * 1024]
                args.append(((X, True, lrT[X][0:17, t * 128:(t + 1) * 128], gkt[:, t, :], gvt[:, t, :],
                              [t_lrT[X], t_gkt, t_gvt, t_gkT, t_gqT]),
                             dict(qT=gqT[:, :, t * 128:(t + 1) * 128], kT=gkT[:, :, t * 128:(t + 1) * 128],
                                  o_dst=osl, o_add=(X == "A"), o_tok=t_ag[g])))
            gla_run(args)

        for g in (3, 2, 1, 0):
            gla_group("B", g)
        for g in (0, 1, 2, 3):
            gla_group("A", g, reuse=(g == 0))
        self.dump("osum", AG[:, :, 1, :, :], [128, 4, 8, 512], BF16, t_ag)
        if "stop_gla" in self.dbg:
            return self.finish_stub(t_mod, modT)

        P.barrier()
        self.reset_region("S", "R2", "R3")
        self.set_rot([0, 1, 2, 3, 4, 5, 6, 7])
        self.use("R3")
        wgo = A([8, 1024], BF16, "wgo"); t_wgo = Tok("wgo"); c_wgo = P.chan()
        for kc in range(8):
            self.dma("pool", wgo[:, kc, :], w_in_v[:, kc, C_GO:C_GO + 1024], c_wgo, [], [t_wgo])
        self.use("R2")
        wga = A([8, 1024], BF16, "wga"); t_wga = Tok("wga"); c_wga = P.chan()
        wba = A([8, 1024], BF16, "wba"); t_wba = Tok("wba"); c_wba = P.chan()
        w_bra_v = w_bra.rearrange("(k p) n -> p k n", p=128)
        w_brg_v = w_brg.rearrange("(k p) n -> p k n", p=128)
        w_out_v = w_out.rearrange("(k p) n -> p k n", p=128)
        for kc in range(8):
            self.dma("pool", wga[:, kc, :], w_in_v[:, kc, C_GA:C_GA + 1024], c_wga, [], [t_wga])
            self.dma("pool", wba[:, kc, :], w_bra_v[:, kc, :], c_wba, [], [t_wba])
        self.use("R3")
        gx = A([4, 1024], F32, "gx"); t_gx = Tok("gx")
        sgs = [A([1024], F32, "sg%d" % i) for i in range(2)]
        t_sgs = [Tok("sg%d" % i) for i in range(2)]
        ssgs = [A([8], F32, "ssg%d" % i) for i in range(2)]
        t_ssgs = [Tok("ssg%d" % i) for i in range(2)]
        o2j = A([256], BF16, "o2j"); t_o2j = Tok("o2j")
        acnt = 0
        for blk in range(4):
            for t in range(4):
                sg, t_sg = sgs[acnt % 2], t_sgs[acnt % 2]
                ssg, t_ssg = ssgs[acnt % 2], t_ssgs[acnt % 2]
                acnt += 1
                osl = AG[:, blk, 1, :, :].rearrange("p a b -> p (a b)")[:, t * 1024:(t + 1) * 1024]
                for hf in range(2):
                    b = self.bank()
                    for kc in range(8):
                        self.mm(ps[b], hxT[:, blk, kc, t * 128:(t + 1) * 128],
                                wgo[:, kc, hf * 512:(hf + 1) * 512], kc == 0, kc == 7,
                                [t_hx[blk], t_wgo], [pt[b]])
                    sgh = sg[:, hf * 512:(hf + 1) * 512]
                    self.act(sgh, ps[b], AF.Silu, [pt[b]], [t_sg])
                    sgh3 = sgh.rearrange("p (h e) -> p h e", h=2)
                    self.tt("dve", sgh3, sgh3, glan.unsqueeze(1).broadcast_to([128, 2, 256]), ALU.mult,
                            [t_sg, t_glan], [t_sg])
                for h in range(4):
                    self.act(o2j, osl[:, h * 256:(h + 1) * 256], AF.Square,
                             [t_ag[blk]], [t_o2j, t_ssg], accum=ssg[:, h:h + 1])
                self.ts("pool", ssg[:, 0:4], ssg[:, 0:4], float(1.0 / 256), float(EPS), ALU.mult, ALU.add,
                        [t_ssg], [t_ssg])
                self.tt("pool", ssg[:, 4:8], ssg[:, 0:4], nhalf.broadcast_to([128, 4]), ALU.pow,
                        [t_ssg, t_nhalf], [t_ssg])
                for h in range(4):
                    self.stt("dve", gx[:, t, h * 256:(h + 1) * 256], osl[:, h * 256:(h + 1) * 256],
                             ssg[:, 4 + h:5 + h], sg[:, h * 256:(h + 1) * 256], ALU.mult, ALU.mult,
                             [t_ag[blk], t_ssg, t_sg], [t_gx])
            for t in range(4):
                for half in range(2):
                    b = self.bank()
                    for q in range(4):
                        kc = half * 4 + q
                        self.tr(ps[b][:, q * 128:(q + 1) * 128], gx[:, t, kc * 128:(kc + 1) * 128], ident,
                                [t_gx, t_ident], [pt[b]])
                    dstv = AG[:, blk, 1, half * 4:half * 4 + 4, t * 128:(t + 1) * 128]
                    self.cp("act" if half == 0 else "dve", dstv,
                            ps[b].rearrange("p (q t) -> p q t", q=4), [pt[b]], [t_ag[blk]])
        self.dump("glaT", AG[:, :, 1, :, :], [128, 4, 8, 512], BF16, t_ag)
        self.chk("a")
        P.barrier()
        self.reset_region("S", "R3")
        self.use("R3")
        yT = A([8, 512], BF16, "yT"); t_yT = Tok("yT")
        sgts = [A([512], F32, "sgt%d" % i) for i in range(2)]
        t_sgts = [Tok("sgt%d" % i) for i in range(2)]
        self.sgc = 0
        wo = A([8, 1024], BF16, "wo"); t_wo = Tok("wo"); c_wo = P.chan()
        for kc in range(8):
            self.dma("pool", wo[:, kc, :], w_out_v[:, kc, :], c_wo, [], [t_wo])
        for kc in range(8):
            self.tt("dve", wo[:, kc, :], wo[:, kc, :], gt1bc, ALU.mult, [t_wo, t_gt1], [t_wo])

        def gated_proj(blk, wgate, t_wgate, wbr, t_wbr, src_half, fc, dst, dst_toks, add_src=None,
                       add_toks=()):
            bg = proj_fm(wgate, t_wgate, fc * 128, 128, lambda kc: hxT[:, blk, kc, :], 512, [t_hx[blk]])
            sgt, t_sgt = sgts[self.sgc % 2], t_sgts[self.sgc % 2]
            self.sgc += 1
            self.act(sgt, ps[bg], AF.Sigmoid, [pt[bg]], [t_sgt])
            bp = proj_fm(wbr, t_wbr, fc * 128, 128, lambda kc: AG[:, blk, src_half, kc, :], 512,
                         [t_ag[blk]])
            if add_src is None:
                self.tt("dve", dst, ps[bp], sgt, ALU.mult, [pt[bp], t_sgt], dst_toks)
            else:
                self.tt("dve", sgt, ps[bp], sgt, ALU.mult, [pt[bp], t_sgt], [t_sgt])
                self.tt("dve", dst, sgt, add_src, ALU.add, [t_sgt] + list(add_toks), dst_toks)

        for blk in range(4):
            for fc in range(8):
                gated_proj(blk, wga, t_wga, wba, t_wba, 0, fc, yT[:, fc, :], [t_yT])
            self.cp("dve", AG[:, blk, 0, :, :], yT, [t_yT], [t_ag[blk]])
        self.dump("y1T", AG[:, :, 0, :, :], [128, 4, 8, 512], BF16, t_ag)
        self.chk("b1")
        P.barrier()
        self.reset_region("S", "R2")
        self.use("R2")
        wgg = A([8, 1024], BF16, "wgg"); t_wgg = Tok("wgg"); c_wgg = P.chan()
        wbg = A([8, 1024], BF16, "wbg"); t_wbg = Tok("wbg"); c_wbg = P.chan()
        self.use("R3")
        for kc in range(8):
            self.dma("pool", wgg[:, kc, :], w_in_v[:, kc, C_GG:C_GG + 1024], c_wgg, [], [t_wgg])
            self.dma("pool", wbg[:, kc, :], w_brg_v[:, kc, :], c_wbg, [], [t_wbg])
        xt = [A([1024], F32, "xtb%d" % i) for i in range(2)]
        t_xt = [Tok("xtb%d" % i) for i in range(2)]
        c_xt = [P.chan() for _ in range(2)]
        xn = A([1024], F32, "xn2"); t_xn = Tok("xn2")
        sqj = A([1024], BF16, "sqj2"); t_sqj = Tok("sqj2")
        x2v = [AG[:, blk].rearrange("p a b c -> p (a b c)").bitcast(F32).rearrange(
            "p (t d) -> p t d", t=4) for blk in range(4)]
        xcount = 0

        def gp_gen(blk):
            for fc in range(8):
                gated_proj(blk, wgg, t_wgg, wbg, t_wbg, 1, fc, yT[:, fc, :], [t_yT],
                           add_src=AG[:, blk, 0, fc, :], add_toks=[t_ag[blk]])
                yield fc

        cur = gp_gen(0)
        for blk in range(4):
            for _ in cur:
                pass
            for t in range(4):
                xi = xcount % 2
                xcount += 1
                r0 = blk * 512 + t * 128
                self.dma("sp", xt[xi], xs[r0:r0 + 128, :], c_xt[xi], [], [t_xt[xi]])
                for hf in range(2):
                    b = self.bank()
                    for kc in range(8):
                        self.mm(ps[b], yT[:, kc, t * 128:(t + 1) * 128], wo[:, kc, hf * 512:(hf + 1) * 512],
                                kc == 0, kc == 7, [t_yT, t_wo], [pt[b]])
                    xh = xt[xi][:, hf * 512:(hf + 1) * 512]
                    self.tt("dve", x2v[blk][:, t, hf * 512:(hf + 1) * 512], ps[b], xh, ALU.add,
                            [pt[b], t_xt[xi]], [t_ag[blk]])
            cur = gp_gen(blk + 1) if blk + 1 < 4 else iter(())
            for t in range(4):
                norm_transpose(x2v[blk][:, t, :], 0, a2, lambda kc: modT[:, 24 + kc, 0:1],
                               lambda kc, t=t, blk=blk: hxT[:, blk, kc, t * 128:(t + 1) * 128],
                               [t_hx[blk]], from_sbuf_tok=t_ag[blk])
                next(cur, None)
                next(cur, None)
        self.dump("x2", AG.rearrange("p a b c d -> p (a b c d)").bitcast(F32), [128, 16384], F32, t_ag)
        self.dump("hmT", hxT, [128, 4, 8, 512], BF16, t_hx)
        self.chk("b2")

        P.barrier()
        self.reset_region("S", "R2", "R3")
        self.use("R2")
        w1q = [A([8, 1024], BF16, "w1q%d" % i) for i in range(2)]
        self.use("R3")
        w2q = [A([8, 1024], BF16, "w2q%d" % i) for i in range(2)]
        t_w1q = [Tok("w1q%d" % i) for i in range(2)]
        t_w2q = [Tok("w2q%d" % i) for i in range(2)]
        c_w1q = [P.chan() for _ in range(2)]
        c_w2q = [P.chan() for _ in range(2)]
        h1 = A([8, 512], BF16, "h1"); t_h1 = Tok("h1")
        rl = [A([512], F32, "rl%d" % i) for i in range(2)]
        t_rl = [Tok("rl%d" % i) for i in range(2)]
        self.use("S")
        tmp = A([512], F32, "mtmp"); t_tmp = Tok("mtmp")
        ost = [A([1024], F32, "ost%d" % i) for i in range(2)]
        t_ost = [Tok("ost%d" % i) for i in range(2)]
        c_ost = [P.chan() for _ in range(2)]
        self.final_chans.extend(c_ost)
        w_m1_v = w_m1.rearrange("(k p) n -> p k n", p=128)
        w_m2_v = w_m2.rearrange("(k p) n -> p k n", p=128)
        ocount = 0
        rcount = 0
        for q in range(4):
            wi = q % 2
            for kc in range(8):
                self.dma("pool", w1q[wi][:, kc, :], w_m1_v[:, kc, q * 1024:(q + 1) * 1024], c_w1q[wi],
                         [], [t_w1q[wi]])
            for kc in range(8):
                self.dma("pool", w2q[wi][:, kc, :], w_m2_v[:, q * 8 + kc, :], c_w2q[wi], [], [t_w2q[wi]])
            for blk in range(4):
                for fc in range(8):
                    b = proj_fm(w1q[wi], t_w1q[wi], fc * 128, 128, lambda kc: hxT[:, blk, kc, :], 512,
                                [t_hx[blk]])
                    ri = rcount % 2
                    rcount += 1
                    self.act(rl[ri], ps[b], AF.Relu, [pt[b]], [t_rl[ri]])
                    self.tt("dve", h1[:, fc, :], rl[ri], rl[ri], ALU.mult, [t_rl[ri]], [t_h1])
                for t in range(4):
                    for hf in range(2):
                        b = self.bank()
                        for kc in range(8):
                            self.mm(ps[b], h1[:, kc, t * 128:(t + 1) * 128],
                                    w2q[wi][:, kc, hf * 512:(hf + 1) * 512], kc == 0, kc == 7,
                                    [t_h1, t_w2q[wi]], [pt[b]])
                        self.tt("dve", tmp, ps[b], gt2bc[:, hf * 512:(hf + 1) * 512], ALU.mult,
                                [pt[b], t_gt2], [t_tmp])
                        x2h = x2v[blk][:, t, hf * 512:(hf + 1) * 512]
                        if q < 3:
                            self.tt("dve", x2h, tmp, x2h, ALU.add, [t_tmp, t_ag[blk]], [t_ag[blk]])
                        else:
                            oi = ocount % 2
                            self.tt("dve", ost[oi][:, hf * 512:(hf + 1) * 512], tmp, x2h, ALU.add,
                                    [t_tmp, t_ag[blk]], [t_ost[oi]])
                    if q == 3:
                        oi = ocount % 2
                        ocount += 1
                        r0 = blk * 512 + t * 128
                        self.dma("sp", out_d[r0:r0 + 128, :], ost[oi], c_ost[oi], [t_ost[oi]], [])
        self.finalize()
        return self.nc

    def finish_stub(self, tok, ap):
        self.reset_region("R3")
        z = self.alloc([1024], F32, "zstub", region="R3")
        tz = Tok("z")
        self.memset("dve", z, 0.0, [tz])
        c = self.P.chan()
        self.final_chans.append(c)
        for i in range(16):
            self.dma("sp", self.out_d[i * 128:(i + 1) * 128, :], z, c, [tz], [])
        self.finalize()
        return self.nc

    def finalize(self):
        self.P.emit(self.nc, self.stack, self.final_chans)
        self.stack.close()


def _perm_half_swap(nheads):
    idx = []
    for h in range(nheads):
        for d in range(128):
            axis, rem = divmod(d, 64)
            half, f = divmod(rem, 32)
            idx.append(h * 128 + axis * 64 + (1 - half) * 32 + f)
    return np.array(idx)


def _consts(h):
    ident = np.eye(128, dtype=np.float32)
    r = np.arange(128)[:, None]
    c = np.arange(128)[None, :]
    s = np.float32(-1.0 / 16.0)
    tri = np.zeros((128, 4, 128), np.float32)
    tri[:, 0, :] = (r <= c) * s
    tri[:, 1, :] = (r > c) * s
    tri[:, 2, :] = (r >= c) * s
    tri[:, 3, :] = (r < c) * s
    msk = np.zeros((128, 2, 128), np.float32)
    msk[:, 0, :] = (r <= c)
    msk[:, 1, :] = (r >= c)
    half = 64
    freqs = (10000.0 ** (-np.arange(0, half, 2, dtype=np.float32) / half)).astype(np.float32)
    ropec = np.zeros((128, 2, 72), np.float32)
    for d in range(128):
        axis, rem = divmod(d, 64)
        hf, f = divmod(rem, 32)
        for i in range(64):
            pos = i if h == 0 else 63 - i
            ang = np.float32(pos) * freqs[f]
            ropec[d, 0, i] = np.cos(ang)
            ropec[d, 1, i] = (-np.sin(ang)) if hf == 0 else np.sin(ang)
        ropec[d, 0, 64:72] = 1.0
        ropec[d, 1, 64:72] = 0.0
    return ident, tri, msk, ropec


_NC_CACHE = {}


def _get_nc(dbg=()):
    key = tuple(sorted(dbg))
    if key not in _NC_CACHE:
        b = Builder(dbg)
        nc = b.build()
        _NC_CACHE[key] = (nc, b.dbg_out)
    return _NC_CACHE[key]


def make_in_maps(x, c, ctx, c_ctx, w_ada, b_ada, norm1, w_in, q_norm, k_norm, w_gk_fwd, b_gk_fwd,
                 w_gk_bwd, b_gk_bwd, gla_norm, w_br_attn, w_br_gla, w_out, norm2, w_mlp1, w_mlp2):
    f = lambda a: np.ascontiguousarray(np.asarray(a, dtype=np.float32))
    x = f(x); c = f(c); ctx = f(ctx); c_ctx = f(c_ctx)
    w_in0 = f(w_in)[0]
    off = np.cumsum([0, 256, 256, 512, 1024, 16, 16, 1024, 512, 1024, 1024, 1024])
    ak, av, gk, gv, lrf, lrb, aq, gq, go, ga, gg = [w_in0[:, off[i]:off[i + 1]] for i in range(11)]
    pk = _perm_half_swap(2)
    pq = _perm_half_swap(8)
    pd = _perm_half_swap(1)
    qn = f(q_norm)[0]; kn = f(k_norm)[0]
    qkg = np.ascontiguousarray(np.stack([qn, qn[pd], kn, kn[pd]], axis=1))
    win = {}
    wgk = {}
    for h in (0, 1):
        lra, lrbb = (lrf, lrb) if h == 0 else (lrb, lrf)
        win[h] = np.ascontiguousarray(np.concatenate(
            [ak, ak[:, pk], av, gk, gv, lra, lrbb, gq, aq, aq[:, pq], go, ga, gg], axis=1))
        assert win[h].shape[1] == NIN
        wf = np.concatenate([f(w_gk_fwd)[0], f(b_gk_fwd)[0][None, :]], axis=0)
        wb = np.concatenate([f(w_gk_bwd)[0], f(b_gk_bwd)[0][None, :]], axis=0)
        wgk[h] = np.ascontiguousarray(np.stack([wf, wb] if h == 0 else [wb, wf], axis=0))
    consts = {h: _consts(h) for h in (0, 1)}
    shared = dict(w_ada=f(w_ada)[0], b_ada=f(b_ada)[0], norm1=f(norm1)[0], norm2=f(norm2)[0],
                  gla_norm=f(gla_norm)[0], w_br_attn=f(w_br_attn)[0], w_br_gla=f(w_br_gla)[0],
                  w_out=f(w_out)[0], w_mlp1=f(w_mlp1)[0], w_mlp2=f(w_mlp2)[0], qkg=qkg)
    in_maps = []
    for core in range(8):
        b, h = divmod(core, 2)
        xb = x[b] if h == 0 else x[b][::-1]
        cb = ctx[b] if h == 0 else ctx[b][::-1]
        ident, tri, msk, ropec = consts[h]
        m = dict(shared)
        m.update(xs=np.ascontiguousarray(xb), ctx=np.ascontiguousarray(cb),
                 cvec=np.ascontiguousarray(np.stack([c[b], c_ctx], axis=0)),
                 w_in=win[h], wgk=wgk[h], ident=ident, tri=tri, msk=msk, ropec=ropec)
        in_maps.append(m)
    return in_maps


def assemble(results):
    out = np.empty((4, SEQ, D), np.float32)
    for core in range(8):
        b, h = divmod(core, 2)
        o = np.asarray(results[core]["out"], dtype=np.float32)
        if h == 0:
            out[b, 0:OWN] = o
        else:
            out[b, OWN:SEQ] = o[::-1]
    return out


def kernel(**inputs):
    nc, _ = _get_nc(())
    in_maps = make_in_maps(**inputs)
    res = run_bass_kernel_spmd(nc, in_maps, core_ids=list(range(8)))
    return assemble(res.results)
```

```python
import numpy as np
from contextlib import ExitStack
import concourse.bass as bass
import concourse.mybir as mybir
from concourse.bass_utils import run_bass_kernel_spmd

F32 = mybir.dt.float32
BF16 = mybir.dt.bfloat16
AF = mybir.ActivationFunctionType
ALU = mybir.AluOpType

D = 1024
SEQ = 4096
OWN = 2048
CTXL = 256
NKEY = SEQ + CTXL
EPS = 1e-6
C_AK, C_AKP, C_AV, C_GK, C_GV, C_LRA, C_LRB, C_GQ, C_AQ, C_AQP, C_GO, C_GA, C_GG = (
    0, 256, 512, 768, 1280, 2304, 2320, 2336, 2848, 3872, 4896, 5920, 6944)
NIN = 7968
SAME_WIN = 3


class Tok:
    __slots__ = ("name", "w", "r", "excl")

    def __init__(self, name, excl=False):
        self.name = name
        self.w = None
        self.r = []
        self.excl = excl


class Chan:
    def __init__(self, idx):
        self.idx = idx
        self.count = 0
        self.sem = None


class Op:
    __slots__ = ("fn", "deps", "signal", "chan")

    def __init__(self, fn, deps, chan):
        self.fn = fn
        self.deps = deps
        self.signal = False
        self.chan = chan


ENGS = ("pe", "act", "dve", "pool", "sp")


class Prog:
    def __init__(self):
        self.ops = {e: [] for e in ENGS}
        self.chans = []
        self.wm = {e: {} for e in ENGS}

    def chan(self):
        c = Chan(len(self.chans))
        self.chans.append(c)
        return c

    def add(self, eng, fn, reads=(), writes=(), chan=None):
        idx = len(self.ops[eng])
        deps = []
        for t in reads:
            if t.w is not None:
                deps.append(t.w)
            if t.excl:
                deps.extend(t.r)
        for t in writes:
            if t.w is not None:
                deps.append(t.w)
            deps.extend(t.r)
        need = []
        wm = self.wm[eng]
        best = {}
        for d in deps:
            if d[0] == "e":
                _, e2, i2 = d
                if e2 == eng:
                    if chan is not None:
                        pass
                    elif eng in ("pe", "sp"):
                        continue
                    elif idx - i2 > SAME_WIN:
                        continue
                key = ("e", e2)
                val = i2
            else:
                _, c, v = d
                if chan is not None and c is chan:
                    continue
                key = ("c", c.idx)
                val = v
            if wm.get(key, -1) >= val:
                continue
            if key not in best or best[key][0] < val:
                best[key] = (val, d)
        for key, (val, d) in best.items():
            wm[key] = val
            need.append(d)
            if d[0] == "e":
                self.ops[d[1]][d[2]].signal = True
        op = Op(fn, need, chan)
        self.ops[eng].append(op)
        if chan is None:
            ref = ("e", eng, idx)
        else:
            chan.count += 16
            ref = ("c", chan, chan.count)
        for t in reads:
            if t.excl:
                t.r = [ref]
            else:
                t.r.append(ref)
        for t in writes:
            t.w = ref
            t.r = []
        return op

    def barrier(self):
        refs = []
        for e in ENGS:
            if self.ops[e]:
                for i in range(len(self.ops[e]) - 1, -1, -1):
                    if self.ops[e][i].chan is None and self.ops[e][i].fn is not None:
                        refs.append(("e", e, i))
                        break
        for c in self.chans:
            if c.count:
                refs.append(("c", c, c.count))
        bt = Tok("barrier")
        for e in ENGS:
            need = []
            wm = self.wm[e]
            for d in refs:
                if d[0] == "e":
                    if d[1] == e:
                        continue
                    key = ("e", d[1]); val = d[2]
                else:
                    key = ("c", d[1].idx); val = d[2]
                if wm.get(key, -1) >= val:
                    continue
                wm[key] = val
                need.append(d)
                if d[0] == "e":
                    self.ops[d[1]][d[2]].signal = True
            self.ops[e].append(Op(None, need, None))

    def emit(self, nc, stack, final_chans):
        sems = {e: stack.enter_context(nc.semaphore("s_" + e)) for e in ENGS}
        for c in self.chans:
            c.sem = stack.enter_context(nc.semaphore("c%d" % c.idx))
        pref = {}
        for e in ENGS:
            cnt = 0
            p = []
            for op in self.ops[e]:
                if op.signal:
                    cnt += 1
                p.append(cnt)
            pref[e] = p
        block = stack.enter_context(nc.Block())
        handles = {"pe": block.tensor, "act": block.scalar, "dve": block.vector,
                   "pool": block.gpsimd, "sp": block.sync}

        def make(e):
            def body(eng):
                for op in self.ops[e]:
                    for d in op.deps:
                        if d[0] == "e":
                            eng.wait_ge(sems[d[1]], pref[d[1]][d[2]])
                        else:
                            eng.wait_ge(d[1].sem, d[2])
                    if op.fn is None:
                        continue
                    ins = op.fn(eng)
                    if op.signal:
                        ins.then_inc(sems[e], 1)
                    if op.chan is not None:
                        ins.then_inc(op.chan.sem, 16)
                if e == "sp":
                    for c in final_chans:
                        if c.count:
                            eng.wait_ge(c.sem, c.count)
            return body

        for e in ENGS:
            handles[e](make(e))


class StopBuild(Exception):
    pass


class Builder:
    def __init__(self, dbg=()):
        self.dbg = set(dbg)
        self.nc = bass.Bass("TRN2", target_bir_lowering=False)
        self.P = Prog()
        self.stack = ExitStack()
        self.dram = {}
        self.dbg_out = []

    def din(self, name, shape, dt=F32):
        ap = self.nc.dram_tensor(name, list(shape), dt, kind="ExternalInput").ap()
        self.dram[name] = ap
        return ap

    def init_arena(self):
        nc = self.nc
        self.ARENA_BYTES = 207 * 1024
        self.arena = self.stack.enter_context(
            nc.sbuf_tensor("arena", [128, self.ARENA_BYTES // 2], BF16))
        self.regions = {}
        self.def_region("P", 0, self.ARENA_BYTES)
        self.cur_region = "P"
        self.psum = [self.stack.enter_context(nc.psum_tensor("ps%d" % i, [128, 512], F32))[:]
                     for i in range(8)]
        self.pstok = [Tok("ps%d" % i, excl=True) for i in range(8)]
        self.rot = list(range(8))
        self.rot_i = 0

    def alloc(self, shape, dt, name="", parts=128, region=None):
        esz = 4 if dt == F32 else 2
        n = int(np.prod(shape))
        nbytes = (n * esz + 63) // 64 * 64
        if region is None:
            region = self.cur_region
        r = self.regions[region]
        off = r[1]
        r[1] += nbytes
        assert r[1] <= r[2], ("SBUF overflow", region, name, r[1] - r[2])
        v = self.arena[0:parts, off // 2: off // 2 + n * esz // 2]
        if dt == F32:
            v = v.bitcast(F32)
        if len(shape) == 2:
            v = v.rearrange("p (a b) -> p a b", a=shape[0])
        elif len(shape) == 3:
            v = v.rearrange("p (a b c) -> p a b c", a=shape[0], b=shape[1])
        elif len(shape) == 4:
            v = v.rearrange("p (a b c d) -> p a b c d", a=shape[0], b=shape[1], c=shape[2])
        return v

    def def_region(self, name, start, end):
        self.regions[name] = [start, start, end]

    def reset_region(self, *names):
        for n in names:
            self.regions[n][1] = self.regions[n][0]

    def use(self, name):
        self.cur_region = name

    def set_rot(self, banks):
        self.rot = list(banks)
        self.rot_i = 0

    def bank(self):
        b = self.rot[self.rot_i % len(self.rot)]
        self.rot_i += 1
        return b

    def mm(self, out, lhsT, rhs, start, stop, reads, writes, tile_position=None):
        if tile_position is not None:
            return self.P.add("pe", lambda e: e.matmul(out, lhsT, rhs, start=start, stop=stop,
                                                       tile_position=tile_position), reads, writes)
        return self.P.add("pe", lambda e: e.matmul(out, lhsT, rhs, start=start, stop=stop),
                          reads, writes)

    def tr(self, out, in_, ident, reads, writes):
        return self.P.add("pe", lambda e: e.transpose(out, in_, ident), reads, writes)

    def act(self, out, in_, func, reads, writes, bias=None, scale=None, accum=None):
        kw = {}
        if bias is not None:
            kw["bias"] = bias
        if scale is not None:
            kw["scale"] = scale
        if accum is not None:
            kw["accum_out"] = accum
        return self.P.add("act", lambda e: e.activation(out, in_, func, **kw), reads, writes)

    def tt(self, eng, out, in0, in1, op, reads, writes):
        return self.P.add(eng, lambda e: e.tensor_tensor(out, in0, in1, op), reads, writes)

    def ts(self, eng, out, in0, s1, s2, op0, op1, reads, writes):
        if op1 is None:
            return self.P.add(eng, lambda e: e.tensor_scalar(out, in0, s1, None, op0),
                              reads, writes)
        return self.P.add(eng, lambda e: e.tensor_scalar(out, in0, s1, s2, op0, op1),
                          reads, writes)

    def stt(self, eng, out, in0, scalar, in1, op0, op1, reads, writes):
        return self.P.add(eng, lambda e: e.scalar_tensor_tensor(out, in0, scalar, in1, op0, op1),
                          reads, writes)

    def cp(self, eng, out, in_, reads, writes):
        if eng == "act":
            return self.P.add("act", lambda e: e.copy(out, in_), reads, writes)
        return self.P.add(eng, lambda e: e.tensor_copy(out, in_), reads, writes)

    def recip(self, out, in_, reads, writes):
        return self.P.add("dve", lambda e: e.reciprocal(out, in_), reads, writes)

    def memset(self, eng, ap, val, writes):
        return self.P.add(eng, lambda e: e.memset(ap, val), (), writes)

    def dma(self, q, out, in_, chan, reads, writes, slow=False):
        if slow:
            return self.P.add(q, lambda e: e.dma_start(out=out, in_=in_,
                                                       allow_slow_non_contiguous=True),
                              reads, writes, chan=chan)
        return self.P.add(q, lambda e: e.dma_start(out=out, in_=in_), reads, writes, chan=chan)

    def dump(self, name, ap, shape, dt, tok):
        if name not in self.dbg:
            return
        o = self.nc.dram_tensor("dbg_" + name, list(shape), dt, kind="ExternalOutput").ap()
        c = self.P.chan()
        self.final_chans.append(c)
        self.dma("sp", o, ap, c, [tok] if not isinstance(tok, (list, tuple)) else list(tok), [])
        self.dbg_out.append("dbg_" + name)

    def build(self):
        try:
            return self._build_body()
        except StopBuild:
            return self.finish_stub(None, None)

    def chk(self, label):
        if ("stop_" + label) in self.dbg:
            raise StopBuild()

    def _build_body(self):
        nc = self.nc
        P = self.P
        din = self.din
        self.final_chans = []
        xs = din("xs", [SEQ, D])
        ctx = din("ctx", [CTXL, D])
        cvec = din("cvec", [2, D])
        w_ada = din("w_ada", [D, 6 * D])
        b_ada = din("b_ada", [6 * D])
        norm1 = din("norm1", [D])
        norm2 = din("norm2", [D])
        w_in = din("w_in", [D, NIN])
        wgk = din("wgk", [2, 17, 512])
        qkg = din("qkg", [128, 4])
        gla_norm = din("gla_norm", [256])
        w_bra = din("w_br_attn", [D, D])
        w_brg = din("w_br_gla", [D, D])
        w_out = din("w_out", [D, D])
        w_m1 = din("w_mlp1", [D, 4 * D])
        w_m2 = din("w_mlp2", [4 * D, D])
        ident_d = din("ident", [128, 128])
        tri_d = din("tri", [128, 4, 128])
        msk_d = din("msk", [128, 2, 128])
        ropec_d = din("ropec", [128, 2, 72])
        out_d = nc.dram_tensor("out", [OWN, D], F32, kind="ExternalOutput").ap()
        self.out_d = out_d

        self.init_arena()
        A = self.alloc
        ps = self.psum
        pt = self.pstok

        ident = A([128], F32, "ident"); t_ident = Tok("ident")
        onesf = A([128], F32, "onesf"); t_onesf = Tok("onesf")
        onesb = A([128], BF16, "onesb"); t_onesb = Tok("onesb")
        tri = A([4, 128], BF16, "tri"); t_tri = Tok("tri")
        msk = A([2, 128], F32, "msk"); t_msk = Tok("msk")
        ropec = A([2, 72], F32, "ropec"); t_ropec = Tok("ropec")
        qkgs = A([4], F32, "qkg"); t_qkg = Tok("qkg")
        GT = A([2, 2, 72], F32, "GT"); t_GT = Tok("GT")
        scT = A([8, 2], BF16, "scT"); t_scT = Tok("scT")
        modT = A([48, 2], F32, "modT"); t_mod = Tok("mod")
        a1 = A([8], F32, "a1"); ac = A([8], F32, "ac"); a2 = A([8], F32, "a2"); t_av = Tok("avec"); t_av2 = Tok("avec2")
        gt1bc = A([1024], F32, "gt1bc"); t_gt1 = Tok("gt1bc")
        gt2bc = A([1024], F32, "gt2bc"); t_gt2 = Tok("gt2bc")
        glan = A([256], F32, "glan"); t_glan = Tok("glan")
        wgka = A([2, 512], BF16, "wgka", parts=32); t_wgka = Tok("wgka")
        nhalf = A([1], F32, "nhalf"); t_nhalf = Tok("nhalf")
        tiny = A([16], F32, "tiny"); t_tiny = Tok("tiny")

        self.dma("pool", tri, tri_d, P.chan(), [], [t_tri])
        for (dst, src, tk) in ((ident, ident_d, t_ident), (msk, msk_d, t_msk),
                               (ropec, ropec_d, t_ropec), (qkgs, qkg, t_qkg)):
            self.dma("sp", dst, src, P.chan(), [], [tk])
        vst = A([128], F32, "vst", parts=128); t_vst = Tok("vst")
        vT = A([80], F32, "vT"); t_vT = Tok("vT")
        self.memset("dve", vst, 0.0, [t_vst])
        c_v = P.chan()
        self.dma("sp", vst[0:48], b_ada.rearrange("(k p) -> k p", p=128), c_v, [], [t_vst])
        self.dma("sp", vst[48:56], norm1.rearrange("(k p) -> k p", p=128), c_v, [], [t_vst])
        self.dma("sp", vst[56:64], norm2.rearrange("(k p) -> k p", p=128), c_v, [], [t_vst])
        self.dma("sp", vst[64:80], cvec.rearrange("r (k p) -> (r k) p", p=128), c_v, [], [t_vst])
        self.dma("sp", glan, gla_norm.partition_broadcast(128), P.chan(), [], [t_glan])
        c_wgk = P.chan()
        self.dma("pool", wgka[0:17], wgk.rearrange("x r n -> r x n"), c_wgk, [], [t_wgka])
        self.memset("dve", onesf, 1.0, [t_onesf])
        self.memset("dve", onesb, 1.0, [t_onesb])
        self.memset("dve", nhalf, -0.5, [t_nhalf])
        self.tr(ps[0][:, 0:128], vst, ident, [t_vst, t_ident], [pt[0]])
        self.cp("dve", vT, ps[0][:, 0:80], [pt[0]], [t_vT])
        badaT = vT[:, 0:48]; n1T = vT[:, 48:56]; n2T = vT[:, 56:64]
        cT = vT[:, 64:80].rearrange("p (r k) -> p k r", r=2)
        t_bada = t_vT; t_nT = t_vT; t_cT = t_vT

        hxT = A([4, 8, 512], BF16, "hxT_own")
        t_hx = [Tok("hxT%d" % j) for j in range(4)]
        t_ag = [Tok("AG%d" % j) for j in range(4)]

        SA = A([4, 256], F32, "SA"); SB = A([4, 256], F32, "SB")
        SAb = A([4, 256], BF16, "SAb"); SBb = A([4, 256], BF16, "SBb")
        t_S = {"A": Tok("SA"), "B": Tok("SB")}
        t_Sb = {"A": Tok("SAb"), "B": Tok("SBb")}
        Sf = {"A": SA, "B": SB}
        Sb = {"A": SAb, "B": SBb}
        pend = self.regions["P"][1]
        s_start = pend - 12288
        self.def_region("S", s_start, pend)
        self.def_region("R1", pend, pend + 65536)
        self.def_region("R2", pend + 65536, pend + 65536 + 34816)
        self.def_region("R3", pend + 65536 + 34816, self.ARENA_BYTES)
        self.use("R2")
        KT = A([2, NKEY], BF16, "KT")
        V = A([34, 256], BF16, "V")
        t_kv = [Tok("kv%d" % g) for g in range(9)]

        self.use("R1")
        ws1 = A([8, 2336], BF16, "ws1"); t_ws1 = Tok("ws1"); c_ws1 = P.chan()
        xt = [A([1024], F32, "xt%d" % i) for i in range(2)]
        t_xt = [Tok("xt%d" % i) for i in range(2)]
        c_xt = [P.chan() for _ in range(2)]
        wada = [A([8, 512], BF16, "wada%d" % i) for i in range(2)]
        t_wada = [Tok("wada%d" % i) for i in range(2)]
        c_wada = [P.chan() for _ in range(2)]
        Tkc = A([2, 256], F32, "Tkc"); t_Tkc = Tok("Tkc")
        self.use("R3")
        xn = A([1024], F32, "xn"); t_xn = Tok("xn")
        hxg = A([8, 512], BF16, "hxg"); t_hxg = Tok("hxg")
        gkt = A([4, 512], BF16, "gkt"); t_gkt = Tok("gkt")
        gvt = A([4, 1024], BF16, "gvt"); t_gvt = Tok("gvt")
        lrT = {"A": A([512], BF16, "lrTA", parts=32), "B": A([512], BF16, "lrTB", parts=32)}
        t_lrT = {"A": Tok("lrTA"), "B": Tok("lrTB")}
        Tk = A([2, 512], F32, "Tk"); t_Tk = Tok("Tk")
        r_sq = A([512], BF16, "r_sq"); t_rsq = Tok("r_sq")
        r_rs = A([512], F32, "r_rs"); t_rrs = Tok("r_rs")
        r_t1 = A([512], F32, "r_t1"); t_rt1 = Tok("r_t1")
        r_t2 = A([512], F32, "r_t2"); t_rt2 = Tok("r_t2")
        def make_gset(i, full):
            G = {"sp": (A([512], BF16, "g_sp%d" % i), Tok("g_sp")),
                 "EC": (A([512], F32, "g_EC%d" % i), Tok("g_EC")),
                 "kh": (A([512], BF16, "g_kh%d" % i), Tok("g_kh")),
                 "EL": (A([4], F32, "g_EL%d" % i), Tok("g_EL"))}
            if full:
                G["E1"] = (A([512], F32, "g_E1%d" % i), Tok("g_E1"))
                G["E2"] = (A([512], F32, "g_E2%d" % i), Tok("g_E2"))
                G["qt"] = (A([4, 128], BF16, "g_qt%d" % i), Tok("g_qt"))
                G["kt"] = (A([4, 128], BF16, "g_kt%d" % i), Tok("g_kt"))
                G["AT"] = (A([4, 128], BF16, "g_AT%d" % i), Tok("g_AT"))
            return G
        gsets = [make_gset(0, False),
                 {"sp": (r_rs.bitcast(BF16)[:, 0:512], t_rrs), "EC": (r_t1, t_rt1), "kh": (r_sq, t_rsq),
                  "EL": (A([4], F32, "g_EL1"), Tok("g_EL1"))}]
        self.gcnt = 0
        sqj = r_t2.bitcast(BF16)
        t_sqj = t_rt2
        diag = A([128], F32, "diag"); t_diag = Tok("diag")

        for X in ("A", "B"):
            self.memset("dve", lrT[X], 1.0, [t_lrT[X]])
            self.memset("dve", Sf[X], 0.0, [t_S[X]])
            self.memset("dve", Sb[X], 0.0, [t_Sb[X]])

        w_in_v = w_in.rearrange("(k p) n -> p k n", p=128)

        self.set_rot([0, 1, 2, 3, 4, 5, 6, 7])
        ec = tiny
        ecv = tiny.rearrange("p (a b) -> p a b", a=8)
        self.act(ecv, cT, AF.Exp, [t_cT], [t_tiny], scale=-1.0)
        self.ts("dve", ecv, ecv, 1.0, None, ALU.add, None, [t_tiny], [t_tiny])
        self.recip(ecv, ecv, [t_tiny], [t_tiny])
        self.tt("dve", scT, ecv, cT, ALU.mult, [t_tiny, t_cT], [t_scT])

        w_ada_v = w_ada.rearrange("(k p) n -> p k n", p=128)
        self.set_rot([0, 1, 2, 3, 4, 5, 6])
        bmod = 7
        modps = ps[bmod][:, 0:96].rearrange("p (a b) -> p a b", a=48)

        def ada_dma(cb):
            for kc in range(8):
                self.dma("pool", wada[cb % 2][:, kc, :], w_ada_v[:, kc, cb * 512:(cb + 1) * 512],
                         c_wada[cb % 2], [], [t_wada[cb % 2]])

        def ada_mm(cb):
            wb = wada[cb % 2]
            for fc in range(4):
                j = cb * 4 + fc
                for kc in range(8):
                    self.mm(modps[:, j, :], wb[:, kc, fc * 128:(fc + 1) * 128], scT[:, kc, :],
                            kc == 0, kc == 7, [t_wada[cb % 2], t_scT], [pt[bmod]])

        ada_dma(0)
        ada_dma(1)
        for kc in range(8):
            self.dma("pool", ws1[:, kc, :], w_in_v[:, kc, 0:2336], c_ws1, [], [t_ws1])
        for cb in range(4):
            ada_mm(cb)
            ada_dma(cb + 2)
        self.tt("dve", modT[:, 0:16, :], modps[:, 0:16, :],
                badaT[:, 0:16].unsqueeze(2).broadcast_to([128, 16, 2]), ALU.add,
                [pt[bmod], t_bada], [t_mod])
        self.ts("dve", a1, modT[:, 8:16, 0], 1.0, 32.0, ALU.add, ALU.mult, [t_mod], [t_av])
        self.tt("dve", a1, a1, n1T, ALU.mult, [t_av, t_nT], [t_av])
        self.ts("dve", ac, modT[:, 8:16, 1], 1.0, 32.0, ALU.add, ALU.mult, [t_mod], [t_av])
        self.tt("dve", ac, ac, n1T, ALU.mult, [t_av, t_nT], [t_av])

        def ada_part2():
            for cb in range(4, 12):
                ada_mm(cb)
                if cb + 2 < 12:
                    ada_dma(cb + 2)
            self.tt("dve", modT[:, 16:48, :], modps[:, 16:48, :],
                    badaT[:, 16:48].unsqueeze(2).broadcast_to([128, 32, 2]), ALU.add,
                    [pt[bmod], t_bada], [t_mod])
            self.ts("dve", a2, modT[:, 32:40, 0], 1.0, 32.0, ALU.add, ALU.mult, [t_mod], [t_av2])
            self.tt("dve", a2, a2, n2T, ALU.mult, [t_av2, t_nT], [t_av2])
            for (dst, tk, j0) in ((gt1bc, t_gt1, 16), (gt2bc, t_gt2, 40)):
                for half in range(2):
                    b = self.bank()
                    for q in range(4):
                        kc = half * 4 + q
                        self.ts("dve", diag, ident, modT[:, j0 + kc, 0:1], None, ALU.mult, None,
                                [t_ident, t_mod], [t_diag])
                        self.mm(ps[b][:, q * 128:(q + 1) * 128], onesf, diag, True, True,
                                [t_onesf, t_diag], [pt[b]])
                    self.cp("dve", dst[:, half * 512:(half + 1) * 512], ps[b], [pt[b]], [tk])
        SQ128 = float(np.sqrt(128.0))
        self.ts("dve", GT[:, 0, 0, :], ropec[:, 0, :], qkgs[:, 0:1], None, ALU.mult, None,
                [t_ropec, t_qkg], [t_GT])
        self.ts("dve", GT[:, 0, 1, :], ropec[:, 1, :], qkgs[:, 1:2], None, ALU.mult, None,
                [t_ropec, t_qkg], [t_GT])
        self.ts("dve", GT[:, 1, 0, :], ropec[:, 0, :], qkgs[:, 2:3], SQ128, ALU.mult, ALU.mult,
                [t_ropec, t_qkg], [t_GT])
        self.ts("dve", GT[:, 1, 1, :], ropec[:, 1, :], qkgs[:, 3:4], SQ128, ALU.mult, ALU.mult,
                [t_ropec, t_qkg], [t_GT])
        Tk4 = Tk.rearrange("p c (r w) -> p c r w", r=8)
        self.cp("dve", Tk4[64:128], GT[64:128, 1, :, 0:64].unsqueeze(2).broadcast_to([64, 2, 8, 64]),
                [t_GT], [t_Tk])
        Tkc4 = Tkc.rearrange("p c (r w) -> p c r w", r=4)
        self.cp("dve", Tkc4, GT[:, 1, :, 64:68].unsqueeze(3).broadcast_to([128, 2, 4, 64]),
                [t_GT], [t_Tkc])

        if "stop_setup" in self.dbg:
            return self.finish_stub(t_mod, modT)
        def norm_transpose(src_ap, xt_i, avec, shcol, dst_fn, dst_toks, from_sbuf_tok=None):
            xin = src_ap
            rt = [t_xt[xt_i]] if from_sbuf_tok is None else [from_sbuf_tok]
            ssc = tiny[:, 0:1]
            self.act(sqj, xin, AF.Square, rt, [t_sqj, t_tiny], accum=ssc)
            self.ts("pool", tiny[:, 1:2], ssc, float(D * EPS), None, ALU.add, None, [t_tiny], [t_tiny])
            self.tt("pool", tiny[:, 2:3], tiny[:, 1:2], nhalf, ALU.pow, [t_tiny, t_nhalf], [t_tiny])
            self.act(xn, xin, AF.Copy, rt + [t_tiny], [t_xn], scale=tiny[:, 2:3])
            for half in range(2):
                b = self.bank()
                for q in range(4):
                    kc = half * 4 + q
                    self.tr(ps[b][:, q * 128:(q + 1) * 128], xn[:, kc * 128:(kc + 1) * 128], ident,
                            [t_xn, t_ident], [pt[b]])
                for q in range(4):
                    kc = half * 4 + q
                    dst = dst_fn(kc)
                    if half == 0:
                        self.act(dst, ps[b][:, q * 128:(q + 1) * 128], AF.Identity,
                                 [pt[b], t_av, t_av2, t_mod], dst_toks,
                                 scale=avec[:, kc:kc + 1], bias=shcol(kc))
                    else:
                        self.ts("dve", dst, ps[b][:, q * 128:(q + 1) * 128], avec[:, kc:kc + 1],
                                shcol(kc), ALU.mult, ALU.add, [pt[b], t_av, t_av2, t_mod], dst_toks)

        def rope_norm(b0, b1, n, Tc, Ts, t_tab, dst, dst_toks, eps_scaled):
            self.act(r_sq[:, 0:n], ps[b0][:, 0:n], AF.Square, [pt[b0]], [t_rsq])
            bs = self.bank()
            self.mm(ps[bs][:, 0:n], onesb, r_sq[:, 0:n], True, True, [t_onesb, t_rsq], [pt[bs]])
            self.chk("r1")
            import os
            EXP = os.environ.get("EXP", "")
            if EXP == "copyfirst":
                self.cp("dve", r_t1[:, 0:n], ps[b0][:, 0:n], [pt[b0]], [t_rt1])
            elif EXP == "serial":
                self.cp("dve", r_t1[:, 0:n], ps[b0][:, 0:n], [pt[b0], t_rsq], [t_rt1])
                self.chk("r1b")
                self.chk("r1b")
                self.tt("dve", r_t1[:, 0:n], r_t1[:, 0:n], Tc, ALU.mult, [t_rt1, t_tab], [t_rt1])
                self.chk("r1c")
            else:
                self.tt("dve", r_t1[:, 0:n], ps[b0][:, 0:n], Tc, ALU.mult, [pt[b0], t_tab], [t_rt1])
            self.chk("r1a")
            self.tt("dve", r_t2[:, 0:n], ps[b1][:, 0:n], Ts, ALU.mult, [pt[b1], t_tab], [t_rt2])
            self.chk("r2")
            self.act(r_rs[:, 0:n], ps[bs][:, 0:n], AF.Ln, [pt[bs]], [t_rrs], bias=eps_scaled)
            self.chk("r3")
            self.act(r_rs[:, 0:n], r_rs[:, 0:n], AF.Exp, [t_rrs], [t_rrs], scale=-0.5)
            self.chk("r4")
            self.tt("dve", r_t1[:, 0:n], r_t1[:, 0:n], r_t2[:, 0:n], ALU.add, [t_rt1, t_rt2], [t_rt1])
            self.tt("dve", dst, r_t1[:, 0:n], r_rs[:, 0:n], ALU.mult, [t_rt1, t_rrs], dst_toks)

        def proj_fm(w, t_w, c0, ncols_chunk, rhs_fn, n, rhs_toks):
            b = self.bank()
            for kc in range(8):
                self.mm(ps[b][0:ncols_chunk, 0:n], w[:, kc, c0:c0 + ncols_chunk], rhs_fn(kc),
                        kc == 0, kc == 7, [t_w] + rhs_toks, [pt[b]])
            return b

        def gla_stage1(X, full, lr_ap, gk_tile, gv_tile, t_in, qT=None, kT=None, o_dst=None,
                       o_add=False, o_tok=None):
            G = gsets[self.gcnt % len(gsets)]
            self.gcnt += 1
            g_sp, t_gsp = G["sp"]; g_EC, t_gEC = G["EC"]; g_kh, t_gkh = G["kh"]; g_EL, t_gEL = G["EL"]
            xi = 0 if X == "A" else 1
            cum = tri[:, 2 * xi, :]
            cmat = tri[:, 2 * xi + 1, :]
            last = 127 if X == "A" else 0
            bz = self.bank()
            self.mm(ps[bz], lr_ap, wgka[0:17, xi, :], True, True, t_in + [t_wgka], [pt[bz]])
            self.act(g_sp, ps[bz], AF.Exp, [pt[bz]], [t_gsp], scale=-1.0)
            self.act(g_sp, g_sp, AF.Ln, [t_gsp], [t_gsp], bias=1.0)
            bc = self.bank()
            self.mm(ps[bc], cmat, g_sp, True, True, [t_tri, t_gsp], [pt[bc]])
            if full:
                bb = self.bank()
                for h in range(4):
                    self.mm(ps[bb][:, h * 128:(h + 1) * 128], g_sp[:, h * 128:(h + 1) * 128], cum,
                            True, True, [t_gsp, t_tri], [pt[bb]])
            else:
                bl = self.bank()
                for h in range(4):
                    self.mm(ps[bl][:, h:h + 1], g_sp[:, h * 128:(h + 1) * 128],
                            cum[:, last:last + 1], True, True, [t_gsp, t_tri], [pt[bl]])
            self.act(g_EC, ps[bc], AF.Exp, [pt[bc]], [t_gEC])
            self.tt("dve", g_kh, gk_tile, g_EC, ALU.mult, t_in + [t_gEC], [t_gkh])
            st = dict(X=X, full=full, G=G, gv_tile=gv_tile, t_in=t_in, o_dst=o_dst, o_add=o_add,
                      o_tok=o_tok, xi=xi)
            if full:
                g_E1, t_gE1 = G["E1"]; g_E2, t_gE2 = G["E2"]; g_qt, t_gqtl = G["qt"]
                g_kt, t_gktl = G["kt"]
                self.act(g_E1, ps[bb], AF.Exp, [pt[bb]], [t_gE1])
                self.act(g_E2, ps[bb], AF.Exp, [pt[bb]], [t_gE2], scale=-1.0)
                E1v = g_E1.rearrange("p (h t) -> p h t", h=4)
                E2v = g_E2.rearrange("p (h t) -> p h t", h=4)
                self.tt("dve", g_qt, qT, E1v, ALU.mult, t_in + [t_gE1], [t_gqtl])
                self.tt("dve", g_kt, kT, E2v, ALU.mult, t_in + [t_gE2], [t_gktl])
                st["el"] = lambda h: g_E1[:, h * 128 + last: h * 128 + last + 1]
                st["el_tok"] = t_gE1
            else:
                self.act(g_EL, ps[bl][:, 0:4], AF.Exp, [pt[bl]], [t_gEL])
                st["el"] = lambda h: g_EL[:, h:h + 1]
                st["el_tok"] = t_gEL
            return st

        def gla_stage2(st):
            X = st["X"]; G = st["G"]; gv_tile = st["gv_tile"]; t_in = st["t_in"]; xi = st["xi"]
            g_kh, t_gkh = G["kh"]
            if st["full"]:
                g_qt, t_gqtl = G["qt"]; g_kt, t_gktl = G["kt"]; g_AT, t_gAT = G["AT"]
                ba = self.bank()
                for h in range(4):
                    self.mm(ps[ba][:, h * 128:(h + 1) * 128], g_kt[:, h, :], g_qt[:, h, :],
                            True, True, [t_gktl, t_gqtl], [pt[ba]])
                self.tt("dve", g_AT, ps[ba].rearrange("p (h t) -> p h t", h=4),
                        msk[:, xi, :].unsqueeze(1).broadcast_to([128, 4, 128]), ALU.mult,
                        [pt[ba], t_msk], [t_gAT])
                for hp in range(2):
                    bo = self.bank()
                    for hh in range(2):
                        h = hp * 2 + hh
                        self.mm(ps[bo][:, hh * 256:(hh + 1) * 256], g_qt[:, h, :], Sb[X][:, h, :],
                                True, False, [t_gqtl, t_Sb[X]], [pt[bo]])
                        self.mm(ps[bo][:, hh * 256:(hh + 1) * 256], g_AT[:, h, :],
                                gv_tile[:, h * 256:(h + 1) * 256], False, True,
                                [t_gAT] + t_in, [pt[bo]])
                    od = st["o_dst"][:, hp * 512:(hp + 1) * 512]
                    if st["o_add"]:
                        self.tt("dve", od, ps[bo], od, ALU.add, [pt[bo], st["o_tok"]], [st["o_tok"]])
                    else:
                        self.cp("act", od, ps[bo], [pt[bo]], [st["o_tok"]])
            el = st["el"]; el_tok = st["el_tok"]
            for hp in range(2):
                bu = self.bank()
                for hh in range(2):
                    h = hp * 2 + hh
                    self.mm(ps[bu][:, hh * 256:(hh + 1) * 256], g_kh[:, h * 128:(h + 1) * 128],
                            gv_tile[:, h * 256:(h + 1) * 256], True, True, [t_gkh] + t_in, [pt[bu]])
                for hh in range(2):
                    h = hp * 2 + hh
                    self.stt("dve", Sf[X][:, h, :], Sf[X][:, h, :], el(h),
                             ps[bu][:, hh * 256:(hh + 1) * 256], ALU.mult, ALU.add,
                             [t_S[X], el_tok, pt[bu]], [t_S[X]])
            self.cp("act", Sb[X], Sf[X], [t_S[X]], [t_Sb[X]])

        def gla_run(step_args, inter=None):
            prev = None
            for a in step_args:
                cur = gla_stage1(*a[0], **a[1])
                if inter is not None:
                    next(inter, None)
                if prev is not None:
                    gla_stage2(prev)
                prev = cur
            if prev is not None:
                gla_stage2(prev)
            if inter is not None:
                for _ in inter:
                    pass

        wf0 = wada[0].rearrange("p a b -> p (a b)")
        wf1 = wada[1].rearrange("p a b -> p (a b)")
        bsets = [
            dict(gkt=gkt, t_gkt=t_gkt, gvt=gvt, t_gvt=t_gvt, lrT=lrT, t_lrT=t_lrT),
            dict(gkt=wf0[:, 0:2048].rearrange("p (t n) -> p t n", t=4), t_gkt=t_wada[0],
                 gvt=wf1.rearrange("p (t n) -> p t n", t=4), t_gvt=t_wada[1],
                 lrT={"A": wf0[0:32, 2048:2560], "B": wf0[0:32, 2560:3072]},
                 t_lrT={"A": t_wada[0], "B": t_wada[0]}),
        ]
        self.xcount = 0

        def front_tiles(kind, g):
            ntile = 2 if kind == "ctx" else 4
            avec = ac if kind == "ctx" else a1
            rcol = 1 if kind == "ctx" else 0
            shcol = lambda kc, rcol=rcol: modT[:, kc, rcol:rcol + 1]
            src = ctx if kind == "ctx" else xs
            for t in range(ntile):
                xi = self.xcount % 2
                self.xcount += 1
                r0 = t * 128 if kind == "ctx" else g * 512 + t * 128
                self.dma("sp", xt[xi], src[r0:r0 + 128, :], c_xt[xi], [], [t_xt[xi]])
                if kind == "own":
                    dst_fn = lambda kc, t=t, g=g: hxT[:, g, kc, t * 128:(t + 1) * 128]
                    dtoks = [t_hx[g]]
                else:
                    dst_fn = lambda kc, t=t: hxg[:, kc, t * 128:(t + 1) * 128]
                    dtoks = [t_hxg]
                norm_transpose(xt[xi], xi, avec, shcol, dst_fn, dtoks)
                yield t

        def front_proj(kind, g, bs):
            ntile = 2 if kind == "ctx" else 4
            n = ntile * 128
            keyoff = SEQ if kind == "ctx" else g * 512
            if kind == "own":
                rhs_fn = lambda kc, g=g: hxT[:, g, kc, :]
                rtoks = [t_hx[g]]
                lhs_fn = lambda kc, t, g=g: hxT[:, g, kc, t * 128:(t + 1) * 128]
            else:
                rhs_fn = lambda kc, n=n: hxg[:, kc, 0:n]
                rtoks = [t_hxg]
                lhs_fn = lambda kc, t: hxg[:, kc, t * 128:(t + 1) * 128]
            if kind == "ctx":
                Tc, Ts, t_tab = Tkc[:, 0, :], Tkc[:, 1, :], t_Tkc
            else:
                self.cp("dve", Tk4[0:64],
                        GT[0:64, 1, :, g * 8:(g + 1) * 8].unsqueeze(3).broadcast_to([64, 2, 8, 64]),
                        [t_GT], [t_Tk])
                Tc, Ts, t_tab = Tk[:, 0, :], Tk[:, 1, :], t_Tk
            gi = 8 if kind == "ctx" else g
            for kvh in range(2):
                b0 = proj_fm(ws1, t_ws1, C_AK + kvh * 128, 128, rhs_fn, n, rtoks)
                b1 = proj_fm(ws1, t_ws1, C_AKP + kvh * 128, 128, rhs_fn, n, rtoks)
                rope_norm(b0, b1, n, Tc, Ts, t_tab, KT[:, kvh, keyoff:keyoff + n], [t_kv[gi]],
                          float(128 * EPS))
            for t in range(ntile):
                b = self.bank()
                for kc in range(8):
                    self.mm(ps[b][:, 0:256], lhs_fn(kc, t), ws1[:, kc, C_AV:C_AV + 256],
                            kc == 0, kc == 7, rtoks + [t_ws1], [pt[b]])
                self.cp("act", V[:, keyoff // 128 + t, :], ps[b][:, 0:256], [pt[b]], [t_kv[gi]])
            if kind == "own":
                return
            for t in range(ntile):
                b = self.bank()
                for kc in range(8):
                    self.mm(ps[b], lhs_fn(kc, t), ws1[:, kc, C_GK:C_GK + 512],
                            kc == 0, kc == 7, rtoks + [t_ws1], [pt[b]])
                self.cp("dve", bs["gkt"][:, t, :], ps[b], [pt[b]], [bs["t_gkt"]])
                for hf in range(2):
                    b = self.bank()
                    for kc in range(8):
                        self.mm(ps[b], lhs_fn(kc, t),
                                ws1[:, kc, C_GV + hf * 512:C_GV + (hf + 1) * 512],
                                kc == 0, kc == 7, rtoks + [t_ws1], [pt[b]])
                    self.cp("act", bs["gvt"][:, t, hf * 512:(hf + 1) * 512], ps[b], [pt[b]],
                            [bs["t_gvt"]])
            dirs = ("A", "B") if kind == "ctx" else ("B",)
            for X in dirs:
                c0 = C_LRA if X == "A" else C_LRB
                b = proj_fm(ws1, t_ws1, c0, 16, rhs_fn, n, rtoks)
                self.cp("dve", bs["lrT"][X][0:16, 0:n], ps[b][0:16, 0:n], [pt[b]], [bs["t_lrT"][X]])

        def steps(kind, g, bs, inter=None):
            ntile = 2 if kind == "ctx" else 4
            dirs = ("A", "B") if kind == "ctx" else ("B",)
            args = []
            for X in dirs:
                order = range(ntile) if X == "A" else range(ntile - 1, -1, -1)
                for t in order:
                    args.append(((X, False, bs["lrT"][X][0:17, t * 128:(t + 1) * 128], bs["gkt"][:, t, :],
                                  bs["gvt"][:, t, :], [bs["t_lrT"][X], bs["t_gkt"], bs["t_gvt"]]), {}))
            gla_run(args, inter)

        def front(kind, g, bs):
            for _ in front_tiles(kind, g):
                pass
            front_proj(kind, g, bs)

        front("ctx", 8, bsets[0])
        ada_part2()
        for X in ("A", "B"):
            self.memset("dve", bsets[1]["lrT"][X], 1.0, [t_wada[0]])
        front("oth", 7, bsets[1])
        steps("ctx", 8, bsets[0], front_tiles("oth", 6))
        front_proj("oth", 6, bsets[0])
        steps("oth", 7, bsets[1], front_tiles("oth", 5))
        front_proj("oth", 5, bsets[1])
        steps("oth", 6, bsets[0], front_tiles("oth", 4))
        front_proj("oth", 4, bsets[0])
        steps("oth", 5, bsets[1], front_tiles("own", 0))
        front_proj("own", 0, None)
        steps("oth", 4, bsets[0], front_tiles("own", 1))
        front_proj("own", 1, None)
        for g in (2, 3):
            front("own", g, None)
        self.dump("modT", modT, [128, 48, 2], F32, t_mod)
        self.dump("gt1bc", gt1bc, [128, 1024], F32, t_gt1)
        self.dump("KT", KT, [128, 2, NKEY], BF16, t_kv)
        self.dump("V", V, [128, 34, 256], BF16, t_kv)
        self.dump("hxT", hxT, [128, 4, 8, 512], BF16, t_hx)
        self.dump("SA", SA, [128, 4, 256], F32, t_S["A"])
        self.dump("SB", SB, [128, 4, 256], F32, t_S["B"])

        if "stop_s1" in self.dbg:
            return self.finish_stub(t_mod, modT)

        P.barrier()
        self.reset_region("R1", "R3")
        self.use("R1")
        AG = A([4, 2, 8, 512], BF16, "AG")
        self.use("R3")
        wq = A([8, 1024], BF16, "wq"); t_wq = Tok("wq"); c_wq = P.chan()
        Qbs = [A([4, 512], BF16, "Qb%d" % i) for i in range(2)]
        t_Qbs = [Tok("Qb%d" % i) for i in range(2)]
        Tq = A([2, 512], F32, "Tq"); t_Tq = Tok("Tq")
        r_sq = A([512], BF16, "r_sq2"); t_rsq = Tok("r_sq2")
        r_rs = A([512], F32, "r_rs2"); t_rrs = Tok("r_rs2")
        r_t1 = A([512], F32, "r_t12"); t_rt1 = Tok("r_t12")
        r_t2 = A([512], F32, "r_t22"); t_rt2 = Tok("r_t22")
        PTN = 4
        PT = [A([512], BF16, "PT%d" % i) for i in range(PTN)]
        t_PT = [Tok("PT%d" % i) for i in range(PTN)]
        sst = r_t2; t_sst = t_rt2
        ones32 = A([128], F32, "ones32"); t_ones32 = Tok("ones32")
        self.memset("dve", ones32, 1.0 / 32.0, [t_ones32])
        rec = A([512], F32, "rec"); t_rec = Tok("rec")
        Tq4 = Tq.rearrange("p c (r w) -> p c r w", r=8)
        self.cp("dve", Tq4[64:128], GT[64:128, 0, :, 0:64].unsqueeze(2).broadcast_to([64, 2, 8, 64]),
                [t_GT], [t_Tq])
        self.set_rot([0, 1, 2, 3])
        iters = [(hh, blk) for hh in range(2) for blk in range(4)]
        self.wq_loaded = -1

        def emit_q(it):
            hh, blk = iters[it]
            if self.wq_loaded != hh:
                self.wq_loaded = hh
                for kc in range(8):
                    self.dma("pool", wq[:, kc, 0:512],
                             w_in_v[:, kc, C_AQ + hh * 512:C_AQ + (hh + 1) * 512], c_wq, [], [t_wq])
                    self.dma("pool", wq[:, kc, 512:1024],
                             w_in_v[:, kc, C_AQP + hh * 512:C_AQP + (hh + 1) * 512], c_wq, [], [t_wq])
            self.cp("dve", Tq4[0:64],
                    GT[0:64, 0, :, blk * 8:(blk + 1) * 8].unsqueeze(3).broadcast_to([64, 2, 8, 64]),
                    [t_GT], [t_Tq])
            rhs_fn = lambda kc: hxT[:, blk, kc, :]
            for hl in range(4):
                b0 = proj_fm(wq, t_wq, hl * 128, 128, rhs_fn, 512, [t_hx[blk]])
                b1 = proj_fm(wq, t_wq, 512 + hl * 128, 128, rhs_fn, 512, [t_hx[blk]])
                rope_norm(b0, b1, 512, Tq[:, 0, :], Tq[:, 1, :], t_Tq, Qbs[it % 2][:, hl, :],
                          [t_Qbs[it % 2]], float(128 * EPS))

        self.pcount = 0

        def emit_unit(it, hl):
            hh, blk = iters[it]
            Qb, t_Qb = Qbs[it % 2], t_Qbs[it % 2]
            h = hh * 4 + hl
            kvh = h // 4
            bo = 4 + (self.pcount % 2) * 2
            bsum = bo + 1
            self.pcount += 1

            def smm(kt):
                b = self.bank()
                gi = 8 if kt >= 32 else kt // 4
                self.mm(ps[b], KT[:, kvh, kt * 128:(kt + 1) * 128], Qb[:, hl, :], True, True,
                        [t_kv[gi], t_Qb], [pt[b]])
                return b
            bcur = smm(0)
            bnext = None
            for kt in range(34):
                pi = kt % PTN
                self.act(PT[pi], ps[bcur], AF.Exp, [pt[bcur]], [t_PT[pi]])
                if kt + 1 < 34:
                    bnext = smm(kt + 1)
                gi = 8 if kt >= 32 else kt // 4
                self.mm(ps[bo], V[:, kt, kvh * 128:(kvh + 1) * 128], PT[pi], kt == 0, kt == 33,
                        [t_kv[gi], t_PT[pi]], [pt[bo]])
                self.mm(ps[bsum], onesb, PT[pi], kt == 0, kt == 33,
                        [t_onesb, t_PT[pi]], [pt[bsum]])
                bcur = bnext
            self.recip(rec, ps[bsum], [pt[bsum]], [t_rec])
            self.tt("dve", AG[:, blk, 0, h, :], ps[bo], rec, ALU.mult, [pt[bo], t_rec],
                    [t_ag[blk]])

        emit_q(0)
        for it in range(len(iters)):
            emit_unit(it, 0)
            emit_unit(it, 1)
            if it + 1 < len(iters):
                emit_q(it + 1)
            emit_unit(it, 2)
            emit_unit(it, 3)
        self.dump("attnT", AG[:, :, 0, :, :], [128, 4, 8, 512], BF16, t_ag)
        if "stop_att" in self.dbg:
            return self.finish_stub(t_mod, modT)

        P.barrier()
        self.reset_region("R2", "R3")
        self.set_rot([0, 1, 2, 3, 4, 5, 6, 7])
        self.use("R2")
        wg = A([8, 2080], BF16, "wg"); t_wg = Tok("wg"); c_wg = P.chan()
        self.use("R3")
        WG0 = 768
        gkT = A([4, 512], BF16, "gkT"); t_gkT = Tok("gkT")
        gqT = A([4, 512], BF16, "gqT"); t_gqT = Tok("gqT")
        gkt = A([4, 512], BF16, "gkt2"); t_gkt = Tok("gkt2")
        gvt = A([4, 1024], BF16, "gvt2"); t_gvt = Tok("gvt2")
        lrT = {"A": A([512], BF16, "lrTA2", parts=32), "B": A([512], BF16, "lrTB2", parts=32)}
        t_lrT = {"A": Tok("lrTA2"), "B": Tok("lrTB2")}
        gsets = [make_gset(10 + i, True) for i in range(2)]
        for X in ("A", "B"):
            self.memset("dve", lrT[X], 1.0, [t_lrT[X]])
        for kc in range(8):
            self.dma("pool", wg[:, kc, :], w_in_v[:, kc, WG0:WG0 + 2080], c_wg, [], [t_wg])

        def gla_group(X, g, reuse=False):
            rhs_fn = lambda kc: hxT[:, g, kc, :]
            rtoks = [t_hx[g]]
            for h in range(0 if reuse else 4):
                b = proj_fm(wg, t_wg, C_GK - WG0 + h * 128, 128, rhs_fn, 512, rtoks)
                self.cp("act", gkT[:, h, :], ps[b], [pt[b]], [t_gkT])
                b = proj_fm(wg, t_wg, C_GQ - WG0 + h * 128, 128, rhs_fn, 512, rtoks)
                self.ts("dve", gqT[:, h, :], ps[b], float(128.0 ** -0.5), None, ALU.mult, None,
                        [pt[b]], [t_gqT])
            for t in range(0 if reuse else 4):
                b = self.bank()
                for kc in range(8):
                    self.mm(ps[b], hxT[:, g, kc, t * 128:(t + 1) * 128],
                            wg[:, kc, C_GK - WG0:C_GK - WG0 + 512], kc == 0, kc == 7,
                            rtoks + [t_wg], [pt[b]])
                self.cp("dve", gkt[:, t, :], ps[b], [pt[b]], [t_gkt])
                for hf in range(2):
                    b = self.bank()
                    for kc in range(8):
                        self.mm(ps[b], hxT[:, g, kc, t * 128:(t + 1) * 128],
                                wg[:, kc, C_GV - WG0 + hf * 512:C_GV - WG0 + (hf + 1) * 512],
                                kc == 0, kc == 7, rtoks + [t_wg], [pt[b]])
                    self.cp("act", gvt[:, t, hf * 512:(hf + 1) * 512], ps[b], [pt[b]], [t_gvt])
            c0 = (C_LRA if X == "A" else C_LRB) - WG0
            b = proj_fm(wg, t_wg, c0, 16, rhs_fn, 512, rtoks)
            self.cp("dve", lrT[X][0:16, :], ps[b][0:16, :], [pt[b]], [t_lrT[X]])
            order = range(4) if X == "A" else range(3, -1, -1)
            args = []
            for t in order:
                osl = AG[:, g, 1, :, :].rearrange("p a b -> p (a b)")[:, t * 1024:(t + 1) * 1024]
                args.append(((X, True, lrT[X][0:17, t * 128:(t + 1) * 128], gkt[:, t, :], gvt[:, t, :],
                              [t_lrT[X], t_gkt, t_gvt, t_gkT, t_gqT]),
                             dict(qT=gqT[:, :, t * 128:(t + 1) * 128], kT=gkT[:, :, t * 128:(t + 1) * 128],
                                  o_dst=osl, o_add=(X == "A"), o_tok=t_ag[g])))
            gla_run(args)

        for g in (3, 2, 1, 0):
            gla_group("B", g)
        for g in (0, 1, 2, 3):
            gla_group("A", g, reuse=(g == 0))
        self.dump("osum", AG[:, :, 1, :, :], [128, 4, 8, 512], BF16, t_ag)
        if "stop_gla" in self.dbg:
            return self.finish_stub(t_mod, modT)

        P.barrier()
        self.reset_region("S", "R2", "R3")
        self.set_rot([0, 1, 2, 3, 4, 5, 6, 7])
        self.use("R3")
        wgo = A([8, 1024], BF16, "wgo"); t_wgo = Tok("wgo"); c_wgo = P.chan()
        for kc in range(8):
            self.dma("pool", wgo[:, kc, :], w_in_v[:, kc, C_GO:C_GO + 1024], c_wgo, [], [t_wgo])
        self.use("R2")
        wga = A([8, 1024], BF16, "wga"); t_wga = Tok("wga"); c_wga = P.chan()
        wba = A([8, 1024], BF16, "wba"); t_wba = Tok("wba"); c_wba = P.chan()
        w_bra_v = w_bra.rearrange("(k p) n -> p k n", p=128)
        w_brg_v = w_brg.rearrange("(k p) n -> p k n", p=128)
        w_out_v = w_out.rearrange("(k p) n -> p k n", p=128)
        for kc in range(8):
            self.dma("pool", wga[:, kc, :], w_in_v[:, kc, C_GA:C_GA + 1024], c_wga, [], [t_wga])
            self.dma("pool", wba[:, kc, :], w_bra_v[:, kc, :], c_wba, [], [t_wba])
        self.use("R3")
        gx = A([4, 1024], F32, "gx"); t_gx = Tok("gx")
        sgs = [A([1024], F32, "sg%d" % i) for i in range(2)]
        t_sgs = [Tok("sg%d" % i) for i in range(2)]
        ssgs = [A([8], F32, "ssg%d" % i) for i in range(2)]
        t_ssgs = [Tok("ssg%d" % i) for i in range(2)]
        o2j = A([256], BF16, "o2j"); t_o2j = Tok("o2j")
        acnt = 0
        for blk in range(4):
            for t in range(4):
                sg, t_sg = sgs[acnt % 2], t_sgs[acnt % 2]
                ssg, t_ssg = ssgs[acnt % 2], t_ssgs[acnt % 2]
                acnt += 1
                osl = AG[:, blk, 1, :, :].rearrange("p a b -> p (a b)")[:, t * 1024:(t + 1) * 1024]
                for hf in range(2):
                    b = self.bank()
                    for kc in range(8):
                        self.mm(ps[b], hxT[:, blk, kc, t * 128:(t + 1) * 128],
                                wgo[:, kc, hf * 512:(hf + 1) * 512], kc == 0, kc == 7,
                                [t_hx[blk], t_wgo], [pt[b]])
                    sgh = sg[:, hf * 512:(hf + 1) * 512]
                    self.act(sgh, ps[b], AF.Silu, [pt[b]], [t_sg])
                    sgh3 = sgh.rearrange("p (h e) -> p h e", h=2)
                    self.tt("dve", sgh3, sgh3, glan.unsqueeze(1).broadcast_to([128, 2, 256]), ALU.mult,
                            [t_sg, t_glan], [t_sg])
                for h in range(4):
                    self.act(o2j, osl[:, h * 256:(h + 1) * 256], AF.Square,
                             [t_ag[blk]], [t_o2j, t_ssg], accum=ssg[:, h:h + 1])
                self.ts("pool", ssg[:, 0:4], ssg[:, 0:4], float(1.0 / 256), float(EPS), ALU.mult, ALU.add,
                        [t_ssg], [t_ssg])
                self.tt("pool", ssg[:, 4:8], ssg[:, 0:4], nhalf.broadcast_to([128, 4]), ALU.pow,
                        [t_ssg, t_nhalf], [t_ssg])
                for h in range(4):
                    self.stt("dve", gx[:, t, h * 256:(h + 1) * 256], osl[:, h * 256:(h + 1) * 256],
                             ssg[:, 4 + h:5 + h], sg[:, h * 256:(h + 1) * 256], ALU.mult, ALU.mult,
                             [t_ag[blk], t_ssg, t_sg], [t_gx])
            for t in range(4):
                for half in range(2):
                    b = self.bank()
                    for q in range(4):
                        kc = half * 4 + q
                        self.tr(ps[b][:, q * 128:(q + 1) * 128], gx[:, t, kc * 128:(kc + 1) * 128], ident,
                                [t_gx, t_ident], [pt[b]])
                    dstv = AG[:, blk, 1, half * 4:half * 4 + 4, t * 128:(t + 1) * 128]
                    self.cp("act" if half == 0 else "dve", dstv,
                            ps[b].rearrange("p (q t) -> p q t", q=4), [pt[b]], [t_ag[blk]])
        self.dump("glaT", AG[:, :, 1, :, :], [128, 4, 8, 512], BF16, t_ag)
        self.chk("a")
        P.barrier()
        self.reset_region("S", "R3")
        self.use("R3")
        yT = A([8, 512], BF16, "yT"); t_yT = Tok("yT")
        sgts = [A([512], F32, "sgt%d" % i) for i in range(2)]
        t_sgts = [Tok("sgt%d" % i) for i in range(2)]
        self.sgc = 0
        wo = A([8, 1024], BF16, "wo"); t_wo = Tok("wo"); c_wo = P.chan()
        for kc in range(8):
            self.dma("pool", wo[:, kc, :], w_out_v[:, kc, :], c_wo, [], [t_wo])
        for kc in range(8):
            self.tt("dve", wo[:, kc, :], wo[:, kc, :], gt1bc, ALU.mult, [t_wo, t_gt1], [t_wo])

        def gated_proj(blk, wgate, t_wgate, wbr, t_wbr, src_half, fc, dst, dst_toks, add_src=None,
                       add_toks=()):
            bg = proj_fm(wgate, t_wgate, fc * 128, 128, lambda kc: hxT[:, blk, kc, :], 512, [t_hx[blk]])
            sgt, t_sgt = sgts[self.sgc % 2], t_sgts[self.sgc % 2]
            self.sgc += 1
            self.act(sgt, ps[bg], AF.Sigmoid, [pt[bg]], [t_sgt])
            bp = proj_fm(wbr, t_wbr, fc * 128, 128, lambda kc: AG[:, blk, src_half, kc, :], 512,
                         [t_ag[blk]])
            if add_src is None:
                self.tt("dve", dst, ps[bp], sgt, ALU.mult, [pt[bp], t_sgt], dst_toks)
            else:
                self.tt("dve", sgt, ps[bp], sgt, ALU.mult, [pt[bp], t_sgt], [t_sgt])
                self.tt("dve", dst, sgt, add_src, ALU.add, [t_sgt] + list(add_toks), dst_toks)

        for blk in range(4):
            for fc in range(8):
                gated_proj(blk, wga, t_wga, wba, t_wba, 0, fc, yT[:, fc, :], [t_yT])
            self.cp("dve", AG[:, blk, 0, :, :], yT, [t_yT], [t_ag[blk]])
        self.dump("y1T", AG[:, :, 0, :, :], [128, 4, 8, 512], BF16, t_ag)
        self.chk("b1")
        P.barrier()
        self.reset_region("S", "R2")
        self.use("R2")
        wgg = A([8, 1024], BF16, "wgg"); t_wgg = Tok("wgg"); c_wgg = P.chan()
        wbg = A([8, 1024], BF16, "wbg"); t_wbg = Tok("wbg"); c_wbg = P.chan()
        self.use("R3")
        for kc in range(8):
            self.dma("pool", wgg[:, kc, :], w_in_v[:, kc, C_GG:C_GG + 1024], c_wgg, [], [t_wgg])
            self.dma("pool", wbg[:, kc, :], w_brg_v[:, kc, :], c_wbg, [], [t_wbg])
        xt = [A([1024], F32, "xtb%d" % i) for i in range(2)]
        t_xt = [Tok("xtb%d" % i) for i in range(2)]
        c_xt = [P.chan() for _ in range(2)]
        xn = A([1024], F32, "xn2"); t_xn = Tok("xn2")
        sqj = A([1024], BF16, "sqj2"); t_sqj = Tok("sqj2")
        x2v = [AG[:, blk].rearrange("p a b c -> p (a b c)").bitcast(F32).rearrange(
            "p (t d) -> p t d", t=4) for blk in range(4)]
        xcount = 0

        def gp_gen(blk):
            for fc in range(8):
                gated_proj(blk, wgg, t_wgg, wbg, t_wbg, 1, fc, yT[:, fc, :], [t_yT],
                           add_src=AG[:, blk, 0, fc, :], add_toks=[t_ag[blk]])
                yield fc

        cur = gp_gen(0)
        for blk in range(4):
            for _ in cur:
                pass
            for t in range(4):
                xi = xcount % 2
                xcount += 1
                r0 = blk * 512 + t * 128
                self.dma("sp", xt[xi], xs[r0:r0 + 128, :], c_xt[xi], [], [t_xt[xi]])
                for hf in range(2):
                    b = self.bank()
                    for kc in range(8):
                        self.mm(ps[b], yT[:, kc, t * 128:(t + 1) * 128], wo[:, kc, hf * 512:(hf + 1) * 512],
                                kc == 0, kc == 7, [t_yT, t_wo], [pt[b]])
                    xh = xt[xi][:, hf * 512:(hf + 1) * 512]
                    self.tt("dve", x2v[blk][:, t, hf * 512:(hf + 1) * 512], ps[b], xh, ALU.add,
                            [pt[b], t_xt[xi]], [t_ag[blk]])
            cur = gp_gen(blk + 1) if blk + 1 < 4 else iter(())
            for t in range(4):
                norm_transpose(x2v[blk][:, t, :], 0, a2, lambda kc: modT[:, 24 + kc, 0:1],
                               lambda kc, t=t, blk=blk: hxT[:, blk, kc, t * 128:(t + 1) * 128],
                               [t_hx[blk]], from_sbuf_tok=t_ag[blk])
                next(cur, None)
                next(cur, None)
        self.dump("x2", AG.rearrange("p a b c d -> p (a b c d)").bitcast(F32), [128, 16384], F32, t_ag)
        self.dump("hmT", hxT, [128, 4, 8, 512], BF16, t_hx)
        self.chk("b2")

        P.barrier()
        self.reset_region("S", "R2", "R3")
        self.use("R2")
        w1q = [A([8, 1024], BF16, "w1q%d" % i) for i in range(2)]
        self.use("R3")
        w2q = [A([8, 1024], BF16, "w2q%d" % i) for i in range(2)]
        t_w1q = [Tok("w1q%d" % i) for i in range(2)]
        t_w2q = [Tok("w2q%d" % i) for i in range(2)]
        c_w1q = [P.chan() for _ in range(2)]
        c_w2q = [P.chan() for _ in range(2)]
        h1 = A([8, 512], BF16, "h1"); t_h1 = Tok("h1")
        rl = [A([512], F32, "rl%d" % i) for i in range(2)]
        t_rl = [Tok("rl%d" % i) for i in range(2)]
        self.use("S")
        tmp = A([512], F32, "mtmp"); t_tmp = Tok("mtmp")
        ost = [A([1024], F32, "ost%d" % i) for i in range(2)]
        t_ost = [Tok("ost%d" % i) for i in range(2)]
        c_ost = [P.chan() for _ in range(2)]
        self.final_chans.extend(c_ost)
        w_m1_v = w_m1.rearrange("(k p) n -> p k n", p=128)
        w_m2_v = w_m2.rearrange("(k p) n -> p k n", p=128)
        ocount = 0
        rcount = 0
        for q in range(4):
            wi = q % 2
            for kc in range(8):
                self.dma("pool", w1q[wi][:, kc, :], w_m1_v[:, kc, q * 1024:(q + 1) * 1024], c_w1q[wi],
                         [], [t_w1q[wi]])
            for kc in range(8):
                self.dma("pool", w2q[wi][:, kc, :], w_m2_v[:, q * 8 + kc, :], c_w2q[wi], [], [t_w2q[wi]])
            for blk in range(4):
                for fc in range(8):
                    b = proj_fm(w1q[wi], t_w1q[wi], fc * 128, 128, lambda kc: hxT[:, blk, kc, :], 512,
                                [t_hx[blk]])
                    ri = rcount % 2
                    rcount += 1
                    self.act(rl[ri], ps[b], AF.Relu, [pt[b]], [t_rl[ri]])
                    self.tt("dve", h1[:, fc, :], rl[ri], rl[ri], ALU.mult, [t_rl[ri]], [t_h1])
                for t in range(4):
                    for hf in range(2):
                        b = self.bank()
                        for kc in range(8):
                            self.mm(ps[b], h1[:, kc, t * 128:(t + 1) * 128],
                                    w2q[wi][:, kc, hf * 512:(hf + 1) * 512], kc == 0, kc == 7,
                                    [t_h1, t_w2q[wi]], [pt[b]])
                        self.tt("dve", tmp, ps[b], gt2bc[:, hf * 512:(hf + 1) * 512], ALU.mult,
                                [pt[b], t_gt2], [t_tmp])
                        x2h = x2v[blk][:, t, hf * 512:(hf + 1) * 512]
                        if q < 3:
                            self.tt("dve", x2h, tmp, x2h, ALU.add, [t_tmp, t_ag[blk]], [t_ag[blk]])
                        else:
                            oi = ocount % 2
                            self.tt("dve", ost[oi][:, hf * 512:(hf + 1) * 512], tmp, x2h, ALU.add,
                                    [t_tmp, t_ag[blk]], [t_ost[oi]])
                    if q == 3:
                        oi = ocount % 2
                        ocount += 1
                        r0 = blk * 512 + t * 128
                        self.dma("sp", out_d[r0:r0 + 128, :], ost[oi], c_ost[oi], [t_ost[oi]], [])
        self.finalize()
        return self.nc

    def finish_stub(self, tok, ap):
        self.reset_region("R3")
        z = self.alloc([1024], F32, "zstub", region="R3")
        tz = Tok("z")
        self.memset("dve", z, 0.0, [tz])
        c = self.P.chan()
        self.final_chans.append(c)
        for i in range(16):
            self.dma("sp", self.out_d[i * 128:(i + 1) * 128, :], z, c, [tz], [])
        self.finalize()
        return self.nc

    def finalize(self):
        self.P.emit(self.nc, self.stack, self.final_chans)
        self.stack.close()


def _perm_half_swap(nheads):
    idx = []
    for h in range(nheads):
        for d in range(128):
            axis, rem = divmod(d, 64)
            half, f = divmod(rem, 32)
            idx.append(h * 128 + axis * 64 + (1 - half) * 32 + f)
    return np.array(idx)


def _consts(h):
    ident = np.eye(128, dtype=np.float32)
    r = np.arange(128)[:, None]
    c = np.arange(128)[None, :]
    s = np.float32(-1.0 / 16.0)
    tri = np.zeros((128, 4, 128), np.float32)
    tri[:, 0, :] = (r <= c) * s
    tri[:, 1, :] = (r > c) * s
    tri[:, 2, :] = (r >= c) * s
    tri[:, 3, :] = (r < c) * s
    msk = np.zeros((128, 2, 128), np.float32)
    msk[:, 0, :] = (r <= c)
    msk[:, 1, :] = (r >= c)
    half = 64
    freqs = (10000.0 ** (-np.arange(0, half, 2, dtype=np.float32) / half)).astype(np.float32)
    ropec = np.zeros((128, 2, 72), np.float32)
    for d in range(128):
        axis, rem = divmod(d, 64)
        hf, f = divmod(rem, 32)
        for i in range(64):
            pos = i if h == 0 else 63 - i
            ang = np.float32(pos) * freqs[f]
            ropec[d, 0, i] = np.cos(ang)
            ropec[d, 1, i] = (-np.sin(ang)) if hf == 0 else np.sin(ang)
        ropec[d, 0, 64:72] = 1.0
        ropec[d, 1, 64:72] = 0.0
    return ident, tri, msk, ropec


_NC_CACHE = {}


def _get_nc(dbg=()):
    key = tuple(sorted(dbg))
    if key not in _NC_CACHE:
        b = Builder(dbg)
        nc = b.build()
        _NC_CACHE[key] = (nc, b.dbg_out)
    return _NC_CACHE[key]


def make_in_maps(x, c, ctx, c_ctx, w_ada, b_ada, norm1, w_in, q_norm, k_norm, w_gk_fwd, b_gk_fwd,
                 w_gk_bwd, b_gk_bwd, gla_norm, w_br_attn, w_br_gla, w_out, norm2, w_mlp1, w_mlp2):
    f = lambda a: np.ascontiguousarray(np.asarray(a, dtype=np.float32))
    x = f(x); c = f(c); ctx = f(ctx); c_ctx = f(c_ctx)
    w_in0 = f(w_in)[0]
    off = np.cumsum([0, 256, 256, 512, 1024, 16, 16, 1024, 512, 1024, 1024, 1024])
    ak, av, gk, gv, lrf, lrb, aq, gq, go, ga, gg = [w_in0[:, off[i]:off[i + 1]] for i in range(11)]
    pk = _perm_half_swap(2)
    pq = _perm_half_swap(8)
    pd = _perm_half_swap(1)
    qn = f(q_norm)[0]; kn = f(k_norm)[0]
    qkg = np.ascontiguousarray(np.stack([qn, qn[pd], kn, kn[pd]], axis=1))
    win = {}
    wgk = {}
    for h in (0, 1):
        lra, lrbb = (lrf, lrb) if h == 0 else (lrb, lrf)
        win[h] = np.ascontiguousarray(np.concatenate(
            [ak, ak[:, pk], av, gk, gv, lra, lrbb, gq, aq, aq[:, pq], go, ga, gg], axis=1))
        assert win[h].shape[1] == NIN
        wf = np.concatenate([f(w_gk_fwd)[0], f(b_gk_fwd)[0][None, :]], axis=0)
        wb = np.concatenate([f(w_gk_bwd)[0], f(b_gk_bwd)[0][None, :]], axis=0)
        wgk[h] = np.ascontiguousarray(np.stack([wf, wb] if h == 0 else [wb, wf], axis=0))
    consts = {h: _consts(h) for h in (0, 1)}
    shared = dict(w_ada=f(w_ada)[0], b_ada=f(b_ada)[0], norm1=f(norm1)[0], norm2=f(norm2)[0],
                  gla_norm=f(gla_norm)[0], w_br_attn=f(w_br_attn)[0], w_br_gla=f(w_br_gla)[0],
                  w_out=f(w_out)[0], w_mlp1=f(w_mlp1)[0], w_mlp2=f(w_mlp2)[0], qkg=qkg)
    in_maps = []
    for core in range(8):
        b, h = divmod(core, 2)
        xb = x[b] if h == 0 else x[b][::-1]
        cb = ctx[b] if h == 0 else ctx[b][::-1]
        ident, tri, msk, ropec = consts[h]
        m = dict(shared)
        m.update(xs=np.ascontiguousarray(xb), ctx=np.ascontiguousarray(cb),
                 cvec=np.ascontiguousarray(np.stack([c[b], c_ctx], axis=0)),
                 w_in=win[h], wgk=wgk[h], ident=ident, tri=tri, msk=msk, ropec=ropec)
        in_maps.append(m)
    return in_maps


def assemble(results):
    out = np.empty((4, SEQ, D), np.float32)
    for core in range(8):
        b, h = divmod(core, 2)
        o = np.asarray(results[core]["out"], dtype=np.float32)
        if h == 0:
            out[b, 0:OWN] = o
        else:
            out[b, OWN:SEQ] = o[::-1]
    return out


def kernel(**inputs):
    nc, _ = _get_nc(())
    in_maps = make_in_maps(**inputs)
    res = run_bass_kernel_spmd(nc, in_maps, core_ids=list(range(8)))
    return assemble(res.results)
```

```python
import numpy as np
from contextlib import ExitStack
import concourse.bass as bass
import concourse.mybir as mybir
from concourse.bass_utils import run_bass_kernel_spmd

F32 = mybir.dt.float32
BF16 = mybir.dt.bfloat16
AF = mybir.ActivationFunctionType
ALU = mybir.AluOpType

D = 1024
SEQ = 4096
OWN = 2048
CTXL = 256
NKEY = SEQ + CTXL
EPS = 1e-6
C_AK, C_AKP, C_AV, C_GK, C_GV, C_LRA, C_LRB, C_GQ, C_AQ, C_AQP, C_GO, C_GA, C_GG = (
    0, 256, 512, 768, 1280, 2304, 2320, 2336, 2848, 3872, 4896, 5920, 6944)
NIN = 7968
SAME_WIN = 3


class Tok:
    __slots__ = ("name", "w", "r", "excl")

    def __init__(self, name, excl=False):
        self.name = name
        self.w = None
        self.r = []
        self.excl = excl


class Chan:
    def __init__(self, idx):
        self.idx = idx
        self.count = 0
        self.sem = None


class Op:
    __slots__ = ("fn", "deps", "signal", "chan")

    def __init__(self, fn, deps, chan):
        self.fn = fn
        self.deps = deps
        self.signal = False
        self.chan = chan


ENGS = ("pe", "act", "dve", "pool", "sp")


class Prog:
    def __init__(self):
        self.ops = {e: [] for e in ENGS}
        self.chans = []
        self.wm = {e: {} for e in ENGS}

    def chan(self):
        c = Chan(len(self.chans))
        self.chans.append(c)
        return c

    def add(self, eng, fn, reads=(), writes=(), chan=None):
        idx = len(self.ops[eng])
        deps = []
        for t in reads:
            if t.w is not None:
                deps.append(t.w)
            if t.excl:
                deps.extend(t.r)
        for t in writes:
            if t.w is not None:
                deps.append(t.w)
            deps.extend(t.r)
        need = []
        wm = self.wm[eng]
        best = {}
        for d in deps:
            if d[0] == "e":
                _, e2, i2 = d
                if e2 == eng:
                    if chan is not None:
                        pass
                    elif eng in ("pe", "sp"):
                        continue
                    elif idx - i2 > SAME_WIN:
                        continue
                key = ("e", e2)
                val = i2
            else:
                _, c, v = d
                if chan is not None and c is chan:
                    continue
                key = ("c", c.idx)
                val = v
            if wm.get(key, -1) >= val:
                continue
            if key not in best or best[key][0] < val:
                best[key] = (val, d)
        for key, (val, d) in best.items():
            wm[key] = val
            need.append(d)
            if d[0] == "e":
                self.ops[d[1]][d[2]].signal = True
        op = Op(fn, need, chan)
        self.ops[eng].append(op)
        if chan is None:
            ref = ("e", eng, idx)
        else:
            chan.count += 16
            ref = ("c", chan, chan.count)
        for t in reads:
            if t.excl:
                t.r = [ref]
            else:
                t.r.append(ref)
        for t in writes:
            t.w = ref
            t.r = []
        return op

    def barrier(self):
        refs = []
        for e in ENGS:
            if self.ops[e]:
                for i in range(len(self.ops[e]) - 1, -1, -1):
                    if self.ops[e][i].chan is None and self.ops[e][i].fn is not None:
                        refs.append(("e", e, i))
                        break
        for c in self.chans:
            if c.count:
                refs.append(("c", c, c.count))
        bt = Tok("barrier")
        for e in ENGS:
            need = []
            wm = self.wm[e]
            for d in refs:
                if d[0] == "e":
                    if d[1] == e:
                        continue
                    key = ("e", d[1]); val = d[2]
                else:
                    key = ("c", d[1].idx); val = d[2]
                if wm.get(key, -1) >= val:
                    continue
                wm[key] = val
                need.append(d)
                if d[0] == "e":
                    self.ops[d[1]][d[2]].signal = True
            self.ops[e].append(Op(None, need, None))

    def emit(self, nc, stack, final_chans):
        sems = {e: stack.enter_context(nc.semaphore("s_" + e)) for e in ENGS}
        for c in self.chans:
            c.sem = stack.enter_context(nc.semaphore("c%d" % c.idx))
        pref = {}
        for e in ENGS:
            cnt = 0
            p = []
            for op in self.ops[e]:
                if op.signal:
                    cnt += 1
                p.append(cnt)
            pref[e] = p
        block = stack.enter_context(nc.Block())
        handles = {"pe": block.tensor, "act": block.scalar, "dve": block.vector,
                   "pool": block.gpsimd, "sp": block.sync}

        def make(e):
            def body(eng):
                for op in self.ops[e]:
                    for d in op.deps:
                        if d[0] == "e":
                            eng.wait_ge(sems[d[1]], pref[d[1]][d[2]])
                        else:
                            eng.wait_ge(d[1].sem, d[2])
                    if op.fn is None:
                        continue
                    ins = op.fn(eng)
                    if op.signal:
                        ins.then_inc(sems[e], 1)
                    if op.chan is not None:
                        ins.then_inc(op.chan.sem, 16)
                if e == "sp":
                    for c in final_chans:
                        if c.count:
                            eng.wait_ge(c.sem, c.count)
            return body

        for e in ENGS:
            handles[e](make(e))


class StopBuild(Exception):
    pass


class Builder:
    def __init__(self, dbg=()):
        self.dbg = set(dbg)
        self.nc = bass.Bass("TRN2", target_bir_lowering=False)
        self.P = Prog()
        self.stack = ExitStack()
        self.dram = {}
        self.dbg_out = []

    def din(self, name, shape, dt=F32):
        ap = self.nc.dram_tensor(name, list(shape), dt, kind="ExternalInput").ap()
        self.dram[name] = ap
        return ap

    def init_arena(self):
        nc = self.nc
        self.ARENA_BYTES = 207 * 1024
        self.arena = self.stack.enter_context(
            nc.sbuf_tensor("arena", [128, self.ARENA_BYTES // 2], BF16))
        self.regions = {}
        self.def_region("P", 0, self.ARENA_BYTES)
        self.cur_region = "P"
        self.psum = [self.stack.enter_context(nc.psum_tensor("ps%d" % i, [128, 512], F32))[:]
                     for i in range(8)]
        self.pstok = [Tok("ps%d" % i, excl=True) for i in range(8)]
        self.rot = list(range(8))
        self.rot_i = 0

    def alloc(self, shape, dt, name="", parts=128, region=None):
        esz = 4 if dt == F32 else 2
        n = int(np.prod(shape))
        nbytes = (n * esz + 63) // 64 * 64
        if region is None:
            region = self.cur_region
        r = self.regions[region]
        off = r[1]
        r[1] += nbytes
        assert r[1] <= r[2], ("SBUF overflow", region, name, r[1] - r[2])
        v = self.arena[0:parts, off // 2: off // 2 + n * esz // 2]
        if dt == F32:
            v = v.bitcast(F32)
        if len(shape) == 2:
            v = v.rearrange("p (a b) -> p a b", a=shape[0])
        elif len(shape) == 3:
            v = v.rearrange("p (a b c) -> p a b c", a=shape[0], b=shape[1])
        elif len(shape) == 4:
            v = v.rearrange("p (a b c d) -> p a b c d", a=shape[0], b=shape[1], c=shape[2])
        return v

    def def_region(self, name, start, end):
        self.regions[name] = [start, start, end]

    def reset_region(self, *names):
        for n in names:
            self.regions[n][1] = self.regions[n][0]

    def use(self, name):
        self.cur_region = name

    def set_rot(self, banks):
        self.rot = list(banks)
        self.rot_i = 0

    def bank(self):
        b = self.rot[self.rot_i % len(self.rot)]
        self.rot_i += 1
        return b

    def mm(self, out, lhsT, rhs, start, stop, reads, writes, tile_position=None):
        if tile_position is not None:
            return self.P.add("pe", lambda e: e.matmul(out, lhsT, rhs, start=start, stop=stop,
                                                       tile_position=tile_position), reads, writes)
        return self.P.add("pe", lambda e: e.matmul(out, lhsT, rhs, start=start, stop=stop),
                          reads, writes)

    def tr(self, out, in_, ident, reads, writes):
        return self.P.add("pe", lambda e: e.transpose(out, in_, ident), reads, writes)

    def act(self, out, in_, func, reads, writes, bias=None, scale=None, accum=None):
        kw = {}
        if bias is not None:
            kw["bias"] = bias
        if scale is not None:
            kw["scale"] = scale
        if accum is not None:
            kw["accum_out"] = accum
        return self.P.add("act", lambda e: e.activation(out, in_, func, **kw), reads, writes)

    def tt(self, eng, out, in0, in1, op, reads, writes):
        return self.P.add(eng, lambda e: e.tensor_tensor(out, in0, in1, op), reads, writes)

    def ts(self, eng, out, in0, s1, s2, op0, op1, reads, writes):
        if op1 is None:
            return self.P.add(eng, lambda e: e.tensor_scalar(out, in0, s1, None, op0),
                              reads, writes)
        return self.P.add(eng, lambda e: e.tensor_scalar(out, in0, s1, s2, op0, op1),
                          reads, writes)

    def stt(self, eng, out, in0, scalar, in1, op0, op1, reads, writes):
        return self.P.add(eng, lambda e: e.scalar_tensor_tensor(out, in0, scalar, in1, op0, op1),
                          reads, writes)

    def cp(self, eng, out, in_, reads, writes):
        if eng == "act":
            return self.P.add("act", lambda e: e.copy(out, in_), reads, writes)
        return self.P.add(eng, lambda e: e.tensor_copy(out, in_), reads, writes)

    def recip(self, out, in_, reads, writes):
        return self.P.add("dve", lambda e: e.reciprocal(out, in_), reads, writes)

    def memset(self, eng, ap, val, writes):
        return self.P.add(eng, lambda e: e.memset(ap, val), (), writes)

    def dma(self, q, out, in_, chan, reads, writes, slow=False):
        if slow:
            return self.P.add(q, lambda e: e.dma_start(out=out, in_=in_,
                                                       allow_slow_non_contiguous=True),
                              reads, writes, chan=chan)
        return self.P.add(q, lambda e: e.dma_start(out=out, in_=in_), reads, writes, chan=chan)

    def dump(self, name, ap, shape, dt, tok):
        if name not in self.dbg:
            return
        o = self.nc.dram_tensor("dbg_" + name, list(shape), dt, kind="ExternalOutput").ap()
        c = self.P.chan()
        self.final_chans.append(c)
        self.dma("sp", o, ap, c, [tok] if not isinstance(tok, (list, tuple)) else list(tok), [])
        self.dbg_out.append("dbg_" + name)

    def build(self):
        try:
            return self._build_body()
        except StopBuild:
            return self.finish_stub(None, None)

    def chk(self, label):
        if ("stop_" + label) in self.dbg:
            raise StopBuild()

    def _build_body(self):
        nc = self.nc
        P = self.P
        din = self.din
        self.final_chans = []
        xs = din("xs", [SEQ, D])
        ctx = din("ctx", [CTXL, D])
        cvec = din("cvec", [2, D])
        w_ada = din("w_ada", [D, 6 * D])
        b_ada = din("b_ada", [6 * D])
        norm1 = din("norm1", [D])
        norm2 = din("norm2", [D])
        w_in = din("w_in", [D, NIN])
        wgk = din("wgk", [2, 17, 512])
        qkg = din("qkg", [128, 4])
        gla_norm = din("gla_norm", [256])
        w_bra = din("w_br_attn", [D, D])
        w_brg = din("w_br_gla", [D, D])
        w_out = din("w_out", [D, D])
        w_m1 = din("w_mlp1", [D, 4 * D])
        w_m2 = din("w_mlp2", [4 * D, D])
        ident_d = din("ident", [128, 128])
        tri_d = din("tri", [128, 4, 128])
        msk_d = din("msk", [128, 2, 128])
        ropec_d = din("ropec", [128, 2, 72])
        out_d = nc.dram_tensor("out", [OWN, D], F32, kind="ExternalOutput").ap()
        self.out_d = out_d

        self.init_arena()
        A = self.alloc
        ps = self.psum
        pt = self.pstok

        ident = A([128], F32, "ident"); t_ident = Tok("ident")
        onesf = A([128], F32, "onesf"); t_onesf = Tok("onesf")
        onesb = A([128], BF16, "onesb"); t_onesb = Tok("onesb")
        tri = A([4, 128], BF16, "tri"); t_tri = Tok("tri")
        msk = A([2, 128], F32, "msk"); t_msk = Tok("msk")
        ropec = A([2, 72], F32, "ropec"); t_ropec = Tok("ropec")
        qkgs = A([4], F32, "qkg"); t_qkg = Tok("qkg")
        GT = A([2, 2, 72], F32, "GT"); t_GT = Tok("GT")
        scT = A([8, 2], BF16, "scT"); t_scT = Tok("scT")
        modT = A([48, 2], F32, "modT"); t_mod = Tok("mod")
        a1 = A([8], F32, "a1"); ac = A([8], F32, "ac"); a2 = A([8], F32, "a2"); t_av = Tok("avec"); t_av2 = Tok("avec2")
        gt1bc = A([1024], F32, "gt1bc"); t_gt1 = Tok("gt1bc")
        gt2bc = A([1024], F32, "gt2bc"); t_gt2 = Tok("gt2bc")
        glan = A([256], F32, "glan"); t_glan = Tok("glan")
        wgka = A([2, 512], BF16, "wgka", parts=32); t_wgka = Tok("wgka")
        nhalf = A([1], F32, "nhalf"); t_nhalf = Tok("nhalf")
        tiny = A([16], F32, "tiny"); t_tiny = Tok("tiny")

        self.dma("pool", tri, tri_d, P.chan(), [], [t_tri])
        for (dst, src, tk) in ((ident, ident_d, t_ident), (msk, msk_d, t_msk),
                               (ropec, ropec_d, t_ropec), (qkgs, qkg, t_qkg)):
            self.dma("sp", dst, src, P.chan(), [], [tk])
        vst = A([128], F32, "vst", parts=128); t_vst = Tok("vst")
        vT = A([80], F32, "vT"); t_vT = Tok("vT")
        self.memset("dve", vst, 0.0, [t_vst])
        c_v = P.chan()
        self.dma("sp", vst[0:48], b_ada.rearrange("(k p) -> k p", p=128), c_v, [], [t_vst])
        self.dma("sp", vst[48:56], norm1.rearrange("(k p) -> k p", p=128), c_v, [], [t_vst])
        self.dma("sp", vst[56:64], norm2.rearrange("(k p) -> k p", p=128), c_v, [], [t_vst])
        self.dma("sp", vst[64:80], cvec.rearrange("r (k p) -> (r k) p", p=128), c_v, [], [t_vst])
        self.dma("sp", glan, gla_norm.partition_broadcast(128), P.chan(), [], [t_glan])
        c_wgk = P.chan()
        self.dma("pool", wgka[0:17], wgk.rearrange("x r n -> r x n"), c_wgk, [], [t_wgka])
        self.memset("dve", onesf, 1.0, [t_onesf])
        self.memset("dve", onesb, 1.0, [t_onesb])
        self.memset("dve", nhalf, -0.5, [t_nhalf])
        self.tr(ps[0][:, 0:128], vst, ident, [t_vst, t_ident], [pt[0]])
        self.cp("dve", vT, ps[0][:, 0:80], [pt[0]], [t_vT])
        badaT = vT[:, 0:48]; n1T = vT[:, 48:56]; n2T = vT[:, 56:64]
        cT = vT[:, 64:80].rearrange("p (r k) -> p k r", r=2)
        t_bada = t_vT; t_nT = t_vT; t_cT = t_vT

        hxT = A([4, 8, 512], BF16, "hxT_own")
        t_hx = [Tok("hxT%d" % j) for j in range(4)]
        t_ag = [Tok("AG%d" % j) for j in range(4)]

        SA = A([4, 256], F32, "SA"); SB = A([4, 256], F32, "SB")
        SAb = A([4, 256], BF16, "SAb"); SBb = A([4, 256], BF16, "SBb")
        t_S = {"A": Tok("SA"), "B": Tok("SB")}
        t_Sb = {"A": Tok("SAb"), "B": Tok("SBb")}
        Sf = {"A": SA, "B": SB}
        Sb = {"A": SAb, "B": SBb}
        pend = self.regions["P"][1]
        s_start = pend - 12288
        self.def_region("S", s_start, pend)
        self.def_region("R1", pend, pend + 65536)
        self.def_region("R2", pend + 65536, pend + 65536 + 34816)
        self.def_region("R3", pend + 65536 + 34816, self.ARENA_BYTES)
        self.use("R2")
        KT = A([2, NKEY], BF16, "KT")
        V = A([34, 256], BF16, "V")
        t_kv = [Tok("kv%d" % g) for g in range(9)]

        self.use("R1")
        ws1 = A([8, 2336], BF16, "ws1"); t_ws1 = Tok("ws1"); c_ws1 = P.chan()
        xt = [A([1024], F32, "xt%d" % i) for i in range(2)]
        t_xt = [Tok("xt%d" % i) for i in range(2)]
        c_xt = [P.chan() for _ in range(2)]
        wada = [A([8, 512], BF16, "wada%d" % i) for i in range(2)]
        t_wada = [Tok("wada%d" % i) for i in range(2)]
        c_wada = [P.chan() for _ in range(2)]
        Tkc = A([2, 256], F32, "Tkc"); t_Tkc = Tok("Tkc")
        self.use("R3")
        xn = A([1024], F32, "xn"); t_xn = Tok("xn")
        hxg = A([8, 512], BF16, "hxg"); t_hxg = Tok("hxg")
        gkt = A([4, 512], BF16, "gkt"); t_gkt = Tok("gkt")
        gvt = A([4, 1024], BF16, "gvt"); t_gvt = Tok("gvt")
        lrT = {"A": A([512], BF16, "lrTA", parts=32), "B": A([512], BF16, "lrTB", parts=32)}
        t_lrT = {"A": Tok("lrTA"), "B": Tok("lrTB")}
        Tk = A([2, 512], F32, "Tk"); t_Tk = Tok("Tk")
        r_sq = A([512], BF16, "r_sq"); t_rsq = Tok("r_sq")
        r_rs = A([512], F32, "r_rs"); t_rrs = Tok("r_rs")
        r_t1 = A([512], F32, "r_t1"); t_rt1 = Tok("r_t1")
        r_t2 = A([512], F32, "r_t2"); t_rt2 = Tok("r_t2")
        def make_gset(i, full):
            G = {"sp": (A([512], BF16, "g_sp%d" % i), Tok("g_sp")),
                 "EC": (A([512], F32, "g_EC%d" % i), Tok("g_EC")),
                 "kh": (A([512], BF16, "g_kh%d" % i), Tok("g_kh")),
                 "EL": (A([4], F32, "g_EL%d" % i), Tok("g_EL"))}
            if full:
                G["E1"] = (A([512], F32, "g_E1%d" % i), Tok("g_E1"))
                G["E2"] = (A([512], F32, "g_E2%d" % i), Tok("g_E2"))
                G["qt"] = (A([4, 128], BF16, "g_qt%d" % i), Tok("g_qt"))
                G["kt"] = (A([4, 128], BF16, "g_kt%d" % i), Tok("g_kt"))
                G["AT"] = (A([4, 128], BF16, "g_AT%d" % i), Tok("g_AT"))
            return G
        gsets = [make_gset(0, False),
                 {"sp": (r_rs.bitcast(BF16)[:, 0:512], t_rrs), "EC": (r_t1, t_rt1), "kh": (r_sq, t_rsq),
                  "EL": (A([4], F32, "g_EL1"), Tok("g_EL1"))}]
        self.gcnt = 0
        sqj = r_t2.bitcast(BF16)
        t_sqj = t_rt2
        diag = A([128], F32, "diag"); t_diag = Tok("diag")

        for X in ("A", "B"):
            self.memset("dve", lrT[X], 1.0, [t_lrT[X]])
            self.memset("dve", Sf[X], 0.0, [t_S[X]])
            self.memset("dve", Sb[X], 0.0, [t_Sb[X]])

        w_in_v = w_in.rearrange("(k p) n -> p k n", p=128)

        self.set_rot([0, 1, 2, 3, 4, 5, 6, 7])
        ec = tiny
        ecv = tiny.rearrange("p (a b) -> p a b", a=8)
        self.act(ecv, cT, AF.Exp, [t_cT], [t_tiny], scale=-1.0)
        self.ts("dve", ecv, ecv, 1.0, None, ALU.add, None, [t_tiny], [t_tiny])
        self.recip(ecv, ecv, [t_tiny], [t_tiny])
        self.tt("dve", scT, ecv, cT, ALU.mult, [t_tiny, t_cT], [t_scT])

        w_ada_v = w_ada.rearrange("(k p) n -> p k n", p=128)
        self.set_rot([0, 1, 2, 3, 4, 5, 6])
        bmod = 7
        modps = ps[bmod][:, 0:96].rearrange("p (a b) -> p a b", a=48)

        def ada_dma(cb):
            for kc in range(8):
                self.dma("pool", wada[cb % 2][:, kc, :], w_ada_v[:, kc, cb * 512:(cb + 1) * 512],
                         c_wada[cb % 2], [], [t_wada[cb % 2]])

        def ada_mm(cb):
            wb = wada[cb % 2]
            for fc in range(4):
                j = cb * 4 + fc
                for kc in range(8):
                    self.mm(modps[:, j, :], wb[:, kc, fc * 128:(fc + 1) * 128], scT[:, kc, :],
                            kc == 0, kc == 7, [t_wada[cb % 2], t_scT], [pt[bmod]])

        ada_dma(0)
        ada_dma(1)
        for kc in range(8):
            self.dma("pool", ws1[:, kc, :], w_in_v[:, kc, 0:2336], c_ws1, [], [t_ws1])
        for cb in range(4):
            ada_mm(cb)
            ada_dma(cb + 2)
        self.tt("dve", modT[:, 0:16, :], modps[:, 0:16, :],
                badaT[:, 0:16].unsqueeze(2).broadcast_to([128, 16, 2]), ALU.add,
                [pt[bmod], t_bada], [t_mod])
        self.ts("dve", a1, modT[:, 8:16, 0], 1.0, 32.0, ALU.add, ALU.mult, [t_mod], [t_av])
        self.tt("dve", a1, a1, n1T, ALU.mult, [t_av, t_nT], [t_av])
        self.ts("dve", ac, modT[:, 8:16, 1], 1.0, 32.0, ALU.add, ALU.mult, [t_mod], [t_av])
        self.tt("dve", ac, ac, n1T, ALU.mult, [t_av, t_nT], [t_av])

        def ada_part2():
            for cb in range(4, 12):
                ada_mm(cb)
                if cb + 2 < 12:
                    ada_dma(cb + 2)
            self.tt("dve", modT[:, 16:48, :], modps[:, 16:48, :],
                    badaT[:, 16:48].unsqueeze(2).broadcast_to([128, 32, 2]), ALU.add,
                    [pt[bmod], t_bada], [t_mod])
            self.ts("dve", a2, modT[:, 32:40, 0], 1.0, 32.0, ALU.add, ALU.mult, [t_mod], [t_av2])
            self.tt("dve", a2, a2, n2T, ALU.mult, [t_av2, t_nT], [t_av2])
            for (dst, tk, j0) in ((gt1bc, t_gt1, 16), (gt2bc, t_gt2, 40)):
                for half in range(2):
                    b = self.bank()
                    for q in range(4):
                        kc = half * 4 + q
                        self.ts("dve", diag, ident, modT[:, j0 + kc, 0:1], None, ALU.mult, None,
                                [t_ident, t_mod], [t_diag])
                        self.mm(ps[b][:, q * 128:(q + 1) * 128], onesf, diag, True, True,
                                [t_onesf, t_diag], [pt[b]])
                    self.cp("dve", dst[:, half * 512:(half + 1) * 512], ps[b], [pt[b]], [tk])
        SQ128 = float(np.sqrt(128.0))
        self.ts("dve", GT[:, 0, 0, :], ropec[:, 0, :], qkgs[:, 0:1], None, ALU.mult, None,
                [t_ropec, t_qkg], [t_GT])
        self.ts("dve", GT[:, 0, 1, :], ropec[:, 1, :], qkgs[:, 1:2], None, ALU.mult, None,
                [t_ropec, t_qkg], [t_GT])
        self.ts("dve", GT[:, 1, 0, :], ropec[:, 0, :], qkgs[:, 2:3], SQ128, ALU.mult, ALU.mult,
                [t_ropec, t_qkg], [t_GT])
        self.ts("dve", GT[:, 1, 1, :], ropec[:, 1, :], qkgs[:, 3:4], SQ128, ALU.mult, ALU.mult,
                [t_ropec, t_qkg], [t_GT])
        Tk4 = Tk.rearrange("p c (r w) -> p c r w", r=8)
        self.cp("dve", Tk4[64:128], GT[64:128, 1, :, 0:64].unsqueeze(2).broadcast_to([64, 2, 8, 64]),
                [t_GT], [t_Tk])
        Tkc4 = Tkc.rearrange("p c (r w) -> p c r w", r=4)
        self.cp("dve", Tkc4, GT[:, 1, :, 64:68].unsqueeze(3).broadcast_to([128, 2, 4, 64]),
                [t_GT], [t_Tkc])

        if "stop_setup" in self.dbg:
            return self.finish_stub(t_mod, modT)
        def norm_transpose(src_ap, xt_i, avec, shcol, dst_fn, dst_toks, from_sbuf_tok=None):
            xin = src_ap
            rt = [t_xt[xt_i]] if from_sbuf_tok is None else [from_sbuf_tok]
            ssc = tiny[:, 0:1]
            self.act(sqj, xin, AF.Square, rt, [t_sqj, t_tiny], accum=ssc)
            self.ts("pool", tiny[:, 1:2], ssc, float(D * EPS), None, ALU.add, None, [t_tiny], [t_tiny])
            self.tt("pool", tiny[:, 2:3], tiny[:, 1:2], nhalf, ALU.pow, [t_tiny, t_nhalf], [t_tiny])
            self.act(xn, xin, AF.Copy, rt + [t_tiny], [t_xn], scale=tiny[:, 2:3])
            for half in range(2):
                b = self.bank()
                for q in range(4):
                    kc = half * 4 + q
                    self.tr(ps[b][:, q * 128:(q + 1) * 128], xn[:, kc * 128:(kc + 1) * 128], ident,
                            [t_xn, t_ident], [pt[b]])
                for q in range(4):
                    kc = half * 4 + q
                    dst = dst_fn(kc)
                    if half == 0:
                        self.act(dst, ps[b][:, q * 128:(q + 1) * 128], AF.Identity,
                                 [pt[b], t_av, t_av2, t_mod], dst_toks,
                                 scale=avec[:, kc:kc + 1], bias=shcol(kc))
                    else:
                        self.ts("dve", dst, ps[b][:, q * 128:(q + 1) * 128], avec[:, kc:kc + 1],
                                shcol(kc), ALU.mult, ALU.add, [pt[b], t_av, t_av2, t_mod], dst_toks)

        def rope_norm(b0, b1, n, Tc, Ts, t_tab, dst, dst_toks, eps_scaled):
            self.act(r_sq[:, 0:n], ps[b0][:, 0:n], AF.Square, [pt[b0]], [t_rsq])
            bs = self.bank()
            self.mm(ps[bs][:, 0:n], onesb, r_sq[:, 0:n], True, True, [t_onesb, t_rsq], [pt[bs]])
            self.chk("r1")
            import os
            EXP = os.environ.get("EXP", "")
            if EXP == "copyfirst":
                self.cp("dve", r_t1[:, 0:n], ps[b0][:, 0:n], [pt[b0]], [t_rt1])
            elif EXP == "serial":
                self.cp("dve", r_t1[:, 0:n], ps[b0][:, 0:n], [pt[b0], t_rsq], [t_rt1])
                self.chk("r1b")
                self.chk("r1b")
                self.tt("dve", r_t1[:, 0:n], r_t1[:, 0:n], Tc, ALU.mult, [t_rt1, t_tab], [t_rt1])
                self.chk("r1c")
            else:
                self.tt("dve", r_t1[:, 0:n], ps[b0][:, 0:n], Tc, ALU.mult, [pt[b0], t_tab], [t_rt1])
            self.chk("r1a")
            self.tt("dve", r_t2[:, 0:n], ps[b1][:, 0:n], Ts, ALU.mult, [pt[b1], t_tab], [t_rt2])
            self.chk("r2")
            self.act(r_rs[:, 0:n], ps[bs][:, 0:n], AF.Ln, [pt[bs]], [t_rrs], bias=eps_scaled)
            self.chk("r3")
            self.act(r_rs[:, 0:n], r_rs[:, 0:n], AF.Exp, [t_rrs], [t_rrs], scale=-0.5)
            self.chk("r4")
            self.tt("dve", r_t1[:, 0:n], r_t1[:, 0:n], r_t2[:, 0:n], ALU.add, [t_rt1, t_rt2], [t_rt1])
            self.tt("dve", dst, r_t1[:, 0:n], r_rs[:, 0:n], ALU.mult, [t_rt1, t_rrs], dst_toks)

        def proj_fm(w, t_w, c0, ncols_chunk, rhs_fn, n, rhs_toks):
            b = self.bank()
            for kc in range(8):
                self.mm(ps[b][0:ncols_chunk, 0:n], w[:, kc, c0:c0 + ncols_chunk], rhs_fn(kc),
                        kc == 0, kc == 7, [t_w] + rhs_toks, [pt[b]])
            return b

        def gla_stage1(X, full, lr_ap, gk_tile, gv_tile, t_in, qT=None, kT=None, o_dst=None,
                       o_add=False, o_tok=None):
            G = gsets[self.gcnt % len(gsets)]
            self.gcnt += 1
            g_sp, t_gsp = G["sp"]; g_EC, t_gEC = G["EC"]; g_kh, t_gkh = G["kh"]; g_EL, t_gEL = G["EL"]
            xi = 0 if X == "A" else 1
            cum = tri[:, 2 * xi, :]
            cmat = tri[:, 2 * xi + 1, :]
            last = 127 if X == "A" else 0
            bz = self.bank()
            self.mm(ps[bz], lr_ap, wgka[0:17, xi, :], True, True, t_in + [t_wgka], [pt[bz]])
            self.act(g_sp, ps[bz], AF.Exp, [pt[bz]], [t_gsp], scale=-1.0)
            self.act(g_sp, g_sp, AF.Ln, [t_gsp], [t_gsp], bias=1.0)
            bc = self.bank()
            self.mm(ps[bc], cmat, g_sp, True, True, [t_tri, t_gsp], [pt[bc]])
            if full:
                bb = self.bank()
                for h in range(4):
                    self.mm(ps[bb][:, h * 128:(h + 1) * 128], g_sp[:, h * 128:(h + 1) * 128], cum,
                            True, True, [t_gsp, t_tri], [pt[bb]])
            else:
                bl = self.bank()
                for h in range(4):
                    self.mm(ps[bl][:, h:h + 1], g_sp[:, h * 128:(h + 1) * 128],
                            cum[:, last:last + 1], True, True, [t_gsp, t_tri], [pt[bl]])
            self.act(g_EC, ps[bc], AF.Exp, [pt[bc]], [t_gEC])
            self.tt("dve", g_kh, gk_tile, g_EC, ALU.mult, t_in + [t_gEC], [t_gkh])
            st = dict(X=X, full=full, G=G, gv_tile=gv_tile, t_in=t_in, o_dst=o_dst, o_add=o_add,
                      o_tok=o_tok, xi=xi)
            if full:
                g_E1, t_gE1 = G["E1"]; g_E2, t_gE2 = G["E2"]; g_qt, t_gqtl = G["qt"]
                g_kt, t_gktl = G["kt"]
                self.act(g_E1, ps[bb], AF.Exp, [pt[bb]], [t_gE1])
                self.act(g_E2, ps[bb], AF.Exp, [pt[bb]], [t_gE2], scale=-1.0)
                E1v = g_E1.rearrange("p (h t) -> p h t", h=4)
                E2v = g_E2.rearrange("p (h t) -> p h t", h=4)
                self.tt("dve", g_qt, qT, E1v, ALU.mult, t_in + [t_gE1], [t_gqtl])
                self.tt("dve", g_kt, kT, E2v, ALU.mult, t_in + [t_gE2], [t_gktl])
                st["el"] = lambda h: g_E1[:, h * 128 + last: h * 128 + last + 1]
                st["el_tok"] = t_gE1
            else:
                self.act(g_EL, ps[bl][:, 0:4], AF.Exp, [pt[bl]], [t_gEL])
                st["el"] = lambda h: g_EL[:, h:h + 1]
                st["el_tok"] = t_gEL
            return st

        def gla_stage2(st):
            X = st["X"]; G = st["G"]; gv_tile = st["gv_tile"]; t_in = st["t_in"]; xi = st["xi"]
            g_kh, t_gkh = G["kh"]
            if st["full"]:
                g_qt, t_gqtl = G["qt"]; g_kt, t_gktl = G["kt"]; g_AT, t_gAT = G["AT"]
                ba = self.bank()
                for h in range(4):
                    self.mm(ps[ba][:, h * 128:(h + 1) * 128], g_kt[:, h, :], g_qt[:, h, :],
                            True, True, [t_gktl, t_gqtl], [pt[ba]])
                self.tt("dve", g_AT, ps[ba].rearrange("p (h t) -> p h t", h=4),
                        msk[:, xi, :].unsqueeze(1).broadcast_to([128, 4, 128]), ALU.mult,
                        [pt[ba], t_msk], [t_gAT])
                for hp in range(2):
                    bo = self.bank()
                    for hh in range(2):
                        h = hp * 2 + hh
                        self.mm(ps[bo][:, hh * 256:(hh + 1) * 256], g_qt[:, h, :], Sb[X][:, h, :],
                                True, False, [t_gqtl, t_Sb[X]], [pt[bo]])
                        self.mm(ps[bo][:, hh * 256:(hh + 1) * 256], g_AT[:, h, :],
                                gv_tile[:, h * 256:(h + 1) * 256], False, True,
                                [t_gAT] + t_in, [pt[bo]])
                    od = st["o_dst"][:, hp * 512:(hp + 1) * 512]
                    if st["o_add"]:
                        self.tt("dve", od, ps[bo], od, ALU.add, [pt[bo], st["o_tok"]], [st["o_tok"]])
                    else:
                        self.cp("act", od, ps[bo], [pt[bo]], [st["o_tok"]])
            el = st["el"]; el_tok = st["el_tok"]
            for hp in range(2):
                bu = self.bank()
                for hh in range(2):
                    h = hp * 2 + hh
                    self.mm(ps[bu][:, hh * 256:(hh + 1) * 256], g_kh[:, h * 128:(h + 1) * 128],
                            gv_tile[:, h * 256:(h + 1) * 256], True, True, [t_gkh] + t_in, [pt[bu]])
                for hh in range(2):
                    h = hp * 2 + hh
                    self.stt("dve", Sf[X][:, h, :], Sf[X][:, h, :], el(h),
                             ps[bu][:, hh * 256:(hh + 1) * 256], ALU.mult, ALU.add,
                             [t_S[X], el_tok, pt[bu]], [t_S[X]])
            self.cp("act", Sb[X], Sf[X], [t_S[X]], [t_Sb[X]])

        def gla_run(step_args, inter=None):
            prev = None
            for a in step_args:
                cur = gla_stage1(*a[0], **a[1])
                if inter is not None:
                    next(inter, None)
                if prev is not None:
                    gla_stage2(prev)
                prev = cur
            if prev is not None:
                gla_stage2(prev)
            if inter is not None:
                for _ in inter:
                    pass

        wf0 = wada[0].rearrange("p a b -> p (a b)")
        wf1 = wada[1].rearrange("p a b -> p (a b)")
        bsets = [
            dict(gkt=gkt, t_gkt=t_gkt, gvt=gvt, t_gvt=t_gvt, lrT=lrT, t_lrT=t_lrT),
            dict(gkt=wf0[:, 0:2048].rearrange("p (t n) -> p t n", t=4), t_gkt=t_wada[0],
                 gvt=wf1.rearrange("p (t n) -> p t n", t=4), t_gvt=t_wada[1],
                 lrT={"A": wf0[0:32, 2048:2560], "B": wf0[0:32, 2560:3072]},
                 t_lrT={"A": t_wada[0], "B": t_wada[0]}),
        ]
        self.xcount = 0

        def front_tiles(kind, g):
            ntile = 2 if kind == "ctx" else 4
            avec = ac if kind == "ctx" else a1
            rcol = 1 if kind == "ctx" else 0
            shcol = lambda kc, rcol=rcol: modT[:, kc, rcol:rcol + 1]
            src = ctx if kind == "ctx" else xs
            for t in range(ntile):
                xi = self.xcount % 2
                self.xcount += 1
                r0 = t * 128 if kind == "ctx" else g * 512 + t * 128
                self.dma("sp", xt[xi], src[r0:r0 + 128, :], c_xt[xi], [], [t_xt[xi]])
                if kind == "own":
                    dst_fn = lambda kc, t=t, g=g: hxT[:, g, kc, t * 128:(t + 1) * 128]
                    dtoks = [t_hx[g]]
                else:
                    dst_fn = lambda kc, t=t: hxg[:, kc, t * 128:(t + 1) * 128]
                    dtoks = [t_hxg]
                norm_transpose(xt[xi], xi, avec, shcol, dst_fn, dtoks)
                yield t

        def front_proj(kind, g, bs):
            ntile = 2 if kind == "ctx" else 4
            n = ntile * 128
            keyoff = SEQ if kind == "ctx" else g * 512
            if kind == "own":
                rhs_fn = lambda kc, g=g: hxT[:, g, kc, :]
                rtoks = [t_hx[g]]
                lhs_fn = lambda kc, t, g=g: hxT[:, g, kc, t * 128:(t + 1) * 128]
            else:
                rhs_fn = lambda kc, n=n: hxg[:, kc, 0:n]
                rtoks = [t_hxg]
                lhs_fn = lambda kc, t: hxg[:, kc, t * 128:(t + 1) * 128]
            if kind == "ctx":
                Tc, Ts, t_tab = Tkc[:, 0, :], Tkc[:, 1, :], t_Tkc
            else:
                self.cp("dve", Tk4[0:64],
                        GT[0:64, 1, :, g * 8:(g + 1) * 8].unsqueeze(3).broadcast_to([64, 2, 8, 64]),
                        [t_GT], [t_Tk])
                Tc, Ts, t_tab = Tk[:, 0, :], Tk[:, 1, :], t_Tk
            gi = 8 if kind == "ctx" else g
            for kvh in range(2):
                b0 = proj_fm(ws1, t_ws1, C_AK + kvh * 128, 128, rhs_fn, n, rtoks)
                b1 = proj_fm(ws1, t_ws1, C_AKP + kvh * 128, 128, rhs_fn, n, rtoks)
                rope_norm(b0, b1, n, Tc, Ts, t_tab, KT[:, kvh, keyoff:keyoff + n], [t_kv[gi]],
                          float(128 * EPS))
            for t in range(ntile):
                b = self.bank()
                for kc in range(8):
                    self.mm(ps[b][:, 0:256], lhs_fn(kc, t), ws1[:, kc, C_AV:C_AV + 256],
                            kc == 0, kc == 7, rtoks + [t_ws1], [pt[b]])
                self.cp("act", V[:, keyoff // 128 + t, :], ps[b][:, 0:256], [pt[b]], [t_kv[gi]])
            if kind == "own":
                return
            for t in range(ntile):
                b = self.bank()
                for kc in range(8):
                    self.mm(ps[b], lhs_fn(kc, t), ws1[:, kc, C_GK:C_GK + 512],
                            kc == 0, kc == 7, rtoks + [t_ws1], [pt[b]])
                self.cp("dve", bs["gkt"][:, t, :], ps[b], [pt[b]], [bs["t_gkt"]])
                for hf in range(2):
                    b = self.bank()
                    for kc in range(8):
                        self.mm(ps[b], lhs_fn(kc, t),
                                ws1[:, kc, C_GV + hf * 512:C_GV + (hf + 1) * 512],
                                kc == 0, kc == 7, rtoks + [t_ws1], [pt[b]])
                    self.cp("act", bs["gvt"][:, t, hf * 512:(hf + 1) * 512], ps[b], [pt[b]],
                            [bs["t_gvt"]])
            dirs = ("A", "B") if kind == "ctx" else ("B",)
            for X in dirs:
                c0 = C_LRA if X == "A" else C_LRB
                b = proj_fm(ws1, t_ws1, c0, 16, rhs_fn, n, rtoks)
                self.cp("dve", bs["lrT"][X][0:16, 0:n], ps[b][0:16, 0:n], [pt[b]], [bs["t_lrT"][X]])

        def steps(kind, g, bs, inter=None):
            ntile = 2 if kind == "ctx" else 4
            dirs = ("A", "B") if kind == "ctx" else ("B",)
            args = []
            for X in dirs:
                order = range(ntile) if X == "A" else range(ntile - 1, -1, -1)
                for t in order:
                    args.append(((X, False, bs["lrT"][X][0:17, t * 128:(t + 1) * 128], bs["gkt"][:, t, :],
                                  bs["gvt"][:, t, :], [bs["t_lrT"][X], bs["t_gkt"], bs["t_gvt"]]), {}))
            gla_run(args, inter)

        def front(kind, g, bs):
            for _ in front_tiles(kind, g):
                pass
            front_proj(kind, g, bs)

        front("ctx", 8, bsets[0])
        ada_part2()
        for X in ("A", "B"):
            self.memset("dve", bsets[1]["lrT"][X], 1.0, [t_wada[0]])
        front("oth", 7, bsets[1])
        steps("ctx", 8, bsets[0], front_tiles("oth", 6))
        front_proj("oth", 6, bsets[0])
        steps("oth", 7, bsets[1], front_tiles("oth", 5))
        front_proj("oth", 5, bsets[1])
        steps("oth", 6, bsets[0], front_tiles("oth", 4))
        front_proj("oth", 4, bsets[0])
        steps("oth", 5, bsets[1], front_tiles("own", 0))
        front_proj("own", 0, None)
        steps("oth", 4, bsets[0], front_tiles("own", 1))
        front_proj("own", 1, None)
        for g in (2, 3):
            front("own", g, None)
        self.dump("modT", modT, [128, 48, 2], F32, t_mod)
        self.dump("gt1bc", gt1bc, [128, 1024], F32, t_gt1)
        self.dump("KT", KT, [128, 2, NKEY], BF16, t_kv)
        self.dump("V", V, [128, 34, 256], BF16, t_kv)
        self.dump("hxT", hxT, [128, 4, 8, 512], BF16, t_hx)
        self.dump("SA", SA, [128, 4, 256], F32, t_S["A"])
        self.dump("SB", SB, [128, 4, 256], F32, t_S["B"])

        if "stop_s1" in self.dbg:
            return self.finish_stub(t_mod, modT)

        P.barrier()
        self.reset_region("R1", "R3")
        self.use("R1")
        AG = A([4, 2, 8, 512], BF16, "AG")
        self.use("R3")
        wq = A([8, 1024], BF16, "wq"); t_wq = Tok("wq"); c_wq = P.chan()
        Qbs = [A([4, 512], BF16, "Qb%d" % i) for i in range(2)]
        t_Qbs = [Tok("Qb%d" % i) for i in range(2)]
        Tq = A([2, 512], F32, "Tq"); t_Tq = Tok("Tq")
        r_sq = A([512], BF16, "r_sq2"); t_rsq = Tok("r_sq2")
        r_rs = A([512], F32, "r_rs2"); t_rrs = Tok("r_rs2")
        r_t1 = A([512], F32, "r_t12"); t_rt1 = Tok("r_t12")
        r_t2 = A([512], F32, "r_t22"); t_rt2 = Tok("r_t22")
        PTN = 4
        PT = [A([512], BF16, "PT%d" % i) for i in range(PTN)]
        t_PT = [Tok("PT%d" % i) for i in range(PTN)]
        sst = r_t2; t_sst = t_rt2
        ones32 = A([128], F32, "ones32"); t_ones32 = Tok("ones32")
        self.memset("dve", ones32, 1.0 / 32.0, [t_ones32])
        rec = A([512], F32, "rec"); t_rec = Tok("rec")
        Tq4 = Tq.rearrange("p c (r w) -> p c r w", r=8)
        self.cp("dve", Tq4[64:128], GT[64:128, 0, :, 0:64].unsqueeze(2).broadcast_to([64, 2, 8, 64]),
                [t_GT], [t_Tq])
        self.set_rot([0, 1, 2, 3])
        iters = [(hh, blk) for hh in range(2) for blk in range(4)]
        self.wq_loaded = -1

        def emit_q(it):
            hh, blk = iters[it]
            if self.wq_loaded != hh:
                self.wq_loaded = hh
                for kc in range(8):
                    self.dma("pool", wq[:, kc, 0:512],
                             w_in_v[:, kc, C_AQ + hh * 512:C_AQ + (hh + 1) * 512], c_wq, [], [t_wq])
                    self.dma("pool", wq[:, kc, 512:1024],
                             w_in_v[:, kc, C_AQP + hh * 512:C_AQP + (hh + 1) * 512], c_wq, [], [t_wq])
            self.cp("dve", Tq4[0:64],
                    GT[0:64, 0, :, blk * 8:(blk + 1) * 8].unsqueeze(3).broadcast_to([64, 2, 8, 64]),
                    [t_GT], [t_Tq])
            rhs_fn = lambda kc: hxT[:, blk, kc, :]
            for hl in range(4):
                b0 = proj_fm(wq, t_wq, hl * 128, 128, rhs_fn, 512, [t_hx[blk]])
                b1 = proj_fm(wq, t_wq, 512 + hl * 128, 128, rhs_fn, 512, [t_hx[blk]])
                rope_norm(b0, b1, 512, Tq[:, 0, :], Tq[:, 1, :], t_Tq, Qbs[it % 2][:, hl, :],
                          [t_Qbs[it % 2]], float(128 * EPS))

        self.pcount = 0

        def emit_unit(it, hl):
            hh, blk = iters[it]
            Qb, t_Qb = Qbs[it % 2], t_Qbs[it % 2]
            h = hh * 4 + hl
            kvh = h // 4
            bo = 4 + (self.pcount % 2) * 2
            bsum = bo + 1
            self.pcount += 1

            def smm(kt):
                b = self.bank()
                gi = 8 if kt >= 32 else kt // 4
                self.mm(ps[b], KT[:, kvh, kt * 128:(kt + 1) * 128], Qb[:, hl, :], True, True,
                        [t_kv[gi], t_Qb], [pt[b]])
                return b
            bcur = smm(0)
            bnext = None
            for kt in range(34):
                pi = kt % PTN
                self.act(PT[pi], ps[bcur], AF.Exp, [pt[bcur]], [t_PT[pi]])
                if kt + 1 < 34:
                    bnext = smm(kt + 1)
                gi = 8 if kt >= 32 else kt // 4
                self.mm(ps[bo], V[:, kt, kvh * 128:(kvh + 1) * 128], PT[pi], kt == 0, kt == 33,
                        [t_kv[gi], t_PT[pi]], [pt[bo]])
                self.mm(ps[bsum], onesb, PT[pi], kt == 0, kt == 33,
                        [t_onesb, t_PT[pi]], [pt[bsum]])
                bcur = bnext
            self.recip(rec, ps[bsum], [pt[bsum]], [t_rec])
            self.tt("dve", AG[:, blk, 0, h, :], ps[bo], rec, ALU.mult, [pt[bo], t_rec],
                    [t_ag[blk]])

        emit_q(0)
        for it in range(len(iters)):
            emit_unit(it, 0)
            emit_unit(it, 1)
            if it + 1 < len(iters):
                emit_q(it + 1)
            emit_unit(it, 2)
            emit_unit(it, 3)
        self.dump("attnT", AG[:, :, 0, :, :], [128, 4, 8, 512], BF16, t_ag)
        if "stop_att" in self.dbg:
            return self.finish_stub(t_mod, modT)

        P.barrier()
        self.reset_region("R2", "R3")
        self.set_rot([0, 1, 2, 3, 4, 5, 6, 7])
        self.use("R2")
        wg = A([8, 2080], BF16, "wg"); t_wg = Tok("wg"); c_wg = P.chan()
        self.use("R3")
        WG0 = 768
        gkT = A([4, 512], BF16, "gkT"); t_gkT = Tok("gkT")
        gqT = A([4, 512], BF16, "gqT"); t_gqT = Tok("gqT")
        gkt = A([4, 512], BF16, "gkt2"); t_gkt = Tok("gkt2")
        gvt = A([4, 1024], BF16, "gvt2"); t_gvt = Tok("gvt2")
        lrT = {"A": A([512], BF16, "lrTA2", parts=32), "B": A([512], BF16, "lrTB2", parts=32)}
        t_lrT = {"A": Tok("lrTA2"), "B": Tok("lrTB2")}
        gsets = [make_gset(10 + i, True) for i in range(2)]
        for X in ("A", "B"):
            self.memset("dve", lrT[X], 1.0, [t_lrT[X]])
        for kc in range(8):
            self.dma("pool", wg[:, kc, :], w_in_v[:, kc, WG0:WG0 + 2080], c_wg, [], [t_wg])

        def gla_group(X, g, reuse=False):
            rhs_fn = lambda kc: hxT[:, g, kc, :]
            rtoks = [t_hx[g]]
            for h in range(0 if reuse else 4):
                b = proj_fm(wg, t_wg, C_GK - WG0 + h * 128, 128, rhs_fn, 512, rtoks)
                self.cp("act", gkT[:, h, :], ps[b], [pt[b]], [t_gkT])
                b = proj_fm(wg, t_wg, C_GQ - WG0 + h * 128, 128, rhs_fn, 512, rtoks)
                self.ts("dve", gqT[:, h, :], ps[b], float(128.0 ** -0.5), None, ALU.mult, None,
                        [pt[b]], [t_gqT])
            for t in range(0 if reuse else 4):
                b = self.bank()
                for kc in range(8):
                    self.mm(ps[b], hxT[:, g, kc, t * 128:(t + 1) * 128],
                            wg[:, kc, C_GK - WG0:C_GK - WG0 + 512], kc == 0, kc == 7,
                            rtoks + [t_wg], [pt[b]])
                self.cp("dve", gkt[:, t, :], ps[b], [pt[b]], [t_gkt])
                for hf in range(2):
                    b = self.bank()
                    for kc in range(8):
                        self.mm(ps[b], hxT[:, g, kc, t * 128:(t + 1) * 128],
                                wg[:, kc, C_GV - WG0 + hf * 512:C_GV - WG0 + (hf + 1) * 512],
                                kc == 0, kc == 7, rtoks + [t_wg], [pt[b]])
                    self.cp("act", gvt[:, t, hf * 512:(hf + 1) * 512], ps[b], [pt[b]], [t_gvt])
            c0 = (C_LRA if X == "A" else C_LRB) - WG0
            b = proj_fm(wg, t_wg, c0, 16, rhs_fn, 512, rtoks)
            self.cp("dve", lrT[X][0:16, :], ps[b][0:16, :], [pt[b]], [t_lrT[X]])
            order = range(4) if X == "A" else range(3, -1, -1)
            args = []
            for t in order:
                osl = AG[:, g, 1, :, :].rearrange("p a b -> p (a b)")[:, t * 1024:(t + 1) * 1024]
                args.append(((X, True, lrT[X][0:17, t * 128:(t + 1) * 128], gkt[:, t, :], gvt[:, t, :],
                              [t_lrT[X], t_gkt, t_gvt, t_gkT, t_gqT]),
                             dict(qT=gqT[:, :, t * 128:(t + 1) * 128], kT=gkT[:, :, t * 128:(t + 1) * 128],
                                  o_dst=osl, o_add=(X == "A"), o_tok=t_ag[g])))
            gla_run(args)

        for g in (3, 2, 1, 0):
            gla_group("B", g)
        for g in (0, 1, 2, 3):
            gla_group("A", g, reuse=(g == 0))
        self.dump("osum", AG[:, :, 1, :, :], [128, 4, 8, 512], BF16, t_ag)
        if "stop_gla" in self.dbg:
            return self.finish_stub(t_mod, modT)

        P.barrier()
        self.reset_region("S", "R2", "R3")
        self.set_rot([0, 1, 2, 3, 4, 5, 6, 7])
        self.use("R3")
        wgo = A([8, 1024], BF16, "wgo"); t_wgo = Tok("wgo"); c_wgo = P.chan()
        for kc in range(8):
            self.dma("pool", wgo[:, kc, :], w_in_v[:, kc, C_GO:C_GO + 1024], c_wgo, [], [t_wgo])
        self.use("R2")
        wga = A([8, 1024], BF16, "wga"); t_wga = Tok("wga"); c_wga = P.chan()
        wba = A([8, 1024], BF16, "wba"); t_wba = Tok("wba"); c_wba = P.chan()
        w_bra_v = w_bra.rearrange("(k p) n -> p k n", p=128)
        w_brg_v = w_brg.rearrange("(k p) n -> p k n", p=128)
        w_out_v = w_out.rearrange("(k p) n -> p k n", p=128)
        for kc in range(8):
            self.dma("pool", wga[:, kc, :], w_in_v[:, kc, C_GA:C_GA + 1024], c_wga, [], [t_wga])
            self.dma("pool", wba[:, kc, :], w_bra_v[:, kc, :], c_wba, [], [t_wba])
        self.use("R3")
        gxs = [A([4, 1024], BF16, "gx%d" % i) for i in range(2)]
        t_gxs = [Tok("gx%d" % i) for i in range(2)]
        identb = A([128], BF16, "identb"); t_identb = Tok("identb")
        self.cp("dve", identb, ident, [t_ident], [t_identb])
        sgs = [A([1024], F32, "sg%d" % i) for i in range(2)]
        t_sgs = [Tok("sg%d" % i) for i in range(2)]
        ssgs = [A([8], F32, "ssg%d" % i) for i in range(2)]
        t_ssgs = [Tok("ssg%d" % i) for i in range(2)]
        o2j = A([256], BF16, "o2j"); t_o2j = Tok("o2j")
        self.acnt = 0

        def a_elem(blk, t):
            gx, t_gx = gxs[blk % 2], t_gxs[blk % 2]
            sg, t_sg = sgs[self.acnt % 2], t_sgs[self.acnt % 2]
            ssg, t_ssg = ssgs[self.acnt % 2], t_ssgs[self.acnt % 2]
            self.acnt += 1
            osl = AG[:, blk, 1, :, :].rearrange("p a b -> p (a b)")[:, t * 1024:(t + 1) * 1024]
            for hf in range(2):
                b = self.bank()
                for kc in range(8):
                    self.mm(ps[b], hxT[:, blk, kc, t * 128:(t + 1) * 128],
                            wgo[:, kc, hf * 512:(hf + 1) * 512], kc == 0, kc == 7,
                            [t_hx[blk], t_wgo], [pt[b]])
                sgh = sg[:, hf * 512:(hf + 1) * 512]
                self.act(sgh, ps[b], AF.Silu, [pt[b]], [t_sg])
                sgh3 = sgh.rearrange("p (h e) -> p h e", h=2)
                self.tt("dve", sgh3, sgh3, glan.unsqueeze(1).broadcast_to([128, 2, 256]), ALU.mult,
                        [t_sg, t_glan], [t_sg])
            for h in range(4):
                self.act(o2j, osl[:, h * 256:(h + 1) * 256], AF.Square,
                         [t_ag[blk]], [t_o2j, t_ssg], accum=ssg[:, h:h + 1])
            self.ts("pool", ssg[:, 0:4], ssg[:, 0:4], float(1.0 / 256), float(EPS), ALU.mult, ALU.add,
                    [t_ssg], [t_ssg])
            self.tt("pool", ssg[:, 4:8], ssg[:, 0:4], nhalf.broadcast_to([128, 4]), ALU.pow,
                    [t_ssg, t_nhalf], [t_ssg])
            for h in range(4):
                self.stt("dve", gx[:, t, h * 256:(h + 1) * 256], osl[:, h * 256:(h + 1) * 256],
                         ssg[:, 4 + h:5 + h], sg[:, h * 256:(h + 1) * 256], ALU.mult, ALU.mult,
                         [t_ag[blk], t_ssg, t_sg], [t_gx])

        def a_tr(blk, t):
            gx, t_gx = gxs[blk % 2], t_gxs[blk % 2]
            b = self.bank()
            psb = ps[b].bitcast(BF16)
            for kc in range(8):
                self.tr(psb[:, kc * 128:(kc + 1) * 128], gx[:, t, kc * 128:(kc + 1) * 128], identb,
                        [t_gx, t_identb], [pt[b]])
            dstv = AG[:, blk, 1, :, t * 128:(t + 1) * 128]
            self.cp("act" if t % 2 == 0 else "dve", dstv,
                    psb.rearrange("p (q t) -> p q t", q=8), [pt[b]], [t_ag[blk]])

        for t in range(4):
            a_elem(0, t)
        for blk in range(4):
            for t in range(4):
                if blk + 1 < 4:
                    a_elem(blk + 1, t)
                a_tr(blk, t)
        self.dump("glaT", AG[:, :, 1, :, :], [128, 4, 8, 512], BF16, t_ag)
        self.chk("a")
        P.barrier()
        self.reset_region("S", "R3")
        self.use("R3")
        yT = A([8, 512], BF16, "yT"); t_yT = Tok("yT")
        sgts = [A([512], F32, "sgt%d" % i) for i in range(2)]
        t_sgts = [Tok("sgt%d" % i) for i in range(2)]
        self.sgc = 0
        wo = A([8, 1024], BF16, "wo"); t_wo = Tok("wo"); c_wo = P.chan()
        for kc in range(8):
            self.dma("pool", wo[:, kc, :], w_out_v[:, kc, :], c_wo, [], [t_wo])
        for kc in range(8):
            self.tt("dve", wo[:, kc, :], wo[:, kc, :], gt1bc, ALU.mult, [t_wo, t_gt1], [t_wo])

        def gated_proj(blk, wgate, t_wgate, wbr, t_wbr, src_half, fc, dst, dst_toks, add_src=None,
                       add_toks=()):
            bg = proj_fm(wgate, t_wgate, fc * 128, 128, lambda kc: hxT[:, blk, kc, :], 512, [t_hx[blk]])
            sgt, t_sgt = sgts[self.sgc % 2], t_sgts[self.sgc % 2]
            self.sgc += 1
            self.act(sgt, ps[bg], AF.Sigmoid, [pt[bg]], [t_sgt])
            bp = proj_fm(wbr, t_wbr, fc * 128, 128, lambda kc: AG[:, blk, src_half, kc, :], 512,
                         [t_ag[blk]])
            if add_src is None:
                self.tt("dve", dst, ps[bp], sgt, ALU.mult, [pt[bp], t_sgt], dst_toks)
            else:
                self.tt("dve", sgt, ps[bp], sgt, ALU.mult, [pt[bp], t_sgt], [t_sgt])
                self.tt("dve", dst, sgt, add_src, ALU.add, [t_sgt] + list(add_toks), dst_toks)

        for blk in range(4):
            for fc in range(8):
                gated_proj(blk, wga, t_wga, wba, t_wba, 0, fc, yT[:, fc, :], [t_yT])
            self.cp("dve", AG[:, blk, 0, :, :], yT, [t_yT], [t_ag[blk]])
        self.dump("y1T", AG[:, :, 0, :, :], [128, 4, 8, 512], BF16, t_ag)
        self.chk("b1")
        P.barrier()
        self.reset_region("S", "R2")
        self.use("R2")
        wgg = A([8, 1024], BF16, "wgg"); t_wgg = Tok("wgg"); c_wgg = P.chan()
        wbg = A([8, 1024], BF16, "wbg"); t_wbg = Tok("wbg"); c_wbg = P.chan()
        self.use("R3")
        for kc in range(8):
            self.dma("pool", wgg[:, kc, :], w_in_v[:, kc, C_GG:C_GG + 1024], c_wgg, [], [t_wgg])
            self.dma("pool", wbg[:, kc, :], w_brg_v[:, kc, :], c_wbg, [], [t_wbg])
        xt = [A([1024], F32, "xtb%d" % i) for i in range(2)]
        t_xt = [Tok("xtb%d" % i) for i in range(2)]
        c_xt = [P.chan() for _ in range(2)]
        xn = A([1024], F32, "xn2"); t_xn = Tok("xn2")
        sqj = A([1024], BF16, "sqj2"); t_sqj = Tok("sqj2")
        x2v = [AG[:, blk].rearrange("p a b c -> p (a b c)").bitcast(F32).rearrange(
            "p (t d) -> p t d", t=4) for blk in range(4)]
        xcount = 0

        def gp_gen(blk):
            for fc in range(8):
                gated_proj(blk, wgg, t_wgg, wbg, t_wbg, 1, fc, yT[:, fc, :], [t_yT],
                           add_src=AG[:, blk, 0, fc, :], add_toks=[t_ag[blk]])
                yield fc

        cur = gp_gen(0)
        for blk in range(4):
            for _ in cur:
                pass
            for t in range(4):
                xi = xcount % 2
                xcount += 1
                r0 = blk * 512 + t * 128
                self.dma("sp", xt[xi], xs[r0:r0 + 128, :], c_xt[xi], [], [t_xt[xi]])
                for hf in range(2):
                    b = self.bank()
                    for kc in range(8):
                        self.mm(ps[b], yT[:, kc, t * 128:(t + 1) * 128], wo[:, kc, hf * 512:(hf + 1) * 512],
                                kc == 0, kc == 7, [t_yT, t_wo], [pt[b]])
                    xh = xt[xi][:, hf * 512:(hf + 1) * 512]
                    self.tt("dve", x2v[blk][:, t, hf * 512:(hf + 1) * 512], ps[b], xh, ALU.add,
                            [pt[b], t_xt[xi]], [t_ag[blk]])
            cur = gp_gen(blk + 1) if blk + 1 < 4 else iter(())
            for t in range(4):
                norm_transpose(x2v[blk][:, t, :], 0, a2, lambda kc: modT[:, 24 + kc, 0:1],
                               lambda kc, t=t, blk=blk: hxT[:, blk, kc, t * 128:(t + 1) * 128],
                               [t_hx[blk]], from_sbuf_tok=t_ag[blk])
                next(cur, None)
                next(cur, None)
        self.dump("x2", AG.rearrange("p a b c d -> p (a b c d)").bitcast(F32), [128, 16384], F32, t_ag)
        self.dump("hmT", hxT, [128, 4, 8, 512], BF16, t_hx)
        self.chk("b2")

        P.barrier()
        self.reset_region("S", "R2", "R3")
        self.use("R2")
        w1q = [A([8, 1024], BF16, "w1q%d" % i) for i in range(2)]
        self.use("R3")
        w2q = [A([8, 1024], BF16, "w2q%d" % i) for i in range(2)]
        t_w1q = [Tok("w1q%d" % i) for i in range(2)]
        t_w2q = [Tok("w2q%d" % i) for i in range(2)]
        c_w1q = [P.chan() for _ in range(2)]
        c_w2q = [P.chan() for _ in range(2)]
        h1 = A([8, 512], BF16, "h1"); t_h1 = Tok("h1")
        rl = [A([512], F32, "rl%d" % i) for i in range(2)]
        t_rl = [Tok("rl%d" % i) for i in range(2)]
        self.use("S")
        tmp = A([512], F32, "mtmp"); t_tmp = Tok("mtmp")
        ost = [A([1024], F32, "ost%d" % i) for i in range(2)]
        t_ost = [Tok("ost%d" % i) for i in range(2)]
        c_ost = [P.chan() for _ in range(2)]
        self.final_chans.extend(c_ost)
        w_m1_v = w_m1.rearrange("(k p) n -> p k n", p=128)
        w_m2_v = w_m2.rearrange("(k p) n -> p k n", p=128)
        ocount = 0
        rcount = 0
        for q in range(4):
            wi = q % 2
            for kc in range(8):
                self.dma("pool", w1q[wi][:, kc, :], w_m1_v[:, kc, q * 1024:(q + 1) * 1024], c_w1q[wi],
                         [], [t_w1q[wi]])
            for kc in range(8):
                self.dma("pool", w2q[wi][:, kc, :], w_m2_v[:, q * 8 + kc, :], c_w2q[wi], [], [t_w2q[wi]])
            for blk in range(4):
                for fc in range(8):
                    b = proj_fm(w1q[wi], t_w1q[wi], fc * 128, 128, lambda kc: hxT[:, blk, kc, :], 512,
                                [t_hx[blk]])
                    ri = rcount % 2
                    rcount += 1
                    self.act(rl[ri], ps[b], AF.Relu, [pt[b]], [t_rl[ri]])
                    self.tt("dve", h1[:, fc, :], rl[ri], rl[ri], ALU.mult, [t_rl[ri]], [t_h1])
                for t in range(4):
                    for hf in range(2):
                        b = self.bank()
                        for kc in range(8):
                            self.mm(ps[b], h1[:, kc, t * 128:(t + 1) * 128],
                                    w2q[wi][:, kc, hf * 512:(hf + 1) * 512], kc == 0, kc == 7,
                                    [t_h1, t_w2q[wi]], [pt[b]])
                        self.tt("dve", tmp, ps[b], gt2bc[:, hf * 512:(hf + 1) * 512], ALU.mult,
                                [pt[b], t_gt2], [t_tmp])
                        x2h = x2v[blk][:, t, hf * 512:(hf + 1) * 512]
                        if q < 3:
                            self.tt("dve", x2h, tmp, x2h, ALU.add, [t_tmp, t_ag[blk]], [t_ag[blk]])
                        else:
                            oi = ocount % 2
                            self.tt("dve", ost[oi][:, hf * 512:(hf + 1) * 512], tmp, x2h, ALU.add,
                                    [t_tmp, t_ag[blk]], [t_ost[oi]])
                    if q == 3:
                        oi = ocount % 2
                        ocount += 1
                        r0 = blk * 512 + t * 128
                        self.dma("sp", out_d[r0:r0 + 128, :], ost[oi], c_ost[oi], [t_ost[oi]], [])
        self.finalize()
        return self.nc

    def finish_stub(self, tok, ap):
        self.reset_region("R3")
        z = self.alloc([1024], F32, "zstub", region="R3")
        tz = Tok("z")
        self.memset("dve", z, 0.0, [tz])
        c = self.P.chan()
        self.final_chans.append(c)
        for i in range(16):
            self.dma("sp", self.out_d[i * 128:(i + 1) * 128, :], z, c, [tz], [])
        self.finalize()
        return self.nc

    def finalize(self):
        self.P.emit(self.nc, self.stack, self.final_chans)
        self.stack.close()


def _perm_half_swap(nheads):
    idx = []
    for h in range(nheads):
        for d in range(128):
            axis, rem = divmod(d, 64)
            half, f = divmod(rem, 32)
            idx.append(h * 128 + axis * 64 + (1 - half) * 32 + f)
    return np.array(idx)


def _consts(h):
    ident = np.eye(128, dtype=np.float32)
    r = np.arange(128)[:, None]
    c = np.arange(128)[None, :]
    s = np.float32(-1.0 / 16.0)
    tri = np.zeros((128, 4, 128), np.float32)
    tri[:, 0, :] = (r <= c) * s
    tri[:, 1, :] = (r > c) * s
    tri[:, 2, :] = (r >= c) * s
    tri[:, 3, :] = (r < c) * s
    msk = np.zeros((128, 2, 128), np.float32)
    msk[:, 0, :] = (r <= c)
    msk[:, 1, :] = (r >= c)
    half = 64
    freqs = (10000.0 ** (-np.arange(0, half, 2, dtype=np.float32) / half)).astype(np.float32)
    ropec = np.zeros((128, 2, 72), np.float32)
    for d in range(128):
        axis, rem = divmod(d, 64)
        hf, f = divmod(rem, 32)
        for i in range(64):
            pos = i if h == 0 else 63 - i
            ang = np.float32(pos) * freqs[f]
            ropec[d, 0, i] = np.cos(ang)
            ropec[d, 1, i] = (-np.sin(ang)) if hf == 0 else np.sin(ang)
        ropec[d, 0, 64:72] = 1.0
        ropec[d, 1, 64:72] = 0.0
    return ident, tri, msk, ropec


_NC_CACHE = {}


def _get_nc(dbg=()):
    key = tuple(sorted(dbg))
    if key not in _NC_CACHE:
        b = Builder(dbg)
        nc = b.build()
        _NC_CACHE[key] = (nc, b.dbg_out)
    return _NC_CACHE[key]


def make_in_maps(x, c, ctx, c_ctx, w_ada, b_ada, norm1, w_in, q_norm, k_norm, w_gk_fwd, b_gk_fwd,
                 w_gk_bwd, b_gk_bwd, gla_norm, w_br_attn, w_br_gla, w_out, norm2, w_mlp1, w_mlp2):
    f = lambda a: np.ascontiguousarray(np.asarray(a, dtype=np.float32))
    x = f(x); c = f(c); ctx = f(ctx); c_ctx = f(c_ctx)
    w_in0 = f(w_in)[0]
    off = np.cumsum([0, 256, 256, 512, 1024, 16, 16, 1024, 512, 1024, 1024, 1024])
    ak, av, gk, gv, lrf, lrb, aq, gq, go, ga, gg = [w_in0[:, off[i]:off[i + 1]] for i in range(11)]
    pk = _perm_half_swap(2)
    pq = _perm_half_swap(8)
    pd = _perm_half_swap(1)
    qn = f(q_norm)[0]; kn = f(k_norm)[0]
    qkg = np.ascontiguousarray(np.stack([qn, qn[pd], kn, kn[pd]], axis=1))
    win = {}
    wgk = {}
    for h in (0, 1):
        lra, lrbb = (lrf, lrb) if h == 0 else (lrb, lrf)
        win[h] = np.ascontiguousarray(np.concatenate(
            [ak, ak[:, pk], av, gk, gv, lra, lrbb, gq, aq, aq[:, pq], go, ga, gg], axis=1))
        assert win[h].shape[1] == NIN
        wf = np.concatenate([f(w_gk_fwd)[0], f(b_gk_fwd)[0][None, :]], axis=0)
        wb = np.concatenate([f(w_gk_bwd)[0], f(b_gk_bwd)[0][None, :]], axis=0)
        wgk[h] = np.ascontiguousarray(np.stack([wf, wb] if h == 0 else [wb, wf], axis=0))
    consts = {h: _consts(h) for h in (0, 1)}
    shared = dict(w_ada=f(w_ada)[0], b_ada=f(b_ada)[0], norm1=f(norm1)[0], norm2=f(norm2)[0],
                  gla_norm=f(gla_norm)[0], w_br_attn=f(w_br_attn)[0], w_br_gla=f(w_br_gla)[0],
                  w_out=f(w_out)[0], w_mlp1=f(w_mlp1)[0], w_mlp2=f(w_mlp2)[0], qkg=qkg)
    in_maps = []
    for core in range(8):
        b, h = divmod(core, 2)
        xb = x[b] if h == 0 else x[b][::-1]
        cb = ctx[b] if h == 0 else ctx[b][::-1]
        ident, tri, msk, ropec = consts[h]
        m = dict(shared)
        m.update(xs=np.ascontiguousarray(xb), ctx=np.ascontiguousarray(cb),
                 cvec=np.ascontiguousarray(np.stack([c[b], c_ctx], axis=0)),
                 w_in=win[h], wgk=wgk[h], ident=ident, tri=tri, msk=msk, ropec=ropec)
        in_maps.append(m)
    return in_maps


def assemble(results):
    out = np.empty((4, SEQ, D), np.float32)
    for core in range(8):
        b, h = divmod(core, 2)
        o = np.asarray(results[core]["out"], dtype=np.float32)
        if h == 0:
            out[b, 0:OWN] = o
        else:
            out[b, OWN:SEQ] = o[::-1]
    return out


def kernel(**inputs):
    nc, _ = _get_nc(())
    in_maps = make_in_maps(**inputs)
    res = run_bass_kernel_spmd(nc, in_maps, core_ids=list(range(8)))
    return assemble(res.results)
```

```python
import numpy as np
from contextlib import ExitStack
import concourse.bass as bass
import concourse.mybir as mybir
from concourse.bass_utils import run_bass_kernel_spmd

F32 = mybir.dt.float32
BF16 = mybir.dt.bfloat16
AF = mybir.ActivationFunctionType
ALU = mybir.AluOpType

D = 1024
SEQ = 4096
OWN = 2048
CTXL = 256
NKEY = SEQ + CTXL
EPS = 1e-6
C_AK, C_AKP, C_AV, C_GK, C_GV, C_LRA, C_LRB, C_GQ, C_AQ, C_AQP, C_GO, C_GA, C_GG = (
    0, 256, 512, 768, 1280, 2304, 2320, 2336, 2848, 3872, 4896, 5920, 6944)
NIN = 7968
SAME_WIN = 3


class Tok:
    __slots__ = ("name", "w", "r", "excl")

    def __init__(self, name, excl=False):
        self.name = name
        self.w = None
        self.r = []
        self.excl = excl


class Chan:
    def __init__(self, idx):
        self.idx = idx
        self.count = 0
        self.sem = None


class Op:
    __slots__ = ("fn", "deps", "signal", "chan")

    def __init__(self, fn, deps, chan):
        self.fn = fn
        self.deps = deps
        self.signal = False
        self.chan = chan


ENGS = ("pe", "act", "dve", "pool", "sp")


class Prog:
    def __init__(self):
        self.ops = {e: [] for e in ENGS}
        self.chans = []
        self.wm = {e: {} for e in ENGS}

    def chan(self):
        c = Chan(len(self.chans))
        self.chans.append(c)
        return c

    def add(self, eng, fn, reads=(), writes=(), chan=None):
        idx = len(self.ops[eng])
        deps = []
        for t in reads:
            if t.w is not None:
                deps.append(t.w)
            if t.excl:
                deps.extend(t.r)
        for t in writes:
            if t.w is not None:
                deps.append(t.w)
            deps.extend(t.r)
        need = []
        wm = self.wm[eng]
        best = {}
        for d in deps:
            if d[0] == "e":
                _, e2, i2 = d
                if e2 == eng:
                    if chan is not None:
                        pass
                    elif eng in ("pe", "sp"):
                        continue
                    elif idx - i2 > SAME_WIN:
                        continue
                key = ("e", e2)
                val = i2
            else:
                _, c, v = d
                if chan is not None and c is chan:
                    continue
                key = ("c", c.idx)
                val = v
            if wm.get(key, -1) >= val:
                continue
            if key not in best or best[key][0] < val:
                best[key] = (val, d)
        for key, (val, d) in best.items():
            wm[key] = val
            need.append(d)
            if d[0] == "e":
                self.ops[d[1]][d[2]].signal = True
        op = Op(fn, need, chan)
        self.ops[eng].append(op)
        if chan is None:
            ref = ("e", eng, idx)
        else:
            chan.count += 16
            ref = ("c", chan, chan.count)
        for t in reads:
            if t.excl:
                t.r = [ref]
            else:
                t.r.append(ref)
        for t in writes:
            t.w = ref
            t.r = []
        return op

    def barrier(self):
        refs = []
        for e in ENGS:
            if self.ops[e]:
                for i in range(len(self.ops[e]) - 1, -1, -1):
                    if self.ops[e][i].chan is None and self.ops[e][i].fn is not None:
                        refs.append(("e", e, i))
                        break
        for c in self.chans:
            if c.count:
                refs.append(("c", c, c.count))
        bt = Tok("barrier")
        for e in ENGS:
            need = []
            wm = self.wm[e]
            for d in refs:
                if d[0] == "e":
                    if d[1] == e:
                        continue
                    key = ("e", d[1]); val = d[2]
                else:
                    key = ("c", d[1].idx); val = d[2]
                if wm.get(key, -1) >= val:
                    continue
                wm[key] = val
                need.append(d)
                if d[0] == "e":
                    self.ops[d[1]][d[2]].signal = True
            self.ops[e].append(Op(None, need, None))

    def emit(self, nc, stack, final_chans):
        sems = {e: stack.enter_context(nc.semaphore("s_" + e)) for e in ENGS}
        for c in self.chans:
            c.sem = stack.enter_context(nc.semaphore("c%d" % c.idx))
        pref = {}
        for e in ENGS:
            cnt = 0
            p = []
            for op in self.ops[e]:
                if op.signal:
                    cnt += 1
                p.append(cnt)
            pref[e] = p
        block = stack.enter_context(nc.Block())
        handles = {"pe": block.tensor, "act": block.scalar, "dve": block.vector,
                   "pool": block.gpsimd, "sp": block.sync}

        def make(e):
            def body(eng):
                for op in self.ops[e]:
                    for d in op.deps:
                        if d[0] == "e":
                            eng.wait_ge(sems[d[1]], pref[d[1]][d[2]])
                        else:
                            eng.wait_ge(d[1].sem, d[2])
                    if op.fn is None:
                        continue
                    ins = op.fn(eng)
                    if op.signal:
                        ins.then_inc(sems[e], 1)
                    if op.chan is not None:
                        ins.then_inc(op.chan.sem, 16)
                if e == "sp":
                    for c in final_chans:
                        if c.count:
                            eng.wait_ge(c.sem, c.count)
            return body

        for e in ENGS:
            handles[e](make(e))


class StopBuild(Exception):
    pass


class Builder:
    def __init__(self, dbg=()):
        self.dbg = set(dbg)
        self.nc = bass.Bass("TRN2", target_bir_lowering=False)
        self.P = Prog()
        self.stack = ExitStack()
        self.dram = {}
        self.dbg_out = []

    def din(self, name, shape, dt=F32):
        ap = self.nc.dram_tensor(name, list(shape), dt, kind="ExternalInput").ap()
        self.dram[name] = ap
        return ap

    def init_arena(self):
        nc = self.nc
        self.ARENA_BYTES = 207 * 1024
        self.arena = self.stack.enter_context(
            nc.sbuf_tensor("arena", [128, self.ARENA_BYTES // 2], BF16))
        self.regions = {}
        self.def_region("P", 0, self.ARENA_BYTES)
        self.cur_region = "P"
        self.psum = [self.stack.enter_context(nc.psum_tensor("ps%d" % i, [128, 512], F32))[:]
                     for i in range(8)]
        self.pstok = [Tok("ps%d" % i, excl=True) for i in range(8)]
        self.rot = list(range(8))
        self.rot_i = 0

    def alloc(self, shape, dt, name="", parts=128, region=None):
        esz = 4 if dt == F32 else 2
        n = int(np.prod(shape))
        nbytes = (n * esz + 63) // 64 * 64
        if region is None:
            region = self.cur_region
        r = self.regions[region]
        off = r[1]
        r[1] += nbytes
        assert r[1] <= r[2], ("SBUF overflow", region, name, r[1] - r[2])
        v = self.arena[0:parts, off // 2: off // 2 + n * esz // 2]
        if dt == F32:
            v = v.bitcast(F32)
        if len(shape) == 2:
            v = v.rearrange("p (a b) -> p a b", a=shape[0])
        elif len(shape) == 3:
            v = v.rearrange("p (a b c) -> p a b c", a=shape[0], b=shape[1])
        elif len(shape) == 4:
            v = v.rearrange("p (a b c d) -> p a b c d", a=shape[0], b=shape[1], c=shape[2])
        return v

    def def_region(self, name, start, end):
        self.regions[name] = [start, start, end]

    def reset_region(self, *names):
        for n in names:
            self.regions[n][1] = self.regions[n][0]

    def use(self, name):
        self.cur_region = name

    def set_rot(self, banks):
        self.rot = list(banks)
        self.rot_i = 0

    def bank(self):
        b = self.rot[self.rot_i % len(self.rot)]
        self.rot_i += 1
        return b

    def mm(self, out, lhsT, rhs, start, stop, reads, writes, tile_position=None):
        if tile_position is not None:
            return self.P.add("pe", lambda e: e.matmul(out, lhsT, rhs, start=start, stop=stop,
                                                       tile_position=tile_position), reads, writes)
        return self.P.add("pe", lambda e: e.matmul(out, lhsT, rhs, start=start, stop=stop),
                          reads, writes)

    def tr(self, out, in_, ident, reads, writes):
        return self.P.add("pe", lambda e: e.transpose(out, in_, ident), reads, writes)

    def act(self, out, in_, func, reads, writes, bias=None, scale=None, accum=None):
        kw = {}
        if bias is not None:
            kw["bias"] = bias
        if scale is not None:
            kw["scale"] = scale
        if accum is not None:
            kw["accum_out"] = accum
        return self.P.add("act", lambda e: e.activation(out, in_, func, **kw), reads, writes)

    def tt(self, eng, out, in0, in1, op, reads, writes):
        return self.P.add(eng, lambda e: e.tensor_tensor(out, in0, in1, op), reads, writes)

    def ts(self, eng, out, in0, s1, s2, op0, op1, reads, writes):
        if op1 is None:
            return self.P.add(eng, lambda e: e.tensor_scalar(out, in0, s1, None, op0),
                              reads, writes)
        return self.P.add(eng, lambda e: e.tensor_scalar(out, in0, s1, s2, op0, op1),
                          reads, writes)

    def stt(self, eng, out, in0, scalar, in1, op0, op1, reads, writes):
        return self.P.add(eng, lambda e: e.scalar_tensor_tensor(out, in0, scalar, in1, op0, op1),
                          reads, writes)

    def cp(self, eng, out, in_, reads, writes):
        if eng == "act":
            return self.P.add("act", lambda e: e.copy(out, in_), reads, writes)
        return self.P.add(eng, lambda e: e.tensor_copy(out, in_), reads, writes)

    def recip(self, out, in_, reads, writes):
        return self.P.add("dve", lambda e: e.reciprocal(out, in_), reads, writes)

    def memset(self, eng, ap, val, writes):
        return self.P.add(eng, lambda e: e.memset(ap, val), (), writes)

    def dma(self, q, out, in_, chan, reads, writes, slow=False):
        if slow:
            return self.P.add(q, lambda e: e.dma_start(out=out, in_=in_,
                                                       allow_slow_non_contiguous=True),
                              reads, writes, chan=chan)
        return self.P.add(q, lambda e: e.dma_start(out=out, in_=in_), reads, writes, chan=chan)

    def dump(self, name, ap, shape, dt, tok):
        if name not in self.dbg:
            return
        o = self.nc.dram_tensor("dbg_" + name, list(shape), dt, kind="ExternalOutput").ap()
        c = self.P.chan()
        self.final_chans.append(c)
        self.dma("sp", o, ap, c, [tok] if not isinstance(tok, (list, tuple)) else list(tok), [])
        self.dbg_out.append("dbg_" + name)

    def build(self):
        try:
            return self._build_body()
        except StopBuild:
            return self.finish_stub(None, None)

    def chk(self, label):
        if ("stop_" + label) in self.dbg:
            raise StopBuild()

    def _build_body(self):
        nc = self.nc
        P = self.P
        din = self.din
        self.final_chans = []
        xs = din("xs", [SEQ, D])
        ctx = din("ctx", [CTXL, D])
        cvec = din("cvec", [2, D])
        w_ada = din("w_ada", [D, 6 * D])
        b_ada = din("b_ada", [6 * D])
        norm1 = din("norm1", [D])
        norm2 = din("norm2", [D])
        w_in = din("w_in", [D, NIN])
        wgk = din("wgk", [2, 17, 512])
        qkg = din("qkg", [128, 4])
        gla_norm = din("gla_norm", [256])
        w_bra = din("w_br_attn", [D, D])
        w_brg = din("w_br_gla", [D, D])
        w_out = din("w_out", [D, D])
        w_m1 = din("w_mlp1", [D, 4 * D])
        w_m2 = din("w_mlp2", [4 * D, D])
        ident_d = din("ident", [128, 128])
        tri_d = din("tri", [128, 4, 128])
        msk_d = din("msk", [128, 2, 128])
        ropec_d = din("ropec", [128, 2, 72])
        out_d = nc.dram_tensor("out", [OWN, D], F32, kind="ExternalOutput").ap()
        self.out_d = out_d

        self.init_arena()
        A = self.alloc
        ps = self.psum
        pt = self.pstok

        ident = A([128], F32, "ident"); t_ident = Tok("ident")
        onesf = A([128], F32, "onesf"); t_onesf = Tok("onesf")
        onesb = A([128], BF16, "onesb"); t_onesb = Tok("onesb")
        tri = A([4, 128], BF16, "tri"); t_tri = Tok("tri")
        msk = A([2, 128], F32, "msk"); t_msk = Tok("msk")
        ropec = A([2, 72], F32, "ropec"); t_ropec = Tok("ropec")
        qkgs = A([4], F32, "qkg"); t_qkg = Tok("qkg")
        GT = A([2, 2, 72], F32, "GT"); t_GT = Tok("GT")
        scT = A([8, 2], BF16, "scT"); t_scT = Tok("scT")
        modT = A([48, 2], F32, "modT"); t_mod = Tok("mod")
        a1 = A([8], F32, "a1"); ac = A([8], F32, "ac"); a2 = A([8], F32, "a2"); t_av = Tok("avec"); t_av2 = Tok("avec2")
        gt1bc = A([1024], F32, "gt1bc"); t_gt1 = Tok("gt1bc")
        gt2bc = A([1024], F32, "gt2bc"); t_gt2 = Tok("gt2bc")
        glan = A([256], F32, "glan"); t_glan = Tok("glan")
        wgka = A([2, 512], BF16, "wgka", parts=32); t_wgka = Tok("wgka")
        nhalf = A([1], F32, "nhalf"); t_nhalf = Tok("nhalf")
        tiny = A([16], F32, "tiny"); t_tiny = Tok("tiny")

        self.dma("pool", tri, tri_d, P.chan(), [], [t_tri])
        for (dst, src, tk) in ((ident, ident_d, t_ident), (msk, msk_d, t_msk),
                               (ropec, ropec_d, t_ropec), (qkgs, qkg, t_qkg)):
            self.dma("sp", dst, src, P.chan(), [], [tk])
        vst = A([128], F32, "vst", parts=128); t_vst = Tok("vst")
        vT = A([80], F32, "vT"); t_vT = Tok("vT")
        self.memset("dve", vst, 0.0, [t_vst])
        c_v = P.chan()
        self.dma("sp", vst[0:48], b_ada.rearrange("(k p) -> k p", p=128), c_v, [], [t_vst])
        self.dma("sp", vst[48:56], norm1.rearrange("(k p) -> k p", p=128), c_v, [], [t_vst])
        self.dma("sp", vst[56:64], norm2.rearrange("(k p) -> k p", p=128), c_v, [], [t_vst])
        self.dma("sp", vst[64:80], cvec.rearrange("r (k p) -> (r k) p", p=128), c_v, [], [t_vst])
        self.dma("sp", glan, gla_norm.partition_broadcast(128), P.chan(), [], [t_glan])
        c_wgk = P.chan()
        self.dma("pool", wgka[0:17], wgk.rearrange("x r n -> r x n"), c_wgk, [], [t_wgka])
        self.memset("dve", onesf, 1.0, [t_onesf])
        self.memset("dve", onesb, 1.0, [t_onesb])
        self.memset("dve", nhalf, -0.5, [t_nhalf])
        self.tr(ps[0][:, 0:128], vst, ident, [t_vst, t_ident], [pt[0]])
        self.cp("dve", vT, ps[0][:, 0:80], [pt[0]], [t_vT])
        badaT = vT[:, 0:48]; n1T = vT[:, 48:56]; n2T = vT[:, 56:64]
        cT = vT[:, 64:80].rearrange("p (r k) -> p k r", r=2)
        t_bada = t_vT; t_nT = t_vT; t_cT = t_vT

        hxT = A([4, 8, 512], BF16, "hxT_own")
        t_hx = [Tok("hxT%d" % j) for j in range(4)]
        t_ag = [Tok("AG%d" % j) for j in range(4)]

        SA = A([4, 256], F32, "SA"); SB = A([4, 256], F32, "SB")
        SAb = A([4, 256], BF16, "SAb"); SBb = A([4, 256], BF16, "SBb")
        t_S = {"A": Tok("SA"), "B": Tok("SB")}
        t_Sb = {"A": Tok("SAb"), "B": Tok("SBb")}
        Sf = {"A": SA, "B": SB}
        Sb = {"A": SAb, "B": SBb}
        pend = self.regions["P"][1]
        s_start = pend - 12288
        self.def_region("S", s_start, pend)
        self.def_region("R1", pend, pend + 65536)
        self.def_region("R2", pend + 65536, pend + 65536 + 34816)
        self.def_region("R3", pend + 65536 + 34816, self.ARENA_BYTES)
        self.use("R2")
        KT = A([2, NKEY], BF16, "KT")
        V = A([34, 256], BF16, "V")
        t_kv = [Tok("kv%d" % g) for g in range(9)]

        self.use("R1")
        ws1 = A([8, 2336], BF16, "ws1"); t_ws1 = Tok("ws1"); c_ws1 = P.chan()
        xt = [A([1024], F32, "xt%d" % i) for i in range(2)]
        t_xt = [Tok("xt%d" % i) for i in range(2)]
        c_xt = [P.chan() for _ in range(2)]
        wada = [A([8, 512], BF16, "wada%d" % i) for i in range(2)]
        t_wada = [Tok("wada%d" % i) for i in range(2)]
        c_wada = [P.chan() for _ in range(2)]
        Tkc = A([2, 256], F32, "Tkc"); t_Tkc = Tok("Tkc")
        self.use("R3")
        xn = A([1024], F32, "xn"); t_xn = Tok("xn")
        hxg = A([8, 512], BF16, "hxg"); t_hxg = Tok("hxg")
        gkt = A([4, 512], BF16, "gkt"); t_gkt = Tok("gkt")
        gvt = A([4, 1024], BF16, "gvt"); t_gvt = Tok("gvt")
        lrT = {"A": A([512], BF16, "lrTA", parts=32), "B": A([512], BF16, "lrTB", parts=32)}
        t_lrT = {"A": Tok("lrTA"), "B": Tok("lrTB")}
        Tk = A([2, 512], F32, "Tk"); t_Tk = Tok("Tk")
        r_sq = A([512], BF16, "r_sq"); t_rsq = Tok("r_sq")
        r_rs = A([512], F32, "r_rs"); t_rrs = Tok("r_rs")
        r_t1 = A([512], F32, "r_t1"); t_rt1 = Tok("r_t1")
        r_t2 = A([512], F32, "r_t2"); t_rt2 = Tok("r_t2")
        def make_gset(i, full):
            G = {"sp": (A([512], BF16, "g_sp%d" % i), Tok("g_sp")),
                 "EC": (A([512], F32, "g_EC%d" % i), Tok("g_EC")),
                 "kh": (A([512], BF16, "g_kh%d" % i), Tok("g_kh")),
                 "EL": (A([4], F32, "g_EL%d" % i), Tok("g_EL"))}
            if full:
                G["E1"] = (A([512], F32, "g_E1%d" % i), Tok("g_E1"))
                G["E2"] = (A([512], F32, "g_E2%d" % i), Tok("g_E2"))
                G["qt"] = (A([4, 128], BF16, "g_qt%d" % i), Tok("g_qt"))
                G["kt"] = (A([4, 128], BF16, "g_kt%d" % i), Tok("g_kt"))
                G["AT"] = (A([4, 128], BF16, "g_AT%d" % i), Tok("g_AT"))
            return G
        gsets = [make_gset(0, False),
                 {"sp": (r_rs.bitcast(BF16)[:, 0:512], t_rrs), "EC": (r_t1, t_rt1), "kh": (r_sq, t_rsq),
                  "EL": (A([4], F32, "g_EL1"), Tok("g_EL1"))}]
        self.gcnt = 0
        sqj = r_t2.bitcast(BF16)
        t_sqj = t_rt2
        diag = A([128], F32, "diag"); t_diag = Tok("diag")

        for X in ("A", "B"):
            self.memset("dve", lrT[X], 1.0, [t_lrT[X]])
            self.memset("dve", Sf[X], 0.0, [t_S[X]])
            self.memset("dve", Sb[X], 0.0, [t_Sb[X]])

        w_in_v = w_in.rearrange("(k p) n -> p k n", p=128)

        self.set_rot([0, 1, 2, 3, 4, 5, 6, 7])
        ec = tiny
        ecv = tiny.rearrange("p (a b) -> p a b", a=8)
        self.act(ecv, cT, AF.Exp, [t_cT], [t_tiny], scale=-1.0)
        self.ts("dve", ecv, ecv, 1.0, None, ALU.add, None, [t_tiny], [t_tiny])
        self.recip(ecv, ecv, [t_tiny], [t_tiny])
        self.tt("dve", scT, ecv, cT, ALU.mult, [t_tiny, t_cT], [t_scT])

        w_ada_v = w_ada.rearrange("(k p) n -> p k n", p=128)
        self.set_rot([0, 1, 2, 3, 4, 5, 6])
        bmod = 7
        modps = ps[bmod][:, 0:96].rearrange("p (a b) -> p a b", a=48)

        def ada_dma(cb):
            for kc in range(8):
                self.dma("pool", wada[cb % 2][:, kc, :], w_ada_v[:, kc, cb * 512:(cb + 1) * 512],
                         c_wada[cb % 2], [], [t_wada[cb % 2]])

        def ada_mm(cb):
            wb = wada[cb % 2]
            for fc in range(4):
                j = cb * 4 + fc
                for kc in range(8):
                    self.mm(modps[:, j, :], wb[:, kc, fc * 128:(fc + 1) * 128], scT[:, kc, :],
                            kc == 0, kc == 7, [t_wada[cb % 2], t_scT], [pt[bmod]])

        ada_dma(0)
        ada_dma(1)
        for kc in range(8):
            self.dma("pool", ws1[:, kc, :], w_in_v[:, kc, 0:2336], c_ws1, [], [t_ws1])
        for cb in range(4):
            ada_mm(cb)
            ada_dma(cb + 2)
        self.tt("dve", modT[:, 0:16, :], modps[:, 0:16, :],
                badaT[:, 0:16].unsqueeze(2).broadcast_to([128, 16, 2]), ALU.add,
                [pt[bmod], t_bada], [t_mod])
        self.ts("dve", a1, modT[:, 8:16, 0], 1.0, 32.0, ALU.add, ALU.mult, [t_mod], [t_av])
        self.tt("dve", a1, a1, n1T, ALU.mult, [t_av, t_nT], [t_av])
        self.ts("dve", ac, modT[:, 8:16, 1], 1.0, 32.0, ALU.add, ALU.mult, [t_mod], [t_av])
        self.tt("dve", ac, ac, n1T, ALU.mult, [t_av, t_nT], [t_av])

        def ada_part2():
            for cb in range(4, 12):
                ada_mm(cb)
                if cb + 2 < 12:
                    ada_dma(cb + 2)
            self.tt("dve", modT[:, 16:48, :], modps[:, 16:48, :],
                    badaT[:, 16:48].unsqueeze(2).broadcast_to([128, 32, 2]), ALU.add,
                    [pt[bmod], t_bada], [t_mod])
            self.ts("dve", a2, modT[:, 32:40, 0], 1.0, 32.0, ALU.add, ALU.mult, [t_mod], [t_av2])
            self.tt("dve", a2, a2, n2T, ALU.mult, [t_av2, t_nT], [t_av2])
            for (dst, tk, j0) in ((gt1bc, t_gt1, 16), (gt2bc, t_gt2, 40)):
                for half in range(2):
                    b = self.bank()
                    for q in range(4):
                        kc = half * 4 + q
                        self.ts("dve", diag, ident, modT[:, j0 + kc, 0:1], None, ALU.mult, None,
                                [t_ident, t_mod], [t_diag])
                        self.mm(ps[b][:, q * 128:(q + 1) * 128], onesf, diag, True, True,
                                [t_onesf, t_diag], [pt[b]])
                    self.cp("dve", dst[:, half * 512:(half + 1) * 512], ps[b], [pt[b]], [tk])
        SQ128 = float(np.sqrt(128.0))
        self.ts("dve", GT[:, 0, 0, :], ropec[:, 0, :], qkgs[:, 0:1], None, ALU.mult, None,
                [t_ropec, t_qkg], [t_GT])
        self.ts("dve", GT[:, 0, 1, :], ropec[:, 1, :], qkgs[:, 1:2], None, ALU.mult, None,
                [t_ropec, t_qkg], [t_GT])
        self.ts("dve", GT[:, 1, 0, :], ropec[:, 0, :], qkgs[:, 2:3], SQ128, ALU.mult, ALU.mult,
                [t_ropec, t_qkg], [t_GT])
        self.ts("dve", GT[:, 1, 1, :], ropec[:, 1, :], qkgs[:, 3:4], SQ128, ALU.mult, ALU.mult,
                [t_ropec, t_qkg], [t_GT])
        Tk4 = Tk.rearrange("p c (r w) -> p c r w", r=8)
        self.cp("dve", Tk4[64:128], GT[64:128, 1, :, 0:64].unsqueeze(2).broadcast_to([64, 2, 8, 64]),
                [t_GT], [t_Tk])
        Tkc4 = Tkc.rearrange("p c (r w) -> p c r w", r=4)
        self.cp("dve", Tkc4, GT[:, 1, :, 64:68].unsqueeze(3).broadcast_to([128, 2, 4, 64]),
                [t_GT], [t_Tkc])

        if "stop_setup" in self.dbg:
            return self.finish_stub(t_mod, modT)
        def norm_transpose(src_ap, xt_i, avec, shcol, dst_fn, dst_toks, from_sbuf_tok=None):
            xin = src_ap
            rt = [t_xt[xt_i]] if from_sbuf_tok is None else [from_sbuf_tok]
            ssc = tiny[:, 0:1]
            self.act(sqj, xin, AF.Square, rt, [t_sqj, t_tiny], accum=ssc)
            self.ts("pool", tiny[:, 1:2], ssc, float(D * EPS), None, ALU.add, None, [t_tiny], [t_tiny])
            self.tt("pool", tiny[:, 2:3], tiny[:, 1:2], nhalf, ALU.pow, [t_tiny, t_nhalf], [t_tiny])
            self.act(xn, xin, AF.Copy, rt + [t_tiny], [t_xn], scale=tiny[:, 2:3])
            for half in range(2):
                b = self.bank()
                for q in range(4):
                    kc = half * 4 + q
                    self.tr(ps[b][:, q * 128:(q + 1) * 128], xn[:, kc * 128:(kc + 1) * 128], ident,
                            [t_xn, t_ident], [pt[b]])
                for q in range(4):
                    kc = half * 4 + q
                    dst = dst_fn(kc)
                    if half == 0:
                        self.act(dst, ps[b][:, q * 128:(q + 1) * 128], AF.Identity,
                                 [pt[b], t_av, t_av2, t_mod], dst_toks,
                                 scale=avec[:, kc:kc + 1], bias=shcol(kc))
                    else:
                        self.ts("dve", dst, ps[b][:, q * 128:(q + 1) * 128], avec[:, kc:kc + 1],
                                shcol(kc), ALU.mult, ALU.add, [pt[b], t_av, t_av2, t_mod], dst_toks)

        def rope_norm(b0, b1, n, Tc, Ts, t_tab, dst, dst_toks, eps_scaled):
            self.act(r_sq[:, 0:n], ps[b0][:, 0:n], AF.Square, [pt[b0]], [t_rsq])
            bs = self.bank()
            self.mm(ps[bs][:, 0:n], onesb, r_sq[:, 0:n], True, True, [t_onesb, t_rsq], [pt[bs]])
            self.chk("r1")
            import os
            EXP = os.environ.get("EXP", "")
            if EXP == "copyfirst":
                self.cp("dve", r_t1[:, 0:n], ps[b0][:, 0:n], [pt[b0]], [t_rt1])
            elif EXP == "serial":
                self.cp("dve", r_t1[:, 0:n], ps[b0][:, 0:n], [pt[b0], t_rsq], [t_rt1])
                self.chk("r1b")
                self.chk("r1b")
                self.tt("dve", r_t1[:, 0:n], r_t1[:, 0:n], Tc, ALU.mult, [t_rt1, t_tab], [t_rt1])
                self.chk("r1c")
            else:
                self.tt("dve", r_t1[:, 0:n], ps[b0][:, 0:n], Tc, ALU.mult, [pt[b0], t_tab], [t_rt1])
            self.chk("r1a")
            self.tt("dve", r_t2[:, 0:n], ps[b1][:, 0:n], Ts, ALU.mult, [pt[b1], t_tab], [t_rt2])
            self.chk("r2")
            self.act(r_rs[:, 0:n], ps[bs][:, 0:n], AF.Ln, [pt[bs]], [t_rrs], bias=eps_scaled)
            self.chk("r3")
            self.act(r_rs[:, 0:n], r_rs[:, 0:n], AF.Exp, [t_rrs], [t_rrs], scale=-0.5)
            self.chk("r4")
            self.tt("dve", r_t1[:, 0:n], r_t1[:, 0:n], r_t2[:, 0:n], ALU.add, [t_rt1, t_rt2], [t_rt1])
            self.tt("dve", dst, r_t1[:, 0:n], r_rs[:, 0:n], ALU.mult, [t_rt1, t_rrs], dst_toks)

        def proj_fm(w, t_w, c0, ncols_chunk, rhs_fn, n, rhs_toks):
            b = self.bank()
            for kc in range(8):
                self.mm(ps[b][0:ncols_chunk, 0:n], w[:, kc, c0:c0 + ncols_chunk], rhs_fn(kc),
                        kc == 0, kc == 7, [t_w] + rhs_toks, [pt[b]])
            return b

        def gla_stage1(X, full, lr_ap, gk_tile, gv_tile, t_in, qT=None, kT=None, o_dst=None,
                       o_add=False, o_tok=None):
            G = gsets[self.gcnt % len(gsets)]
            self.gcnt += 1
            g_sp, t_gsp = G["sp"]; g_EC, t_gEC = G["EC"]; g_kh, t_gkh = G["kh"]; g_EL, t_gEL = G["EL"]
            xi = 0 if X == "A" else 1
            cum = tri[:, 2 * xi, :]
            cmat = tri[:, 2 * xi + 1, :]
            last = 127 if X == "A" else 0
            bz = self.bank()
            self.mm(ps[bz], lr_ap, wgka[0:17, xi, :], True, True, t_in + [t_wgka], [pt[bz]])
            self.act(g_sp, ps[bz], AF.Exp, [pt[bz]], [t_gsp], scale=-1.0)
            self.act(g_sp, g_sp, AF.Ln, [t_gsp], [t_gsp], bias=1.0)
            bc = self.bank()
            self.mm(ps[bc], cmat, g_sp, True, True, [t_tri, t_gsp], [pt[bc]])
            if full:
                bb = self.bank()
                for h in range(4):
                    self.mm(ps[bb][:, h * 128:(h + 1) * 128], g_sp[:, h * 128:(h + 1) * 128], cum,
                            True, True, [t_gsp, t_tri], [pt[bb]])
            else:
                bl = self.bank()
                for h in range(4):
                    self.mm(ps[bl][:, h:h + 1], g_sp[:, h * 128:(h + 1) * 128],
                            cum[:, last:last + 1], True, True, [t_gsp, t_tri], [pt[bl]])
            self.act(g_EC, ps[bc], AF.Exp, [pt[bc]], [t_gEC])
            self.tt("dve", g_kh, gk_tile, g_EC, ALU.mult, t_in + [t_gEC], [t_gkh])
            st = dict(X=X, full=full, G=G, gv_tile=gv_tile, t_in=t_in, o_dst=o_dst, o_add=o_add,
                      o_tok=o_tok, xi=xi)
            if full:
                g_E1, t_gE1 = G["E1"]; g_E2, t_gE2 = G["E2"]; g_qt, t_gqtl = G["qt"]
                g_kt, t_gktl = G["kt"]
                self.act(g_E1, ps[bb], AF.Exp, [pt[bb]], [t_gE1])
                self.act(g_E2, ps[bb], AF.Exp, [pt[bb]], [t_gE2], scale=-1.0)
                E1v = g_E1.rearrange("p (h t) -> p h t", h=4)
                E2v = g_E2.rearrange("p (h t) -> p h t", h=4)
                self.tt("dve", g_qt, qT, E1v, ALU.mult, t_in + [t_gE1], [t_gqtl])
                self.tt("dve", g_kt, kT, E2v, ALU.mult, t_in + [t_gE2], [t_gktl])
                st["el"] = lambda h: g_E1[:, h * 128 + last: h * 128 + last + 1]
                st["el_tok"] = t_gE1
            else:
                self.act(g_EL, ps[bl][:, 0:4], AF.Exp, [pt[bl]], [t_gEL])
                st["el"] = lambda h: g_EL[:, h:h + 1]
                st["el_tok"] = t_gEL
            return st

        def gla_stage2(st):
            X = st["X"]; G = st["G"]; gv_tile = st["gv_tile"]; t_in = st["t_in"]; xi = st["xi"]
            g_kh, t_gkh = G["kh"]
            if st["full"]:
                g_qt, t_gqtl = G["qt"]; g_kt, t_gktl = G["kt"]; g_AT, t_gAT = G["AT"]
                ba = self.bank()
                for h in range(4):
                    self.mm(ps[ba][:, h * 128:(h + 1) * 128], g_kt[:, h, :], g_qt[:, h, :],
                            True, True, [t_gktl, t_gqtl], [pt[ba]])
                self.tt("dve", g_AT, ps[ba].rearrange("p (h t) -> p h t", h=4),
                        msk[:, xi, :].unsqueeze(1).broadcast_to([128, 4, 128]), ALU.mult,
                        [pt[ba], t_msk], [t_gAT])
                for hp in range(2):
                    bo = self.bank()
                    for hh in range(2):
                        h = hp * 2 + hh
                        self.mm(ps[bo][:, hh * 256:(hh + 1) * 256], g_qt[:, h, :], Sb[X][:, h, :],
                                True, False, [t_gqtl, t_Sb[X]], [pt[bo]])
                        self.mm(ps[bo][:, hh * 256:(hh + 1) * 256], g_AT[:, h, :],
                                gv_tile[:, h * 256:(h + 1) * 256], False, True,
                                [t_gAT] + t_in, [pt[bo]])
                    od = st["o_dst"][:, hp * 512:(hp + 1) * 512]
                    if st["o_add"]:
                        self.tt("dve", od, ps[bo], od, ALU.add, [pt[bo], st["o_tok"]], [st["o_tok"]])
                    else:
                        self.cp("act", od, ps[bo], [pt[bo]], [st["o_tok"]])
            el = st["el"]; el_tok = st["el_tok"]
            for hp in range(2):
                bu = self.bank()
                for hh in range(2):
                    h = hp * 2 + hh
                    self.mm(ps[bu][:, hh * 256:(hh + 1) * 256], g_kh[:, h * 128:(h + 1) * 128],
                            gv_tile[:, h * 256:(h + 1) * 256], True, True, [t_gkh] + t_in, [pt[bu]])
                for hh in range(2):
                    h = hp * 2 + hh
                    self.stt("dve", Sf[X][:, h, :], Sf[X][:, h, :], el(h),
                             ps[bu][:, hh * 256:(hh + 1) * 256], ALU.mult, ALU.add,
                             [t_S[X], el_tok, pt[bu]], [t_S[X]])
            self.cp("act", Sb[X], Sf[X], [t_S[X]], [t_Sb[X]])

        def gla_run(step_args, inter=None):
            prev = None
            for a in step_args:
                cur = gla_stage1(*a[0], **a[1])
                if inter is not None:
                    next(inter, None)
                if prev is not None:
                    gla_stage2(prev)
                prev = cur
            if prev is not None:
                gla_stage2(prev)
            if inter is not None:
                for _ in inter:
                    pass

        wf0 = wada[0].rearrange("p a b -> p (a b)")
        wf1 = wada[1].rearrange("p a b -> p (a b)")
        bsets = [
            dict(gkt=gkt, t_gkt=t_gkt, gvt=gvt, t_gvt=t_gvt, lrT=lrT, t_lrT=t_lrT),
            dict(gkt=wf0[:, 0:2048].rearrange("p (t n) -> p t n", t=4), t_gkt=t_wada[0],
                 gvt=wf1.rearrange("p (t n) -> p t n", t=4), t_gvt=t_wada[1],
                 lrT={"A": wf0[0:32, 2048:2560], "B": wf0[0:32, 2560:3072]},
                 t_lrT={"A": t_wada[0], "B": t_wada[0]}),
        ]
        self.xcount = 0

        def front_tiles(kind, g):
            ntile = 2 if kind == "ctx" else 4
            avec = ac if kind == "ctx" else a1
            rcol = 1 if kind == "ctx" else 0
            shcol = lambda kc, rcol=rcol: modT[:, kc, rcol:rcol + 1]
            src = ctx if kind == "ctx" else xs
            for t in range(ntile):
                xi = self.xcount % 2
                self.xcount += 1
                r0 = t * 128 if kind == "ctx" else g * 512 + t * 128
                self.dma("sp", xt[xi], src[r0:r0 + 128, :], c_xt[xi], [], [t_xt[xi]])
                if kind == "own":
                    dst_fn = lambda kc, t=t, g=g: hxT[:, g, kc, t * 128:(t + 1) * 128]
                    dtoks = [t_hx[g]]
                else:
                    dst_fn = lambda kc, t=t: hxg[:, kc, t * 128:(t + 1) * 128]
                    dtoks = [t_hxg]
                norm_transpose(xt[xi], xi, avec, shcol, dst_fn, dtoks)
                yield t

        def front_proj(kind, g, bs):
            ntile = 2 if kind == "ctx" else 4
            n = ntile * 128
            keyoff = SEQ if kind == "ctx" else g * 512
            if kind == "own":
                rhs_fn = lambda kc, g=g: hxT[:, g, kc, :]
                rtoks = [t_hx[g]]
                lhs_fn = lambda kc, t, g=g: hxT[:, g, kc, t * 128:(t + 1) * 128]
            else:
                rhs_fn = lambda kc, n=n: hxg[:, kc, 0:n]
                rtoks = [t_hxg]
                lhs_fn = lambda kc, t: hxg[:, kc, t * 128:(t + 1) * 128]
            if kind == "ctx":
                Tc, Ts, t_tab = Tkc[:, 0, :], Tkc[:, 1, :], t_Tkc
            else:
                self.cp("dve", Tk4[0:64],
                        GT[0:64, 1, :, g * 8:(g + 1) * 8].unsqueeze(3).broadcast_to([64, 2, 8, 64]),
                        [t_GT], [t_Tk])
                Tc, Ts, t_tab = Tk[:, 0, :], Tk[:, 1, :], t_Tk
            gi = 8 if kind == "ctx" else g
            for kvh in range(2):
                b0 = proj_fm(ws1, t_ws1, C_AK + kvh * 128, 128, rhs_fn, n, rtoks)
                b1 = proj_fm(ws1, t_ws1, C_AKP + kvh * 128, 128, rhs_fn, n, rtoks)
                rope_norm(b0, b1, n, Tc, Ts, t_tab, KT[:, kvh, keyoff:keyoff + n], [t_kv[gi]],
                          float(128 * EPS))
            for t in range(ntile):
                b = self.bank()
                for kc in range(8):
                    self.mm(ps[b][:, 0:256], lhs_fn(kc, t), ws1[:, kc, C_AV:C_AV + 256],
                            kc == 0, kc == 7, rtoks + [t_ws1], [pt[b]])
                self.cp("act", V[:, keyoff // 128 + t, :], ps[b][:, 0:256], [pt[b]], [t_kv[gi]])
            if kind == "own":
                return
            for t in range(ntile):
                b = self.bank()
                for kc in range(8):
                    self.mm(ps[b], lhs_fn(kc, t), ws1[:, kc, C_GK:C_GK + 512],
                            kc == 0, kc == 7, rtoks + [t_ws1], [pt[b]])
                self.cp("dve", bs["gkt"][:, t, :], ps[b], [pt[b]], [bs["t_gkt"]])
                for hf in range(2):
                    b = self.bank()
                    for kc in range(8):
                        self.mm(ps[b], lhs_fn(kc, t),
                                ws1[:, kc, C_GV + hf * 512:C_GV + (hf + 1) * 512],
                                kc == 0, kc == 7, rtoks + [t_ws1], [pt[b]])
                    self.cp("act", bs["gvt"][:, t, hf * 512:(hf + 1) * 512], ps[b], [pt[b]],
                            [bs["t_gvt"]])
            dirs = ("A", "B") if kind == "ctx" else ("B",)
            for X in dirs:
                c0 = C_LRA if X == "A" else C_LRB
                b = proj_fm(ws1, t_ws1, c0, 16, rhs_fn, n, rtoks)
                self.cp("dve", bs["lrT"][X][0:16, 0:n], ps[b][0:16, 0:n], [pt[b]], [bs["t_lrT"][X]])

        def steps(kind, g, bs, inter=None):
            ntile = 2 if kind == "ctx" else 4
            dirs = ("A", "B") if kind == "ctx" else ("B",)
            args = []
            for X in dirs:
                order = range(ntile) if X == "A" else range(ntile - 1, -1, -1)
                for t in order:
                    args.append(((X, False, bs["lrT"][X][0:17, t * 128:(t + 1) * 128], bs["gkt"][:, t, :],
                                  bs["gvt"][:, t, :], [bs["t_lrT"][X], bs["t_gkt"], bs["t_gvt"]]), {}))
            gla_run(args, inter)

        def front(kind, g, bs):
            for _ in front_tiles(kind, g):
                pass
            front_proj(kind, g, bs)

        front("ctx", 8, bsets[0])
        ada_part2()
        for X in ("A", "B"):
            self.memset("dve", bsets[1]["lrT"][X], 1.0, [t_wada[0]])
        front("oth", 7, bsets[1])
        steps("ctx", 8, bsets[0], front_tiles("oth", 6))
        front_proj("oth", 6, bsets[0])
        steps("oth", 7, bsets[1], front_tiles("oth", 5))
        front_proj("oth", 5, bsets[1])
        steps("oth", 6, bsets[0], front_tiles("oth", 4))
        front_proj("oth", 4, bsets[0])
        steps("oth", 5, bsets[1], front_tiles("own", 0))
        front_proj("own", 0, None)
        steps("oth", 4, bsets[0], front_tiles("own", 1))
        front_proj("own", 1, None)
        for g in (2, 3):
            front("own", g, None)
        self.dump("modT", modT, [128, 48, 2], F32, t_mod)
        self.dump("gt1bc", gt1bc, [128, 1024], F32, t_gt1)
        self.dump("KT", KT, [128, 2, NKEY], BF16, t_kv)
        self.dump("V", V, [128, 34, 256], BF16, t_kv)
        self.dump("hxT", hxT, [128, 4, 8, 512], BF16, t_hx)
        self.dump("SA", SA, [128, 4, 256], F32, t_S["A"])
        self.dump("SB", SB, [128, 4, 256], F32, t_S["B"])

        if "stop_s1" in self.dbg:
            return self.finish_stub(t_mod, modT)

        P.barrier()
        self.reset_region("R1", "R3")
        self.use("R1")
        AG = A([4, 2, 8, 512], BF16, "AG")
        self.use("R3")
        wq = A([8, 1024], BF16, "wq"); t_wq = Tok("wq"); c_wq = P.chan()
        Qbs = [A([4, 512], BF16, "Qb%d" % i) for i in range(2)]
        t_Qbs = [Tok("Qb%d" % i) for i in range(2)]
        Tq = A([2, 512], F32, "Tq"); t_Tq = Tok("Tq")
        r_sq = A([512], BF16, "r_sq2"); t_rsq = Tok("r_sq2")
        r_rs = A([512], F32, "r_rs2"); t_rrs = Tok("r_rs2")
        r_t1 = A([512], F32, "r_t12"); t_rt1 = Tok("r_t12")
        r_t2 = A([512], F32, "r_t22"); t_rt2 = Tok("r_t22")
        PTN = 4
        PT = [A([512], BF16, "PT%d" % i) for i in range(PTN)]
        t_PT = [Tok("PT%d" % i) for i in range(PTN)]
        sst = r_t2; t_sst = t_rt2
        ones32 = A([128], F32, "ones32"); t_ones32 = Tok("ones32")
        self.memset("dve", ones32, 1.0 / 32.0, [t_ones32])
        rec = A([512], F32, "rec"); t_rec = Tok("rec")
        Tq4 = Tq.rearrange("p c (r w) -> p c r w", r=8)
        self.cp("dve", Tq4[64:128], GT[64:128, 0, :, 0:64].unsqueeze(2).broadcast_to([64, 2, 8, 64]),
                [t_GT], [t_Tq])
        self.set_rot([0, 1, 2, 3])
        iters = [(hh, blk) for hh in range(2) for blk in range(4)]
        self.wq_loaded = -1

        def emit_q(it):
            hh, blk = iters[it]
            if self.wq_loaded != hh:
                self.wq_loaded = hh
                for kc in range(8):
                    self.dma("pool", wq[:, kc, 0:512],
                             w_in_v[:, kc, C_AQ + hh * 512:C_AQ + (hh + 1) * 512], c_wq, [], [t_wq])
                    self.dma("pool", wq[:, kc, 512:1024],
                             w_in_v[:, kc, C_AQP + hh * 512:C_AQP + (hh + 1) * 512], c_wq, [], [t_wq])
            self.cp("dve", Tq4[0:64],
                    GT[0:64, 0, :, blk * 8:(blk + 1) * 8].unsqueeze(3).broadcast_to([64, 2, 8, 64]),
                    [t_GT], [t_Tq])
            rhs_fn = lambda kc: hxT[:, blk, kc, :]
            for hl in range(4):
                b0 = proj_fm(wq, t_wq, hl * 128, 128, rhs_fn, 512, [t_hx[blk]])
                b1 = proj_fm(wq, t_wq, 512 + hl * 128, 128, rhs_fn, 512, [t_hx[blk]])
                rope_norm(b0, b1, 512, Tq[:, 0, :], Tq[:, 1, :], t_Tq, Qbs[it % 2][:, hl, :],
                          [t_Qbs[it % 2]], float(128 * EPS))

        self.pcount = 0

        def emit_unit(it, hl):
            hh, blk = iters[it]
            Qb, t_Qb = Qbs[it % 2], t_Qbs[it % 2]
            h = hh * 4 + hl
            kvh = h // 4
            bo = 4 + (self.pcount % 2) * 2
            bsum = bo + 1
            self.pcount += 1

            def smm(kt):
                b = self.bank()
                gi = 8 if kt >= 32 else kt // 4
                self.mm(ps[b], KT[:, kvh, kt * 128:(kt + 1) * 128], Qb[:, hl, :], True, True,
                        [t_kv[gi], t_Qb], [pt[b]])
                return b
            bcur = smm(0)
            bnext = None
            for kt in range(34):
                pi = kt % PTN
                self.act(PT[pi], ps[bcur], AF.Exp, [pt[bcur]], [t_PT[pi]])
                if kt + 1 < 34:
                    bnext = smm(kt + 1)
                gi = 8 if kt >= 32 else kt // 4
                self.mm(ps[bo], V[:, kt, kvh * 128:(kvh + 1) * 128], PT[pi], kt == 0, kt == 33,
                        [t_kv[gi], t_PT[pi]], [pt[bo]])
                self.mm(ps[bsum], onesb, PT[pi], kt == 0, kt == 33,
                        [t_onesb, t_PT[pi]], [pt[bsum]])
                bcur = bnext
            self.recip(rec, ps[bsum], [pt[bsum]], [t_rec])
            self.tt("dve", AG[:, blk, 0, h, :], ps[bo], rec, ALU.mult, [pt[bo], t_rec],
                    [t_ag[blk]])

        emit_q(0)
        for it in range(len(iters)):
            emit_unit(it, 0)
            emit_unit(it, 1)
            if it + 1 < len(iters):
                emit_q(it + 1)
            emit_unit(it, 2)
            emit_unit(it, 3)
        self.dump("attnT", AG[:, :, 0, :, :], [128, 4, 8, 512], BF16, t_ag)
        if "stop_att" in self.dbg:
            return self.finish_stub(t_mod, modT)

        P.barrier()
        self.reset_region("R2", "R3")
        self.set_rot([0, 1, 2, 3, 4, 5, 6, 7])
        self.use("R2")
        wg = A([8, 2080], BF16, "wg"); t_wg = Tok("wg"); c_wg = P.chan()
        self.use("R3")
        WG0 = 768
        psets = []
        for i in range(2):
            psets.append(dict(
                gkT=A([4, 256], BF16, "gkT%d" % i), t_gkT=Tok("gkT%d" % i),
                gqT=A([4, 256], BF16, "gqT%d" % i), t_gqT=Tok("gqT%d" % i),
                gkt=A([2, 512], BF16, "gktp%d" % i), t_gkt=Tok("gktp%d" % i),
                gvt=A([2, 1024], BF16, "gvtp%d" % i), t_gvt=Tok("gvtp%d" % i),
                lrT={"A": A([256], BF16, "lrTAp%d" % i, parts=32), "B": A([256], BF16, "lrTBp%d" % i, parts=32)},
                t_lrT={"A": Tok("lrTAp%d" % i), "B": Tok("lrTBp%d" % i)}))
        gsets = [make_gset(10 + i, True) for i in range(2)]
        for ps_ in psets:
            for X in ("A", "B"):
                self.memset("dve", ps_["lrT"][X], 1.0, [ps_["t_lrT"][X]])
        for kc in range(8):
            self.dma("pool", wg[:, kc, :], w_in_v[:, kc, WG0:WG0 + 2080], c_wg, [], [t_wg])

        def gla_proj(X, g, hh, S_, reuse=False):
            rhs_fn = lambda kc: hxT[:, g, kc, hh * 256:(hh + 1) * 256]
            rtoks = [t_hx[g]]
            if not reuse:
                for h in range(4):
                    b = proj_fm(wg, t_wg, C_GK - WG0 + h * 128, 128, rhs_fn, 256, rtoks)
                    self.cp("act", S_["gkT"][:, h, :], ps[b][:, 0:256], [pt[b]], [S_["t_gkT"]])
                    b = proj_fm(wg, t_wg, C_GQ - WG0 + h * 128, 128, rhs_fn, 256, rtoks)
                    self.ts("dve", S_["gqT"][:, h, :], ps[b][:, 0:256], float(128.0 ** -0.5), None,
                            ALU.mult, None, [pt[b]], [S_["t_gqT"]])
            c0 = (C_LRA if X == "A" else C_LRB) - WG0
            b = proj_fm(wg, t_wg, c0, 16, rhs_fn, 256, rtoks)
            self.cp("dve", S_["lrT"][X][0:16, :], ps[b][0:16, 0:256], [pt[b]], [S_["t_lrT"][X]])
            yield 0
            if not reuse:
                for tl in range(2):
                    t = 2 * hh + tl
                    b = self.bank()
                    for kc in range(8):
                        self.mm(ps[b], hxT[:, g, kc, t * 128:(t + 1) * 128],
                                wg[:, kc, C_GK - WG0:C_GK - WG0 + 512], kc == 0, kc == 7,
                                rtoks + [t_wg], [pt[b]])
                    self.cp("dve", S_["gkt"][:, tl, :], ps[b], [pt[b]], [S_["t_gkt"]])
                    for hf in range(2):
                        b = self.bank()
                        for kc in range(8):
                            self.mm(ps[b], hxT[:, g, kc, t * 128:(t + 1) * 128],
                                    wg[:, kc, C_GV - WG0 + hf * 512:C_GV - WG0 + (hf + 1) * 512],
                                    kc == 0, kc == 7, rtoks + [t_wg], [pt[b]])
                        self.cp("act", S_["gvt"][:, tl, hf * 512:(hf + 1) * 512], ps[b], [pt[b]],
                                [S_["t_gvt"]])
            yield 1

        def gla_args(X, g, hh, S_):
            order = (0, 1) if X == "A" else (1, 0)
            args = []
            for tl in order:
                t = 2 * hh + tl
                osl = AG[:, g, 1, :, :].rearrange("p a b -> p (a b)")[:, t * 1024:(t + 1) * 1024]
                args.append(((X, True, S_["lrT"][X][0:17, tl * 128:(tl + 1) * 128], S_["gkt"][:, tl, :],
                              S_["gvt"][:, tl, :],
                              [S_["t_lrT"][X], S_["t_gkt"], S_["t_gvt"], S_["t_gkT"], S_["t_gqT"]]),
                             dict(qT=S_["gqT"][:, :, tl * 128:(tl + 1) * 128],
                                  kT=S_["gkT"][:, :, tl * 128:(tl + 1) * 128],
                                  o_dst=osl, o_add=(X == "A"), o_tok=t_ag[g])))
            return args

        units = [("B", g, hh) for g in (3, 2, 1, 0) for hh in (1, 0)] + \
                [("A", g, hh) for g in (0, 1, 2, 3) for hh in (0, 1)]
        setidx = []
        for u, (X, g, hh) in enumerate(units):
            if u == 0:
                setidx.append(0)
            elif X == "A" and g == 0 and hh == 0:
                setidx.append(setidx[-1])
            else:
                setidx.append(1 - setidx[-1])
        for _ in gla_proj(*units[0], psets[setidx[0]]):
            pass
        for u, (X, g, hh) in enumerate(units):
            inter = None
            if u + 1 < len(units):
                Xn, gn, hn = units[u + 1]
                reuse = (setidx[u + 1] == setidx[u])
                if reuse:
                    for _ in gla_proj(Xn, gn, hn, psets[setidx[u + 1]], reuse=True):
                        pass
                else:
                    inter = gla_proj(Xn, gn, hn, psets[setidx[u + 1]])
            gla_run(gla_args(X, g, hh, psets[setidx[u]]), inter)
        self.dump("osum", AG[:, :, 1, :, :], [128, 4, 8, 512], BF16, t_ag)
        if "stop_gla" in self.dbg:
            return self.finish_stub(t_mod, modT)

        P.barrier()
        self.reset_region("S", "R2", "R3")
        self.set_rot([0, 1, 2, 3, 4, 5, 6, 7])
        self.use("R3")
        wgo = A([8, 1024], BF16, "wgo"); t_wgo = Tok("wgo"); c_wgo = P.chan()
        for kc in range(8):
            self.dma("pool", wgo[:, kc, :], w_in_v[:, kc, C_GO:C_GO + 1024], c_wgo, [], [t_wgo])
        self.use("R2")
        wga = A([8, 1024], BF16, "wga"); t_wga = Tok("wga"); c_wga = P.chan()
        wba = A([8, 1024], BF16, "wba"); t_wba = Tok("wba"); c_wba = P.chan()
        w_bra_v = w_bra.rearrange("(k p) n -> p k n", p=128)
        w_brg_v = w_brg.rearrange("(k p) n -> p k n", p=128)
        w_out_v = w_out.rearrange("(k p) n -> p k n", p=128)
        for kc in range(8):
            self.dma("pool", wga[:, kc, :], w_in_v[:, kc, C_GA:C_GA + 1024], c_wga, [], [t_wga])
            self.dma("pool", wba[:, kc, :], w_bra_v[:, kc, :], c_wba, [], [t_wba])
        self.use("R3")
        gxs = [A([4, 1024], BF16, "gx%d" % i) for i in range(2)]
        t_gxs = [Tok("gx%d" % i) for i in range(2)]
        identb = A([128], BF16, "identb"); t_identb = Tok("identb")
        self.cp("dve", identb, ident, [t_ident], [t_identb])
        sgs = [A([1024], F32, "sg%d" % i) for i in range(2)]
        t_sgs = [Tok("sg%d" % i) for i in range(2)]
        ssgs = [A([8], F32, "ssg%d" % i) for i in range(2)]
        t_ssgs = [Tok("ssg%d" % i) for i in range(2)]
        o2j = A([256], BF16, "o2j"); t_o2j = Tok("o2j")
        self.acnt = 0

        def a_elem(blk, t):
            gx, t_gx = gxs[blk % 2], t_gxs[blk % 2]
            sg, t_sg = sgs[self.acnt % 2], t_sgs[self.acnt % 2]
            ssg, t_ssg = ssgs[self.acnt % 2], t_ssgs[self.acnt % 2]
            self.acnt += 1
            osl = AG[:, blk, 1, :, :].rearrange("p a b -> p (a b)")[:, t * 1024:(t + 1) * 1024]
            for hf in range(2):
                b = self.bank()
                for kc in range(8):
                    self.mm(ps[b], hxT[:, blk, kc, t * 128:(t + 1) * 128],
                            wgo[:, kc, hf * 512:(hf + 1) * 512], kc == 0, kc == 7,
                            [t_hx[blk], t_wgo], [pt[b]])
                sgh = sg[:, hf * 512:(hf + 1) * 512]
                self.act(sgh, ps[b], AF.Silu, [pt[b]], [t_sg])
                sgh3 = sgh.rearrange("p (h e) -> p h e", h=2)
                self.tt("dve", sgh3, sgh3, glan.unsqueeze(1).broadcast_to([128, 2, 256]), ALU.mult,
                        [t_sg, t_glan], [t_sg])
            for h in range(4):
                self.act(o2j, osl[:, h * 256:(h + 1) * 256], AF.Square,
                         [t_ag[blk]], [t_o2j, t_ssg], accum=ssg[:, h:h + 1])
            self.ts("pool", ssg[:, 0:4], ssg[:, 0:4], float(1.0 / 256), float(EPS), ALU.mult, ALU.add,
                    [t_ssg], [t_ssg])
            self.tt("pool", ssg[:, 4:8], ssg[:, 0:4], nhalf.broadcast_to([128, 4]), ALU.pow,
                    [t_ssg, t_nhalf], [t_ssg])
            for h in range(4):
                self.stt("dve", gx[:, t, h * 256:(h + 1) * 256], osl[:, h * 256:(h + 1) * 256],
                         ssg[:, 4 + h:5 + h], sg[:, h * 256:(h + 1) * 256], ALU.mult, ALU.mult,
                         [t_ag[blk], t_ssg, t_sg], [t_gx])

        def a_tr(blk, t):
            gx, t_gx = gxs[blk % 2], t_gxs[blk % 2]
            b = self.bank()
            psb = ps[b].bitcast(BF16)
            for kc in range(8):
                self.tr(psb[:, kc * 128:(kc + 1) * 128], gx[:, t, kc * 128:(kc + 1) * 128], identb,
                        [t_gx, t_identb], [pt[b]])
            dstv = AG[:, blk, 1, :, t * 128:(t + 1) * 128]
            self.cp("act" if t % 2 == 0 else "dve", dstv,
                    psb.rearrange("p (q t) -> p q t", q=8), [pt[b]], [t_ag[blk]])

        for t in range(4):
            a_elem(0, t)
        for blk in range(4):
            for t in range(4):
                if blk + 1 < 4:
                    a_elem(blk + 1, t)
                a_tr(blk, t)
        self.dump("glaT", AG[:, :, 1, :, :], [128, 4, 8, 512], BF16, t_ag)
        self.chk("a")
        P.barrier()
        self.reset_region("S", "R3")
        self.use("R3")
        yT = A([8, 512], BF16, "yT"); t_yT = Tok("yT")
        sgts = [A([512], F32, "sgt%d" % i) for i in range(2)]
        t_sgts = [Tok("sgt%d" % i) for i in range(2)]
        self.sgc = 0
        wo = A([8, 1024], BF16, "wo"); t_wo = Tok("wo"); c_wo = P.chan()
        for kc in range(8):
            self.dma("pool", wo[:, kc, :], w_out_v[:, kc, :], c_wo, [], [t_wo])
        for kc in range(8):
            self.tt("dve", wo[:, kc, :], wo[:, kc, :], gt1bc, ALU.mult, [t_wo, t_gt1], [t_wo])

        def gated_proj(blk, wgate, t_wgate, wbr, t_wbr, src_half, fc, dst, dst_toks, add_src=None,
                       add_toks=()):
            bg = proj_fm(wgate, t_wgate, fc * 128, 128, lambda kc: hxT[:, blk, kc, :], 512, [t_hx[blk]])
            sgt, t_sgt = sgts[self.sgc % 2], t_sgts[self.sgc % 2]
            self.sgc += 1
            self.act(sgt, ps[bg], AF.Sigmoid, [pt[bg]], [t_sgt])
            bp = proj_fm(wbr, t_wbr, fc * 128, 128, lambda kc: AG[:, blk, src_half, kc, :], 512,
                         [t_ag[blk]])
            if add_src is None:
                self.tt("dve", dst, ps[bp], sgt, ALU.mult, [pt[bp], t_sgt], dst_toks)
            else:
                self.tt("dve", sgt, ps[bp], sgt, ALU.mult, [pt[bp], t_sgt], [t_sgt])
                self.tt("dve", dst, sgt, add_src, ALU.add, [t_sgt] + list(add_toks), dst_toks)

        for blk in range(4):
            for fc in range(8):
                gated_proj(blk, wga, t_wga, wba, t_wba, 0, fc, yT[:, fc, :], [t_yT])
            self.cp("dve", AG[:, blk, 0, :, :], yT, [t_yT], [t_ag[blk]])
        self.dump("y1T", AG[:, :, 0, :, :], [128, 4, 8, 512], BF16, t_ag)
        self.chk("b1")
        P.barrier()
        self.reset_region("S", "R2")
        self.use("R2")
        wgg = A([8, 1024], BF16, "wgg"); t_wgg = Tok("wgg"); c_wgg = P.chan()
        wbg = A([8, 1024], BF16, "wbg"); t_wbg = Tok("wbg"); c_wbg = P.chan()
        self.use("R3")
        for kc in range(8):
            self.dma("pool", wgg[:, kc, :], w_in_v[:, kc, C_GG:C_GG + 1024], c_wgg, [], [t_wgg])
            self.dma("pool", wbg[:, kc, :], w_brg_v[:, kc, :], c_wbg, [], [t_wbg])
        xt = [A([1024], F32, "xtb%d" % i) for i in range(2)]
        t_xt = [Tok("xtb%d" % i) for i in range(2)]
        c_xt = [P.chan() for _ in range(2)]
        xn = A([1024], F32, "xn2"); t_xn = Tok("xn2")
        sqj = A([1024], BF16, "sqj2"); t_sqj = Tok("sqj2")
        x2v = [AG[:, blk].rearrange("p a b c -> p (a b c)").bitcast(F32).rearrange(
            "p (t d) -> p t d", t=4) for blk in range(4)]
        xcount = 0

        def gp_gen(blk):
            for fc in range(8):
                gated_proj(blk, wgg, t_wgg, wbg, t_wbg, 1, fc, yT[:, fc, :], [t_yT],
                           add_src=AG[:, blk, 0, fc, :], add_toks=[t_ag[blk]])
                yield fc

        cur = gp_gen(0)
        for blk in range(4):
            for _ in cur:
                pass
            for t in range(4):
                xi = xcount % 2
                xcount += 1
                r0 = blk * 512 + t * 128
                self.dma("sp", xt[xi], xs[r0:r0 + 128, :], c_xt[xi], [], [t_xt[xi]])
                for hf in range(2):
                    b = self.bank()
                    for kc in range(8):
                        self.mm(ps[b], yT[:, kc, t * 128:(t + 1) * 128], wo[:, kc, hf * 512:(hf + 1) * 512],
                                kc == 0, kc == 7, [t_yT, t_wo], [pt[b]])
                    xh = xt[xi][:, hf * 512:(hf + 1) * 512]
                    self.tt("dve", x2v[blk][:, t, hf * 512:(hf + 1) * 512], ps[b], xh, ALU.add,
                            [pt[b], t_xt[xi]], [t_ag[blk]])
            cur = gp_gen(blk + 1) if blk + 1 < 4 else iter(())
            for t in range(4):
                norm_transpose(x2v[blk][:, t, :], 0, a2, lambda kc: modT[:, 24 + kc, 0:1],
                               lambda kc, t=t, blk=blk: hxT[:, blk, kc, t * 128:(t + 1) * 128],
                               [t_hx[blk]], from_sbuf_tok=t_ag[blk])
                next(cur, None)
                next(cur, None)
        self.dump("x2", AG.rearrange("p a b c d -> p (a b c d)").bitcast(F32), [128, 16384], F32, t_ag)
        self.dump("hmT", hxT, [128, 4, 8, 512], BF16, t_hx)
        self.chk("b2")

        P.barrier()
        self.reset_region("S", "R2", "R3")
        self.use("R2")
        w1q = [A([8, 1024], BF16, "w1q%d" % i) for i in range(2)]
        self.use("R3")
        w2q = [A([8, 1024], BF16, "w2q%d" % i) for i in range(2)]
        t_w1q = [Tok("w1q%d" % i) for i in range(2)]
        t_w2q = [Tok("w2q%d" % i) for i in range(2)]
        c_w1q = [P.chan() for _ in range(2)]
        c_w2q = [P.chan() for _ in range(2)]
        h1 = A([8, 512], BF16, "h1"); t_h1 = Tok("h1")
        rl = [A([512], F32, "rl%d" % i) for i in range(2)]
        t_rl = [Tok("rl%d" % i) for i in range(2)]
        self.use("S")
        tmp = A([512], F32, "mtmp"); t_tmp = Tok("mtmp")
        ost = [A([1024], F32, "ost%d" % i) for i in range(2)]
        t_ost = [Tok("ost%d" % i) for i in range(2)]
        c_ost = [P.chan() for _ in range(2)]
        self.final_chans.extend(c_ost)
        w_m1_v = w_m1.rearrange("(k p) n -> p k n", p=128)
        w_m2_v = w_m2.rearrange("(k p) n -> p k n", p=128)
        ocount = 0
        rcount = 0
        for q in range(4):
            wi = q % 2
            for kc in range(8):
                self.dma("pool", w1q[wi][:, kc, :], w_m1_v[:, kc, q * 1024:(q + 1) * 1024], c_w1q[wi],
                         [], [t_w1q[wi]])
            for kc in range(8):
                self.dma("pool", w2q[wi][:, kc, :], w_m2_v[:, q * 8 + kc, :], c_w2q[wi], [], [t_w2q[wi]])
            for blk in range(4):
                for fc in range(8):
                    b = proj_fm(w1q[wi], t_w1q[wi], fc * 128, 128, lambda kc: hxT[:, blk, kc, :], 512,
                                [t_hx[blk]])
                    ri = rcount % 2
                    rcount += 1
                    self.act(rl[ri], ps[b], AF.Relu, [pt[b]], [t_rl[ri]])
                    self.tt("dve", h1[:, fc, :], rl[ri], rl[ri], ALU.mult, [t_rl[ri]], [t_h1])
                for t in range(4):
                    for hf in range(2):
                        b = self.bank()
                        for kc in range(8):
                            self.mm(ps[b], h1[:, kc, t * 128:(t + 1) * 128],
                                    w2q[wi][:, kc, hf * 512:(hf + 1) * 512], kc == 0, kc == 7,
                                    [t_h1, t_w2q[wi]], [pt[b]])
                        self.tt("dve", tmp, ps[b], gt2bc[:, hf * 512:(hf + 1) * 512], ALU.mult,
                                [pt[b], t_gt2], [t_tmp])
                        x2h = x2v[blk][:, t, hf * 512:(hf + 1) * 512]
                        if q < 3:
                            self.tt("dve", x2h, tmp, x2h, ALU.add, [t_tmp, t_ag[blk]], [t_ag[blk]])
                        else:
                            oi = ocount % 2
                            self.tt("dve", ost[oi][:, hf * 512:(hf + 1) * 512], tmp, x2h, ALU.add,
                                    [t_tmp, t_ag[blk]], [t_ost[oi]])
                    if q == 3:
                        oi = ocount % 2
                        ocount += 1
                        r0 = blk * 512 + t * 128
                        self.dma("sp", out_d[r0:r0 + 128, :], ost[oi], c_ost[oi], [t_ost[oi]], [])
        self.finalize()
        return self.nc

    def finish_stub(self, tok, ap):
        self.reset_region("R3")
        z = self.alloc([1024], F32, "zstub", region="R3")
        tz = Tok("z")
        self.memset("dve", z, 0.0, [tz])
        c = self.P.chan()
        self.final_chans.append(c)
        for i in range(16):
            self.dma("sp", self.out_d[i * 128:(i + 1) * 128, :], z, c, [tz], [])
        self.finalize()
        return self.nc

    def finalize(self):
        self.P.emit(self.nc, self.stack, self.final_chans)
        self.stack.close()


def _perm_half_swap(nheads):
    idx = []
    for h in range(nheads):
        for d in range(128):
            axis, rem = divmod(d, 64)
            half, f = divmod(rem, 32)
            idx.append(h * 128 + axis * 64 + (1 - half) * 32 + f)
    return np.array(idx)


def _consts(h):
    ident = np.eye(128, dtype=np.float32)
    r = np.arange(128)[:, None]
    c = np.arange(128)[None, :]
    s = np.float32(-1.0 / 16.0)
    tri = np.zeros((128, 4, 128), np.float32)
    tri[:, 0, :] = (r <= c) * s
    tri[:, 1, :] = (r > c) * s
    tri[:, 2, :] = (r >= c) * s
    tri[:, 3, :] = (r < c) * s
    msk = np.zeros((128, 2, 128), np.float32)
    msk[:, 0, :] = (r <= c)
    msk[:, 1, :] = (r >= c)
    half = 64
    freqs = (10000.0 ** (-np.arange(0, half, 2, dtype=np.float32) / half)).astype(np.float32)
    ropec = np.zeros((128, 2, 72), np.float32)
    for d in range(128):
        axis, rem = divmod(d, 64)
        hf, f = divmod(rem, 32)
        for i in range(64):
            pos = i if h == 0 else 63 - i
            ang = np.float32(pos) * freqs[f]
            ropec[d, 0, i] = np.cos(ang)
            ropec[d, 1, i] = (-np.sin(ang)) if hf == 0 else np.sin(ang)
        ropec[d, 0, 64:72] = 1.0
        ropec[d, 1, 64:72] = 0.0
    return ident, tri, msk, ropec


_NC_CACHE = {}


def _get_nc(dbg=()):
    key = tuple(sorted(dbg))
    if key not in _NC_CACHE:
        b = Builder(dbg)
        nc = b.build()
        _NC_CACHE[key] = (nc, b.dbg_out)
    return _NC_CACHE[key]


def make_in_maps(x, c, ctx, c_ctx, w_ada, b_ada, norm1, w_in, q_norm, k_norm, w_gk_fwd, b_gk_fwd,
                 w_gk_bwd, b_gk_bwd, gla_norm, w_br_attn, w_br_gla, w_out, norm2, w_mlp1, w_mlp2):
    f = lambda a: np.ascontiguousarray(np.asarray(a, dtype=np.float32))
    x = f(x); c = f(c); ctx = f(ctx); c_ctx = f(c_ctx)
    w_in0 = f(w_in)[0]
    off = np.cumsum([0, 256, 256, 512, 1024, 16, 16, 1024, 512, 1024, 1024, 1024])
    ak, av, gk, gv, lrf, lrb, aq, gq, go, ga, gg = [w_in0[:, off[i]:off[i + 1]] for i in range(11)]
    pk = _perm_half_swap(2)
    pq = _perm_half_swap(8)
    pd = _perm_half_swap(1)
    qn = f(q_norm)[0]; kn = f(k_norm)[0]
    qkg = np.ascontiguousarray(np.stack([qn, qn[pd], kn, kn[pd]], axis=1))
    win = {}
    wgk = {}
    for h in (0, 1):
        lra, lrbb = (lrf, lrb) if h == 0 else (lrb, lrf)
        win[h] = np.ascontiguousarray(np.concatenate(
            [ak, ak[:, pk], av, gk, gv, lra, lrbb, gq, aq, aq[:, pq], go, ga, gg], axis=1))
        assert win[h].shape[1] == NIN
        wf = np.concatenate([f(w_gk_fwd)[0], f(b_gk_fwd)[0][None, :]], axis=0)
        wb = np.concatenate([f(w_gk_bwd)[0], f(b_gk_bwd)[0][None, :]], axis=0)
        wgk[h] = np.ascontiguousarray(np.stack([wf, wb] if h == 0 else [wb, wf], axis=0))
    consts = {h: _consts(h) for h in (0, 1)}
    shared = dict(w_ada=f(w_ada)[0], b_ada=f(b_ada)[0], norm1=f(norm1)[0], norm2=f(norm2)[0],
                  gla_norm=f(gla_norm)[0], w_br_attn=f(w_br_attn)[0], w_br_gla=f(w_br_gla)[0],
                  w_out=f(w_out)[0], w_mlp1=f(w_mlp1)[0], w_mlp2=f(w_mlp2)[0], qkg=qkg)
    in_maps = []
    for core in range(8):
        b, h = divmod(core, 2)
        xb = x[b] if h == 0 else x[b][::-1]
        cb = ctx[b] if h == 0 else ctx[b][::-1]
        ident, tri, msk, ropec = consts[h]
        m = dict(shared)
        m.update(xs=np.ascontiguousarray(xb), ctx=np.ascontiguousarray(cb),
                 cvec=np.ascontiguousarray(np.stack([c[b], c_ctx], axis=0)),
                 w_in=win[h], wgk=wgk[h], ident=ident, tri=tri, msk=msk, ropec=ropec)
        in_maps.append(m)
    return in_maps


def assemble(results):
    out = np.empty((4, SEQ, D), np.float32)
    for core in range(8):
        b, h = divmod(core, 2)
        o = np.asarray(results[core]["out"], dtype=np.float32)
        if h == 0:
            out[b, 0:OWN] = o
        else:
            out[b, OWN:SEQ] = o[::-1]
    return out


def kernel(**inputs):
    nc, _ = _get_nc(())
    in_maps = make_in_maps(**inputs)
    res = run_bass_kernel_spmd(nc, in_maps, core_ids=list(range(8)))
    return assemble(res.results)
```
